# Optimizing a Trainium2 kernel written in Bass

```python
import math
import jax, jax.numpy as jnp
from jax import lax
import numpy as np

D_MODEL = 1024
BATCH = 16
SEQ = 2048
DEPTH = 4

GRID_W = 64
CTX_LEN = 256

HG_HEADS = 8
HG_DK = 128
HG_DV = 128
HG_WIDTH = HG_HEADS * HG_DK
HG_CHUNK = 32

AT_HEADS = 8
AT_KV_HEADS = 2
AT_HEAD_DIM = 128
AT_GROUP = AT_HEADS // AT_KV_HEADS
AT_WIDTH = AT_HEADS * AT_HEAD_DIM
AT_KV_WIDTH = AT_KV_HEADS * AT_HEAD_DIM
AT_BLOCK = 128
ROPE_THETA = 10000.0

MB_INNER = 2 * D_MODEL
MB_HEAD_DIM = 64
MB_HEADS = MB_INNER // MB_HEAD_DIM
MB_STATE = 128
MB_GROUPS = 4
MB_HEADS_PER_GROUP = MB_HEADS // MB_GROUPS
MB_CONV = 5
MB_CHUNK = 128
MB_BC_WIDTH = MB_GROUPS * MB_STATE
MB_CONV_DIM = MB_INNER + 2 * MB_BC_WIDTH

FFN_HIDDEN = ((8 * D_MODEL + 3 * 256 - 1) // (3 * 256)) * 256

IN_SIZES = (HG_WIDTH,) * 5 + (AT_WIDTH, AT_KV_WIDTH, AT_KV_WIDTH) + (MB_INNER, MB_CONV_DIM, 2 * MB_HEADS) + (3 * D_MODEL,)
IN_WIDTH = sum(IN_SIZES)
IN_SPLITS = tuple(int(s) for s in np.cumsum(IN_SIZES)[:-1])

DN_ALPHA = (2 * DEPTH) ** 0.25
DN_BETA = (8 * DEPTH) ** -0.25
LN_EPS = 1e-5
RMS_EPS = 1e-6
F32 = jnp.float32

kernel_name = "hybrid_hgrn2_gqa_ssd_dit_prefix"


def layer_norm(x, g, b):
    xf = x.astype(F32)
    mu = jnp.mean(xf, axis=-1, keepdims=True)
    var = jnp.mean(jnp.square(xf - mu), axis=-1, keepdims=True)
    return ((xf - mu) * lax.rsqrt(var + LN_EPS) * g + b).astype(x.dtype)


def rms_norm(x, w):
    xf = x.astype(F32)
    return (xf * lax.rsqrt(jnp.mean(xf * xf, axis=-1, keepdims=True) + RMS_EPS) * w).astype(x.dtype)


def flip(a):
    return jnp.flip(a, axis=1)


def to_chunks(a, size):
    n, t = a.shape[:2]
    return jnp.moveaxis(a.reshape(n, t // size, size, *a.shape[2:]), 1, 0)


def from_chunks(a):
    a = jnp.moveaxis(a, 0, 1)
    return a.reshape(a.shape[0], -1, *a.shape[3:])


def axial_rope(n_tokens):
    rows = n_tokens // GRID_W
    row, col = jnp.meshgrid(jnp.arange(rows, dtype=F32), jnp.arange(GRID_W, dtype=F32), indexing="ij")
    n_pairs = AT_HEAD_DIM // 4
    inv_freq = ROPE_THETA ** (-jnp.arange(n_pairs, dtype=F32) / n_pairs)
    ang = jnp.concatenate([row.reshape(-1, 1) * inv_freq, col.reshape(-1, 1) * inv_freq], axis=-1)
    return jnp.cos(ang), jnp.sin(ang)


def apply_rope(x, cos, sin):
    xp = x.astype(F32).reshape(*x.shape[:-1], -1, 2)
    x1, x2 = xp[..., 0], xp[..., 1]
    c, s = cos[None, :, None, :], sin[None, :, None, :]
    out = jnp.stack([x1 * c - x2 * s, x1 * s + x2 * c], axis=-1)
    return out.reshape(x.shape).astype(x.dtype)


def hgrn2_forget(f_pre, lb):
    log_f = jnp.logaddexp(jnp.log(lb), jnp.log1p(-lb) + jax.nn.log_sigmoid(f_pre))
    k = (1.0 - lb) * jax.nn.sigmoid(-f_pre)
    return log_f, k


def hgrn2_inputs(q, f_fwd, f_bwd, i, lb_f, lb_b):
    n, t = q.shape[:2]
    heads = lambda a: a.reshape(n, t, HG_HEADS, -1).astype(F32)
    q = jax.nn.silu(heads(q)) * HG_DK ** -0.5
    lf_f, k_f = hgrn2_forget(heads(f_fwd), lb_f)
    lf_b, k_b = hgrn2_forget(heads(f_bwd), lb_b)
    return q, heads(i), lf_f, k_f, lf_b, k_b


def hgrn2_scan(q, k, v, log_f, state0):
    L = HG_CHUNK
    causal = jnp.tril(jnp.ones((L, L), dtype=bool))[None, :, :, None, None]

    def step(state, blk):
        qc, kc, vc, gc = blk
        b = jnp.cumsum(gc, axis=1)
        diff = b[:, :, None] - b[:, None, :]
        decay = jnp.exp(jnp.where(causal, diff, -jnp.inf))
        attn = jnp.einsum("bthk,bshk,btshk->bhts", qc, kc, decay)
        o = jnp.einsum("bhts,bshv->bthv", attn, vc) + jnp.einsum("bthk,bhkv->bthv", qc * jnp.exp(b), state)
        b_end = b[:, -1]
        new = jnp.exp(b_end)[..., None] * state + jnp.einsum(
            "bshk,bshv->bhkv", kc * jnp.exp(b_end[:, None] - b), vc)
        return new, o

    state, o = lax.scan(step, state0, (to_chunks(q, L), to_chunks(k, L), to_chunks(v, L), to_chunks(log_f, L)))
    return from_chunks(o), state


def hgrn2_out(o, g, w):
    n, t = g.shape[:2]
    o = rms_norm(o, w) * jax.nn.silu(g.reshape(n, t, HG_HEADS, HG_DV).astype(F32))
    return o.reshape(n, t, HG_WIDTH).astype(g.dtype)


def gqa_attend(q, k, v):
    n, tq = q.shape[:2]
    qg = q.reshape(n, tq, AT_KV_HEADS, AT_GROUP, AT_HEAD_DIM)
    s = jnp.einsum("bqkgd,bskd->bkgqs", qg, k) * (AT_HEAD_DIM ** -0.5)
    p = jax.nn.softmax(s.astype(F32), axis=-1).astype(v.dtype)
    o = jnp.einsum("bkgqs,bskd->bqkgd", p, v)
    return o.reshape(n, tq, AT_WIDTH)


def dwconv_centred(u, w):
    return lax.conv_general_dilated(
        u, w[:, None, :].astype(u.dtype), window_strides=(1,),
        padding=[(MB_CONV // 2, MB_CONV // 2)], dimension_numbers=("NWC", "WIO", "NWC"),
        feature_group_count=u.shape[-1])


def mamba_inputs(xbc, dt_raw, conv_w, conv_b, dt_bias):
    n, t = xbc.shape[:2]
    u = jax.nn.silu(dwconv_centred(xbc, conv_w) + conv_b)
    xs, bm, cm = jnp.split(u, [MB_INNER, MB_INNER + MB_BC_WIDTH], axis=-1)
    xs = xs.reshape(n, t, MB_HEADS, MB_HEAD_DIM).astype(F32)
    bm = bm.reshape(n, t, MB_GROUPS, MB_STATE).astype(F32)
    cm = cm.reshape(n, t, MB_GROUPS, MB_STATE).astype(F32)
    dt = jax.nn.softplus(dt_raw.reshape(n, t, 2, MB_HEADS).astype(F32) + dt_bias.astype(F32))
    return xs, bm, cm, dt[:, :, 0], dt[:, :, 1]


def ssd_scan(xs, dt, a, bm, cm, state0):
    L = MB_CHUNK
    G, HPG = MB_GROUPS, MB_HEADS_PER_GROUP
    causal = jnp.tril(jnp.ones((L, L), dtype=bool))[None, :, :, None]

    def step(state, blk):
        xc, dtc, bc, cc = blk
        n = xc.shape[0]
        acum = jnp.cumsum(dtc * a, axis=1)
        seg = acum[:, :, None, :] - acum[:, None, :, :]
        decay = jnp.exp(jnp.where(causal, seg, -jnp.inf)).reshape(n, L, L, G, HPG)
        xdt = (xc * dtc[..., None]).reshape(n, L, G, HPG, MB_HEAD_DIM)
        cb = jnp.einsum("btgn,bsgn->btsg", cc, bc)
        y = jnp.einsum("btsg,btsgh,bsghp->btghp", cb, decay, xdt)
        sg = state.reshape(n, G, HPG, MB_HEAD_DIM, MB_STATE)
        y = y + jnp.einsum("btgn,bghpn,btgh->btghp", cc, sg, jnp.exp(acum).reshape(n, L, G, HPG))
        a_end = acum[:, -1]
        w_end = jnp.exp(a_end[:, None] - acum).reshape(n, L, G, HPG)
        new = jnp.exp(a_end)[..., None, None] * state + jnp.einsum(
            "bsgn,bsgh,bsghp->bghpn", bc, w_end, xdt).reshape(n, MB_HEADS, MB_HEAD_DIM, MB_STATE)
        return new, y.reshape(n, L, MB_HEADS, MB_HEAD_DIM)

    state, y = lax.scan(step, state0, (to_chunks(xs, L), to_chunks(dt, L), to_chunks(bm, L), to_chunks(cm, L)))
    return from_chunks(y), state


def mamba_out(y, xs, z, d, w):
    n, t = z.shape[:2]
    y = (y + d.astype(F32)[:, None] * xs).reshape(n, t, MB_INNER) * jax.nn.silu(z.astype(F32))
    yg = y.reshape(n, t, MB_GROUPS, -1)
    yg = yg * lax.rsqrt(jnp.mean(yg * yg, axis=-1, keepdims=True) + RMS_EPS)
    return (yg.reshape(n, t, MB_INNER) * w).astype(z.dtype)


def mixer_sublayer(h_lat, h_ctx, w_in, lb_f, lb_b, hg_gnorm, at_qnorm, at_knorm,
                   mb_conv_w, mb_conv_b, mb_dt_bias, mb_a_log, mb_d, mb_norm,
                   w_br_hg, w_br_at, w_br_mb, w_out, rope_cos, rope_sin, with_ctx):
    n, t = h_lat.shape[:2]
    tc = h_ctx.shape[1]
    (hq_l, hff_l, hfb_l, hi_l, hg_l, aq_l, ak_l, av_l, mz_l, mx_l, mdt_l, gt_l) = jnp.split(h_lat @ w_in, IN_SPLITS, axis=-1)
    (hq_c, hff_c, hfb_c, hi_c, hg_c, aq_c, ak_c, av_c, mz_c, mx_c, mdt_c, gt_c) = jnp.split(h_ctx @ w_in, IN_SPLITS, axis=-1)

    q_l, i_l, lf_lf, k_lf, lf_lb, k_lb = hgrn2_inputs(hq_l, hff_l, hfb_l, hi_l, lb_f, lb_b)
    q_c, i_c, lf_cf, k_cf, lf_cb, k_cb = hgrn2_inputs(hq_c, hff_c, hfb_c, hi_c, lb_f, lb_b)
    s0 = jnp.zeros((n, HG_HEADS, HG_DK, HG_DV), F32)
    oc_f, s_f = hgrn2_scan(q_c, k_cf, i_c, lf_cf, s0)
    oc_b, s_b = hgrn2_scan(flip(q_c), flip(k_cb), flip(i_c), flip(lf_cb), s0)
    ol_f, _ = hgrn2_scan(q_l, k_lf, i_l, lf_lf, s_f)
    ol_b, _ = hgrn2_scan(flip(q_l), flip(k_lb), flip(i_l), flip(lf_lb), s_b)
    o_hg_l = hgrn2_out(ol_f + flip(ol_b), hg_l, hg_gnorm)

    heads = lambda a, h: a.reshape(a.shape[0], a.shape[1], h, AT_HEAD_DIM)
    q_l = apply_rope(rms_norm(heads(aq_l, AT_HEADS), at_qnorm), rope_cos, rope_sin)
    k_l = apply_rope(rms_norm(heads(ak_l, AT_KV_HEADS), at_knorm), rope_cos, rope_sin)
    k_c = rms_norm(heads(ak_c, AT_KV_HEADS), at_knorm)
    v_c = heads(av_c, AT_KV_HEADS)
    k_all = jnp.concatenate([k_c, k_l], axis=1)
    v_all = jnp.concatenate([v_c, heads(av_l, AT_KV_HEADS)], axis=1)
    o_at_l = from_chunks(lax.map(lambda qb: gqa_attend(qb, k_all, v_all), to_chunks(q_l, AT_BLOCK)))

    a = -jnp.exp(mb_a_log.astype(F32))
    xs_c, bm_c, cm_c, dtf_c, dtb_c = mamba_inputs(mx_c, mdt_c, mb_conv_w, mb_conv_b, mb_dt_bias)
    xs_l, bm_l, cm_l, dtf_l, dtb_l = mamba_inputs(mx_l, mdt_l, mb_conv_w, mb_conv_b, mb_dt_bias)
    z0 = jnp.zeros((n, MB_HEADS, MB_HEAD_DIM, MB_STATE), F32)
    yc_f, st_f = ssd_scan(xs_c, dtf_c, a[0], bm_c, cm_c, z0)
    yc_b, st_b = ssd_scan(flip(xs_c), flip(dtb_c), a[1], flip(bm_c), flip(cm_c), z0)
    yl_f, _ = ssd_scan(xs_l, dtf_l, a[0], bm_l, cm_l, st_f)
    yl_b, _ = ssd_scan(flip(xs_l), flip(dtb_l), a[1], flip(bm_l), flip(cm_l), st_b)
    o_mb_l = mamba_out(yl_f + flip(yl_b), xs_l, mz_l, mb_d, mb_norm)

    def merge(o_hg, o_at, o_mb, gates):
        g_hg, g_at, g_mb = jnp.split(jax.nn.sigmoid(gates), 3, axis=-1)
        y = g_hg * (o_hg @ w_br_hg) + g_at * (o_at @ w_br_at) + g_mb * (o_mb @ w_br_mb)
        return y @ w_out

    y_lat = merge(o_hg_l, o_at_l, o_mb_l, gt_l)
    if not with_ctx:
        return y_lat, None
    o_hg_c = hgrn2_out(oc_f + flip(oc_b), hg_c, hg_gnorm)
    o_at_c = gqa_attend(rms_norm(heads(aq_c, AT_HEADS), at_qnorm), k_c, v_c)
    o_mb_c = mamba_out(yc_f + flip(yc_b), xs_c, mz_c, mb_d, mb_norm)
    y_ctx = merge(o_hg_c, o_at_c, o_mb_c, gt_c)
    return y_lat, y_ctx


def swiglu_ffn(h, w_in, w_out):
    gate, up = jnp.split(h @ w_in, 2, axis=-1)
    return (jax.nn.silu(gate) * up) @ w_out


def setup_inputs(seed: int = 0) -> dict:
    key = jax.random.key(seed)
    ks = jax.random.split(key, 28)
    nrm = lambda k, shape, scale: jax.random.normal(k, shape, F32) * scale
    gain = lambda k, shape: 1.0 + 0.02 * jax.random.normal(k, shape, F32)
    dt0 = jnp.exp(jax.random.uniform(ks[13], (DEPTH, 2, MB_HEADS), F32, math.log(1e-3), math.log(1e-1)))
    return {
        "x": nrm(ks[0], (BATCH, SEQ, D_MODEL), 1.0),
        "c": nrm(ks[1], (BATCH, D_MODEL), 1.0),
        "ctx": nrm(ks[2], (BATCH, CTX_LEN, D_MODEL), 1.0),
        "c_ctx": nrm(ks[3], (D_MODEL,), 1.0),
        "w_mod": nrm(ks[4], (DEPTH, D_MODEL, 6 * D_MODEL), 0.5 * D_MODEL ** -0.5),
        "b_mod": nrm(ks[5], (DEPTH, 6 * D_MODEL), 0.02),
        "w_in": nrm(ks[6], (DEPTH, D_MODEL, IN_WIDTH), D_MODEL ** -0.5),
        "hg_lb": nrm(ks[7], (DEPTH, 2, HG_WIDTH), 0.1),
        "hg_gnorm": gain(ks[8], (DEPTH, HG_DV)),
        "at_qnorm": gain(ks[9], (DEPTH, AT_HEAD_DIM)),
        "at_knorm": gain(ks[10], (DEPTH, AT_HEAD_DIM)),
        "mb_conv_w": nrm(ks[11], (DEPTH, MB_CONV, MB_CONV_DIM), MB_CONV ** -0.5),
        "mb_conv_b": nrm(ks[12], (DEPTH, MB_CONV_DIM), 0.02),
        "mb_dt_bias": dt0 + jnp.log(-jnp.expm1(-dt0)),
        "mb_a_log": jnp.log(jax.random.uniform(ks[14], (DEPTH, 2, MB_HEADS), F32, 1.0, 16.0)),
        "mb_d": gain(ks[15], (DEPTH, MB_HEADS)),
        "mb_norm": gain(ks[16], (DEPTH, MB_INNER)),
        "w_br_hg": nrm(ks[17], (DEPTH, HG_WIDTH, D_MODEL), HG_WIDTH ** -0.5),
        "w_br_at": nrm(ks[18], (DEPTH, AT_WIDTH, D_MODEL), AT_WIDTH ** -0.5),
        "w_br_mb": nrm(ks[19], (DEPTH, MB_INNER, D_MODEL), MB_INNER ** -0.5),
        "w_out": nrm(ks[20], (DEPTH, D_MODEL, D_MODEL), DN_BETA * D_MODEL ** -0.5),
        "ln1_g": gain(ks[21], (DEPTH, D_MODEL)),
        "ln1_b": nrm(ks[22], (DEPTH, D_MODEL), 0.02),
        "w_ffn_in": nrm(ks[23], (DEPTH, D_MODEL, 2 * FFN_HIDDEN), D_MODEL ** -0.5),
        "w_ffn_out": nrm(ks[24], (DEPTH, FFN_HIDDEN, D_MODEL), DN_BETA * FFN_HIDDEN ** -0.5),
        "ln2_g": gain(ks[25], (DEPTH, D_MODEL)),
        "ln2_b": nrm(ks[26], (DEPTH, D_MODEL), 0.02),
    }


def reference(x, c, ctx, c_ctx, w_mod, b_mod, w_in, hg_lb, hg_gnorm, at_qnorm, at_knorm,
              mb_conv_w, mb_conv_b, mb_dt_bias, mb_a_log, mb_d, mb_norm,
              w_br_hg, w_br_at, w_br_mb, w_out, ln1_g, ln1_b, w_ffn_in, w_ffn_out, ln2_g, ln2_b):
    rope_cos, rope_sin = axial_rope(x.shape[1])
    lbs = jnp.cumsum(jax.nn.softmax(hg_lb.astype(F32), axis=0), axis=0)
    lbs = lbs - lbs[0]
    for l in range(DEPTH):
        with_ctx = l < DEPTH - 1
        mod_l = (jax.nn.silu(c) @ w_mod[l] + b_mod[l])[:, None, :]
        mod_c = jax.nn.silu(c_ctx) @ w_mod[l] + b_mod[l]
        sh1, sc1, g1, sh2, sc2, g2 = jnp.split(mod_l, 6, axis=-1)
        csh1, csc1, cg1, csh2, csc2, cg2 = jnp.split(mod_c, 6, axis=-1)
        y_l, y_c = mixer_sublayer(
            x * (1.0 + sc1) + sh1, ctx * (1.0 + csc1) + csh1, w_in[l],
            lbs[l, 0].reshape(HG_HEADS, HG_DK), lbs[l, 1].reshape(HG_HEADS, HG_DK), hg_gnorm[l],
            at_qnorm[l], at_knorm[l], mb_conv_w[l], mb_conv_b[l], mb_dt_bias[l], mb_a_log[l],
            mb_d[l], mb_norm[l], w_br_hg[l], w_br_at[l], w_br_mb[l], w_out[l],
            rope_cos, rope_sin, with_ctx)
        x = layer_norm(DN_ALPHA * x + g1 * y_l, ln1_g[l], ln1_b[l])
        x = layer_norm(DN_ALPHA * x + g2 * swiglu_ffn(x * (1.0 + sc2) + sh2, w_ffn_in[l], w_ffn_out[l]),
                       ln2_g[l], ln2_b[l])
        if with_ctx:
            ctx = layer_norm(DN_ALPHA * ctx + cg1 * y_c, ln1_g[l], ln1_b[l])
            ctx = layer_norm(DN_ALPHA * ctx + cg2 * swiglu_ffn(ctx * (1.0 + csc2) + csh2, w_ffn_in[l], w_ffn_out[l]),
                             ln2_g[l], ln2_b[l])
    return x
```

```python
from contextlib import ExitStack
import numpy as np
import concourse.bass as bass
import concourse.mybir as mybir
from concourse.bass_utils import run_bass_kernel_spmd

F32 = mybir.dt.float32
BF16 = mybir.dt.bfloat16
AF = mybir.ActivationFunctionType
ALU = mybir.AluOpType
AX = mybir.AxisListType

SAME_ENGINE_SYNC = True
N_DMA_SEMS = 16

DEPTH = 4
DM = 1024
NT = 18
TCTX = 2
TOK = NT * 128
FFH = 2816
IN_W = 14912
O_HQ, O_HFF, O_HFB, O_HI, O_HG = 0, 1024, 2048, 3072, 4096
O_AQ, O_AK, O_AV = 5120, 6144, 6400
O_MZ, O_MX, O_MDT, O_GT = 6656, 8704, 11776, 11840
DN_ALPHA = (2 * DEPTH) ** 0.25
LN_EPS = 1e-5
RMS_EPS = 1e-6

C_ID, C_ONES, C_LE, C_GE, C_GT, C_LT, C_HDF, C_HDB, C_HMF, C_HMB, C_SEL, C_CI = (
    0, 128, 256, 384, 512, 640, 768, 896, 1024, 1152, 1280, 1664)
NCONST = 1664 + 4


def make_consts():
    a = np.arange(128)[:, None]
    b = np.arange(128)[None, :]
    bd = (a // 32) == (b // 32)
    blocks = [a == b, np.ones((128, 128), bool), a <= b, a >= b, a > b, a < b,
              bd & (a > b), bd & (a < b), bd & (a <= b), bd & (a >= b)]
    sel = np.zeros((128, 3 * 128), bool)
    for r in range(3):
        sel[r, r * 128:(r + 1) * 128] = True
    ci = (a // 32) == np.arange(4)[None, :]
    return np.concatenate(blocks + [sel, ci], axis=1).astype(np.float32)


def make_rope():
    rows = 2048 // 64
    row, col = np.meshgrid(np.arange(rows, dtype=np.float32), np.arange(64, dtype=np.float32), indexing="ij")
    n_pairs = 32
    inv_freq = (np.float32(10000.0) ** (-np.arange(n_pairs, dtype=np.float32) / np.float32(n_pairs))).astype(np.float32)
    ang = np.concatenate([row.reshape(-1, 1) * inv_freq, col.reshape(-1, 1) * inv_freq], axis=-1).astype(np.float32)
    return np.stack([np.cos(ang), np.sin(ang)]).astype(np.float32)


class Res:
    __slots__ = ("w", "r")

    def __init__(self):
        self.w = None
        self.r = {}


class Tile:
    def __init__(self, t, nres=1):
        self.t = t
        self.res = [Res() for _ in range(nres)]

    def __getitem__(self, k):
        return self.t[k]

    @property
    def r(self):
        return self.res[0]


def _res(items):
    out = []
    for it in items:
        if isinstance(it, Tile):
            out.extend(it.res)
        elif isinstance(it, Res):
            out.append(it)
        elif it is None:
            pass
        else:
            out.extend(_res(it))
    return out


class Prog:
    def __init__(self, nc, stack):
        self.nc = nc
        self.eng = {"pe": nc.tensor, "act": nc.scalar, "dve": nc.vector, "pool": nc.gpsimd, "sp": nc.sync}
        self.sem = {}
        for e in self.eng:
            self.sem[e] = stack.enter_context(nc.semaphore("s_" + e))
        self.dma_sems = {}
        for q in ("sp", "act", "pool"):
            self.dma_sems[q] = [("d", q, i) for i in range(N_DMA_SEMS)]
            for k in self.dma_sems[q]:
                self.sem[k] = stack.enter_context(nc.semaphore("d_%s_%d" % (q, k[2])))
        self.count = {k: 0 for k in self.sem}
        self.known = {e: {} for e in self.eng}
        self.vc = {}
        self.dma_rr = {q: 0 for q in self.dma_sems}
        self.n_instr = 0
        self.n_wait = 0
        self.stack = None
        self._dres = {}
        self._uid = 0

    def sb(self, shape, dtype, nres=1, name=None):
        self._uid += 1
        t = self.stack.enter_context(self.nc.sbuf_tensor("%s_%d" % (name or "t", self._uid), list(shape), dtype))
        return Tile(t, nres)

    def ps(self, shape, dtype=F32, name=None):
        self._uid += 1
        t = self.stack.enter_context(self.nc.psum_tensor("%s_%d" % (name or "p", self._uid), list(shape), dtype))
        return Tile(t)

    def dres(self, name, idx=0):
        k = (name, idx)
        if k not in self._dres:
            self._dres[k] = Res()
        return self._dres[k]

    def _deps(self, E, reads, writes):
        deps = {}
        for r in reads:
            if r.w is not None and deps.get(r.w[0], 0) < r.w[1]:
                deps[r.w[0]] = r.w[1]
        for w in writes:
            if w.w is not None and deps.get(w.w[0], 0) < w.w[1]:
                deps[w.w[0]] = w.w[1]
            for k, v in w.r.items():
                if deps.get(k, 0) < v:
                    deps[k] = v
        kn = self.known[E]
        out = []
        for k, v in deps.items():
            if k == E and (E == "pe" or E == "sp" or not SAME_ENGINE_SYNC):
                continue
            if kn.get(k, 0) >= v:
                continue
            out.append((k, v))
        return out

    def _wait(self, E, waits):
        eng = self.eng[E]
        kn = self.known[E]
        for k, v in waits:
            eng.wait_ge(self.sem[k], v)
            self.n_wait += 1
            snap = self.vc.get((k, v))
            if snap is not None:
                for kk, vv in snap.items():
                    if kn.get(kk, 0) < vv:
                        kn[kk] = vv
            if kn.get(k, 0) < v:
                kn[k] = v

    def _finish(self, E, ev, ins, inc, reads, writes):
        ins.then_inc(self.sem[ev[0]], inc)
        snap = dict(self.known[E])
        snap[ev[0]] = ev[1]
        self.vc[ev] = snap
        k, v = ev
        for r in reads:
            if r.r.get(k, 0) < v:
                r.r[k] = v
        for w in writes:
            w.w = ev
            w.r = {}
        self.n_instr += 1

    def op(self, E, fn, R=(), W=()):
        reads = _res(R)
        writes = _res(W)
        self._wait(E, self._deps(E, reads, writes))
        ins = fn(self.eng[E])
        self.count[E] += 1
        ev = (E, self.count[E])
        self._finish(E, ev, ins, 1, reads, writes)

    def dma(self, out, in_, R=(), W=(), q="sp", **kw):
        reads = _res(R)
        writes = _res(W)
        self._wait(q, self._deps(q, reads, writes))
        key = self.dma_sems[q][self.dma_rr[q] % N_DMA_SEMS]
        self.dma_rr[q] += 1
        ins = self.eng[q].dma_start(out=out, in_=in_, **kw)
        self.count[key] += 16
        ev = (key, self.count[key])
        self._finish(q, ev, ins, 16, reads, writes)

    def barrier(self):
        targets = [(k, v) for k, v in self.count.items() if v > 0]
        for E in self.eng:
            waits = [(k, v) for k, v in targets if self.known[E].get(k, 0) < v and not (k == E and E == "pe")]
            self._wait(E, waits)

    def wait_all_on(self, E="sp"):
        waits = [(k, v) for k, v in self.count.items() if v > 0 and k != E and self.known[E].get(k, 0) < v]
        self._wait(E, waits)

    def mm(self, out, lhsT, rhs, start=True, stop=True, R=(), W=()):
        self.op("pe", lambda e: e.matmul(out, lhsT, rhs, start=start, stop=stop), R, W)

    def tr(self, out, in_, ident, R=(), W=()):
        self.op("pe", lambda e: e.transpose(out, in_, ident), R, W)

    def act(self, out, in_, func, R=(), W=(), bias=0.0, scale=1.0, accum_out=None):
        if accum_out is None:
            self.op("act", lambda e: e.activation(out, in_, func, bias=bias, scale=scale), R, W)
        else:
            self.op("act", lambda e: e.activation(out, in_, func, bias=bias, scale=scale, accum_out=accum_out), R, W)

    def tt(self, out, in0, in1, op, R=(), W=(), E="dve"):
        self.op(E, lambda e: e.tensor_tensor(out, in0, in1, op), R, W)

    def ts(self, out, in0, s1, s2, op0, op1=None, R=(), W=(), E="dve"):
        if op1 is None:
            self.op(E, lambda e: e.tensor_scalar(out, in0, s1, None, op0), R, W)
        else:
            self.op(E, lambda e: e.tensor_scalar(out, in0, s1, s2, op0, op1), R, W)

    def stt(self, out, in0, scalar, in1, op0, op1, R=(), W=()):
        self.op("dve", lambda e: e.scalar_tensor_tensor(out, in0, scalar, in1, op0, op1), R, W)

    def cp(self, out, in_, R=(), W=(), E="dve"):
        if E == "act":
            self.op("act", lambda e: e.activation(out, in_, AF.Copy), R, W)
        else:
            self.op(E, lambda e: e.tensor_copy(out, in_), R, W)


WSHAPES = [
    ("w_mod", (DM, 6 * DM)), ("b_mod", (6 * DM,)), ("w_in", (DM, IN_W)), ("hg_gnorm", (128,)),
    ("at_qnorm", (128,)), ("at_knorm", (128,)), ("mb_conv_w", (5, 3072)), ("mb_conv_b", (3072,)),
    ("mb_dt_bias", (2, 32)), ("mb_a_log", (2, 32)), ("mb_d", (32,)), ("mb_norm", (2048,)),
    ("w_br_hg", (1024, DM)), ("w_br_at", (1024, DM)), ("w_br_mb", (2048, DM)), ("w_out", (DM, DM)),
    ("ln1_g", (DM,)), ("ln1_b", (DM,)), ("w_ffn_in", (DM, 2 * FFH)), ("w_ffn_out", (FFH, DM)),
    ("ln2_g", (DM,)), ("ln2_b", (DM,)),
]


def build(depth=DEPTH, dbg=None):
    dbg = dbg or {}
    dump = dbg.get("dump", set())
    phases = dbg.get("phases", {"p0", "hg", "at", "mb", "merge", "ffn"})
    seqs = dbg.get("seqs", [0, 1])
    nc = bass.Bass("TRN2", target_bir_lowering=False)

    def din(name, shape):
        return nc.dram_tensor(name, list(shape), F32, kind="ExternalInput").ap()

    x_in = din("x", [2, 2048, DM])
    ctx_in = din("ctx", [2, 256, DM])
    c3T = din("c3T", [128, 8, 3])
    consts_d = din("consts", [128, NCONST])
    rope_d = din("rope", [2, 2048, 64])
    hg_lb_d = din("hg_lb", [4, 2, 1024])
    Wd = {n: din(n, (depth,) + s) for n, s in WSHAPES}
    out_d = nc.dram_tensor("out", [2, 2048, DM], F32, kind="ExternalOutput").ap()

    def dscr(name, shape, dt):
        kind = "ExternalOutput" if name in dump else "Internal"
        return nc.dram_tensor(name, list(shape), dt, kind=kind).ap()

    xres = dscr("xres", [2, NT, 128, DM], F32)
    x1s = dscr("x1s", [NT, 128, DM], F32)
    hg_of = dscr("hg_of", [NT, 128, 1024], F32)
    ohgT = dscr("ohgT", [NT, 128, 1024], BF16)
    oatT = dscr("oatT", [128, 8, TOK], BF16)
    ombT = dscr("ombT", [NT, 128, 2048], BF16)
    mb_xs = dscr("mb_xs", [NT, 128, 2048], BF16)
    mb_yf = dscr("mb_yf", [NT, 128, 2048], F32)
    ymTd = dscr("ymT", [NT, 128, 1024], BF16)
    lbs_d = dscr("lbs", [4, 2, 1024], F32)
    modrow_d = dscr("modrow", [3, 6 * DM], F32)
    ffn_act = dscr("ffn_act", [NT, 128, 22 * 128], BF16)

    with ExitStack() as top:
        P = Prog(nc, top)
        P.stack = top
        cst = P.sb([128, NCONST], F32, name="cst")
        P.dma(cst[:], consts_d, W=[cst])
        cstb = P.sb([128, 256], BF16, name="cstb")
        P.cp(cstb[:], cst[:, 0:256], R=[cst], W=[cstb])
        identF = cst[:, C_ID:C_ID + 128]
        onesF = cst[:, C_ONES:C_ONES + 128]
        identB = cstb[:, 0:128]
        onesB = cstb[:, 128:256]
        scT = P.sb([128, 8, 3], F32, name="scT")
        P.dma(scT[:], c3T, W=[scT])
        P.act(scT[:], scT[:], AF.Silu, R=[scT], W=[scT])
        hT = P.sb([128, 8, TOK], BF16, nres=NT, name="hT")

        def xsrc(l, s, t):
            if l == 0:
                return (ctx_in[s, t * 128:(t + 1) * 128, :] if t < TCTX else x_in[s, (t - TCTX) * 128:(t - TCTX + 1) * 128, :]), None
            return xres[s, t], P.dres("xres", (s, t))

        with ExitStack() as ph:
            P.stack = ph
            e = P.sb([128, 4, 16], F32, name="lb_e")
            P.dma(e[:].rearrange("p l (d j) -> p l d j", d=2),
                  hg_lb_d.rearrange("l d (p j) -> p l d j", p=128), W=[e])
            P.act(e[:], e[:], AF.Exp, R=[e], W=[e])
            s_ = P.sb([128, 16], F32, name="lb_s")
            P.tt(s_[:], e[:, 0, :], e[:, 1, :], ALU.add, R=[e], W=[s_])
            P.tt(s_[:], s_[:], e[:, 2, :], ALU.add, R=[e, s_], W=[s_])
            P.tt(s_[:], s_[:], e[:, 3, :], ALU.add, R=[e, s_], W=[s_])
            P.op("dve", lambda en: en.reciprocal(s_[:], s_[:]), R=[s_], W=[s_])
            P.tt(e[:], e[:], s_[:].rearrange("p (o j) -> p o j", o=1).broadcast_to([128, 4, 16]), ALU.mult, R=[e, s_], W=[e])
            lbt = P.sb([128, 4, 16], F32, name="lb_t")
            P.op("dve", lambda en: en.memset(lbt[:, 0, :], 0.0), W=[lbt])
            P.cp(lbt[:, 1, :], e[:, 1, :], R=[e], W=[lbt])
            P.tt(lbt[:, 2, :], lbt[:, 1, :], e[:, 2, :], ALU.add, R=[e, lbt], W=[lbt])
            P.tt(lbt[:, 3, :], lbt[:, 2, :], e[:, 3, :], ALU.add, R=[e, lbt], W=[lbt])
            P.dma(lbs_d.rearrange("l d (p j) -> p l d j", p=128),
                  lbt[:].rearrange("p l (d j) -> p l d j", d=2), R=[lbt], W=[P.dres("lbs")])
            P.barrier()
        P.stack = top

        for l in range(depth):
            last = (l == depth - 1) and not dbg.get('nolast', False)
            with ExitStack() as lay:
                P.stack = lay
                modA = P.sb([128, 48, 3], F32, name="modA")
                modB = P.sb([128, 48, 3], F32, name="modB")
                with ExitStack() as ph:
                    P.stack = ph
                    mrow = P.sb([3, 6 * DM], F32, name="mrow")
                    brow = P.sb([3, 6 * DM], F32, name="brow")
                    P.dma(brow[:], Wd["b_mod"][l:l + 1, :].broadcast_to([3, 6 * DM]), W=[brow])
                    wm = [P.sb([128, 8, 1536], F32, name="wm%d" % i) for i in range(2)]
                    pm = [P.ps([128, 512], name="pm%d" % i) for i in range(2)]
                    for blk in range(4):
                        w_ = wm[blk % 2]
                        P.dma(w_[:], Wd["w_mod"][l].rearrange("(kc p) n -> p kc n", p=128)[:, :, blk * 1536:(blk + 1) * 1536], W=[w_])
                        for j in range(3):
                            pj = pm[j % 2]
                            for kc in range(8):
                                P.mm(pj[0:3, :], scT[:, kc, :], w_[:, kc, j * 512:(j + 1) * 512],
                                     start=(kc == 0), stop=(kc == 7), R=[scT, w_], W=[pj])
                            c0 = blk * 1536 + j * 512
                            P.tt(mrow[:, c0:c0 + 512], pj[0:3, :], brow[:, c0:c0 + 512], ALU.add, R=[pj, brow], W=[mrow])
                    P.dma(modrow_d, mrow[:], R=[mrow], W=[P.dres("modrow")])
                    pT = P.ps([128, 512], name="pmT")
                    for j in range(48):
                        P.tr(pT[:, j * 3:(j + 1) * 3], mrow[:, j * 128:(j + 1) * 128], identF[0:3, 0:3], R=[mrow, cst], W=[pT])
                    P.cp(modA[:].rearrange("p j r -> p (j r)"), pT[:, 0:144], R=[pT], W=[modA])
                    P.ts(modB[:].rearrange("p j r -> p (j r)"), pT[:, 0:144], 1.0, None, ALU.add, R=[pT], W=[modB])
                    P.barrier()
                P.stack = lay

                for s in seqs:
                    with ExitStack() as sq:
                        P.stack = sq
                        def featT(l_, s_, tiles, src_fn, sc_chunk, sh_chunk, ph):
                            xt = [P.sb([128, DM], F32, name="p0x%d" % i) for i in range(2)]
                            pp = [P.ps([128, 512], name="p0p%d" % i) for i in range(2)]
                            for i, t in enumerate(tiles):
                                r = 2 if t < TCTX else s_
                                xb = xt[i % 2]
                                src, sres = src_fn(t)
                                P.dma(xb[:], src, R=[sres], W=[xb])
                                for half in range(2):
                                    pb = pp[half]
                                    for q in range(4):
                                        kc = half * 4 + q
                                        P.tr(pb[:, q * 128:(q + 1) * 128], xb[:, kc * 128:(kc + 1) * 128], identF, R=[xb, cst], W=[pb])
                                    for q in range(4):
                                        kc = half * 4 + q
                                        o = hT[:, kc, t * 128:(t + 1) * 128]
                                        i_ = pb[:, q * 128:(q + 1) * 128]
                                        sc = modB[:, sc_chunk + kc, r:r + 1]
                                        sh = modA[:, sh_chunk + kc, r:r + 1]
                                        if q % 2 == 0:
                                            P.ts(o, i_, sc, sh, ALU.mult, ALU.add, R=[pb, modA, modB], W=[hT.res[t]])
                                        else:
                                            P.act(o, i_, AF.Identity, R=[pb, modA, modB], W=[hT.res[t]], bias=sh, scale=sc)

                        if "p0" in phases:
                            with ExitStack() as ph:
                                P.stack = ph
                                featT(l, s, list(range(NT)), lambda t: xsrc(l, s, t), 8, 0, ph)
                                P.barrier()
                            P.stack = sq
                        if "hg" in phases:
                            with ExitStack() as ph:
                                P.stack = ph
                                phase_hg(P, nc, l, s, Wd, cst, cstb, hT, lbs_d, hg_of, ohgT)
                                P.barrier()
                            P.stack = sq
                        if "at" in phases:
                            with ExitStack() as ph:
                                P.stack = ph
                                phase_at(P, nc, l, s, Wd, cst, cstb, hT, rope_d, oatT)
                                P.barrier()
                            P.stack = sq
                        if "mb" in phases:
                            phase_mb(P, nc, l, s, Wd, cst, cstb, hT, mb_xs, mb_yf, ombT, sq)
                            P.stack = sq
                        if "merge" in phases:
                            phase_merge(P, nc, l, s, last, Wd, cst, cstb, hT, modA, modB, modrow_d, ohgT, oatT, ombT,
                                        ymTd, x1s, lambda t: xsrc(l, s, t), sq)
                            P.stack = sq
                        if "ffn" in phases:
                            phase_ffn(P, nc, l, s, last, Wd, cst, cstb, hT, modrow_d, x1s, xres, out_d, ffn_act)
                            P.stack = sq
                    P.stack = lay
            P.stack = top
        P.wait_all_on("sp")
        build.stats = (P.n_instr, P.n_wait)
    return nc


def _loadw(P, dst, src3, n, q="pool"):
    KC = src3.shape[1]
    step = 2
    for i, kc in enumerate(range(0, KC, step)):
        P.dma(dst[:, kc:kc + step, 0:n], src3[:, kc:kc + step, :], W=[dst.res[i % len(dst.res)]], q=q)


def phase_hg(P, nc, l, s, Wd, cst, cstb, hT, lbs_d, hg_of, ohgT):
    identB = cstb[:, 0:128]
    onesF = cst[:, C_ONES:C_ONES + 128]
    CI = cst[:, C_CI:C_CI + 4]
    Wl = Wd["w_in"][l].rearrange("(kc p) n -> p kc n", p=128)
    Wq = P.sb([128, 8, 1024], BF16, nres=4, name="Wq")
    Wf = P.sb([128, 8, 1024], BF16, nres=4, name="Wf")
    Wi = P.sb([128, 8, 1024], BF16, nres=4, name="Wi")
    Wg = P.sb([128, 8, 1024], BF16, nres=4, name="Wg")
    _loadw(P, Wf, Wl[:, :, O_HFF:O_HFF + 1024], 1024)
    _loadw(P, Wq, Wl[:, :, O_HQ:O_HQ + 1024], 1024)
    _loadw(P, Wi, Wl[:, :, O_HI:O_HI + 1024], 1024)
    lbb = P.sb([128, 2, 1024], F32, name="lbb")
    oml = P.sb([128, 2, 1024], F32, name="oml")
    P.dma(lbb[:], lbs_d[l:l + 1].broadcast_to([128, 2, 1024]), R=[P.dres("lbs")], W=[lbb])
    P.ts(oml[:], lbb[:], -1.0, 1.0, ALU.mult, ALU.add, R=[lbb], W=[oml])
    gn = P.sb([128, 1], F32, name="gn")
    P.dma(gn[:], Wd["hg_gnorm"][l].rearrange("(p o) -> p o", o=1), W=[gn])
    lf = P.sb([128, 1024], F32, name="lf")
    kk = P.sb([128, 1024], F32, name="kk")
    ex = P.sb([128, 1024], F32, name="ex")
    exn = P.sb([128, 1024], F32, name="exn")
    qs = P.sb([128, 1024], F32, name="qs")
    Kt = P.sb([128, 1024], BF16, name="Kt")
    Qt = P.sb([128, 1024], BF16, name="Qt")
    Vt = P.sb([128, 1024], BF16, name="Vt")
    Ktm = P.sb([128, 4, 1024], BF16, nres=4, name="Ktm")
    KT = P.sb([128, 1024], BF16, name="KT")
    QT = P.sb([128, 1024], BF16, name="QT")
    ec = P.sb([128, 32], F32, name="ec")
    S = P.sb([128, 8, 128], F32, nres=8, name="S")
    Sb = [P.sb([128, 128], BF16, name="Sb%d" % i) for i in range(2)]
    ATm = [P.sb([128, 128], BF16, name="ATm%d" % i) for i in range(2)]
    ost = P.sb([128, 1024], F32, name="ost")
    ofl = P.sb([128, 1024], F32, name="ofl")
    sg = P.sb([128, 1024], F32, name="sg")
    osq = P.sb([128, 1024], F32, name="osq")
    rs = P.sb([128, 1024], F32, name="rs")
    ob = P.sb([128, 1024], BF16, name="ob")
    pA = P.ps([128, 1024], name="pA")
    pE = P.ps([128, 1024], name="pE")
    pT = P.ps([128, 1024], BF16, name="pT")
    pAT = P.ps([128, 512], name="pAT")
    pS = P.ps([128, 512], name="pS")
    pC = P.ps([128, 512], name="pC")
    pCb = pC[:].bitcast(BF16)

    def proj(Wt, t):
        for half in range(2):
            for kc in range(8):
                P.mm(pA[:, half * 512:(half + 1) * 512], hT[:, kc, t * 128:(t + 1) * 128],
                     Wt[:, kc, half * 512:(half + 1) * 512], start=(kc == 0), stop=(kc == 7),
                     R=[hT.res[t], Wt], W=[pA])

    cnt = [0]

    def hg_tile(t, d):
        D = cst[:, (C_HDF if d == 0 else C_HDB):(C_HDF if d == 0 else C_HDB) + 128]
        M = cst[:, (C_HMF if d == 0 else C_HMB):(C_HMF if d == 0 else C_HMB) + 128]
        if d == 1:
            P.dma(ofl[:], hg_of[t], R=[P.dres("hg_of", t)], W=[ofl])
        proj(Wf, t)
        P.act(lf[:], pA[:], AF.Sigmoid, R=[pA], W=[lf])
        P.act(kk[:], pA[:], AF.Sigmoid, R=[pA], W=[kk], scale=-1.0)
        P.tt(lf[:], lf[:], oml[:, d, :], ALU.mult, R=[lf, oml], W=[lf])
        P.tt(lf[:], lf[:], lbb[:, d, :], ALU.add, R=[lf, lbb], W=[lf])
        P.act(lf[:], lf[:], AF.Ln, R=[lf], W=[lf])
        P.tt(kk[:], kk[:], oml[:, d, :], ALU.mult, R=[kk, oml], W=[kk], E="pool")
        for half in range(2):
            P.mm(pE[:, half * 512:(half + 1) * 512], D, lf[:, half * 512:(half + 1) * 512], R=[cst, lf], W=[pE])
        for h in range(8):
            P.mm(pC[:, h * 4:(h + 1) * 4], lf[:, h * 128:(h + 1) * 128], CI, R=[lf, cst], W=[pC])
        P.act(ex[:], pE[:], AF.Exp, R=[pE], W=[ex])
        P.act(exn[:], pE[:], AF.Exp, R=[pE], W=[exn], scale=-1.0)
        P.act(ec[:], pC[:, 0:32], AF.Exp, R=[pC], W=[ec])
        proj(Wq, t)
        P.act(qs[:], pA[:], AF.Silu, R=[pA], W=[qs])
        P.stt(Qt[:], qs[:], float(128 ** -0.5), exn[:], ALU.mult, ALU.mult, R=[qs, exn], W=[Qt])
        P.tt(Kt[:], kk[:], ex[:], ALU.mult, R=[kk, ex], W=[Kt])
        for c in range(4):
            P.ts(Ktm[:, c, :], Kt[:], CI[:, c:c + 1], None, ALU.mult, R=[Kt, cst], W=[Ktm.res[c]], E="pool")
        proj(Wi, t)
        P.cp(Vt[:], pA[:], R=[pA], W=[Vt], E="act")
        for h in range(8):
            P.tr(pT[:, h * 128:(h + 1) * 128], Kt[:, h * 128:(h + 1) * 128], identB, R=[Kt, cstb], W=[pT])
        P.cp(KT[:], pT[:], R=[pT], W=[KT])
        for h in range(8):
            P.tr(pCb[:, h * 128:(h + 1) * 128], Qt[:, h * 128:(h + 1) * 128], identB, R=[Qt, cstb], W=[pC])
        P.cp(QT[:], pCb, R=[pC], W=[QT], E="act")
        chunks = [0, 1, 2, 3] if d == 0 else [3, 2, 1, 0]
        for h in range(8):
            hs = slice(h * 128, (h + 1) * 128)
            i = cnt[0] % 2
            cnt[0] += 1
            P.mm(pAT[:, 0:128], KT[:, hs], QT[:, hs], R=[KT, QT], W=[pAT])
            P.tt(ATm[i][:], pAT[:, 0:128], M, ALU.mult, R=[pAT, cst], W=[ATm[i]])
            P.mm(pE[:, hs], Vt[:, hs], ATm[i][:], start=True, stop=False, R=[Vt, ATm[i]], W=[pE])
            for ci, c in enumerate(chunks):
                j = cnt[0] % 2
                cnt[0] += 1
                e_c = ec[:, h * 4 + c:h * 4 + c + 1]
                P.ts(Sb[j][:], S[:, h, :], e_c, None, ALU.mult, R=[S.res[h], ec], W=[Sb[j]])
                P.mm(pE[:, h * 128 + c * 32:h * 128 + (c + 1) * 32], Sb[j][:], QT[:, h * 128 + c * 32:h * 128 + (c + 1) * 32],
                     start=False, stop=(ci == 3), R=[Sb[j], QT], W=[pE])
                P.mm(pS[:, 0:128], Ktm[:, c, hs], Vt[:, hs], R=[Ktm.res[c], Vt], W=[pS])
                P.stt(S[:, h, :], S[:, h, :], e_c, pS[:, 0:128], ALU.mult, ALU.add, R=[S.res[h], ec, pS], W=[S.res[h]])
        if d == 0:
            P.cp(ost[:], pE[:], R=[pE], W=[ost], E="act")
            P.dma(hg_of[t], ost[:], R=[ost], W=[P.dres("hg_of", t)])
        else:
            for h in range(8):
                for kc in range(8):
                    P.mm(pA[:, h * 128:(h + 1) * 128], Wg[:, kc, h * 128:(h + 1) * 128], hT[:, kc, t * 128:(t + 1) * 128],
                         start=(kc == 0), stop=(kc == 7), R=[Wg, hT.res[t]], W=[pA])
            P.act(sg[:], pA[:], AF.Silu, R=[pA], W=[sg])
            P.tt(ost[:], pE[:], ofl[:], ALU.add, R=[pE, ofl], W=[ost])
            P.act(osq[:], ost[:], AF.Square, R=[ost], W=[osq])
            for half in range(2):
                P.mm(pA[:, half * 512:(half + 1) * 512], onesF, osq[:, half * 512:(half + 1) * 512], R=[cst, osq], W=[pA])
            P.act(rs[:], pA[:], AF.Sqrt, R=[pA], W=[rs], scale=1.0 / 128.0, bias=RMS_EPS)
            P.op("dve", lambda e: e.reciprocal(rs[:], rs[:]), R=[rs], W=[rs])
            P.stt(osq[:], ost[:], gn[:, 0:1], rs[:], ALU.mult, ALU.mult, R=[ost, gn, rs], W=[osq])
            P.tt(ob[:], osq[:], sg[:], ALU.mult, R=[osq, sg], W=[ob])
            P.dma(ohgT[t], ob[:], R=[ob], W=[P.dres("ohgT", t)])

    P.op("dve", lambda e: e.memset(S[:], 0.0), W=[S])
    for t in range(NT):
        hg_tile(t, 0)
    _loadw(P, Wf, Wl[:, :, O_HFB:O_HFB + 1024], 1024)
    _loadw(P, Wg, Wl[:, :, O_HG:O_HG + 1024], 1024)
    P.op("dve", lambda e: e.memset(S[:], 0.0), W=[S])
    for t in list(range(TCTX - 1, -1, -1)) + list(range(NT - 1, TCTX - 1, -1)):
        hg_tile(t, 1)


def phase_at(P, nc, l, s, Wd, cst, cstb, hT, rope_d, oatT):
    identB = cstb[:, 0:128]
    onesB = cstb[:, 128:256]
    Wl = Wd["w_in"][l].rearrange("(kc p) n -> p kc n", p=128)
    Wa = P.sb([128, 8, 1536], BF16, nres=4, name="Wa")
    _loadw(P, Wa, Wl[:, :, O_AQ:O_AQ + 1536], 1536)
    QT = P.sb([128, 8, TOK], BF16, nres=NT, name="QTa")
    KTa = P.sb([128, 2, TOK], BF16, nres=NT, name="KTa")
    Va = P.sb([128, NT, 256], BF16, nres=NT, name="Va")
    with ExitStack() as sub:
        P.stack = sub
        cosT = P.sb([128, 16, 64], F32, name="cosT")
        sinT = P.sb([128, 16, 64], F32, name="sinT")
        P.dma(cosT[:], rope_d[0].rearrange("(t p) j -> p t j", p=128), W=[cosT])
        P.dma(sinT[:], rope_d[1].rearrange("(t p) j -> p t j", p=128), W=[sinT])
        wqk = P.sb([128, 10, 128], F32, name="wqk")
        P.dma(wqk[:, 0:8, :], Wd["at_qnorm"][l:l + 1, :].rearrange("o (h d) -> o h d", h=1).broadcast_to([128, 8, 128]), W=[wqk])
        P.dma(wqk[:, 8:10, :], Wd["at_knorm"][l:l + 1, :].rearrange("o (h d) -> o h d", h=1).broadcast_to([128, 2, 128]), W=[wqk])
        sqt = P.sb([128, 1280], F32, name="sqt")
        xn = P.sb([128, 1280], F32, name="xn")
        ss = P.sb([128, 10], F32, name="ss")
        t1 = P.sb([128, 10, 64], F32, name="t1")
        t2 = P.sb([128, 10, 64], F32, name="t2")
        xr = P.sb([128, 1280], BF16, name="xr")
        pQ = P.ps([128, 1536], name="pQ")
        pT = P.ps([128, 2048], BF16, name="pTa")
        for t in range(NT):
            tc = slice(t * 128, (t + 1) * 128)
            for j in range(3):
                for kc in range(8):
                    P.mm(pQ[:, j * 512:(j + 1) * 512], hT[:, kc, tc], Wa[:, kc, j * 512:(j + 1) * 512],
                         start=(kc == 0), stop=(kc == 7), R=[hT.res[t], Wa], W=[pQ])
            P.cp(Va[:, t, :], pQ[:, 1280:1536], R=[pQ], W=[Va.res[t]], E="act")
            P.act(sqt[:], pQ[:, 0:1280], AF.Square, R=[pQ], W=[sqt])
            P.op("dve", lambda e: e.tensor_reduce(ss[:], sqt[:].rearrange("p (h d) -> p h d", h=10), AX.X, ALU.add), R=[sqt], W=[ss])
            P.act(ss[:], ss[:], AF.Sqrt, R=[ss], W=[ss], scale=1.0 / 128.0, bias=RMS_EPS)
            P.op("dve", lambda e: e.reciprocal(ss[:], ss[:]), R=[ss], W=[ss])
            P.tt(xn[:].rearrange("p (h d) -> p h d", h=10), pQ[:, 0:1280].rearrange("p (h d) -> p h d", h=10),
                 ss[:].rearrange("p (h o) -> p h o", o=1).broadcast_to([128, 10, 128]), ALU.mult, R=[pQ, ss], W=[xn])
            P.tt(xn[:], xn[:], wqk[:].rearrange("p h d -> p (h d)"), ALU.mult, R=[xn, wqk], W=[xn])
            if t >= TCTX:
                tl = t - TCTX
                xv = xn[:].rearrange("p (h j two) -> p h j two", h=10, two=2)
                xo = xr[:].rearrange("p (h j two) -> p h j two", h=10, two=2)
                cb = cosT[:, tl:tl + 1, :].broadcast_to([128, 10, 64])
                sb_ = sinT[:, tl:tl + 1, :].broadcast_to([128, 10, 64])
                P.tt(t1[:], xv[:, :, :, 0], cb, ALU.mult, R=[xn, cosT], W=[t1])
                P.tt(t2[:], xv[:, :, :, 1], sb_, ALU.mult, R=[xn, sinT], W=[t2], E="pool")
                P.tt(xo[:, :, :, 0], t1[:], t2[:], ALU.subtract, R=[t1, t2], W=[xr])
                P.tt(t1[:], xv[:, :, :, 0], sb_, ALU.mult, R=[xn, sinT], W=[t1])
                P.tt(t2[:], xv[:, :, :, 1], cb, ALU.mult, R=[xn, cosT], W=[t2], E="pool")
                P.tt(xo[:, :, :, 1], t1[:], t2[:], ALU.add, R=[t1, t2], W=[xr])
            else:
                P.cp(xr[:], xn[:], R=[xn], W=[xr])
            for j in range(10):
                P.tr(pT[:, j * 128:(j + 1) * 128], xr[:, j * 128:(j + 1) * 128], identB, R=[xr, cstb], W=[pT])
            P.cp(QT[:, :, tc], pT[:, 0:1024].rearrange("p (h k) -> p h k", h=8), R=[pT], W=[QT.res[t]])
            P.cp(KTa[:, :, tc], pT[:, 1024:1280].rearrange("p (h k) -> p h k", h=2), R=[pT], W=[KTa.res[t]], E="act")
        P.barrier()
    with ExitStack() as sub:
        P.stack = sub
        pS = [P.ps([128, 512], name="pSa%d" % i) for i in range(2)]
        pO = P.ps([128, 512], name="pOa")
        pL = P.ps([128, 512], name="pLa")
        Pt = [P.sb([128, 512], BF16, name="Pt%d" % i) for i in range(2)]
        rl = P.sb([128, 512], F32, name="rl")
        oa = [P.sb([128, 512], BF16, name="oa%d" % i) for i in range(2)]
        blocks = [(0, 256, [0, 1])] + [(256 + qb * 512, 512, list(range(NT))) for qb in range(4)]
        k = 0
        sc = float(128 ** -0.5)
        for h in range(8):
            kv = h // 4
            for (q0, nq, kts) in blocks:
                qres = [QT.res[t] for t in range(q0 // 128, (q0 + nq) // 128)]
                for i, kt in enumerate(kts):
                    ps = pS[i % 2]
                    pt = Pt[i % 2]
                    P.mm(ps[:, 0:nq], KTa[:, kv, kt * 128:(kt + 1) * 128], QT[:, h, q0:q0 + nq], R=[KTa.res[kt]] + qres, W=[ps])
                    P.act(pt[:, 0:nq], ps[:, 0:nq], AF.Exp, R=[ps], W=[pt], scale=sc)
                    P.mm(pO[:, 0:nq], Va[:, kt, kv * 128:(kv + 1) * 128], pt[:, 0:nq], start=(i == 0), stop=(i == len(kts) - 1),
                         R=[Va.res[kt], pt], W=[pO])
                    P.mm(pL[:, 0:nq], onesB, pt[:, 0:nq], start=(i == 0), stop=(i == len(kts) - 1), R=[cstb, pt], W=[pL])
                P.op("dve", lambda e: e.reciprocal(rl[:, 0:nq], pL[:, 0:nq]), R=[pL], W=[rl])
                o_ = oa[k % 2]
                k += 1
                P.tt(o_[:, 0:nq], pO[:, 0:nq], rl[:, 0:nq], ALU.mult, R=[pO, rl], W=[o_])
                P.dma(oatT[:, h, q0:q0 + nq], o_[:, 0:nq], R=[o_], W=[P.dres("oatT", (h, q0))])
        P.barrier()


def phase_mb(P, nc, l, s, Wd, cst, cstb, hT, mb_xs, mb_yf, ombT, sq):
    identB = cstb[:, 0:128]
    identF = cst[:, C_ID:C_ID + 128]
    onesF = cst[:, C_ONES:C_ONES + 128]
    Wl = Wd["w_in"][l].rearrange("(kc p) n -> p kc n", p=128)
    with ExitStack() as ph:
        P.stack = ph
        BT = P.sb([128, 4, TOK], BF16, name="BT")
        CT = P.sb([128, 4, TOK], BF16, name="CT")
        with ExitStack() as sub:
            P.stack = sub
            Wx = P.sb([128, 8, 3072], BF16, nres=4, name="Wx")
            _loadw(P, Wx, Wl[:, :, O_MX:O_MX + 3072], 3072)
            cwr = P.sb([120, 128], F32, name="cwr")
            P.dma(cwr[:], Wd["mb_conv_w"][l].rearrange("j (cc p) -> (j cc) p", p=128), W=[cwr])
            cbr = P.sb([24, 128], F32, name="cbr")
            P.dma(cbr[:], Wd["mb_conv_b"][l].rearrange("(cc p) -> cc p", p=128), W=[cbr])
            cw = P.sb([128, 120], F32, name="cw")
            cbias = P.sb([128, 24], F32, name="cbias")
            pX = [P.ps([128, 512], name="pX%d" % i) for i in range(4)]
            pTt = P.ps([128, 1024], BF16, name="pTt")
            pW_ = P.ps([128, 512], name="pW_")
            P.tr(pW_[:, 0:120], cwr[:], identF[0:120, 0:120], R=[cwr, cst], W=[pW_])
            P.cp(cw[:], pW_[:, 0:120], R=[pW_], W=[cw])
            P.tr(pW_[:, 128:152], cbr[:], identF[0:24, 0:24], R=[cbr, cst], W=[pW_])
            P.cp(cbias[:], pW_[:, 128:152], R=[pW_], W=[cbias])
            xb = P.sb([128, 2052], F32, name="xb")
            acc = P.sb([128, 2048], F32, name="acc")
            ub = P.sb([128, 2048], BF16, name="ub")
            stg = [P.sb([128, 8, 128], BF16, name="stg%d" % i) for i in range(2)]
            P.op("pool", lambda e: e.memset(xb[:], 0.0), W=[xb])
            k = 0
            for cc in range(24):
                for (t0, ntile) in ((0, TCTX), (TCTX, NT - TCTX)):
                    N = ntile * 128
                    for b in range((N + 511) // 512):
                        nb = min(512, N - b * 512)
                        c0 = t0 * 128 + b * 512
                        tr_ = [hT.res[t] for t in range(c0 // 128, (c0 + nb) // 128)]
                        for kc in range(8):
                            P.mm(pX[b][:, 0:nb], Wx[:, kc, cc * 128:(cc + 1) * 128], hT[:, kc, c0:c0 + nb],
                                 start=(kc == 0), stop=(kc == 7), R=[Wx] + tr_, W=[pX[b]])
                        P.cp(xb[:, 2 + b * 512:2 + b * 512 + nb], pX[b][:, 0:nb], R=[pX[b]], W=[xb], E=("act" if b % 2 else "dve"))
                    P.op("pool", lambda e: e.memset(xb[:, N + 2:N + 4], 0.0), W=[xb])
                    P.ts(acc[:, 0:N], xb[:, 0:N], cw[:, cc:cc + 1], None, ALU.mult, R=[xb, cw], W=[acc])
                    for j in range(1, 5):
                        P.stt(acc[:, 0:N], xb[:, j:j + N], cw[:, j * 24 + cc:j * 24 + cc + 1], acc[:, 0:N], ALU.mult, ALU.add,
                              R=[xb, cw, acc], W=[acc])
                    if cc < 16:
                        dst, dres_ = ub[:, 0:N], ub
                    elif cc < 20:
                        dst, dres_ = BT[:, cc - 16, t0 * 128:t0 * 128 + N], BT
                    else:
                        dst, dres_ = CT[:, cc - 20, t0 * 128:t0 * 128 + N], CT
                    P.act(dst, acc[:, 0:N], AF.Silu, R=[acc, cbias], W=[dres_], bias=cbias[:, cc:cc + 1])
                    if cc < 16:
                        for g0 in range(0, ntile, 8):
                            ng = min(8, ntile - g0)
                            for j in range(ng):
                                P.tr(pTt[:, j * 128:(j + 1) * 128], ub[:, (g0 + j) * 128:(g0 + j + 1) * 128], identB, R=[ub, cstb], W=[pTt])
                            st = stg[k % 2]
                            k += 1
                            P.cp(st[:, 0:ng, :], pTt[:, 0:ng * 128].rearrange("p (j c) -> p j c", j=ng), R=[pTt], W=[st],
                                 E=("act" if k % 2 else "dve"))
                            ta = t0 + g0
                            P.dma(mb_xs[ta:ta + ng, :, cc * 128:(cc + 1) * 128].rearrange("t p c -> p t c"), st[:, 0:ng, :],
                                  R=[st], W=[P.dres("mb_xs", t) for t in range(ta, ta + ng)])
            P.barrier()
        with ExitStack() as sub:
            P.stack = sub
            Wz = P.sb([128, 8, 2112], BF16, nres=8, name="Wz")
            for i, kc in enumerate(range(0, 8, 2)):
                P.dma(Wz[:, kc:kc + 2, 0:2048], Wl[:, kc:kc + 2, O_MZ:O_MZ + 2048], W=[Wz.res[i]], q="pool")
                P.dma(Wz[:, kc:kc + 2, 2048:2112], Wl[:, kc:kc + 2, O_MDT:O_MDT + 64], W=[Wz.res[4 + i]], q="pool")
            dtb = P.sb([128, 64], F32, name="dtb")
            abc = P.sb([128, 64], F32, name="abc")
            dbc = P.sb([128, 32], F32, name="dbc")
            nrm = P.sb([128, 2048], F32, name="nrm")
            P.dma(dtb[:], Wd["mb_dt_bias"][l:l + 1].rearrange("o d h -> o (d h)").broadcast_to([128, 64]), W=[dtb])
            P.dma(abc[:], Wd["mb_a_log"][l:l + 1].rearrange("o d h -> o (d h)").broadcast_to([128, 64]), W=[abc])
            P.act(abc[:], abc[:], AF.Exp, R=[abc], W=[abc])
            P.ts(abc[:], abc[:], -1.0, None, ALU.mult, R=[abc], W=[abc])
            P.dma(dbc[:], Wd["mb_d"][l:l + 1, :].broadcast_to([128, 32]), W=[dbc])
            P.dma(nrm[:], Wd["mb_norm"][l:l + 1, :].broadcast_to([128, 2048]), W=[nrm])
            xs = P.sb([128, 2048], BF16, name="xs")
            xdt = P.sb([128, 2048], BF16, name="xdt")
            xdtw = P.sb([128, 2048], BF16, name="xdtw")
            dtv = P.sb([128, 64], F32, name="dtv")
            dA = P.sb([128, 32], F32, name="dA")
            e3 = P.sb([128, 96], F32, name="e3")
            cbm = P.sb([128, 512], F32, name="cbm")
            Btk = P.sb([128, 512], BF16, name="Btk")
            rh = [P.sb([128, 512], F32, name="rh%d" % i) for i in range(2)]
            es = [P.sb([128, 512], F32, name="es%d" % i) for i in range(2)]
            MT = P.sb([128, 32, 128], BF16, nres=8, name="MT")
            tmp = P.sb([128, 512], F32, name="tmpy")
            ysb = P.sb([128, 2048], F32, name="ysb")
            ST = P.sb([128, 2048], F32, nres=4, name="ST")
            STb = P.sb([128, 2048], BF16, nres=4, name="STb")
            yfl = P.sb([128, 2048], F32, name="yfl")
            ss4 = P.sb([128, 4], F32, name="ss4")
            oT = P.sb([128, 2048], BF16, name="oTm")
            pM = P.ps([128, 512], name="pM")
            pCB = P.ps([128, 512], name="pCB")
            pCBb = pCB[:].bitcast(BF16)
            pSG = P.ps([128, 512], name="pSG")
            pY = [P.ps([128, 512], name="pY%d" % i) for i in range(4)]
            pI = P.ps([128, 512], name="pI")
            v3 = lambda ap, h: ap.rearrange("p (h q) -> p h q", h=h)
            col = lambda ap: ap.rearrange("p (h o) -> p h o", o=1)
            cnt = [0]

            def mb_tile(t, d):
                tc = slice(t * 128, (t + 1) * 128)
                TRI = cst[:, (C_LE if d == 0 else C_GE):(C_LE if d == 0 else C_GE) + 128]
                STR = cst[:, (C_GT if d == 0 else C_LT):(C_GT if d == 0 else C_LT) + 128]
                P.dma(xs[:], mb_xs[t], R=[P.dres("mb_xs", t)], W=[xs])
                if d == 1:
                    P.dma(yfl[:], mb_yf[t], R=[P.dres("mb_yf", t)], W=[yfl])
                for kc in range(8):
                    P.mm(pM[:, 0:64], hT[:, kc, tc], Wz[:, kc, 2048:2112], start=(kc == 0), stop=(kc == 7), R=[hT.res[t], Wz], W=[pM])
                P.tt(dtv[:], pM[:, 0:64], dtb[:], ALU.add, R=[pM, dtb], W=[dtv])
                P.act(dtv[:], dtv[:], AF.Exp, R=[dtv], W=[dtv])
                P.act(dtv[:], dtv[:], AF.Ln, R=[dtv], W=[dtv], bias=1.0)
                P.tt(dA[:], dtv[:, d * 32:(d + 1) * 32], abc[:, d * 32:(d + 1) * 32], ALU.mult, R=[dtv, abc], W=[dA])
                P.tt(v3(xdt[:], 32), v3(xs[:], 32), col(dtv[:, d * 32:(d + 1) * 32]).broadcast_to([128, 32, 64]), ALU.mult,
                     R=[xs, dtv], W=[xdt])
                P.mm(pM[:, 64:96], TRI, dA[:], R=[cst, dA], W=[pM])
                P.mm(pM[:, 96:128], STR, dA[:], R=[cst, dA], W=[pM])
                P.mm(pM[:, 128:160], onesF, dA[:], R=[cst, dA], W=[pM])
                P.act(e3[:], pM[:, 64:160], AF.Exp, R=[pM], W=[e3])
                P.tt(v3(xdtw[:], 32), v3(xdt[:], 32), col(e3[:, 32:64]).broadcast_to([128, 32, 64]), ALU.mult,
                     R=[xdt, e3], W=[xdtw], E="pool")
                for g in range(4):
                    P.mm(pCB[:, g * 128:(g + 1) * 128], BT[:, g, tc], CT[:, g, tc], R=[BT, CT], W=[pCB])
                P.tt(v3(cbm[:], 4), v3(pCB[:], 4), TRI.rearrange("p (o t) -> p o t", o=1).broadcast_to([128, 4, 128]), ALU.mult,
                     R=[pCB, cst], W=[cbm])
                for g in range(4):
                    P.tr(pCBb[:, g * 128:(g + 1) * 128], BT[:, g, tc], identB, R=[BT, cstb], W=[pCB])
                P.cp(Btk[:], pCBb[:, 0:512], R=[pCB], W=[Btk], E="act")
                for hb in range(8):
                    h0 = hb * 4
                    g = hb // 2
                    i = cnt[0] % 2
                    cnt[0] += 1
                    P.tt(v3(rh[i][:], 4), TRI.rearrange("p (o t) -> p o t", o=1).broadcast_to([128, 4, 128]),
                         col(dA[:, h0:h0 + 4]).broadcast_to([128, 4, 128]), ALU.mult, R=[cst, dA], W=[rh[i]])
                    P.mm(pSG[:], STR, rh[i][:], R=[cst, rh[i]], W=[pSG])
                    P.act(es[i][:], pSG[:], AF.Exp, R=[pSG], W=[es[i]])
                    P.tt(MT[:, h0:h0 + 4, :], v3(es[i][:], 4),
                         cbm[:, g * 128:(g + 1) * 128].rearrange("p (o t) -> p o t", o=1).broadcast_to([128, 4, 128]), ALU.mult,
                         R=[es[i], cbm], W=[MT.res[hb]])
                for h in range(32):
                    P.mm(pY[h // 8][:, (h % 8) * 64:(h % 8 + 1) * 64], MT[:, h, :], xdt[:, h * 64:(h + 1) * 64],
                         R=[MT.res[h // 4], xdt], W=[pY[h // 8]])
                for g in range(4):
                    gs = slice(g * 512, (g + 1) * 512)
                    P.mm(pI[:], CT[:, g, tc], STb[:, gs], R=[CT, STb.res[g]], W=[pI])
                    P.tt(v3(tmp[:], 8), v3(pI[:], 8), col(e3[:, g * 8:(g + 1) * 8]).broadcast_to([128, 8, 64]), ALU.mult,
                         R=[pI, e3], W=[tmp])
                    P.tt(ysb[:, gs], tmp[:], pY[g][:], ALU.add, R=[tmp, pY[g]], W=[ysb])
                for g in range(4):
                    gs = slice(g * 512, (g + 1) * 512)
                    P.mm(pI[:], Btk[:, g * 128:(g + 1) * 128], xdtw[:, gs], R=[Btk, xdtw], W=[pI])
                    P.tt(v3(ST[:, gs], 8), v3(ST[:, gs], 8), col(e3[:, 64 + g * 8:64 + (g + 1) * 8]).broadcast_to([128, 8, 64]), ALU.mult,
                         R=[ST.res[g], e3], W=[ST.res[g]])
                    P.tt(ST[:, gs], ST[:, gs], pI[:], ALU.add, R=[ST.res[g], pI], W=[ST.res[g]])
                    P.cp(STb[:, gs], ST[:, gs], R=[ST.res[g]], W=[STb.res[g]], E="act")
                if d == 0:
                    P.dma(mb_yf[t], ysb[:], R=[ysb], W=[P.dres("mb_yf", t)])
                    return
                P.tt(ysb[:], ysb[:], yfl[:], ALU.add, R=[ysb, yfl], W=[ysb])
                P.tt(v3(yfl[:], 32), v3(xs[:], 32), col(dbc[:]).broadcast_to([128, 32, 64]), ALU.mult, R=[xs, dbc, yfl], W=[yfl])
                P.tt(ysb[:], ysb[:], yfl[:], ALU.add, R=[ysb, yfl], W=[ysb])
                for j in range(4):
                    for kc in range(8):
                        P.mm(pY[j][:], hT[:, kc, tc], Wz[:, kc, j * 512:(j + 1) * 512], start=(kc == 0), stop=(kc == 7),
                             R=[hT.res[t], Wz], W=[pY[j]])
                    P.act(yfl[:, j * 512:(j + 1) * 512], pY[j][:], AF.Silu, R=[pY[j]], W=[yfl])
                P.tt(ysb[:], ysb[:], yfl[:], ALU.mult, R=[ysb, yfl], W=[ysb])
                P.act(yfl[:], ysb[:], AF.Square, R=[ysb], W=[yfl])
                P.op("dve", lambda e: e.tensor_reduce(ss4[:], v3(yfl[:], 4), AX.X, ALU.add), R=[yfl], W=[ss4])
                P.act(ss4[:], ss4[:], AF.Sqrt, R=[ss4], W=[ss4], scale=1.0 / 512.0, bias=RMS_EPS)
                P.op("dve", lambda e: e.reciprocal(ss4[:], ss4[:]), R=[ss4], W=[ss4])
                P.tt(v3(ysb[:], 4), v3(ysb[:], 4), col(ss4[:]).broadcast_to([128, 4, 512]), ALU.mult, R=[ysb, ss4], W=[ysb])
                P.tt(ysb[:], ysb[:], nrm[:], ALU.mult, R=[ysb, nrm], W=[ysb])
                for cc in range(16):
                    P.tr(pY[cc // 4][:, (cc % 4) * 128:(cc % 4 + 1) * 128], ysb[:, cc * 128:(cc + 1) * 128], identF, R=[ysb, cst], W=[pY[cc // 4]])
                for j in range(4):
                    P.cp(oT[:, j * 512:(j + 1) * 512], pY[j][:], R=[pY[j]], W=[oT], E=("act" if j % 2 else "dve"))
                P.dma(ombT[t], oT[:], R=[oT], W=[P.dres("ombT", t)])

            def reset_state():
                P.op("dve", lambda e: e.memset(ST[:], 0.0), W=[ST])
                P.op("pool", lambda e: e.memset(STb[:], 0.0), W=[STb])

            reset_state()
            for t in range(NT):
                mb_tile(t, 0)
            reset_state()
            for t in list(range(TCTX - 1, -1, -1)) + list(range(NT - 1, TCTX - 1, -1)):
                mb_tile(t, 1)
            P.barrier()


def _layernorm(P, out, z, zr, lng, lnb, st, mv):
    for c in range(2):
        P.op("dve", lambda e: e.bn_stats(st[:, c * 6:(c + 1) * 6], z[:, c * 512:(c + 1) * 512]), R=[zr], W=[st])
    P.op("dve", lambda e: e.bn_aggr(mv[:, 0:2], st[:]), R=[st], W=[mv])
    P.act(mv[:, 2:3], mv[:, 1:2], AF.Sqrt, R=[mv], W=[mv], bias=LN_EPS)
    P.op("dve", lambda e: e.reciprocal(mv[:, 2:3], mv[:, 2:3]), R=[mv], W=[mv])
    P.ts(out, z, mv[:, 0:1], mv[:, 2:3], ALU.subtract, ALU.mult, R=[zr, mv], W=[zr])
    P.tt(out, out, lng[:], ALU.mult, R=[zr, lng], W=[zr])
    P.tt(out, out, lnb[:], ALU.add, R=[zr, lnb], W=[zr])


def _gate_bcast(P, cst, modrow_d, c0, s, pG, gbc):
    mr = P.sb([3, 1024], F32, name="mr")
    P.dma(mr[:], modrow_d[:, c0:c0 + 1024], R=[P.dres("modrow")], W=[mr])
    for ri, r in enumerate((s, 2)):
        for half in range(2):
            P.mm(pG[:, half * 512:(half + 1) * 512], cst[0:3, C_SEL + r * 128:C_SEL + (r + 1) * 128], mr[0:3, half * 512:(half + 1) * 512],
                 R=[cst, mr], W=[pG])
        P.cp(gbc[:, ri, :], pG[:], R=[pG], W=[gbc])


def phase_merge(P, nc, l, s, last, Wd, cst, cstb, hT, modA, modB, modrow_d, ohgT, oatT, ombT, ymTd, x1s, xsrc_fn, sq):
    identB = cstb[:, 0:128]
    identF = cst[:, C_ID:C_ID + 128]
    tiles = list(range(TCTX, NT)) if last else list(range(NT))
    Wl = Wd["w_in"][l].rearrange("(kc p) n -> p kc n", p=128)
    with ExitStack() as ph:
        P.stack = ph
        Wgt = P.sb([128, 8, 3072], BF16, nres=4, name="Wgt")
        Wbh = P.sb([128, 8, 1024], BF16, nres=4, name="Wbh")
        Wba = P.sb([128, 8, 1024], BF16, nres=4, name="Wba")
        Wbm = P.sb([128, 16, 1024], BF16, nres=8, name="Wbm")
        _loadw(P, Wgt, Wl[:, :, O_GT:O_GT + 3072], 3072)
        _loadw(P, Wbh, Wd["w_br_hg"][l].rearrange("(kc p) n -> p kc n", p=128), 1024)
        _loadw(P, Wba, Wd["w_br_at"][l].rearrange("(kc p) n -> p kc n", p=128), 1024)
        _loadw(P, Wbm, Wd["w_br_mb"][l].rearrange("(kc p) n -> p kc n", p=128), 1024)
        oh = [P.sb([128, 1024], BF16, name="oh%d" % i) for i in range(2)]
        oa = [P.sb([128, 8, 128], BF16, name="oam%d" % i) for i in range(2)]
        om = [P.sb([128, 2048], BF16, name="om%d" % i) for i in range(2)]
        sgt = P.sb([128, 1024], F32, name="sgt")
        ym = P.sb([128, 1024], F32, name="ym")
        tmp = P.sb([128, 1024], F32, name="tmpm")
        ymb = P.sb([128, 1024], BF16, name="ymb")
        ymT = [P.sb([128, 1024], BF16, name="ymT%d" % i) for i in range(2)]
        pG = P.ps([128, 1024], name="pGm")
        pB = P.ps([128, 1024], name="pBm")
        pT = P.ps([128, 1024], BF16, name="pTm")
        for i, t in enumerate(tiles):
            tc = slice(t * 128, (t + 1) * 128)
            oh_, oa_, om_ = oh[i % 2], oa[i % 2], om[i % 2]
            P.dma(oh_[:], ohgT[t], R=[P.dres("ohgT", t)], W=[oh_])
            P.dma(oa_[:], oatT[:, :, tc], R=[P.dres("oatT", (h, q0)) for h in range(8) for q0 in (0, 256, 768, 1280, 1792)], W=[oa_])
            P.dma(om_[:], ombT[t], R=[P.dres("ombT", t)], W=[om_])
            srcs = [(lambda kc, o=oh_: o[:, kc * 128:(kc + 1) * 128], Wbh, 8, oh_),
                    (lambda kc, o=oa_: o[:, kc, :], Wba, 8, oa_),
                    (lambda kc, o=om_: o[:, kc * 128:(kc + 1) * 128], Wbm, 16, om_)]
            for b, (of, Wb, nk, ot) in enumerate(srcs):
                for half in range(2):
                    for kc in range(8):
                        P.mm(pG[:, half * 512:(half + 1) * 512], hT[:, kc, tc], Wgt[:, kc, b * 1024 + half * 512:b * 1024 + (half + 1) * 512],
                             start=(kc == 0), stop=(kc == 7), R=[hT.res[t], Wgt], W=[pG])
                P.act(sgt[:], pG[:], AF.Sigmoid, R=[pG], W=[sgt])
                for half in range(2):
                    for kc in range(nk):
                        P.mm(pB[:, half * 512:(half + 1) * 512], of(kc), Wb[:, kc, half * 512:(half + 1) * 512],
                             start=(kc == 0), stop=(kc == nk - 1), R=[ot, Wb], W=[pB])
                if b == 0:
                    P.tt(ym[:], pB[:], sgt[:], ALU.mult, R=[pB, sgt], W=[ym])
                else:
                    P.tt(tmp[:], pB[:], sgt[:], ALU.mult, R=[pB, sgt], W=[tmp])
                    if b == 1:
                        P.tt(ym[:], ym[:], tmp[:], ALU.add, R=[ym, tmp], W=[ym])
                    else:
                        P.tt(ymb[:], ym[:], tmp[:], ALU.add, R=[ym, tmp], W=[ymb])
            for kc in range(8):
                P.tr(pT[:, kc * 128:(kc + 1) * 128], ymb[:, kc * 128:(kc + 1) * 128], identB, R=[ymb, cstb], W=[pT])
            yT = ymT[i % 2]
            P.cp(yT[:], pT[:], R=[pT], W=[yT], E="act")
            P.dma(ymTd[t], yT[:], R=[yT], W=[P.dres("ymT", t)])
        P.barrier()
    with ExitStack() as ph:
        P.stack = ph
        Wo = P.sb([128, 8, 1024], BF16, nres=4, name="Wo")
        _loadw(P, Wo, Wd["w_out"][l].rearrange("(kc p) n -> p kc n", p=128), 1024)
        pW = P.ps([128, 1024], name="pWm")
        pp = [P.ps([128, 512], name="pp%d" % i) for i in range(2)]
        gbc = P.sb([128, 2, 1024], F32, name="gbc")
        _gate_bcast(P, cst, modrow_d, 2048, s, pW, gbc)
        lng = P.sb([128, 1024], F32, name="lng")
        lnb = P.sb([128, 1024], F32, name="lnb")
        P.dma(lng[:], Wd["ln1_g"][l:l + 1, :].broadcast_to([128, 1024]), W=[lng])
        P.dma(lnb[:], Wd["ln1_b"][l:l + 1, :].broadcast_to([128, 1024]), W=[lnb])
        yT = [P.sb([128, 1024], BF16, name="yTb%d" % i) for i in range(2)]
        xt = [P.sb([128, 1024], F32, name="xtb%d" % i) for i in range(2)]
        z = [P.sb([128, 1024], F32, name="zb%d" % i) for i in range(2)]
        st = P.sb([128, 12], F32, name="st")
        mv = P.sb([128, 4], F32, name="mv")
        for i, t in enumerate(tiles):
            tc = slice(t * 128, (t + 1) * 128)
            r = 2 if t < TCTX else s
            ri = 1 if t < TCTX else 0
            yT_, xt_, z_ = yT[i % 2], xt[i % 2], z[i % 2]
            P.dma(yT_[:], ymTd[t], R=[P.dres("ymT", t)], W=[yT_])
            src, sres = xsrc_fn(t)
            P.dma(xt_[:], src, R=[sres], W=[xt_])
            for half in range(2):
                for kc in range(8):
                    P.mm(pW[:, half * 512:(half + 1) * 512], yT_[:, kc * 128:(kc + 1) * 128], Wo[:, kc, half * 512:(half + 1) * 512],
                         start=(kc == 0), stop=(kc == 7), R=[yT_, Wo], W=[pW])
            P.tt(z_[:], pW[:], gbc[:, ri, :], ALU.mult, R=[pW, gbc], W=[z_])
            P.stt(z_[:], xt_[:], float(DN_ALPHA), z_[:], ALU.mult, ALU.add, R=[xt_, z_], W=[z_])
            _layernorm(P, z_[:], z_[:], z_, lng, lnb, st, mv)
            P.dma(x1s[t], z_[:], R=[z_], W=[P.dres("x1s", t)])
            for half in range(2):
                pb = pp[half]
                for q in range(4):
                    kc = half * 4 + q
                    P.tr(pb[:, q * 128:(q + 1) * 128], z_[:, kc * 128:(kc + 1) * 128], identF, R=[z_, cst], W=[pb])
                for q in range(4):
                    kc = half * 4 + q
                    o = hT[:, kc, tc]
                    i_ = pb[:, q * 128:(q + 1) * 128]
                    sc = modB[:, 32 + kc, r:r + 1]
                    sh = modA[:, 24 + kc, r:r + 1]
                    if q % 2 == 0:
                        P.ts(o, i_, sc, sh, ALU.mult, ALU.add, R=[pb, modA, modB], W=[hT.res[t]])
                    else:
                        P.act(o, i_, AF.Identity, R=[pb, modA, modB], W=[hT.res[t]], bias=sh, scale=sc)
        P.barrier()


def phase_ffn(P, nc, l, s, last, Wd, cst, cstb, hT, modrow_d, x1s, xres, out_d, ffn_act):
    blocks = ([] if last else [(0, TCTX)]) + [(TCTX + 4 * i, 4) for i in range(4)]
    with ExitStack() as ph:
        P.stack = ph
        W1 = P.sb([128, 8, 2 * FFH], BF16, nres=4, name="W1")
        _loadw(P, W1, Wd["w_ffn_in"][l].rearrange("(kc p) n -> p kc n", p=128), 2 * FFH)
        pGt = [P.ps([128, 512], name="pGt%d" % i) for i in range(2)]
        pUp = [P.ps([128, 512], name="pUp%d" % i) for i in range(2)]
        sgu = [P.sb([128, 512], F32, name="sgu%d" % i) for i in range(2)]
        actT = [P.sb([128, 22, 512], BF16, name="actT%d" % i) for i in range(2)]
        for bi, (t0, ntile) in enumerate(blocks):
            N = ntile * 128
            cols = slice(t0 * 128, t0 * 128 + N)
            tr_ = [hT.res[t] for t in range(t0, t0 + ntile)]
            aT = actT[bi % 2]
            for j in range(22):
                pg, pu, sg_ = pGt[j % 2], pUp[j % 2], sgu[j % 2]
                for kc in range(8):
                    P.mm(pg[:, 0:N], W1[:, kc, j * 128:(j + 1) * 128], hT[:, kc, cols], start=(kc == 0), stop=(kc == 7), R=[W1] + tr_, W=[pg])
                for kc in range(8):
                    P.mm(pu[:, 0:N], W1[:, kc, FFH + j * 128:FFH + (j + 1) * 128], hT[:, kc, cols], start=(kc == 0), stop=(kc == 7),
                         R=[W1] + tr_, W=[pu])
                P.act(sg_[:, 0:N], pg[:, 0:N], AF.Silu, R=[pg], W=[sg_])
                P.tt(aT[:, j, 0:N], sg_[:, 0:N], pu[:, 0:N], ALU.mult, R=[sg_, pu], W=[aT])
            for ti in range(ntile):
                t = t0 + ti
                P.dma(ffn_act[t].rearrange("p (j k) -> p j k", j=22), aT[:, :, ti * 128:(ti + 1) * 128], R=[aT], W=[P.dres("ffn_act", t)])
        P.barrier()
    with ExitStack() as ph:
        P.stack = ph
        W2 = P.sb([128, 22, 1024], BF16, nres=11, name="W2")
        _loadw(P, W2, Wd["w_ffn_out"][l].rearrange("(j p) n -> p j n", p=128), 1024)
        pF = [P.ps([128, 1024], name="pF%d" % i) for i in range(2)]
        gbc = P.sb([128, 2, 1024], F32, name="gbc2")
        _gate_bcast(P, cst, modrow_d, 5120, s, pF[0], gbc)
        lng = P.sb([128, 1024], F32, name="lng2")
        lnb = P.sb([128, 1024], F32, name="lnb2")
        P.dma(lng[:], Wd["ln2_g"][l:l + 1, :].broadcast_to([128, 1024]), W=[lng])
        P.dma(lnb[:], Wd["ln2_b"][l:l + 1, :].broadcast_to([128, 1024]), W=[lnb])
        aT = [P.sb([128, 22, 128], BF16, name="aTl%d" % i) for i in range(2)]
        xt = [P.sb([128, 1024], F32, name="x1l%d" % i) for i in range(2)]
        z = [P.sb([128, 1024], F32, name="z2%d" % i) for i in range(2)]
        st = P.sb([128, 12], F32, name="st2")
        mv = P.sb([128, 4], F32, name="mv2")
        tiles = [t0 + ti for (t0, ntile) in blocks for ti in range(ntile)]
        for i, t in enumerate(tiles):
            ri = 1 if t < TCTX else 0
            a_, xt_, z_, pF_ = aT[i % 2], xt[i % 2], z[i % 2], pF[i % 2]
            P.dma(a_[:], ffn_act[t].rearrange("p (j k) -> p j k", j=22), R=[P.dres("ffn_act", t)], W=[a_])
            P.dma(xt_[:], x1s[t], R=[P.dres("x1s", t)], W=[xt_])
            for half in range(2):
                for j in range(22):
                    P.mm(pF_[:, half * 512:(half + 1) * 512], a_[:, j, :], W2[:, j, half * 512:(half + 1) * 512],
                         start=(j == 0), stop=(j == 21), R=[a_, W2], W=[pF_])
            P.tt(z_[:], pF_[:], gbc[:, ri, :], ALU.mult, R=[pF_, gbc], W=[z_])
            P.stt(z_[:], xt_[:], float(DN_ALPHA), z_[:], ALU.mult, ALU.add, R=[xt_, z_], W=[z_])
            _layernorm(P, z_[:], z_[:], z_, lng, lnb, st, mv)
            if last:
                P.dma(out_d[s, (t - TCTX) * 128:(t - TCTX + 1) * 128, :], z_[:], R=[z_], W=[P.dres("out", (s, t))])
            else:
                P.dma(xres[s, t], z_[:], R=[z_], W=[P.dres("xres", (s, t))])
        P.barrier()


def kernel(**inputs):
    inp = {k: np.asarray(v) for k, v in inputs.items()}
    nc = build(DEPTH)
    consts = make_consts()
    rope = make_rope()
    maps = []
    for core in range(8):
        b0 = 2 * core
        c3 = np.stack([inp["c"][b0], inp["c"][b0 + 1], inp["c_ctx"]])
        c3T = np.ascontiguousarray(c3.reshape(3, 8, 128).transpose(2, 1, 0))
        m = {"x": np.ascontiguousarray(inp["x"][b0:b0 + 2]), "ctx": np.ascontiguousarray(inp["ctx"][b0:b0 + 2]),
             "c3T": c3T, "consts": consts, "rope": rope, "hg_lb": inp["hg_lb"]}
        for n, _ in WSHAPES:
            m[n] = inp[n]
        maps.append(m)
    res = run_bass_kernel_spmd(nc, maps, core_ids=list(range(8)))
    out = np.concatenate([np.asarray(r["out"]) for r in res.results], axis=0)
    return out.astype(np.float32)
```

```python
from contextlib import ExitStack
import numpy as np
import concourse.bass as bass
import concourse.mybir as mybir
from concourse.bass_utils import run_bass_kernel_spmd

F32 = mybir.dt.float32
BF16 = mybir.dt.bfloat16
AF = mybir.ActivationFunctionType
ALU = mybir.AluOpType
AX = mybir.AxisListType

SAME_ENGINE_SYNC = True
N_DMA_SEMS = 16

DEPTH = 4
DM = 1024
NT = 18
TCTX = 2
TOK = NT * 128
FFH = 2816
IN_W = 14912
O_HQ, O_HFF, O_HFB, O_HI, O_HG = 0, 1024, 2048, 3072, 4096
O_AQ, O_AK, O_AV = 5120, 6144, 6400
O_MZ, O_MX, O_MDT, O_GT = 6656, 8704, 11776, 11840
DN_ALPHA = (2 * DEPTH) ** 0.25
LN_EPS = 1e-5
RMS_EPS = 1e-6

C_ID, C_ONES, C_LE, C_GE, C_GT, C_LT, C_HDF, C_HDB, C_HMF, C_HMB, C_SEL, C_CI = (
    0, 128, 256, 384, 512, 640, 768, 896, 1024, 1152, 1280, 1664)
NCONST = 1664 + 4


def make_consts():
    a = np.arange(128)[:, None]
    b = np.arange(128)[None, :]
    bd = (a // 32) == (b // 32)
    blocks = [a == b, np.ones((128, 128), bool), a <= b, a >= b, a > b, a < b,
              bd & (a > b), bd & (a < b), bd & (a <= b), bd & (a >= b)]
    sel = np.zeros((128, 3 * 128), bool)
    for r in range(3):
        sel[r, r * 128:(r + 1) * 128] = True
    ci = (a // 32) == np.arange(4)[None, :]
    return np.concatenate(blocks + [sel, ci], axis=1).astype(np.float32)


def make_rope():
    rows = 2048 // 64
    row, col = np.meshgrid(np.arange(rows, dtype=np.float32), np.arange(64, dtype=np.float32), indexing="ij")
    n_pairs = 32
    inv_freq = (np.float32(10000.0) ** (-np.arange(n_pairs, dtype=np.float32) / np.float32(n_pairs))).astype(np.float32)
    ang = np.concatenate([row.reshape(-1, 1) * inv_freq, col.reshape(-1, 1) * inv_freq], axis=-1).astype(np.float32)
    return np.stack([np.cos(ang), np.sin(ang)]).astype(np.float32)


class Res:
    __slots__ = ("w", "r")

    def __init__(self):
        self.w = None
        self.r = {}


class Tile:
    def __init__(self, t, nres=1):
        self.t = t
        self.res = [Res() for _ in range(nres)]

    def __getitem__(self, k):
        return self.t[k]

    @property
    def r(self):
        return self.res[0]


def _res(items):
    out = []
    for it in items:
        if isinstance(it, Tile):
            out.extend(it.res)
        elif isinstance(it, Res):
            out.append(it)
        elif it is None:
            pass
        else:
            out.extend(_res(it))
    return out


class Prog:
    def __init__(self, nc, stack):
        self.nc = nc
        self.eng = {"pe": nc.tensor, "act": nc.scalar, "dve": nc.vector, "pool": nc.gpsimd, "sp": nc.sync}
        self.sem = {}
        for e in self.eng:
            self.sem[e] = stack.enter_context(nc.semaphore("s_" + e))
        self.dma_sems = {}
        for q in ("sp", "act", "pool"):
            self.dma_sems[q] = [("d", q, i) for i in range(N_DMA_SEMS)]
            for k in self.dma_sems[q]:
                self.sem[k] = stack.enter_context(nc.semaphore("d_%s_%d" % (q, k[2])))
        self.count = {k: 0 for k in self.sem}
        self.known = {e: {} for e in self.eng}
        self.vc = {}
        self.dma_rr = {q: 0 for q in self.dma_sems}
        self.n_instr = 0
        self.n_wait = 0
        self.stack = None
        self._dres = {}
        self._uid = 0

    def sb(self, shape, dtype, nres=1, name=None):
        self._uid += 1
        t = self.stack.enter_context(self.nc.sbuf_tensor("%s_%d" % (name or "t", self._uid), list(shape), dtype))
        return Tile(t, nres)

    def ps(self, shape, dtype=F32, name=None):
        self._uid += 1
        t = self.stack.enter_context(self.nc.psum_tensor("%s_%d" % (name or "p", self._uid), list(shape), dtype))
        return Tile(t)

    def dres(self, name, idx=0):
        k = (name, idx)
        if k not in self._dres:
            self._dres[k] = Res()
        return self._dres[k]

    def _deps(self, E, reads, writes):
        deps = {}
        for r in reads:
            if r.w is not None and deps.get(r.w[0], 0) < r.w[1]:
                deps[r.w[0]] = r.w[1]
        for w in writes:
            if w.w is not None and deps.get(w.w[0], 0) < w.w[1]:
                deps[w.w[0]] = w.w[1]
            for k, v in w.r.items():
                if deps.get(k, 0) < v:
                    deps[k] = v
        kn = self.known[E]
        out = []
        for k, v in deps.items():
            if k == E and (E == "pe" or E == "sp" or not SAME_ENGINE_SYNC):
                continue
            if kn.get(k, 0) >= v:
                continue
            out.append((k, v))
        return out

    def _wait(self, E, waits):
        eng = self.eng[E]
        kn = self.known[E]
        for k, v in waits:
            eng.wait_ge(self.sem[k], v)
            self.n_wait += 1
            snap = self.vc.get((k, v))
            if snap is not None:
                for kk, vv in snap.items():
                    if kn.get(kk, 0) < vv:
                        kn[kk] = vv
            if kn.get(k, 0) < v:
                kn[k] = v

    def _finish(self, E, ev, ins, inc, reads, writes):
        ins.then_inc(self.sem[ev[0]], inc)
        snap = dict(self.known[E])
        snap[ev[0]] = ev[1]
        self.vc[ev] = snap
        k, v = ev
        for r in reads:
            if r.r.get(k, 0) < v:
                r.r[k] = v
        for w in writes:
            w.w = ev
            w.r = {}
        self.n_instr += 1

    def op(self, E, fn, R=(), W=()):
        reads = _res(R)
        writes = _res(W)
        self._wait(E, self._deps(E, reads, writes))
        ins = fn(self.eng[E])
        self.count[E] += 1
        ev = (E, self.count[E])
        self._finish(E, ev, ins, 1, reads, writes)

    def dma(self, out, in_, R=(), W=(), q="sp", **kw):
        reads = _res(R)
        writes = _res(W)
        self._wait(q, self._deps(q, reads, writes))
        key = self.dma_sems[q][self.dma_rr[q] % N_DMA_SEMS]
        self.dma_rr[q] += 1
        if self.count[key] > 0 and self.known[q].get(key, 0) < self.count[key]:
            self._wait(q, [(key, self.count[key])])
        ins = self.eng[q].dma_start(out=out, in_=in_, **kw)
        self.count[key] += 16
        ev = (key, self.count[key])
        self._finish(q, ev, ins, 16, reads, writes)

    def barrier(self):
        targets = [(k, v) for k, v in self.count.items() if v > 0]
        for E in self.eng:
            waits = [(k, v) for k, v in targets if self.known[E].get(k, 0) < v and not (k == E and E == "pe")]
            self._wait(E, waits)

    def wait_all_on(self, E="sp"):
        waits = [(k, v) for k, v in self.count.items() if v > 0 and k != E and self.known[E].get(k, 0) < v]
        self._wait(E, waits)

    def mm(self, out, lhsT, rhs, start=True, stop=True, R=(), W=()):
        self.op("pe", lambda e: e.matmul(out, lhsT, rhs, start=start, stop=stop), R, W)

    def tr(self, out, in_, ident, R=(), W=()):
        self.op("pe", lambda e: e.transpose(out, in_, ident), R, W)

    def act(self, out, in_, func, R=(), W=(), bias=0.0, scale=1.0, accum_out=None):
        if accum_out is None:
            self.op("act", lambda e: e.activation(out, in_, func, bias=bias, scale=scale), R, W)
        else:
            self.op("act", lambda e: e.activation(out, in_, func, bias=bias, scale=scale, accum_out=accum_out), R, W)

    def tt(self, out, in0, in1, op, R=(), W=(), E="dve"):
        self.op(E, lambda e: e.tensor_tensor(out, in0, in1, op), R, W)

    def ts(self, out, in0, s1, s2, op0, op1=None, R=(), W=(), E="dve"):
        if op1 is None:
            self.op(E, lambda e: e.tensor_scalar(out, in0, s1, None, op0), R, W)
        else:
            self.op(E, lambda e: e.tensor_scalar(out, in0, s1, s2, op0, op1), R, W)

    def stt(self, out, in0, scalar, in1, op0, op1, R=(), W=()):
        self.op("dve", lambda e: e.scalar_tensor_tensor(out, in0, scalar, in1, op0, op1), R, W)

    def cp(self, out, in_, R=(), W=(), E="dve"):
        if E == "act":
            self.op("act", lambda e: e.activation(out, in_, AF.Copy), R, W)
        else:
            self.op(E, lambda e: e.tensor_copy(out, in_), R, W)


WSHAPES = [
    ("w_mod", (DM, 6 * DM)), ("b_mod", (6 * DM,)), ("w_in", (DM, IN_W)), ("hg_gnorm", (128,)),
    ("at_qnorm", (128,)), ("at_knorm", (128,)), ("mb_conv_w", (5, 3072)), ("mb_conv_b", (3072,)),
    ("mb_dt_bias", (2, 32)), ("mb_a_log", (2, 32)), ("mb_d", (32,)), ("mb_norm", (2048,)),
    ("w_br_hg", (1024, DM)), ("w_br_at", (1024, DM)), ("w_br_mb", (2048, DM)), ("w_out", (DM, DM)),
    ("ln1_g", (DM,)), ("ln1_b", (DM,)), ("w_ffn_in", (DM, 2 * FFH)), ("w_ffn_out", (FFH, DM)),
    ("ln2_g", (DM,)), ("ln2_b", (DM,)),
]


def build(depth=DEPTH, dbg=None):
    dbg = dbg or {}
    dump = dbg.get("dump", set())
    phases = dbg.get("phases", {"p0", "hg", "at", "mb", "merge", "ffn"})
    seqs = dbg.get("seqs", [0, 1])
    nc = bass.Bass("TRN2", target_bir_lowering=False)

    def din(name, shape):
        return nc.dram_tensor(name, list(shape), F32, kind="ExternalInput").ap()

    x_in = din("x", [2, 2048, DM])
    ctx_in = din("ctx", [2, 256, DM])
    c3T = din("c3T", [128, 8, 3])
    consts_d = din("consts", [128, NCONST])
    rope_d = din("rope", [2, 2048, 64])
    hg_lb_d = din("hg_lb", [4, 2, 1024])
    Wd = {n: din(n, (depth,) + s) for n, s in WSHAPES}
    out_d = nc.dram_tensor("out", [2, 2048, DM], F32, kind="ExternalOutput").ap()

    def dscr(name, shape, dt):
        kind = "ExternalOutput" if name in dump else "Internal"
        return nc.dram_tensor(name, list(shape), dt, kind=kind).ap()

    xres = dscr("xres", [2, NT, 128, DM], F32)
    x1s = dscr("x1s", [NT, 128, DM], F32)
    hg_of = dscr("hg_of", [NT, 128, 1024], F32)
    ohgT = dscr("ohgT", [NT, 128, 1024], BF16)
    oatT = dscr("oatT", [128, 8, TOK], BF16)
    ombT = dscr("ombT", [NT, 128, 2048], BF16)
    mb_xs = dscr("mb_xs", [NT, 128, 2048], BF16)
    mb_yf = dscr("mb_yf", [NT, 128, 2048], F32)
    ymTd = dscr("ymT", [NT, 128, 1024], BF16)
    lbs_d = dscr("lbs", [4, 2, 1024], F32)
    modrow_d = dscr("modrow", [3, 6 * DM], F32)
    ffn_act = dscr("ffn_act", [NT, 128, 22 * 128], BF16)

    with ExitStack() as top:
        P = Prog(nc, top)
        P.stack = top
        cst = P.sb([128, NCONST], F32, name="cst")
        P.dma(cst[:], consts_d, W=[cst])
        cstb = P.sb([128, 256], BF16, name="cstb")
        P.cp(cstb[:], cst[:, 0:256], R=[cst], W=[cstb])
        identF = cst[:, C_ID:C_ID + 128]
        onesF = cst[:, C_ONES:C_ONES + 128]
        identB = cstb[:, 0:128]
        onesB = cstb[:, 128:256]
        scT = P.sb([128, 8, 3], F32, name="scT")
        P.dma(scT[:], c3T, W=[scT])
        P.act(scT[:], scT[:], AF.Silu, R=[scT], W=[scT])
        hT = P.sb([128, 8, TOK], BF16, nres=NT, name="hT")

        def xsrc(l, s, t):
            if l == 0:
                return (ctx_in[s, t * 128:(t + 1) * 128, :] if t < TCTX else x_in[s, (t - TCTX) * 128:(t - TCTX + 1) * 128, :]), None
            return xres[s, t], P.dres("xres", (s, t))

        with ExitStack() as ph:
            P.stack = ph
            e = P.sb([128, 4, 16], F32, name="lb_e")
            P.dma(e[:].rearrange("p l (d j) -> p l d j", d=2),
                  hg_lb_d.rearrange("l d (p j) -> p l d j", p=128), W=[e])
            P.act(e[:], e[:], AF.Exp, R=[e], W=[e])
            s_ = P.sb([128, 16], F32, name="lb_s")
            P.tt(s_[:], e[:, 0, :], e[:, 1, :], ALU.add, R=[e], W=[s_])
            P.tt(s_[:], s_[:], e[:, 2, :], ALU.add, R=[e, s_], W=[s_])
            P.tt(s_[:], s_[:], e[:, 3, :], ALU.add, R=[e, s_], W=[s_])
            P.op("dve", lambda en: en.reciprocal(s_[:], s_[:]), R=[s_], W=[s_])
            P.tt(e[:], e[:], s_[:].rearrange("p (o j) -> p o j", o=1).broadcast_to([128, 4, 16]), ALU.mult, R=[e, s_], W=[e])
            lbt = P.sb([128, 4, 16], F32, name="lb_t")
            P.op("dve", lambda en: en.memset(lbt[:, 0, :], 0.0), W=[lbt])
            P.cp(lbt[:, 1, :], e[:, 1, :], R=[e], W=[lbt])
            P.tt(lbt[:, 2, :], lbt[:, 1, :], e[:, 2, :], ALU.add, R=[e, lbt], W=[lbt])
            P.tt(lbt[:, 3, :], lbt[:, 2, :], e[:, 3, :], ALU.add, R=[e, lbt], W=[lbt])
            P.dma(lbs_d.rearrange("l d (p j) -> p l d j", p=128),
                  lbt[:].rearrange("p l (d j) -> p l d j", d=2), R=[lbt], W=[P.dres("lbs")])
            P.barrier()
        P.stack = top

        for l in range(depth):
            last = (l == depth - 1) and not dbg.get('nolast', False)
            with ExitStack() as lay:
                P.stack = lay
                modA = P.sb([128, 48, 3], F32, name="modA")
                modB = P.sb([128, 48, 3], F32, name="modB")
                with ExitStack() as ph:
                    P.stack = ph
                    mrow = P.sb([3, 6 * DM], F32, name="mrow")
                    brow = P.sb([3, 6 * DM], F32, name="brow")
                    P.dma(brow[:], Wd["b_mod"][l:l + 1, :].broadcast_to([3, 6 * DM]), W=[brow])
                    wm = [P.sb([128, 8, 1536], F32, name="wm%d" % i) for i in range(2)]
                    pm = [P.ps([128, 512], name="pm%d" % i) for i in range(2)]
                    for blk in range(4):
                        w_ = wm[blk % 2]
                        P.dma(w_[:], Wd["w_mod"][l].rearrange("(kc p) n -> p kc n", p=128)[:, :, blk * 1536:(blk + 1) * 1536], W=[w_])
                        for j in range(3):
                            pj = pm[j % 2]
                            for kc in range(8):
                                P.mm(pj[0:3, :], scT[:, kc, :], w_[:, kc, j * 512:(j + 1) * 512],
                                     start=(kc == 0), stop=(kc == 7), R=[scT, w_], W=[pj])
                            c0 = blk * 1536 + j * 512
                            P.tt(mrow[:, c0:c0 + 512], pj[0:3, :], brow[:, c0:c0 + 512], ALU.add, R=[pj, brow], W=[mrow])
                    P.dma(modrow_d, mrow[:], R=[mrow], W=[P.dres("modrow")])
                    pT = P.ps([128, 512], name="pmT")
                    for j in range(48):
                        P.tr(pT[:, j * 3:(j + 1) * 3], mrow[:, j * 128:(j + 1) * 128], identF[0:3, 0:3], R=[mrow, cst], W=[pT])
                    P.cp(modA[:].rearrange("p j r -> p (j r)"), pT[:, 0:144], R=[pT], W=[modA])
                    P.ts(modB[:].rearrange("p j r -> p (j r)"), pT[:, 0:144], 1.0, None, ALU.add, R=[pT], W=[modB])
                    P.barrier()
                P.stack = lay

                for s in seqs:
                    with ExitStack() as sq:
                        P.stack = sq
                        def featT(l_, s_, tiles, src_fn, sc_chunk, sh_chunk, ph):
                            xt = [P.sb([128, DM], F32, name="p0x%d" % i) for i in range(2)]
                            pp = [P.ps([128, 512], name="p0p%d" % i) for i in range(2)]
                            for i, t in enumerate(tiles):
                                r = 2 if t < TCTX else s_
                                xb = xt[i % 2]
                                src, sres = src_fn(t)
                                P.dma(xb[:], src, R=[sres], W=[xb])
                                for half in range(2):
                                    pb = pp[half]
                                    for q in range(4):
                                        kc = half * 4 + q
                                        P.tr(pb[:, q * 128:(q + 1) * 128], xb[:, kc * 128:(kc + 1) * 128], identF, R=[xb, cst], W=[pb])
                                    for q in range(4):
                                        kc = half * 4 + q
                                        o = hT[:, kc, t * 128:(t + 1) * 128]
                                        i_ = pb[:, q * 128:(q + 1) * 128]
                                        sc = modB[:, sc_chunk + kc, r:r + 1]
                                        sh = modA[:, sh_chunk + kc, r:r + 1]
                                        if q % 2 == 0:
                                            P.ts(o, i_, sc, sh, ALU.mult, ALU.add, R=[pb, modA, modB], W=[hT.res[t]])
                                        else:
                                            P.act(o, i_, AF.Identity, R=[pb, modA, modB], W=[hT.res[t]], bias=sh, scale=sc)

                        if "p0" in phases:
                            with ExitStack() as ph:
                                P.stack = ph
                                featT(l, s, list(range(NT)), lambda t: xsrc(l, s, t), 8, 0, ph)
                                P.barrier()
                            P.stack = sq
                        if "hg" in phases:
                            with ExitStack() as ph:
                                P.stack = ph
                                phase_hg(P, nc, l, s, Wd, cst, cstb, hT, lbs_d, hg_of, ohgT)
                                P.barrier()
                            P.stack = sq
                        if "at" in phases:
                            with ExitStack() as ph:
                                P.stack = ph
                                phase_at(P, nc, l, s, Wd, cst, cstb, hT, rope_d, oatT)
                                P.barrier()
                            P.stack = sq
                        if "mb" in phases:
                            phase_mb(P, nc, l, s, Wd, cst, cstb, hT, mb_xs, mb_yf, ombT, sq)
                            P.stack = sq
                        if "merge" in phases:
                            phase_merge(P, nc, l, s, last, Wd, cst, cstb, hT, modA, modB, modrow_d, ohgT, oatT, ombT,
                                        ymTd, x1s, lambda t: xsrc(l, s, t), sq)
                            P.stack = sq
                        if "ffn" in phases:
                            phase_ffn(P, nc, l, s, last, Wd, cst, cstb, hT, modrow_d, x1s, xres, out_d, ffn_act)
                            P.stack = sq
                    P.stack = lay
            P.stack = top
        P.wait_all_on("sp")
        build.stats = (P.n_instr, P.n_wait)
    return nc


def _loadw(P, dst, src3, n, q="pool"):
    KC = src3.shape[1]
    step = 2
    for i, kc in enumerate(range(0, KC, step)):
        P.dma(dst[:, kc:kc + step, 0:n], src3[:, kc:kc + step, :], W=[dst.res[i % len(dst.res)]], q=q)


def _interleave(gens):
    gens = [g for g in gens if g is not None]
    while gens:
        for g in list(gens):
            try:
                next(g)
            except StopIteration:
                gens.remove(g)


def phase_hg(P, nc, l, s, Wd, cst, cstb, hT, lbs_d, hg_of, ohgT):
    identB = cstb[:, 0:128]
    onesF = cst[:, C_ONES:C_ONES + 128]
    CI = cst[:, C_CI:C_CI + 4]
    Wl = Wd["w_in"][l].rearrange("(kc p) n -> p kc n", p=128)
    Wq = P.sb([128, 8, 1024], BF16, nres=4, name="Wq")
    Wf = P.sb([128, 8, 1024], BF16, nres=4, name="Wf")
    Wi = P.sb([128, 8, 1024], BF16, nres=4, name="Wi")
    Wg = P.sb([128, 8, 1024], BF16, nres=4, name="Wg")
    _loadw(P, Wf, Wl[:, :, O_HFF:O_HFF + 1024], 1024)
    _loadw(P, Wq, Wl[:, :, O_HQ:O_HQ + 1024], 1024)
    _loadw(P, Wi, Wl[:, :, O_HI:O_HI + 1024], 1024)
    lbb = P.sb([128, 1024], F32, name="lbb")
    oml = P.sb([128, 1024], F32, name="oml")

    def load_lb(d):
        P.dma(lbb[:], lbs_d[l, d:d + 1, :].broadcast_to([128, 1024]), R=[P.dres("lbs")], W=[lbb])
        P.ts(oml[:], lbb[:], -1.0, 1.0, ALU.mult, ALU.add, R=[lbb], W=[oml])

    load_lb(0)
    gn = P.sb([128, 1], F32, name="gn")
    P.dma(gn[:], Wd["hg_gnorm"][l].rearrange("(p o) -> p o", o=1), W=[gn])
    lf = P.sb([128, 1024], F32, name="lf")
    kk = P.sb([128, 1024], F32, name="kk")
    ex = P.sb([128, 1024], F32, name="ex")
    exn = P.sb([128, 1024], F32, name="exn")
    qs = P.sb([128, 1024], F32, name="qs")
    Kt = [P.sb([128, 1024], BF16, name="Kt%d" % i) for i in range(2)]
    Qt = P.sb([128, 1024], BF16, name="Qt")
    Vt = [P.sb([128, 1024], BF16, name="Vt%d" % i) for i in range(2)]
    Ktm = [P.sb([128, 1024], BF16, name="Ktm%d" % i) for i in range(2)]
    KT = [P.sb([128, 1024], BF16, name="KT%d" % i) for i in range(2)]
    QT = [P.sb([128, 1024], BF16, name="QT%d" % i) for i in range(2)]
    ec = [P.sb([128, 32], F32, name="ec%d" % i) for i in range(2)]
    sg = [P.sb([128, 1024], F32, name="sg%d" % i) for i in range(2)]
    S = P.sb([128, 8, 128], F32, nres=8, name="S")
    Sb = P.sb([128, 8, 128], BF16, nres=8, name="Sb")
    ATm = P.sb([128, 8, 128], BF16, nres=8, name="ATm")
    ost = P.sb([128, 1024], F32, name="ost")
    ofl = P.sb([128, 1024], F32, name="ofl")
    osq = P.sb([128, 1024], F32, name="osq")
    rs = ofl
    ob = P.sb([128, 1024], BF16, name="ob")
    pA = P.ps([128, 1024], name="pA")
    pT = P.ps([128, 1024], BF16, name="pT")
    pC = P.ps([128, 512], name="pC")
    pCb = pC[:].bitcast(BF16)
    pO = P.ps([128, 1024], name="pO")
    pSl = P.ps([128, 1024], name="pSl")
    _r0, _r1 = Res(), Res()
    pSl.res = [_r0] * 4 + [_r1] * 4

    def proj(Wt, t):
        for half in range(2):
            for kc in range(8):
                P.mm(pA[:, half * 512:(half + 1) * 512], hT[:, kc, t * 128:(t + 1) * 128],
                     Wt[:, kc, half * 512:(half + 1) * 512], start=(kc == 0), stop=(kc == 7),
                     R=[hT.res[t], Wt], W=[pA])

    def prologue(t, d, b):
        D = cst[:, (C_HDF if d == 0 else C_HDB):(C_HDF if d == 0 else C_HDB) + 128]
        proj(Wf, t)
        P.act(lf[:], pA[:], AF.Sigmoid, R=[pA], W=[lf])
        P.act(kk[:], pA[:], AF.Sigmoid, R=[pA], W=[kk], scale=-1.0)
        yield
        proj(Wq, t)
        P.tt(lf[:], lf[:], oml[:], ALU.mult, R=[lf, oml], W=[lf])
        P.tt(lf[:], lf[:], lbb[:], ALU.add, R=[lf, lbb], W=[lf])
        P.act(lf[:], lf[:], AF.Ln, R=[lf], W=[lf])
        P.act(qs[:], pA[:], AF.Silu, R=[pA], W=[qs])
        P.tt(kk[:], kk[:], oml[:], ALU.mult, R=[kk, oml], W=[kk])
        yield
        proj(Wi, t)
        P.cp(Vt[b][:], pA[:], R=[pA], W=[Vt[b]], E="act")
        yield
        for half in range(2):
            P.mm(pA[:, half * 512:(half + 1) * 512], D, lf[:, half * 512:(half + 1) * 512], R=[cst, lf], W=[pA])
        for h in range(8):
            P.mm(pC[:, h * 4:(h + 1) * 4], lf[:, h * 128:(h + 1) * 128], CI, R=[lf, cst], W=[pC])
        P.act(ex[:], pA[:], AF.Exp, R=[pA], W=[ex])
        P.act(exn[:], pA[:], AF.Exp, R=[pA], W=[exn], scale=-1.0)
        P.act(ec[b][:], pC[:, 0:32], AF.Exp, R=[pC], W=[ec[b]])
        yield
        P.stt(Qt[:], qs[:], float(128 ** -0.5), exn[:], ALU.mult, ALU.mult, R=[qs, exn], W=[Qt])
        P.tt(Kt[b][:], kk[:], ex[:], ALU.mult, R=[kk, ex], W=[Kt[b]])
        P.ts(Ktm[b][:], Kt[b][:], CI[:, 3:4], None, ALU.mult, R=[Kt[b], cst], W=[Ktm[b]])
        yield
        for h in range(8):
            P.tr(pT[:, h * 128:(h + 1) * 128], Kt[b][:, h * 128:(h + 1) * 128], identB, R=[Kt[b], cstb], W=[pT])
        P.cp(KT[b][:], pT[:], R=[pT], W=[KT[b]])
        for h in range(8):
            P.tr(pCb[:, h * 128:(h + 1) * 128], Qt[:, h * 128:(h + 1) * 128], identB, R=[Qt, cstb], W=[pC])
        P.cp(QT[b][:], pCb, R=[pC], W=[QT[b]], E="act")
        yield
        if d == 1:
            for h in range(8):
                for kc in range(8):
                    P.mm(pA[:, h * 128:(h + 1) * 128], Wg[:, kc, h * 128:(h + 1) * 128], hT[:, kc, t * 128:(t + 1) * 128],
                         start=(kc == 0), stop=(kc == 7), R=[Wg, hT.res[t]], W=[pA])
            P.act(sg[b][:], pA[:], AF.Silu, R=[pA], W=[sg[b]])
            yield

    def scan(t, d, b):
        M = cst[:, (C_HMF if d == 0 else C_HMB):(C_HMF if d == 0 else C_HMB) + 128]
        if d == 1:
            P.dma(ofl[:], hg_of[t], R=[P.dres("hg_of", t)], W=[ofl])
        for h in range(8):
            hs = slice(h * 128, (h + 1) * 128)
            P.mm(pSl[:, hs], KT[b][:, hs], QT[b][:, hs], R=[KT[b], QT[b]], W=[pSl.res[h]])
        for h in range(8):
            hs = slice(h * 128, (h + 1) * 128)
            P.tt(ATm[:, h, :], pSl[:, hs], M, ALU.mult, R=[pSl.res[h], cst], W=[ATm.res[h]])
        yield
        for h in range(8):
            hs = slice(h * 128, (h + 1) * 128)
            P.op("pe", lambda e, hs=hs, h=h: e.matmul(pO[:, hs], Vt[b][:, hs], ATm[:, h, :], start=(h % 4 == 0), stop=False,
                                                       skip_group_check=True), R=[Vt[b], ATm.res[h]], W=[pO])
        chunks = [0, 1, 2, 3] if d == 0 else [3, 2, 1, 0]
        for ci, c in enumerate(chunks):
            for h in range(8):
                e_c = ec[b][:, h * 4 + c:h * 4 + c + 1]
                P.op("act", lambda e, h=h, e_c=e_c: e.activation(Sb[:, h, :], S[:, h, :], AF.Identity, scale=e_c),
                     R=[S.res[h], ec[b]], W=[Sb.res[h]])
            for h in range(8):
                hs = slice(h * 128, (h + 1) * 128)
                if c == 3:
                    P.mm(pSl[:, hs], Ktm[b][:, hs], Vt[b][:, hs], R=[Ktm[b], Vt[b]], W=[pSl.res[h]])
                else:
                    P.mm(pSl[:, hs], Kt[b][c * 32:(c + 1) * 32, hs], Vt[b][c * 32:(c + 1) * 32, hs], R=[Kt[b], Vt[b]], W=[pSl.res[h]])
            for h in range(8):
                cs = slice(h * 128 + c * 32, h * 128 + (c + 1) * 32)
                P.op("pe", lambda e, cs=cs, h=h: e.matmul(pO[:, cs], Sb[:, h, :], QT[b][:, cs], start=False, stop=(ci == 3),
                                                           skip_group_check=True), R=[Sb.res[h], QT[b]], W=[pO])
            for h in range(8):
                hs = slice(h * 128, (h + 1) * 128)
                e_c = ec[b][:, h * 4 + c:h * 4 + c + 1]
                P.stt(S[:, h, :], S[:, h, :], e_c, pSl[:, hs], ALU.mult, ALU.add, R=[S.res[h], ec[b], pSl.res[h]], W=[S.res[h]])
            yield
        if d == 0:
            P.cp(ost[:], pO[:], R=[pO], W=[ost], E="act")
            P.dma(hg_of[t], ost[:], R=[ost], W=[P.dres("hg_of", t)])
        else:
            P.tt(ost[:], pO[:], ofl[:], ALU.add, R=[pO, ofl], W=[ost])
            P.act(osq[:], ost[:], AF.Square, R=[ost], W=[osq])
            for half in range(2):
                P.mm(pSl[:, half * 512:(half + 1) * 512], onesF, osq[:, half * 512:(half + 1) * 512], R=[cst, osq], W=[pSl])
            yield
            P.act(rs[:], pSl[:], AF.Sqrt, R=[pSl], W=[rs], scale=1.0 / 128.0, bias=RMS_EPS)
            P.op("dve", lambda e: e.reciprocal(rs[:], rs[:]), R=[rs], W=[rs])
            P.stt(osq[:], ost[:], gn[:, 0:1], rs[:], ALU.mult, ALU.mult, R=[ost, gn, rs], W=[osq])
            P.tt(ob[:], osq[:], sg[b][:], ALU.mult, R=[osq, sg[b]], W=[ob])
            P.dma(ohgT[t], ob[:], R=[ob], W=[P.dres("ohgT", t)])
        yield

    order = [(t, 0) for t in range(NT)] + [(t, 1) for t in list(range(TCTX - 1, -1, -1)) + list(range(NT - 1, TCTX - 1, -1))]
    P.op("dve", lambda e: e.memset(S[:], 0.0), W=[S])
    _interleave([prologue(order[0][0], order[0][1], 0)])
    for n, (t, d) in enumerate(order):
        nxt = None
        if n + 1 < len(order):
            t2, d2 = order[n + 1]
            if d2 == 1 and d == 0:
                _loadw(P, Wf, Wl[:, :, O_HFB:O_HFB + 1024], 1024)
                _loadw(P, Wg, Wl[:, :, O_HG:O_HG + 1024], 1024)
                load_lb(1)
            nxt = prologue(t2, d2, (n + 1) % 2)
        _interleave([scan(t, d, n % 2), nxt])
        if n + 1 < len(order) and order[n + 1][1] == 1 and d == 0:
            P.op("dve", lambda e: e.memset(S[:], 0.0), W=[S])


def phase_at(P, nc, l, s, Wd, cst, cstb, hT, rope_d, oatT):
    identB = cstb[:, 0:128]
    onesB = cstb[:, 128:256]
    Wl = Wd["w_in"][l].rearrange("(kc p) n -> p kc n", p=128)
    Wa = P.sb([128, 8, 1536], BF16, nres=4, name="Wa")
    _loadw(P, Wa, Wl[:, :, O_AQ:O_AQ + 1536], 1536)
    QT = P.sb([128, 8, TOK], BF16, nres=NT, name="QTa")
    KTa = P.sb([128, 2, TOK], BF16, nres=NT, name="KTa")
    Va = P.sb([128, NT, 256], BF16, nres=NT, name="Va")
    with ExitStack() as sub:
        P.stack = sub
        cosT = P.sb([128, 16, 64], F32, name="cosT")
        sinT = P.sb([128, 16, 64], F32, name="sinT")
        P.dma(cosT[:], rope_d[0].rearrange("(t p) j -> p t j", p=128), W=[cosT])
        P.dma(sinT[:], rope_d[1].rearrange("(t p) j -> p t j", p=128), W=[sinT])
        wqk = P.sb([128, 10, 128], F32, name="wqk")
        P.dma(wqk[:, 0:8, :], Wd["at_qnorm"][l:l + 1, :].rearrange("o (h d) -> o h d", h=1).broadcast_to([128, 8, 128]), W=[wqk])
        P.dma(wqk[:, 8:10, :], Wd["at_knorm"][l:l + 1, :].rearrange("o (h d) -> o h d", h=1).broadcast_to([128, 2, 128]), W=[wqk])
        sqt = P.sb([128, 1280], F32, name="sqt")
        xn = P.sb([128, 1280], F32, name="xn")
        ss = P.sb([128, 10], F32, name="ss")
        t1 = P.sb([128, 10, 64], F32, name="t1")
        t2 = P.sb([128, 10, 64], F32, name="t2")
        xr = P.sb([128, 1280], BF16, name="xr")
        pQ = P.ps([128, 1536], name="pQ")
        pT = P.ps([128, 2048], BF16, name="pTa")
        for t in range(NT):
            tc = slice(t * 128, (t + 1) * 128)
            for j in range(3):
                for kc in range(8):
                    P.mm(pQ[:, j * 512:(j + 1) * 512], hT[:, kc, tc], Wa[:, kc, j * 512:(j + 1) * 512],
                         start=(kc == 0), stop=(kc == 7), R=[hT.res[t], Wa], W=[pQ])
            P.cp(Va[:, t, :], pQ[:, 1280:1536], R=[pQ], W=[Va.res[t]], E="act")
            P.act(sqt[:], pQ[:, 0:1280], AF.Square, R=[pQ], W=[sqt])
            P.op("dve", lambda e: e.tensor_reduce(ss[:], sqt[:].rearrange("p (h d) -> p h d", h=10), AX.X, ALU.add), R=[sqt], W=[ss])
            P.act(ss[:], ss[:], AF.Sqrt, R=[ss], W=[ss], scale=1.0 / 128.0, bias=RMS_EPS)
            P.op("dve", lambda e: e.reciprocal(ss[:], ss[:]), R=[ss], W=[ss])
            P.tt(xn[:].rearrange("p (h d) -> p h d", h=10), pQ[:, 0:1280].rearrange("p (h d) -> p h d", h=10),
                 ss[:].rearrange("p (h o) -> p h o", o=1).broadcast_to([128, 10, 128]), ALU.mult, R=[pQ, ss], W=[xn])
            P.tt(xn[:], xn[:], wqk[:].rearrange("p h d -> p (h d)"), ALU.mult, R=[xn, wqk], W=[xn])
            if t >= TCTX:
                tl = t - TCTX
                xv = xn[:].rearrange("p (h j two) -> p h j two", h=10, two=2)
                xo = xr[:].rearrange("p (h j two) -> p h j two", h=10, two=2)
                cb = cosT[:, tl:tl + 1, :].broadcast_to([128, 10, 64])
                sb_ = sinT[:, tl:tl + 1, :].broadcast_to([128, 10, 64])
                P.tt(t1[:], xv[:, :, :, 0], cb, ALU.mult, R=[xn, cosT], W=[t1])
                P.tt(t2[:], xv[:, :, :, 1], sb_, ALU.mult, R=[xn, sinT], W=[t2])
                P.tt(xo[:, :, :, 0], t1[:], t2[:], ALU.subtract, R=[t1, t2], W=[xr])
                P.tt(t1[:], xv[:, :, :, 0], sb_, ALU.mult, R=[xn, sinT], W=[t1])
                P.tt(t2[:], xv[:, :, :, 1], cb, ALU.mult, R=[xn, cosT], W=[t2])
                P.tt(xo[:, :, :, 1], t1[:], t2[:], ALU.add, R=[t1, t2], W=[xr])
            else:
                P.cp(xr[:], xn[:], R=[xn], W=[xr])
            for j in range(10):
                P.tr(pT[:, j * 128:(j + 1) * 128], xr[:, j * 128:(j + 1) * 128], identB, R=[xr, cstb], W=[pT])
            P.cp(QT[:, :, tc], pT[:, 0:1024].rearrange("p (h k) -> p h k", h=8), R=[pT], W=[QT.res[t]])
            P.cp(KTa[:, :, tc], pT[:, 1024:1280].rearrange("p (h k) -> p h k", h=2), R=[pT], W=[KTa.res[t]], E="act")
        P.barrier()
    with ExitStack() as sub:
        P.stack = sub
        pS = [P.ps([128, 512], name="pSa%d" % i) for i in range(2)]
        pO = P.ps([128, 512], name="pOa")
        pL = P.ps([128, 512], name="pLa")
        Pt = [P.sb([128, 512], BF16, name="Pt%d" % i) for i in range(2)]
        rl = P.sb([128, 512], F32, name="rl")
        oa = [P.sb([128, 512], BF16, name="oa%d" % i) for i in range(2)]
        blocks = [(0, 256, [0, 1])] + [(256 + qb * 512, 512, list(range(NT))) for qb in range(4)]
        k = 0
        sc = float(128 ** -0.5)
        for h in range(8):
            kv = h // 4
            for (q0, nq, kts) in blocks:
                qres = [QT.res[t] for t in range(q0 // 128, (q0 + nq) // 128)]
                for i, kt in enumerate(kts):
                    ps = pS[i % 2]
                    pt = Pt[i % 2]
                    P.mm(ps[:, 0:nq], KTa[:, kv, kt * 128:(kt + 1) * 128], QT[:, h, q0:q0 + nq], R=[KTa.res[kt]] + qres, W=[ps])
                    P.act(pt[:, 0:nq], ps[:, 0:nq], AF.Exp, R=[ps], W=[pt], scale=sc)
                    P.mm(pO[:, 0:nq], Va[:, kt, kv * 128:(kv + 1) * 128], pt[:, 0:nq], start=(i == 0), stop=(i == len(kts) - 1),
                         R=[Va.res[kt], pt], W=[pO])
                    P.mm(pL[:, 0:nq], onesB, pt[:, 0:nq], start=(i == 0), stop=(i == len(kts) - 1), R=[cstb, pt], W=[pL])
                P.op("dve", lambda e: e.reciprocal(rl[:, 0:nq], pL[:, 0:nq]), R=[pL], W=[rl])
                o_ = oa[k % 2]
                k += 1
                P.tt(o_[:, 0:nq], pO[:, 0:nq], rl[:, 0:nq], ALU.mult, R=[pO, rl], W=[o_])
                P.dma(oatT[:, h, q0:q0 + nq], o_[:, 0:nq], R=[o_], W=[P.dres("oatT", (h, q0))])
        P.barrier()


def phase_mb(P, nc, l, s, Wd, cst, cstb, hT, mb_xs, mb_yf, ombT, sq):
    identB = cstb[:, 0:128]
    identF = cst[:, C_ID:C_ID + 128]
    onesF = cst[:, C_ONES:C_ONES + 128]
    Wl = Wd["w_in"][l].rearrange("(kc p) n -> p kc n", p=128)
    with ExitStack() as ph:
        P.stack = ph
        BT = P.sb([128, 4, TOK], BF16, name="BT")
        CT = P.sb([128, 4, TOK], BF16, name="CT")
        with ExitStack() as sub:
            P.stack = sub
            Wx = P.sb([128, 8, 3072], BF16, nres=4, name="Wx")
            _loadw(P, Wx, Wl[:, :, O_MX:O_MX + 3072], 3072)
            cwr = P.sb([120, 128], F32, name="cwr")
            P.dma(cwr[:], Wd["mb_conv_w"][l].rearrange("j (cc p) -> (j cc) p", p=128), W=[cwr])
            cbr = P.sb([24, 128], F32, name="cbr")
            P.dma(cbr[:], Wd["mb_conv_b"][l].rearrange("(cc p) -> cc p", p=128), W=[cbr])
            cw = P.sb([128, 120], F32, name="cw")
            cbias = P.sb([128, 24], F32, name="cbias")
            pX = [P.ps([128, 512], name="pX%d" % i) for i in range(4)]
            pTt = P.ps([128, 1024], BF16, name="pTt")
            pW_ = P.ps([128, 512], name="pW_")
            P.tr(pW_[:, 0:120], cwr[:], identF[0:120, 0:120], R=[cwr, cst], W=[pW_])
            P.cp(cw[:], pW_[:, 0:120], R=[pW_], W=[cw])
            P.tr(pW_[:, 128:152], cbr[:], identF[0:24, 0:24], R=[cbr, cst], W=[pW_])
            P.cp(cbias[:], pW_[:, 128:152], R=[pW_], W=[cbias])
            xb = P.sb([128, 2052], F32, name="xb")
            acc = P.sb([128, 2048], F32, name="acc")
            ub = P.sb([128, 2048], BF16, name="ub")
            stg = [P.sb([128, 8, 128], BF16, name="stg%d" % i) for i in range(2)]
            P.op("pool", lambda e: e.memset(xb[:], 0.0), W=[xb])
            k = 0
            for cc in range(24):
                for (t0, ntile) in ((0, TCTX), (TCTX, NT - TCTX)):
                    N = ntile * 128
                    for b in range((N + 511) // 512):
                        nb = min(512, N - b * 512)
                        c0 = t0 * 128 + b * 512
                        tr_ = [hT.res[t] for t in range(c0 // 128, (c0 + nb) // 128)]
                        for kc in range(8):
                            P.mm(pX[b][:, 0:nb], Wx[:, kc, cc * 128:(cc + 1) * 128], hT[:, kc, c0:c0 + nb],
                                 start=(kc == 0), stop=(kc == 7), R=[Wx] + tr_, W=[pX[b]])
                        P.cp(xb[:, 2 + b * 512:2 + b * 512 + nb], pX[b][:, 0:nb], R=[pX[b]], W=[xb], E=("act" if b % 2 else "dve"))
                    P.op("pool", lambda e: e.memset(xb[:, N + 2:N + 4], 0.0), W=[xb])
                    P.ts(acc[:, 0:N], xb[:, 0:N], cw[:, cc:cc + 1], None, ALU.mult, R=[xb, cw], W=[acc])
                    for j in range(1, 5):
                        P.stt(acc[:, 0:N], xb[:, j:j + N], cw[:, j * 24 + cc:j * 24 + cc + 1], acc[:, 0:N], ALU.mult, ALU.add,
                              R=[xb, cw, acc], W=[acc])
                    if cc < 16:
                        dst, dres_ = ub[:, 0:N], ub
                    elif cc < 20:
                        dst, dres_ = BT[:, cc - 16, t0 * 128:t0 * 128 + N], BT
                    else:
                        dst, dres_ = CT[:, cc - 20, t0 * 128:t0 * 128 + N], CT
                    P.act(dst, acc[:, 0:N], AF.Silu, R=[acc, cbias], W=[dres_], bias=cbias[:, cc:cc + 1])
                    if cc < 16:
                        for g0 in range(0, ntile, 8):
                            ng = min(8, ntile - g0)
                            for j in range(ng):
                                P.tr(pTt[:, j * 128:(j + 1) * 128], ub[:, (g0 + j) * 128:(g0 + j + 1) * 128], identB, R=[ub, cstb], W=[pTt])
                            st = stg[k % 2]
                            k += 1
                            P.cp(st[:, 0:ng, :], pTt[:, 0:ng * 128].rearrange("p (j c) -> p j c", j=ng), R=[pTt], W=[st],
                                 E=("act" if k % 2 else "dve"))
                            ta = t0 + g0
                            P.dma(mb_xs[ta:ta + ng, :, cc * 128:(cc + 1) * 128].rearrange("t p c -> p t c"), st[:, 0:ng, :],
                                  R=[st], W=[P.dres("mb_xs", t) for t in range(ta, ta + ng)])
            P.barrier()
        with ExitStack() as sub:
            P.stack = sub
            Wz = P.sb([128, 8, 2112], BF16, nres=8, name="Wz")
            for i, kc in enumerate(range(0, 8, 2)):
                P.dma(Wz[:, kc:kc + 2, 0:2048], Wl[:, kc:kc + 2, O_MZ:O_MZ + 2048], W=[Wz.res[i]], q="pool")
                P.dma(Wz[:, kc:kc + 2, 2048:2112], Wl[:, kc:kc + 2, O_MDT:O_MDT + 64], W=[Wz.res[4 + i]], q="pool")
            dtb = P.sb([128, 64], F32, name="dtb")
            abc = P.sb([128, 64], F32, name="abc")
            dbc = P.sb([128, 32], F32, name="dbc")
            nrm = P.sb([128, 2048], F32, name="nrm")
            P.dma(dtb[:], Wd["mb_dt_bias"][l:l + 1].rearrange("o d h -> o (d h)").broadcast_to([128, 64]), W=[dtb])
            P.dma(abc[:], Wd["mb_a_log"][l:l + 1].rearrange("o d h -> o (d h)").broadcast_to([128, 64]), W=[abc])
            P.act(abc[:], abc[:], AF.Exp, R=[abc], W=[abc])
            P.ts(abc[:], abc[:], -1.0, None, ALU.mult, R=[abc], W=[abc])
            P.dma(dbc[:], Wd["mb_d"][l:l + 1, :].broadcast_to([128, 32]), W=[dbc])
            P.dma(nrm[:], Wd["mb_norm"][l:l + 1, :].broadcast_to([128, 2048]), W=[nrm])
            xs = P.sb([128, 2048], BF16, name="xs")
            xdt = P.sb([128, 2048], BF16, name="xdt")
            xdtw = P.sb([128, 2048], BF16, name="xdtw")
            dtv = P.sb([128, 64], F32, name="dtv")
            dA = P.sb([128, 32], F32, name="dA")
            e3 = P.sb([128, 96], F32, name="e3")
            cbm = P.sb([128, 512], F32, name="cbm")
            Btk = P.sb([128, 512], BF16, name="Btk")
            rh = [P.sb([128, 512], F32, name="rh%d" % i) for i in range(2)]
            es = [P.sb([128, 512], F32, name="es%d" % i) for i in range(2)]
            MT = P.sb([128, 32, 128], BF16, nres=8, name="MT")
            tmp = P.sb([128, 512], F32, name="tmpy")
            ysb = P.sb([128, 2048], F32, name="ysb")
            ST = P.sb([128, 2048], F32, nres=4, name="ST")
            STb = P.sb([128, 2048], BF16, nres=4, name="STb")
            yfl = P.sb([128, 2048], F32, name="yfl")
            ss4 = P.sb([128, 4], F32, name="ss4")
            oT = P.sb([128, 2048], BF16, name="oTm")
            pM = P.ps([128, 512], name="pM")
            pCB = P.ps([128, 512], name="pCB")
            pCBb = pCB[:].bitcast(BF16)
            pSG = P.ps([128, 512], name="pSG")
            pY = [P.ps([128, 512], name="pY%d" % i) for i in range(4)]
            pI = P.ps([128, 512], name="pI")
            v3 = lambda ap, h: ap.rearrange("p (h q) -> p h q", h=h)
            col = lambda ap: ap.rearrange("p (h o) -> p h o", o=1)
            cnt = [0]

            def mb_tile(t, d):
                tc = slice(t * 128, (t + 1) * 128)
                TRI = cst[:, (C_LE if d == 0 else C_GE):(C_LE if d == 0 else C_GE) + 128]
                STR = cst[:, (C_GT if d == 0 else C_LT):(C_GT if d == 0 else C_LT) + 128]
                P.dma(xs[:], mb_xs[t], R=[P.dres("mb_xs", t)], W=[xs])
                if d == 1:
                    P.dma(yfl[:], mb_yf[t], R=[P.dres("mb_yf", t)], W=[yfl])
                for kc in range(8):
                    P.mm(pM[:, 0:64], hT[:, kc, tc], Wz[:, kc, 2048:2112], start=(kc == 0), stop=(kc == 7), R=[hT.res[t], Wz], W=[pM])
                P.tt(dtv[:], pM[:, 0:64], dtb[:], ALU.add, R=[pM, dtb], W=[dtv])
                P.act(dtv[:], dtv[:], AF.Exp, R=[dtv], W=[dtv])
                P.act(dtv[:], dtv[:], AF.Ln, R=[dtv], W=[dtv], bias=1.0)
                P.tt(dA[:], dtv[:, d * 32:(d + 1) * 32], abc[:, d * 32:(d + 1) * 32], ALU.mult, R=[dtv, abc], W=[dA])
                P.tt(v3(xdt[:], 32), v3(xs[:], 32), col(dtv[:, d * 32:(d + 1) * 32]).broadcast_to([128, 32, 64]), ALU.mult,
                     R=[xs, dtv], W=[xdt])
                P.mm(pM[:, 64:96], TRI, dA[:], R=[cst, dA], W=[pM])
                P.mm(pM[:, 96:128], STR, dA[:], R=[cst, dA], W=[pM])
                P.mm(pM[:, 128:160], onesF, dA[:], R=[cst, dA], W=[pM])
                P.act(e3[:], pM[:, 64:160], AF.Exp, R=[pM], W=[e3])
                P.tt(v3(xdtw[:], 32), v3(xdt[:], 32), col(e3[:, 32:64]).broadcast_to([128, 32, 64]), ALU.mult,
                     R=[xdt, e3], W=[xdtw])
                for g in range(4):
                    P.mm(pCB[:, g * 128:(g + 1) * 128], BT[:, g, tc], CT[:, g, tc], R=[BT, CT], W=[pCB])
                P.tt(v3(cbm[:], 4), v3(pCB[:], 4), TRI.rearrange("p (o t) -> p o t", o=1).broadcast_to([128, 4, 128]), ALU.mult,
                     R=[pCB, cst], W=[cbm])
                for g in range(4):
                    P.tr(pCBb[:, g * 128:(g + 1) * 128], BT[:, g, tc], identB, R=[BT, cstb], W=[pCB])
                P.cp(Btk[:], pCBb[:, 0:512], R=[pCB], W=[Btk], E="act")
                for hb in range(8):
                    h0 = hb * 4
                    g = hb // 2
                    i = cnt[0] % 2
                    cnt[0] += 1
                    P.tt(v3(rh[i][:], 4), TRI.rearrange("p (o t) -> p o t", o=1).broadcast_to([128, 4, 128]),
                         col(dA[:, h0:h0 + 4]).broadcast_to([128, 4, 128]), ALU.mult, R=[cst, dA], W=[rh[i]])
                    P.mm(pSG[:], STR, rh[i][:], R=[cst, rh[i]], W=[pSG])
                    P.act(es[i][:], pSG[:], AF.Exp, R=[pSG], W=[es[i]])
                    P.tt(MT[:, h0:h0 + 4, :], v3(es[i][:], 4),
                         cbm[:, g * 128:(g + 1) * 128].rearrange("p (o t) -> p o t", o=1).broadcast_to([128, 4, 128]), ALU.mult,
                         R=[es[i], cbm], W=[MT.res[hb]])
                for h in range(32):
                    P.mm(pY[h // 8][:, (h % 8) * 64:(h % 8 + 1) * 64], MT[:, h, :], xdt[:, h * 64:(h + 1) * 64],
                         R=[MT.res[h // 4], xdt], W=[pY[h // 8]])
                for g in range(4):
                    gs = slice(g * 512, (g + 1) * 512)
                    P.mm(pI[:], CT[:, g, tc], STb[:, gs], R=[CT, STb.res[g]], W=[pI])
                    P.tt(v3(tmp[:], 8), v3(pI[:], 8), col(e3[:, g * 8:(g + 1) * 8]).broadcast_to([128, 8, 64]), ALU.mult,
                         R=[pI, e3], W=[tmp])
                    P.tt(ysb[:, gs], tmp[:], pY[g][:], ALU.add, R=[tmp, pY[g]], W=[ysb])
                for g in range(4):
                    gs = slice(g * 512, (g + 1) * 512)
                    P.mm(pI[:], Btk[:, g * 128:(g + 1) * 128], xdtw[:, gs], R=[Btk, xdtw], W=[pI])
                    P.tt(v3(ST[:, gs], 8), v3(ST[:, gs], 8), col(e3[:, 64 + g * 8:64 + (g + 1) * 8]).broadcast_to([128, 8, 64]), ALU.mult,
                         R=[ST.res[g], e3], W=[ST.res[g]])
                    P.tt(ST[:, gs], ST[:, gs], pI[:], ALU.add, R=[ST.res[g], pI], W=[ST.res[g]])
                    P.cp(STb[:, gs], ST[:, gs], R=[ST.res[g]], W=[STb.res[g]], E="act")
                if d == 0:
                    P.dma(mb_yf[t], ysb[:], R=[ysb], W=[P.dres("mb_yf", t)])
                    return
                P.tt(ysb[:], ysb[:], yfl[:], ALU.add, R=[ysb, yfl], W=[ysb])
                P.tt(v3(yfl[:], 32), v3(xs[:], 32), col(dbc[:]).broadcast_to([128, 32, 64]), ALU.mult, R=[xs, dbc, yfl], W=[yfl])
                P.tt(ysb[:], ysb[:], yfl[:], ALU.add, R=[ysb, yfl], W=[ysb])
                for j in range(4):
                    for kc in range(8):
                        P.mm(pY[j][:], hT[:, kc, tc], Wz[:, kc, j * 512:(j + 1) * 512], start=(kc == 0), stop=(kc == 7),
                             R=[hT.res[t], Wz], W=[pY[j]])
                    P.act(yfl[:, j * 512:(j + 1) * 512], pY[j][:], AF.Silu, R=[pY[j]], W=[yfl])
                P.tt(ysb[:], ysb[:], yfl[:], ALU.mult, R=[ysb, yfl], W=[ysb])
                P.act(yfl[:], ysb[:], AF.Square, R=[ysb], W=[yfl])
                P.op("dve", lambda e: e.tensor_reduce(ss4[:], v3(yfl[:], 4), AX.X, ALU.add), R=[yfl], W=[ss4])
                P.act(ss4[:], ss4[:], AF.Sqrt, R=[ss4], W=[ss4], scale=1.0 / 512.0, bias=RMS_EPS)
                P.op("dve", lambda e: e.reciprocal(ss4[:], ss4[:]), R=[ss4], W=[ss4])
                P.tt(v3(ysb[:], 4), v3(ysb[:], 4), col(ss4[:]).broadcast_to([128, 4, 512]), ALU.mult, R=[ysb, ss4], W=[ysb])
                P.tt(ysb[:], ysb[:], nrm[:], ALU.mult, R=[ysb, nrm], W=[ysb])
                for cc in range(16):
                    P.tr(pY[cc // 4][:, (cc % 4) * 128:(cc % 4 + 1) * 128], ysb[:, cc * 128:(cc + 1) * 128], identF, R=[ysb, cst], W=[pY[cc // 4]])
                for j in range(4):
                    P.cp(oT[:, j * 512:(j + 1) * 512], pY[j][:], R=[pY[j]], W=[oT], E=("act" if j % 2 else "dve"))
                P.dma(ombT[t], oT[:], R=[oT], W=[P.dres("ombT", t)])

            def reset_state():
                P.op("dve", lambda e: e.memset(ST[:], 0.0), W=[ST])
                P.op("pool", lambda e: e.memset(STb[:], 0.0), W=[STb])

            reset_state()
            for t in range(NT):
                mb_tile(t, 0)
            reset_state()
            for t in list(range(TCTX - 1, -1, -1)) + list(range(NT - 1, TCTX - 1, -1)):
                mb_tile(t, 1)
            P.barrier()


def _layernorm(P, out, z, zr, lng, lnb, st, mv):
    for c in range(2):
        P.op("dve", lambda e: e.bn_stats(st[:, c * 6:(c + 1) * 6], z[:, c * 512:(c + 1) * 512]), R=[zr], W=[st])
    P.op("dve", lambda e: e.bn_aggr(mv[:, 0:2], st[:]), R=[st], W=[mv])
    P.act(mv[:, 2:3], mv[:, 1:2], AF.Sqrt, R=[mv], W=[mv], bias=LN_EPS)
    P.op("dve", lambda e: e.reciprocal(mv[:, 2:3], mv[:, 2:3]), R=[mv], W=[mv])
    P.ts(out, z, mv[:, 0:1], mv[:, 2:3], ALU.subtract, ALU.mult, R=[zr, mv], W=[zr])
    P.tt(out, out, lng[:], ALU.mult, R=[zr, lng], W=[zr])
    P.tt(out, out, lnb[:], ALU.add, R=[zr, lnb], W=[zr])


def _gate_bcast(P, cst, modrow_d, c0, s, pG, gbc):
    mr = P.sb([3, 1024], F32, name="mr")
    P.dma(mr[:], modrow_d[:, c0:c0 + 1024], R=[P.dres("modrow")], W=[mr])
    for ri, r in enumerate((s, 2)):
        for half in range(2):
            P.mm(pG[:, half * 512:(half + 1) * 512], cst[0:3, C_SEL + r * 128:C_SEL + (r + 1) * 128], mr[0:3, half * 512:(half + 1) * 512],
                 R=[cst, mr], W=[pG])
        P.cp(gbc[:, ri, :], pG[:], R=[pG], W=[gbc])


def phase_merge(P, nc, l, s, last, Wd, cst, cstb, hT, modA, modB, modrow_d, ohgT, oatT, ombT, ymTd, x1s, xsrc_fn, sq):
    identB = cstb[:, 0:128]
    identF = cst[:, C_ID:C_ID + 128]
    tiles = list(range(TCTX, NT)) if last else list(range(NT))
    Wl = Wd["w_in"][l].rearrange("(kc p) n -> p kc n", p=128)
    with ExitStack() as ph:
        P.stack = ph
        Wgt = P.sb([128, 8, 3072], BF16, nres=4, name="Wgt")
        Wbh = P.sb([128, 8, 1024], BF16, nres=4, name="Wbh")
        Wba = P.sb([128, 8, 1024], BF16, nres=4, name="Wba")
        Wbm = P.sb([128, 16, 1024], BF16, nres=8, name="Wbm")
        _loadw(P, Wgt, Wl[:, :, O_GT:O_GT + 3072], 3072)
        _loadw(P, Wbh, Wd["w_br_hg"][l].rearrange("(kc p) n -> p kc n", p=128), 1024)
        _loadw(P, Wba, Wd["w_br_at"][l].rearrange("(kc p) n -> p kc n", p=128), 1024)
        _loadw(P, Wbm, Wd["w_br_mb"][l].rearrange("(kc p) n -> p kc n", p=128), 1024)
        oh = [P.sb([128, 1024], BF16, name="oh%d" % i) for i in range(2)]
        oa = [P.sb([128, 8, 128], BF16, name="oam%d" % i) for i in range(2)]
        om = [P.sb([128, 2048], BF16, name="om%d" % i) for i in range(2)]
        sgt = P.sb([128, 1024], F32, name="sgt")
        ym = P.sb([128, 1024], F32, name="ym")
        tmp = P.sb([128, 1024], F32, name="tmpm")
        ymb = P.sb([128, 1024], BF16, name="ymb")
        ymT = [P.sb([128, 1024], BF16, name="ymT%d" % i) for i in range(2)]
        pG = P.ps([128, 1024], name="pGm")
        pB = P.ps([128, 1024], name="pBm")
        pT = P.ps([128, 1024], BF16, name="pTm")
        for i, t in enumerate(tiles):
            tc = slice(t * 128, (t + 1) * 128)
            oh_, oa_, om_ = oh[i % 2], oa[i % 2], om[i % 2]
            P.dma(oh_[:], ohgT[t], R=[P.dres("ohgT", t)], W=[oh_])
            P.dma(oa_[:], oatT[:, :, tc], R=[P.dres("oatT", (h, q0)) for h in range(8) for q0 in (0, 256, 768, 1280, 1792)], W=[oa_])
            P.dma(om_[:], ombT[t], R=[P.dres("ombT", t)], W=[om_])
            srcs = [(lambda kc, o=oh_: o[:, kc * 128:(kc + 1) * 128], Wbh, 8, oh_),
                    (lambda kc, o=oa_: o[:, kc, :], Wba, 8, oa_),
                    (lambda kc, o=om_: o[:, kc * 128:(kc + 1) * 128], Wbm, 16, om_)]
            for b, (of, Wb, nk, ot) in enumerate(srcs):
                for half in range(2):
                    for kc in range(8):
                        P.mm(pG[:, half * 512:(half + 1) * 512], hT[:, kc, tc], Wgt[:, kc, b * 1024 + half * 512:b * 1024 + (half + 1) * 512],
                             start=(kc == 0), stop=(kc == 7), R=[hT.res[t], Wgt], W=[pG])
                P.act(sgt[:], pG[:], AF.Sigmoid, R=[pG], W=[sgt])
                for half in range(2):
                    for kc in range(nk):
                        P.mm(pB[:, half * 512:(half + 1) * 512], of(kc), Wb[:, kc, half * 512:(half + 1) * 512],
                             start=(kc == 0), stop=(kc == nk - 1), R=[ot, Wb], W=[pB])
                if b == 0:
                    P.tt(ym[:], pB[:], sgt[:], ALU.mult, R=[pB, sgt], W=[ym])
                else:
                    P.tt(tmp[:], pB[:], sgt[:], ALU.mult, R=[pB, sgt], W=[tmp])
                    if b == 1:
                        P.tt(ym[:], ym[:], tmp[:], ALU.add, R=[ym, tmp], W=[ym])
                    else:
                        P.tt(ymb[:], ym[:], tmp[:], ALU.add, R=[ym, tmp], W=[ymb])
            for kc in range(8):
                P.tr(pT[:, kc * 128:(kc + 1) * 128], ymb[:, kc * 128:(kc + 1) * 128], identB, R=[ymb, cstb], W=[pT])
            yT = ymT[i % 2]
            P.cp(yT[:], pT[:], R=[pT], W=[yT], E="act")
            P.dma(ymTd[t], yT[:], R=[yT], W=[P.dres("ymT", t)])
        P.barrier()
    with ExitStack() as ph:
        P.stack = ph
        Wo = P.sb([128, 8, 1024], BF16, nres=4, name="Wo")
        _loadw(P, Wo, Wd["w_out"][l].rearrange("(kc p) n -> p kc n", p=128), 1024)
        pW = P.ps([128, 1024], name="pWm")
        pp = [P.ps([128, 512], name="pp%d" % i) for i in range(2)]
        gbc = P.sb([128, 2, 1024], F32, name="gbc")
        _gate_bcast(P, cst, modrow_d, 2048, s, pW, gbc)
        lng = P.sb([128, 1024], F32, name="lng")
        lnb = P.sb([128, 1024], F32, name="lnb")
        P.dma(lng[:], Wd["ln1_g"][l:l + 1, :].broadcast_to([128, 1024]), W=[lng])
        P.dma(lnb[:], Wd["ln1_b"][l:l + 1, :].broadcast_to([128, 1024]), W=[lnb])
        yT = [P.sb([128, 1024], BF16, name="yTb%d" % i) for i in range(2)]
        xt = [P.sb([128, 1024], F32, name="xtb%d" % i) for i in range(2)]
        z = [P.sb([128, 1024], F32, name="zb%d" % i) for i in range(2)]
        st = P.sb([128, 12], F32, name="st")
        mv = P.sb([128, 4], F32, name="mv")
        for i, t in enumerate(tiles):
            tc = slice(t * 128, (t + 1) * 128)
            r = 2 if t < TCTX else s
            ri = 1 if t < TCTX else 0
            yT_, xt_, z_ = yT[i % 2], xt[i % 2], z[i % 2]
            P.dma(yT_[:], ymTd[t], R=[P.dres("ymT", t)], W=[yT_])
            src, sres = xsrc_fn(t)
            P.dma(xt_[:], src, R=[sres], W=[xt_])
            for half in range(2):
                for kc in range(8):
                    P.mm(pW[:, half * 512:(half + 1) * 512], yT_[:, kc * 128:(kc + 1) * 128], Wo[:, kc, half * 512:(half + 1) * 512],
                         start=(kc == 0), stop=(kc == 7), R=[yT_, Wo], W=[pW])
            P.tt(z_[:], pW[:], gbc[:, ri, :], ALU.mult, R=[pW, gbc], W=[z_])
            P.stt(z_[:], xt_[:], float(DN_ALPHA), z_[:], ALU.mult, ALU.add, R=[xt_, z_], W=[z_])
            _layernorm(P, z_[:], z_[:], z_, lng, lnb, st, mv)
            P.dma(x1s[t], z_[:], R=[z_], W=[P.dres("x1s", t)])
            for half in range(2):
                pb = pp[half]
                for q in range(4):
                    kc = half * 4 + q
                    P.tr(pb[:, q * 128:(q + 1) * 128], z_[:, kc * 128:(kc + 1) * 128], identF, R=[z_, cst], W=[pb])
                for q in range(4):
                    kc = half * 4 + q
                    o = hT[:, kc, tc]
                    i_ = pb[:, q * 128:(q + 1) * 128]
                    sc = modB[:, 32 + kc, r:r + 1]
                    sh = modA[:, 24 + kc, r:r + 1]
                    if q % 2 == 0:
                        P.ts(o, i_, sc, sh, ALU.mult, ALU.add, R=[pb, modA, modB], W=[hT.res[t]])
                    else:
                        P.act(o, i_, AF.Identity, R=[pb, modA, modB], W=[hT.res[t]], bias=sh, scale=sc)
        P.barrier()


def phase_ffn(P, nc, l, s, last, Wd, cst, cstb, hT, modrow_d, x1s, xres, out_d, ffn_act):
    blocks = ([] if last else [(0, TCTX)]) + [(TCTX + 4 * i, 4) for i in range(4)]
    with ExitStack() as ph:
        P.stack = ph
        W1 = P.sb([128, 8, 2 * FFH], BF16, nres=4, name="W1")
        _loadw(P, W1, Wd["w_ffn_in"][l].rearrange("(kc p) n -> p kc n", p=128), 2 * FFH)
        pGt = [P.ps([128, 512], name="pGt%d" % i) for i in range(2)]
        pUp = [P.ps([128, 512], name="pUp%d" % i) for i in range(2)]
        sgu = [P.sb([128, 512], F32, name="sgu%d" % i) for i in range(2)]
        actT = [P.sb([128, 22, 512], BF16, name="actT%d" % i) for i in range(2)]
        for bi, (t0, ntile) in enumerate(blocks):
            N = ntile * 128
            cols = slice(t0 * 128, t0 * 128 + N)
            tr_ = [hT.res[t] for t in range(t0, t0 + ntile)]
            aT = actT[bi % 2]
            for j in range(22):
                pg, pu, sg_ = pGt[j % 2], pUp[j % 2], sgu[j % 2]
                for kc in range(8):
                    P.mm(pg[:, 0:N], W1[:, kc, j * 128:(j + 1) * 128], hT[:, kc, cols], start=(kc == 0), stop=(kc == 7), R=[W1] + tr_, W=[pg])
                for kc in range(8):
                    P.mm(pu[:, 0:N], W1[:, kc, FFH + j * 128:FFH + (j + 1) * 128], hT[:, kc, cols], start=(kc == 0), stop=(kc == 7),
                         R=[W1] + tr_, W=[pu])
                P.act(sg_[:, 0:N], pg[:, 0:N], AF.Silu, R=[pg], W=[sg_])
                P.tt(aT[:, j, 0:N], sg_[:, 0:N], pu[:, 0:N], ALU.mult, R=[sg_, pu], W=[aT])
            for ti in range(ntile):
                t = t0 + ti
                P.dma(ffn_act[t].rearrange("p (j k) -> p j k", j=22), aT[:, :, ti * 128:(ti + 1) * 128], R=[aT], W=[P.dres("ffn_act", t)])
        P.barrier()
    with ExitStack() as ph:
        P.stack = ph
        W2 = P.sb([128, 22, 1024], BF16, nres=11, name="W2")
        _loadw(P, W2, Wd["w_ffn_out"][l].rearrange("(j p) n -> p j n", p=128), 1024)
        pF = [P.ps([128, 1024], name="pF%d" % i) for i in range(2)]
        gbc = P.sb([128, 2, 1024], F32, name="gbc2")
        _gate_bcast(P, cst, modrow_d, 5120, s, pF[0], gbc)
        lng = P.sb([128, 1024], F32, name="lng2")
        lnb = P.sb([128, 1024], F32, name="lnb2")
        P.dma(lng[:], Wd["ln2_g"][l:l + 1, :].broadcast_to([128, 1024]), W=[lng])
        P.dma(lnb[:], Wd["ln2_b"][l:l + 1, :].broadcast_to([128, 1024]), W=[lnb])
        aT = [P.sb([128, 22, 128], BF16, name="aTl%d" % i) for i in range(2)]
        xt = [P.sb([128, 1024], F32, name="x1l%d" % i) for i in range(2)]
        z = [P.sb([128, 1024], F32, name="z2%d" % i) for i in range(2)]
        st = P.sb([128, 12], F32, name="st2")
        mv = P.sb([128, 4], F32, name="mv2")
        tiles = [t0 + ti for (t0, ntile) in blocks for ti in range(ntile)]
        for i, t in enumerate(tiles):
            ri = 1 if t < TCTX else 0
            a_, xt_, z_, pF_ = aT[i % 2], xt[i % 2], z[i % 2], pF[i % 2]
            P.dma(a_[:], ffn_act[t].rearrange("p (j k) -> p j k", j=22), R=[P.dres("ffn_act", t)], W=[a_])
            P.dma(xt_[:], x1s[t], R=[P.dres("x1s", t)], W=[xt_])
            for half in range(2):
                for j in range(22):
                    P.mm(pF_[:, half * 512:(half + 1) * 512], a_[:, j, :], W2[:, j, half * 512:(half + 1) * 512],
                         start=(j == 0), stop=(j == 21), R=[a_, W2], W=[pF_])
            P.tt(z_[:], pF_[:], gbc[:, ri, :], ALU.mult, R=[pF_, gbc], W=[z_])
            P.stt(z_[:], xt_[:], float(DN_ALPHA), z_[:], ALU.mult, ALU.add, R=[xt_, z_], W=[z_])
            _layernorm(P, z_[:], z_[:], z_, lng, lnb, st, mv)
            if last:
                P.dma(out_d[s, (t - TCTX) * 128:(t - TCTX + 1) * 128, :], z_[:], R=[z_], W=[P.dres("out", (s, t))])
            else:
                P.dma(xres[s, t], z_[:], R=[z_], W=[P.dres("xres", (s, t))])
        P.barrier()


def kernel(**inputs):
    inp = {k: np.asarray(v) for k, v in inputs.items()}
    nc = build(DEPTH)
    consts = make_consts()
    rope = make_rope()
    maps = []
    for core in range(8):
        b0 = 2 * core
        c3 = np.stack([inp["c"][b0], inp["c"][b0 + 1], inp["c_ctx"]])
        c3T = np.ascontiguousarray(c3.reshape(3, 8, 128).transpose(2, 1, 0))
        m = {"x": np.ascontiguousarray(inp["x"][b0:b0 + 2]), "ctx": np.ascontiguousarray(inp["ctx"][b0:b0 + 2]),
             "c3T": c3T, "consts": consts, "rope": rope, "hg_lb": inp["hg_lb"]}
        for n, _ in WSHAPES:
            m[n] = inp[n]
        maps.append(m)
    res = run_bass_kernel_spmd(nc, maps, core_ids=list(range(8)))
    out = np.concatenate([np.asarray(r["out"]) for r in res.results], axis=0)
    return out.astype(np.float32)
```

```python
from contextlib import ExitStack
import numpy as np
import concourse.bass as bass
import concourse.mybir as mybir
from concourse.bass_utils import run_bass_kernel_spmd

F32 = mybir.dt.float32
BF16 = mybir.dt.bfloat16
AF = mybir.ActivationFunctionType
ALU = mybir.AluOpType
AX = mybir.AxisListType

SAME_ENGINE_SYNC = True
N_DMA_SEMS = 16

DEPTH = 4
DM = 1024
NT = 18
TCTX = 2
TOK = NT * 128
FFH = 2816
IN_W = 14912
O_HQ, O_HFF, O_HFB, O_HI, O_HG = 0, 1024, 2048, 3072, 4096
O_AQ, O_AK, O_AV = 5120, 6144, 6400
O_MZ, O_MX, O_MDT, O_GT = 6656, 8704, 11776, 11840
DN_ALPHA = (2 * DEPTH) ** 0.25
LN_EPS = 1e-5
RMS_EPS = 1e-6

C_ID, C_ONES, C_LE, C_GE, C_GT, C_LT, C_HDF, C_HDB, C_HMF, C_HMB, C_SEL, C_CI = (
    0, 128, 256, 384, 512, 640, 768, 896, 1024, 1152, 1280, 1664)
NCONST = 1664 + 4


def make_consts():
    a = np.arange(128)[:, None]
    b = np.arange(128)[None, :]
    bd = (a // 32) == (b // 32)
    blocks = [a == b, np.ones((128, 128), bool), a <= b, a >= b, a > b, a < b,
              bd & (a > b), bd & (a < b), bd & (a <= b), bd & (a >= b)]
    sel = np.zeros((128, 3 * 128), bool)
    for r in range(3):
        sel[r, r * 128:(r + 1) * 128] = True
    ci = (a // 32) == np.arange(4)[None, :]
    return np.concatenate(blocks + [sel, ci], axis=1).astype(np.float32)


def make_rope():
    rows = 2048 // 64
    row, col = np.meshgrid(np.arange(rows, dtype=np.float32), np.arange(64, dtype=np.float32), indexing="ij")
    n_pairs = 32
    inv_freq = (np.float32(10000.0) ** (-np.arange(n_pairs, dtype=np.float32) / np.float32(n_pairs))).astype(np.float32)
    ang = np.concatenate([row.reshape(-1, 1) * inv_freq, col.reshape(-1, 1) * inv_freq], axis=-1).astype(np.float32)
    return np.stack([np.cos(ang), np.sin(ang)]).astype(np.float32)


class Res:
    __slots__ = ("w", "r")

    def __init__(self):
        self.w = None
        self.r = {}


class Tile:
    def __init__(self, t, nres=1):
        self.t = t
        self.res = [Res() for _ in range(nres)]

    def __getitem__(self, k):
        return self.t[k]

    @property
    def r(self):
        return self.res[0]


def _res(items):
    out = []
    for it in items:
        if isinstance(it, Tile):
            out.extend(it.res)
        elif isinstance(it, Res):
            out.append(it)
        elif it is None:
            pass
        else:
            out.extend(_res(it))
    return out


class Prog:
    def __init__(self, nc, stack):
        self.nc = nc
        self.eng = {"pe": nc.tensor, "act": nc.scalar, "dve": nc.vector, "pool": nc.gpsimd, "sp": nc.sync}
        self.sem = {}
        for e in self.eng:
            self.sem[e] = stack.enter_context(nc.semaphore("s_" + e))
        self.dma_sems = {}
        for q in ("sp", "act", "pool"):
            self.dma_sems[q] = [("d", q, i) for i in range(N_DMA_SEMS)]
            for k in self.dma_sems[q]:
                self.sem[k] = stack.enter_context(nc.semaphore("d_%s_%d" % (q, k[2])))
        self.count = {k: 0 for k in self.sem}
        self.known = {e: {} for e in self.eng}
        self.vc = {}
        self.dma_rr = {q: 0 for q in self.dma_sems}
        self.n_instr = 0
        self.n_wait = 0
        self.stack = None
        self._dres = {}
        self._uid = 0

    def sb(self, shape, dtype, nres=1, name=None):
        self._uid += 1
        t = self.stack.enter_context(self.nc.sbuf_tensor("%s_%d" % (name or "t", self._uid), list(shape), dtype))
        return Tile(t, nres)

    def ps(self, shape, dtype=F32, name=None):
        self._uid += 1
        t = self.stack.enter_context(self.nc.psum_tensor("%s_%d" % (name or "p", self._uid), list(shape), dtype))
        return Tile(t)

    def dres(self, name, idx=0):
        k = (name, idx)
        if k not in self._dres:
            self._dres[k] = Res()
        return self._dres[k]

    def _deps(self, E, reads, writes):
        deps = {}
        for r in reads:
            if r.w is not None and deps.get(r.w[0], 0) < r.w[1]:
                deps[r.w[0]] = r.w[1]
        for w in writes:
            if w.w is not None and deps.get(w.w[0], 0) < w.w[1]:
                deps[w.w[0]] = w.w[1]
            for k, v in w.r.items():
                if deps.get(k, 0) < v:
                    deps[k] = v
        kn = self.known[E]
        out = []
        for k, v in deps.items():
            if k == E and (E == "pe" or E == "sp" or not SAME_ENGINE_SYNC):
                continue
            if kn.get(k, 0) >= v:
                continue
            out.append((k, v))
        return out

    def _wait(self, E, waits):
        eng = self.eng[E]
        kn = self.known[E]
        for k, v in waits:
            eng.wait_ge(self.sem[k], v)
            self.n_wait += 1
            snap = self.vc.get((k, v))
            if snap is not None:
                for kk, vv in snap.items():
                    if kn.get(kk, 0) < vv:
                        kn[kk] = vv
            if kn.get(k, 0) < v:
                kn[k] = v

    def _finish(self, E, ev, ins, inc, reads, writes):
        ins.then_inc(self.sem[ev[0]], inc)
        snap = dict(self.known[E])
        snap[ev[0]] = ev[1]
        self.vc[ev] = snap
        k, v = ev
        for r in reads:
            if r.r.get(k, 0) < v:
                r.r[k] = v
        for w in writes:
            w.w = ev
            w.r = {}
        self.n_instr += 1

    def op(self, E, fn, R=(), W=()):
        reads = _res(R)
        writes = _res(W)
        self._wait(E, self._deps(E, reads, writes))
        ins = fn(self.eng[E])
        self.count[E] += 1
        ev = (E, self.count[E])
        self._finish(E, ev, ins, 1, reads, writes)

    def dma(self, out, in_, R=(), W=(), q="sp", **kw):
        reads = _res(R)
        writes = _res(W)
        self._wait(q, self._deps(q, reads, writes))
        key = self.dma_sems[q][self.dma_rr[q] % N_DMA_SEMS]
        self.dma_rr[q] += 1
        if self.count[key] > 0 and self.known[q].get(key, 0) < self.count[key]:
            self._wait(q, [(key, self.count[key])])
        ins = self.eng[q].dma_start(out=out, in_=in_, **kw)
        self.count[key] += 16
        ev = (key, self.count[key])
        self._finish(q, ev, ins, 16, reads, writes)

    def barrier(self):
        targets = [(k, v) for k, v in self.count.items() if v > 0]
        for E in self.eng:
            waits = [(k, v) for k, v in targets if self.known[E].get(k, 0) < v and not (k == E and E == "pe")]
            self._wait(E, waits)

    def wait_all_on(self, E="sp"):
        waits = [(k, v) for k, v in self.count.items() if v > 0 and k != E and self.known[E].get(k, 0) < v]
        self._wait(E, waits)

    def mm(self, out, lhsT, rhs, start=True, stop=True, R=(), W=()):
        self.op("pe", lambda e: e.matmul(out, lhsT, rhs, start=start, stop=stop), R, W)

    def tr(self, out, in_, ident, R=(), W=()):
        self.op("pe", lambda e: e.transpose(out, in_, ident), R, W)

    def act(self, out, in_, func, R=(), W=(), bias=0.0, scale=1.0, accum_out=None):
        if accum_out is None:
            self.op("act", lambda e: e.activation(out, in_, func, bias=bias, scale=scale), R, W)
        else:
            self.op("act", lambda e: e.activation(out, in_, func, bias=bias, scale=scale, accum_out=accum_out), R, W)

    def tt(self, out, in0, in1, op, R=(), W=(), E="dve"):
        self.op(E, lambda e: e.tensor_tensor(out, in0, in1, op), R, W)

    def ts(self, out, in0, s1, s2, op0, op1=None, R=(), W=(), E="dve"):
        if op1 is None:
            self.op(E, lambda e: e.tensor_scalar(out, in0, s1, None, op0), R, W)
        else:
            self.op(E, lambda e: e.tensor_scalar(out, in0, s1, s2, op0, op1), R, W)

    def stt(self, out, in0, scalar, in1, op0, op1, R=(), W=()):
        self.op("dve", lambda e: e.scalar_tensor_tensor(out, in0, scalar, in1, op0, op1), R, W)

    def cp(self, out, in_, R=(), W=(), E="dve"):
        if E == "act":
            self.op("act", lambda e: e.activation(out, in_, AF.Copy), R, W)
        else:
            self.op(E, lambda e: e.tensor_copy(out, in_), R, W)


WSHAPES = [
    ("w_mod", (DM, 6 * DM)), ("b_mod", (6 * DM,)), ("w_in", (DM, IN_W)), ("hg_gnorm", (128,)),
    ("at_qnorm", (128,)), ("at_knorm", (128,)), ("mb_conv_w", (5, 3072)), ("mb_conv_b", (3072,)),
    ("mb_dt_bias", (2, 32)), ("mb_a_log", (2, 32)), ("mb_d", (32,)), ("mb_norm", (2048,)),
    ("w_br_hg", (1024, DM)), ("w_br_at", (1024, DM)), ("w_br_mb", (2048, DM)), ("w_out", (DM, DM)),
    ("ln1_g", (DM,)), ("ln1_b", (DM,)), ("w_ffn_in", (DM, 2 * FFH)), ("w_ffn_out", (FFH, DM)),
    ("ln2_g", (DM,)), ("ln2_b", (DM,)),
]


def build(depth=DEPTH, dbg=None):
    dbg = dbg or {}
    dump = dbg.get("dump", set())
    phases = dbg.get("phases", {"p0", "hg", "at", "mb", "merge", "ffn"})
    seqs = dbg.get("seqs", [0, 1])
    nc = bass.Bass("TRN2", target_bir_lowering=False)

    def din(name, shape):
        return nc.dram_tensor(name, list(shape), F32, kind="ExternalInput").ap()

    x_in = din("x", [2, 2048, DM])
    ctx_in = din("ctx", [2, 256, DM])
    c3T = din("c3T", [128, 8, 3])
    consts_d = din("consts", [128, NCONST])
    rope_d = din("rope", [2, 2048, 64])
    hg_lb_d = din("hg_lb", [4, 2, 1024])
    Wd = {n: din(n, (depth,) + s) for n, s in WSHAPES}
    out_d = nc.dram_tensor("out", [2, 2048, DM], F32, kind="ExternalOutput").ap()

    def dscr(name, shape, dt):
        kind = "ExternalOutput" if name in dump else "Internal"
        return nc.dram_tensor(name, list(shape), dt, kind=kind).ap()

    xres = dscr("xres", [2, NT, 128, DM], F32)
    x1s = dscr("x1s", [NT, 128, DM], F32)
    hg_of = dscr("hg_of", [NT, 128, 1024], F32)
    ohgT = dscr("ohgT", [NT, 128, 1024], BF16)
    oatT = dscr("oatT", [128, 8, TOK], BF16)
    ombT = dscr("ombT", [NT, 128, 2048], BF16)
    mb_xs = dscr("mb_xs", [NT, 128, 2048], BF16)
    mb_yf = dscr("mb_yf", [NT, 128, 2048], F32)
    ymTd = dscr("ymT", [NT, 128, 1024], BF16)
    lbs_d = dscr("lbs", [4, 2, 1024], F32)
    modrow_d = dscr("modrow", [3, 6 * DM], F32)
    ffn_act = dscr("ffn_act", [NT, 128, 22 * 128], BF16)

    with ExitStack() as top:
        P = Prog(nc, top)
        P.stack = top
        cst = P.sb([128, NCONST], F32, name="cst")
        P.dma(cst[:], consts_d, W=[cst])
        cstb = P.sb([128, 256], BF16, name="cstb")
        P.cp(cstb[:], cst[:, 0:256], R=[cst], W=[cstb])
        identF = cst[:, C_ID:C_ID + 128]
        onesF = cst[:, C_ONES:C_ONES + 128]
        identB = cstb[:, 0:128]
        onesB = cstb[:, 128:256]
        scT = P.sb([128, 8, 3], F32, name="scT")
        P.dma(scT[:], c3T, W=[scT])
        P.act(scT[:], scT[:], AF.Silu, R=[scT], W=[scT])
        hT = P.sb([128, 8, TOK], BF16, nres=NT, name="hT")

        def xsrc(l, s, t):
            if l == 0:
                return (ctx_in[s, t * 128:(t + 1) * 128, :] if t < TCTX else x_in[s, (t - TCTX) * 128:(t - TCTX + 1) * 128, :]), None
            return xres[s, t], P.dres("xres", (s, t))

        with ExitStack() as ph:
            P.stack = ph
            e = P.sb([128, 4, 16], F32, name="lb_e")
            P.dma(e[:].rearrange("p l (d j) -> p l d j", d=2),
                  hg_lb_d.rearrange("l d (p j) -> p l d j", p=128), W=[e])
            P.act(e[:], e[:], AF.Exp, R=[e], W=[e])
            s_ = P.sb([128, 16], F32, name="lb_s")
            P.tt(s_[:], e[:, 0, :], e[:, 1, :], ALU.add, R=[e], W=[s_])
            P.tt(s_[:], s_[:], e[:, 2, :], ALU.add, R=[e, s_], W=[s_])
            P.tt(s_[:], s_[:], e[:, 3, :], ALU.add, R=[e, s_], W=[s_])
            P.op("dve", lambda en: en.reciprocal(s_[:], s_[:]), R=[s_], W=[s_])
            P.tt(e[:], e[:], s_[:].rearrange("p (o j) -> p o j", o=1).broadcast_to([128, 4, 16]), ALU.mult, R=[e, s_], W=[e])
            lbt = P.sb([128, 4, 16], F32, name="lb_t")
            P.op("dve", lambda en: en.memset(lbt[:, 0, :], 0.0), W=[lbt])
            P.cp(lbt[:, 1, :], e[:, 1, :], R=[e], W=[lbt])
            P.tt(lbt[:, 2, :], lbt[:, 1, :], e[:, 2, :], ALU.add, R=[e, lbt], W=[lbt])
            P.tt(lbt[:, 3, :], lbt[:, 2, :], e[:, 3, :], ALU.add, R=[e, lbt], W=[lbt])
            P.dma(lbs_d.rearrange("l d (p j) -> p l d j", p=128),
                  lbt[:].rearrange("p l (d j) -> p l d j", d=2), R=[lbt], W=[P.dres("lbs")])
            P.barrier()
        P.stack = top

        for l in range(depth):
            last = (l == depth - 1) and not dbg.get('nolast', False)
            with ExitStack() as lay:
                P.stack = lay
                modA = P.sb([128, 48, 3], F32, name="modA")
                modB = P.sb([128, 48, 3], F32, name="modB")
                with ExitStack() as ph:
                    P.stack = ph
                    mrow = P.sb([3, 6 * DM], F32, name="mrow")
                    brow = P.sb([3, 6 * DM], F32, name="brow")
                    P.dma(brow[:], Wd["b_mod"][l:l + 1, :].broadcast_to([3, 6 * DM]), W=[brow])
                    wm = [P.sb([128, 8, 1536], F32, name="wm%d" % i) for i in range(2)]
                    pm = [P.ps([128, 512], name="pm%d" % i) for i in range(2)]
                    for blk in range(4):
                        w_ = wm[blk % 2]
                        P.dma(w_[:], Wd["w_mod"][l].rearrange("(kc p) n -> p kc n", p=128)[:, :, blk * 1536:(blk + 1) * 1536], W=[w_])
                        for j in range(3):
                            pj = pm[j % 2]
                            for kc in range(8):
                                P.mm(pj[0:3, :], scT[:, kc, :], w_[:, kc, j * 512:(j + 1) * 512],
                                     start=(kc == 0), stop=(kc == 7), R=[scT, w_], W=[pj])
                            c0 = blk * 1536 + j * 512
                            P.tt(mrow[:, c0:c0 + 512], pj[0:3, :], brow[:, c0:c0 + 512], ALU.add, R=[pj, brow], W=[mrow])
                    P.dma(modrow_d, mrow[:], R=[mrow], W=[P.dres("modrow")])
                    pT = P.ps([128, 512], name="pmT")
                    for j in range(48):
                        P.tr(pT[:, j * 3:(j + 1) * 3], mrow[:, j * 128:(j + 1) * 128], identF[0:3, 0:3], R=[mrow, cst], W=[pT])
                    P.cp(modA[:].rearrange("p j r -> p (j r)"), pT[:, 0:144], R=[pT], W=[modA])
                    P.ts(modB[:].rearrange("p j r -> p (j r)"), pT[:, 0:144], 1.0, None, ALU.add, R=[pT], W=[modB])
                    P.barrier()
                P.stack = lay

                for s in seqs:
                    with ExitStack() as sq:
                        P.stack = sq
                        def featT(l_, s_, tiles, src_fn, sc_chunk, sh_chunk, ph):
                            xt = [P.sb([128, DM], F32, name="p0x%d" % i) for i in range(2)]
                            pp = [P.ps([128, 512], name="p0p%d" % i) for i in range(2)]
                            for i, t in enumerate(tiles):
                                r = 2 if t < TCTX else s_
                                xb = xt[i % 2]
                                src, sres = src_fn(t)
                                P.dma(xb[:], src, R=[sres], W=[xb])
                                for half in range(2):
                                    pb = pp[half]
                                    for q in range(4):
                                        kc = half * 4 + q
                                        P.tr(pb[:, q * 128:(q + 1) * 128], xb[:, kc * 128:(kc + 1) * 128], identF, R=[xb, cst], W=[pb])
                                    for q in range(4):
                                        kc = half * 4 + q
                                        o = hT[:, kc, t * 128:(t + 1) * 128]
                                        i_ = pb[:, q * 128:(q + 1) * 128]
                                        sc = modB[:, sc_chunk + kc, r:r + 1]
                                        sh = modA[:, sh_chunk + kc, r:r + 1]
                                        if q % 2 == 0:
                                            P.ts(o, i_, sc, sh, ALU.mult, ALU.add, R=[pb, modA, modB], W=[hT.res[t]])
                                        else:
                                            P.act(o, i_, AF.Identity, R=[pb, modA, modB], W=[hT.res[t]], bias=sh, scale=sc)

                        if "p0" in phases:
                            with ExitStack() as ph:
                                P.stack = ph
                                featT(l, s, list(range(NT)), lambda t: xsrc(l, s, t), 8, 0, ph)
                                P.barrier()
                            P.stack = sq
                        if "hg" in phases:
                            with ExitStack() as ph:
                                P.stack = ph
                                phase_hg(P, nc, l, s, Wd, cst, cstb, hT, lbs_d, hg_of, ohgT)
                                P.barrier()
                            P.stack = sq
                        if "at" in phases:
                            with ExitStack() as ph:
                                P.stack = ph
                                phase_at(P, nc, l, s, Wd, cst, cstb, hT, rope_d, oatT)
                                P.barrier()
                            P.stack = sq
                        if "mb" in phases:
                            phase_mb(P, nc, l, s, Wd, cst, cstb, hT, mb_xs, mb_yf, ombT, sq)
                            P.stack = sq
                        if "merge" in phases:
                            phase_merge(P, nc, l, s, last, Wd, cst, cstb, hT, modA, modB, modrow_d, ohgT, oatT, ombT,
                                        ymTd, x1s, lambda t: xsrc(l, s, t), sq)
                            P.stack = sq
                        if "ffn" in phases:
                            phase_ffn(P, nc, l, s, last, Wd, cst, cstb, hT, modrow_d, x1s, xres, out_d, ffn_act)
                            P.stack = sq
                    P.stack = lay
            P.stack = top
        P.wait_all_on("sp")
        build.stats = (P.n_instr, P.n_wait)
    return nc


def _loadw(P, dst, src3, n, q="pool"):
    KC = src3.shape[1]
    step = 2
    for i, kc in enumerate(range(0, KC, step)):
        P.dma(dst[:, kc:kc + step, 0:n], src3[:, kc:kc + step, :], W=[dst.res[i % len(dst.res)]], q=q)


def _interleave(gens):
    gens = [g for g in gens if g is not None]
    while gens:
        for g in list(gens):
            try:
                next(g)
            except StopIteration:
                gens.remove(g)


def phase_hg(P, nc, l, s, Wd, cst, cstb, hT, lbs_d, hg_of, ohgT):
    identB = cstb[:, 0:128]
    onesF = cst[:, C_ONES:C_ONES + 128]
    CI = cst[:, C_CI:C_CI + 4]
    Wl = Wd["w_in"][l].rearrange("(kc p) n -> p kc n", p=128)
    Wq = P.sb([128, 8, 1024], BF16, nres=4, name="Wq")
    Wf = P.sb([128, 8, 1024], BF16, nres=4, name="Wf")
    Wi = P.sb([128, 8, 1024], BF16, nres=4, name="Wi")
    Wg = P.sb([128, 8, 1024], BF16, nres=4, name="Wg")
    _loadw(P, Wf, Wl[:, :, O_HFF:O_HFF + 1024], 1024)
    _loadw(P, Wq, Wl[:, :, O_HQ:O_HQ + 1024], 1024)
    _loadw(P, Wi, Wl[:, :, O_HI:O_HI + 1024], 1024)
    lbb = P.sb([128, 1024], F32, name="lbb")
    oml = P.sb([128, 1024], F32, name="oml")

    def load_lb(d):
        P.dma(lbb[:], lbs_d[l, d:d + 1, :].broadcast_to([128, 1024]), R=[P.dres("lbs")], W=[lbb])
        P.ts(oml[:], lbb[:], -1.0, 1.0, ALU.mult, ALU.add, R=[lbb], W=[oml])

    load_lb(0)
    gn = P.sb([128, 1], F32, name="gn")
    P.dma(gn[:], Wd["hg_gnorm"][l].rearrange("(p o) -> p o", o=1), W=[gn])
    lf = P.sb([128, 1024], F32, name="lf")
    kk = P.sb([128, 1024], F32, name="kk")
    ex = P.sb([128, 1024], F32, name="ex")
    exn = P.sb([128, 1024], F32, name="exn")
    qs = P.sb([128, 1024], F32, name="qs")
    Kt = [P.sb([128, 1024], BF16, name="Kt%d" % i) for i in range(2)]
    Qt = P.sb([128, 1024], BF16, name="Qt")
    Vt = [P.sb([128, 1024], BF16, name="Vt%d" % i) for i in range(2)]
    Ktm = [P.sb([128, 1024], BF16, name="Ktm%d" % i) for i in range(2)]
    KT = [P.sb([128, 1024], BF16, name="KT%d" % i) for i in range(2)]
    QT = [P.sb([128, 1024], BF16, name="QT%d" % i) for i in range(2)]
    ec = [P.sb([128, 32], F32, name="ec%d" % i) for i in range(2)]
    sg = [P.sb([128, 1024], F32, name="sg%d" % i) for i in range(2)]
    S = P.sb([128, 8, 128], F32, nres=8, name="S")
    Sb = P.sb([128, 8, 128], BF16, nres=8, name="Sb")
    ATm = P.sb([128, 8, 128], BF16, nres=8, name="ATm")
    ost = P.sb([128, 1024], F32, name="ost")
    ofl = P.sb([128, 1024], F32, name="ofl")
    osq = P.sb([128, 1024], F32, name="osq")
    rs = ofl
    ob = P.sb([128, 1024], BF16, name="ob")
    pA = P.ps([128, 1024], name="pA")
    pT = P.ps([128, 1024], BF16, name="pT")
    pC = P.ps([128, 512], name="pC")
    pCb = pC[:].bitcast(BF16)
    pO = P.ps([128, 1024], name="pO")
    pSl = P.ps([128, 1024], name="pSl")
    _r0, _r1 = Res(), Res()
    pSl.res = [_r0] * 4 + [_r1] * 4

    def proj(Wt, t):
        for half in range(2):
            for kc in range(8):
                P.mm(pA[:, half * 512:(half + 1) * 512], hT[:, kc, t * 128:(t + 1) * 128],
                     Wt[:, kc, half * 512:(half + 1) * 512], start=(kc == 0), stop=(kc == 7),
                     R=[hT.res[t], Wt], W=[pA])

    def prologue(t, d, b):
        D = cst[:, (C_HDF if d == 0 else C_HDB):(C_HDF if d == 0 else C_HDB) + 128]
        proj(Wf, t)
        P.act(lf[:], pA[:], AF.Sigmoid, R=[pA], W=[lf])
        P.act(kk[:], pA[:], AF.Sigmoid, R=[pA], W=[kk], scale=-1.0)
        yield
        proj(Wq, t)
        P.tt(lf[:], lf[:], oml[:], ALU.mult, R=[lf, oml], W=[lf])
        P.tt(lf[:], lf[:], lbb[:], ALU.add, R=[lf, lbb], W=[lf])
        P.act(lf[:], lf[:], AF.Ln, R=[lf], W=[lf])
        P.act(qs[:], pA[:], AF.Silu, R=[pA], W=[qs])
        P.tt(kk[:], kk[:], oml[:], ALU.mult, R=[kk, oml], W=[kk])
        yield
        proj(Wi, t)
        P.cp(Vt[b][:], pA[:], R=[pA], W=[Vt[b]], E="act")
        yield
        for half in range(2):
            P.mm(pA[:, half * 512:(half + 1) * 512], D, lf[:, half * 512:(half + 1) * 512], R=[cst, lf], W=[pA])
        for h in range(8):
            P.mm(pC[:, h * 4:(h + 1) * 4], lf[:, h * 128:(h + 1) * 128], CI, R=[lf, cst], W=[pC])
        P.act(ex[:], pA[:], AF.Exp, R=[pA], W=[ex])
        P.act(exn[:], pA[:], AF.Exp, R=[pA], W=[exn], scale=-1.0)
        P.act(ec[b][:], pC[:, 0:32], AF.Exp, R=[pC], W=[ec[b]])
        yield
        P.stt(Qt[:], qs[:], float(128 ** -0.5), exn[:], ALU.mult, ALU.mult, R=[qs, exn], W=[Qt])
        P.tt(Kt[b][:], kk[:], ex[:], ALU.mult, R=[kk, ex], W=[Kt[b]])
        P.ts(Ktm[b][:], Kt[b][:], CI[:, 3:4], None, ALU.mult, R=[Kt[b], cst], W=[Ktm[b]])
        yield
        for h in range(8):
            P.tr(pT[:, h * 128:(h + 1) * 128], Kt[b][:, h * 128:(h + 1) * 128], identB, R=[Kt[b], cstb], W=[pT])
        P.cp(KT[b][:], pT[:], R=[pT], W=[KT[b]])
        for h in range(8):
            P.tr(pCb[:, h * 128:(h + 1) * 128], Qt[:, h * 128:(h + 1) * 128], identB, R=[Qt, cstb], W=[pC])
        P.cp(QT[b][:], pCb, R=[pC], W=[QT[b]], E="act")
        yield
        if d == 1:
            for h in range(8):
                for kc in range(8):
                    P.mm(pA[:, h * 128:(h + 1) * 128], Wg[:, kc, h * 128:(h + 1) * 128], hT[:, kc, t * 128:(t + 1) * 128],
                         start=(kc == 0), stop=(kc == 7), R=[Wg, hT.res[t]], W=[pA])
            P.act(sg[b][:], pA[:], AF.Silu, R=[pA], W=[sg[b]])
            yield

    def scan(t, d, b):
        M = cst[:, (C_HMF if d == 0 else C_HMB):(C_HMF if d == 0 else C_HMB) + 128]
        if d == 1:
            P.dma(ofl[:], hg_of[t], R=[P.dres("hg_of", t)], W=[ofl])
        for h in range(8):
            hs = slice(h * 128, (h + 1) * 128)
            P.mm(pSl[:, hs], KT[b][:, hs], QT[b][:, hs], R=[KT[b], QT[b]], W=[pSl.res[h]])
        for h in range(8):
            hs = slice(h * 128, (h + 1) * 128)
            P.tt(ATm[:, h, :], pSl[:, hs], M, ALU.mult, R=[pSl.res[h], cst], W=[ATm.res[h]])
        yield
        for h in range(8):
            hs = slice(h * 128, (h + 1) * 128)
            P.op("pe", lambda e, hs=hs, h=h: e.matmul(pO[:, hs], Vt[b][:, hs], ATm[:, h, :], start=(h % 4 == 0), stop=False,
                                                       skip_group_check=True), R=[Vt[b], ATm.res[h]], W=[pO])
        chunks = [0, 1, 2, 3] if d == 0 else [3, 2, 1, 0]
        for ci, c in enumerate(chunks):
            for h in range(8):
                e_c = ec[b][:, h * 4 + c:h * 4 + c + 1]
                P.op("act", lambda e, h=h, e_c=e_c: e.activation(Sb[:, h, :], S[:, h, :], AF.Identity, scale=e_c),
                     R=[S.res[h], ec[b]], W=[Sb.res[h]])
            for h in range(8):
                hs = slice(h * 128, (h + 1) * 128)
                if c == 3:
                    P.mm(pSl[:, hs], Ktm[b][:, hs], Vt[b][:, hs], R=[Ktm[b], Vt[b]], W=[pSl.res[h]])
                else:
                    P.mm(pSl[:, hs], Kt[b][c * 32:(c + 1) * 32, hs], Vt[b][c * 32:(c + 1) * 32, hs], R=[Kt[b], Vt[b]], W=[pSl.res[h]])
            for h in range(8):
                cs = slice(h * 128 + c * 32, h * 128 + (c + 1) * 32)
                P.op("pe", lambda e, cs=cs, h=h: e.matmul(pO[:, cs], Sb[:, h, :], QT[b][:, cs], start=False, stop=(ci == 3),
                                                           skip_group_check=True), R=[Sb.res[h], QT[b]], W=[pO])
            for h in range(8):
                hs = slice(h * 128, (h + 1) * 128)
                e_c = ec[b][:, h * 4 + c:h * 4 + c + 1]
                P.stt(S[:, h, :], S[:, h, :], e_c, pSl[:, hs], ALU.mult, ALU.add, R=[S.res[h], ec[b], pSl.res[h]], W=[S.res[h]])
            yield
        if d == 0:
            P.cp(ost[:], pO[:], R=[pO], W=[ost], E="act")
            P.dma(hg_of[t], ost[:], R=[ost], W=[P.dres("hg_of", t)])
        else:
            P.tt(ost[:], pO[:], ofl[:], ALU.add, R=[pO, ofl], W=[ost])
            P.act(osq[:], ost[:], AF.Square, R=[ost], W=[osq])
            for half in range(2):
                P.mm(pSl[:, half * 512:(half + 1) * 512], onesF, osq[:, half * 512:(half + 1) * 512], R=[cst, osq], W=[pSl])
            yield
            P.act(rs[:], pSl[:], AF.Sqrt, R=[pSl], W=[rs], scale=1.0 / 128.0, bias=RMS_EPS)
            P.op("dve", lambda e: e.reciprocal(rs[:], rs[:]), R=[rs], W=[rs])
            P.stt(osq[:], ost[:], gn[:, 0:1], rs[:], ALU.mult, ALU.mult, R=[ost, gn, rs], W=[osq])
            P.tt(ob[:], osq[:], sg[b][:], ALU.mult, R=[osq, sg[b]], W=[ob])
            P.dma(ohgT[t], ob[:], R=[ob], W=[P.dres("ohgT", t)])
        yield

    order = [(t, 0) for t in range(NT)] + [(t, 1) for t in list(range(TCTX - 1, -1, -1)) + list(range(NT - 1, TCTX - 1, -1))]
    P.op("dve", lambda e: e.memset(S[:], 0.0), W=[S])
    _interleave([prologue(order[0][0], order[0][1], 0)])
    for n, (t, d) in enumerate(order):
        nxt = None
        if n + 1 < len(order):
            t2, d2 = order[n + 1]
            if d2 == 1 and d == 0:
                _loadw(P, Wf, Wl[:, :, O_HFB:O_HFB + 1024], 1024)
                _loadw(P, Wg, Wl[:, :, O_HG:O_HG + 1024], 1024)
                load_lb(1)
            nxt = prologue(t2, d2, (n + 1) % 2)
        _interleave([scan(t, d, n % 2), nxt])
        if n + 1 < len(order) and order[n + 1][1] == 1 and d == 0:
            P.op("dve", lambda e: e.memset(S[:], 0.0), W=[S])


def _pipeline(items, gen_fn):
    active = []

    def rnd():
        for g in list(active):
            try:
                next(g)
            except StopIteration:
                active.remove(g)

    for it in items:
        active.append(gen_fn(it))
        rnd()
    while active:
        rnd()


def phase_at(P, nc, l, s, Wd, cst, cstb, hT, rope_d, oatT):
    identB = cstb[:, 0:128]
    onesB = cstb[:, 128:256]
    Wl = Wd["w_in"][l].rearrange("(kc p) n -> p kc n", p=128)
    Wa = P.sb([128, 8, 1536], BF16, nres=4, name="Wa")
    _loadw(P, Wa, Wl[:, :, O_AQ:O_AQ + 1536], 1536)
    QT = P.sb([128, 8, TOK], BF16, nres=NT, name="QTa")
    KTa = P.sb([128, 2, TOK], BF16, nres=NT, name="KTa")
    Va = P.sb([128, NT, 256], BF16, nres=NT, name="Va")
    with ExitStack() as sub:
        P.stack = sub
        cosT = P.sb([128, 16, 64], F32, name="cosT")
        sinT = P.sb([128, 16, 64], F32, name="sinT")
        P.dma(cosT[:], rope_d[0].rearrange("(t p) j -> p t j", p=128), W=[cosT])
        P.dma(sinT[:], rope_d[1].rearrange("(t p) j -> p t j", p=128), W=[sinT])
        wqk = P.sb([128, 10, 128], F32, name="wqk")
        P.dma(wqk[:, 0:8, :], Wd["at_qnorm"][l:l + 1, :].rearrange("o (h d) -> o h d", h=1).broadcast_to([128, 8, 128]), W=[wqk])
        P.dma(wqk[:, 8:10, :], Wd["at_knorm"][l:l + 1, :].rearrange("o (h d) -> o h d", h=1).broadcast_to([128, 2, 128]), W=[wqk])
        NB = 2
        sqt = [P.sb([128, 1280], F32, name="sqt%d" % i) for i in range(NB)]
        xn = [P.sb([128, 1280], F32, name="xn%d" % i) for i in range(NB)]
        ss = [P.sb([128, 10], F32, name="ss%d" % i) for i in range(NB)]
        t1 = [P.sb([128, 10, 64], F32, name="t1%d" % i) for i in range(NB)]
        t2 = [P.sb([128, 10, 64], F32, name="t2%d" % i) for i in range(NB)]
        xr = [P.sb([128, 1280], BF16, name="xr%d" % i) for i in range(NB)]
        pQ = [P.ps([128, 1536], name="pQ%d" % i) for i in range(NB)]
        pT = P.ps([128, 2048], BF16, name="pTa")

        def prep(t):
            b = t % NB
            tc = slice(t * 128, (t + 1) * 128)
            pQ_, sqt_, xn_, ss_, t1_, t2_, xr_ = pQ[b], sqt[b], xn[b], ss[b], t1[b], t2[b], xr[b]
            for j in range(3):
                for kc in range(8):
                    P.mm(pQ_[:, j * 512:(j + 1) * 512], hT[:, kc, tc], Wa[:, kc, j * 512:(j + 1) * 512],
                         start=(kc == 0), stop=(kc == 7), R=[hT.res[t], Wa], W=[pQ_])
            P.cp(Va[:, t, :], pQ_[:, 1280:1536], R=[pQ_], W=[Va.res[t]], E="act")
            P.act(sqt_[:], pQ_[:, 0:1280], AF.Square, R=[pQ_], W=[sqt_])
            P.op("dve", lambda e: e.tensor_reduce(ss_[:], sqt_[:].rearrange("p (h d) -> p h d", h=10), AX.X, ALU.add), R=[sqt_], W=[ss_])
            P.act(ss_[:], ss_[:], AF.Sqrt, R=[ss_], W=[ss_], scale=1.0 / 128.0, bias=RMS_EPS)
            P.op("dve", lambda e: e.reciprocal(ss_[:], ss_[:]), R=[ss_], W=[ss_])
            yield
            P.tt(xn_[:].rearrange("p (h d) -> p h d", h=10), pQ_[:, 0:1280].rearrange("p (h d) -> p h d", h=10),
                 ss_[:].rearrange("p (h o) -> p h o", o=1).broadcast_to([128, 10, 128]), ALU.mult, R=[pQ_, ss_], W=[xn_])
            P.tt(xn_[:], xn_[:], wqk[:].rearrange("p h d -> p (h d)"), ALU.mult, R=[xn_, wqk], W=[xn_])
            if t >= TCTX:
                tl = t - TCTX
                xv = xn_[:].rearrange("p (h j two) -> p h j two", h=10, two=2)
                xo = xr_[:].rearrange("p (h j two) -> p h j two", h=10, two=2)
                cb = cosT[:, tl:tl + 1, :].broadcast_to([128, 10, 64])
                sb_ = sinT[:, tl:tl + 1, :].broadcast_to([128, 10, 64])
                P.tt(t1_[:], xv[:, :, :, 0], cb, ALU.mult, R=[xn_, cosT], W=[t1_])
                P.tt(t2_[:], xv[:, :, :, 1], sb_, ALU.mult, R=[xn_, sinT], W=[t2_])
                P.tt(xo[:, :, :, 0], t1_[:], t2_[:], ALU.subtract, R=[t1_, t2_], W=[xr_])
                P.tt(t1_[:], xv[:, :, :, 0], sb_, ALU.mult, R=[xn_, sinT], W=[t1_])
                P.tt(t2_[:], xv[:, :, :, 1], cb, ALU.mult, R=[xn_, cosT], W=[t2_])
                P.tt(xo[:, :, :, 1], t1_[:], t2_[:], ALU.add, R=[t1_, t2_], W=[xr_])
            else:
                P.cp(xr_[:], xn_[:], R=[xn_], W=[xr_], E="act")
            yield
            for j in range(10):
                P.tr(pT[:, j * 128:(j + 1) * 128], xr_[:, j * 128:(j + 1) * 128], identB, R=[xr_, cstb], W=[pT])
            P.cp(QT[:, :, tc], pT[:, 0:1024].rearrange("p (h k) -> p h k", h=8), R=[pT], W=[QT.res[t]])
            P.cp(KTa[:, :, tc], pT[:, 1024:1280].rearrange("p (h k) -> p h k", h=2), R=[pT], W=[KTa.res[t]], E="act")
            yield

        _pipeline(list(range(NT)), prep)
        P.barrier()
    with ExitStack() as sub:
        P.stack = sub
        NS = 3
        pS = [P.ps([128, 512], name="pSa%d" % i) for i in range(NS)]
        pO = [P.ps([128, 512], name="pOa%d" % i) for i in range(2)]
        pL = [P.ps([128, 512], name="pLa%d" % i) for i in range(2)]
        Pt = [P.sb([128, 512], BF16, name="Pt%d" % i) for i in range(NS)]
        rl = [P.sb([128, 512], F32, name="rl%d" % i) for i in range(2)]
        oa = [P.sb([128, 512], BF16, name="oa%d" % i) for i in range(2)]
        blocks = [(0, 256, [0, 1])] + [(256 + qb * 512, 512, list(range(NT))) for qb in range(4)]
        sc = float(128 ** -0.5)
        steps = []
        for h in range(8):
            for bi, (q0, nq, kts) in enumerate(blocks):
                for i, kt in enumerate(kts):
                    steps.append((h, bi, q0, nq, kt, i == 0, i == len(kts) - 1))
        jobn = {}
        for (h, bi, *_r) in steps:
            jobn.setdefault((h, bi), len(jobn))

        def s_mm(n):
            h, bi, q0, nq, kt, first, lastk = steps[n]
            kv = h // 4
            qres = [QT.res[t] for t in range(q0 // 128, (q0 + nq) // 128)]
            P.mm(pS[n % NS][:, 0:nq], KTa[:, kv, kt * 128:(kt + 1) * 128], QT[:, h, q0:q0 + nq], R=[KTa.res[kt]] + qres, W=[pS[n % NS]])

        s_mm(0)
        for n in range(len(steps)):
            h, bi, q0, nq, kt, first, lastk = steps[n]
            kv = h // 4
            jb = jobn[(h, bi)] % 2
            if n + 1 < len(steps):
                s_mm(n + 1)
            ps, pt = pS[n % NS], Pt[n % NS]
            P.act(pt[:, 0:nq], ps[:, 0:nq], AF.Exp, R=[ps], W=[pt], scale=sc)
            P.mm(pO[jb][:, 0:nq], Va[:, kt, kv * 128:(kv + 1) * 128], pt[:, 0:nq], start=first, stop=lastk, R=[Va.res[kt], pt], W=[pO[jb]])
            P.mm(pL[jb][:, 0:nq], onesB, pt[:, 0:nq], start=first, stop=lastk, R=[cstb, pt], W=[pL[jb]])
            if lastk:
                P.op("dve", lambda e: e.reciprocal(rl[jb][:, 0:nq], pL[jb][:, 0:nq]), R=[pL[jb]], W=[rl[jb]])
                P.tt(oa[jb][:, 0:nq], pO[jb][:, 0:nq], rl[jb][:, 0:nq], ALU.mult, R=[pO[jb], rl[jb]], W=[oa[jb]])
                P.dma(oatT[:, h, q0:q0 + nq], oa[jb][:, 0:nq], R=[oa[jb]], W=[P.dres("oatT", (h, q0))])
        P.barrier()


def phase_mb(P, nc, l, s, Wd, cst, cstb, hT, mb_xs, mb_yf, ombT, sq):
    identB = cstb[:, 0:128]
    identF = cst[:, C_ID:C_ID + 128]
    onesF = cst[:, C_ONES:C_ONES + 128]
    Wl = Wd["w_in"][l].rearrange("(kc p) n -> p kc n", p=128)
    with ExitStack() as ph:
        P.stack = ph
        BT = P.sb([128, 4, TOK], BF16, name="BT")
        CT = P.sb([128, 4, TOK], BF16, name="CT")
        with ExitStack() as sub:
            P.stack = sub
            Wx = P.sb([128, 8, 3072], BF16, nres=4, name="Wx")
            _loadw(P, Wx, Wl[:, :, O_MX:O_MX + 3072], 3072)
            cwr = P.sb([120, 128], F32, name="cwr")
            P.dma(cwr[:], Wd["mb_conv_w"][l].rearrange("j (cc p) -> (j cc) p", p=128), W=[cwr])
            cbr = P.sb([24, 128], F32, name="cbr")
            P.dma(cbr[:], Wd["mb_conv_b"][l].rearrange("(cc p) -> cc p", p=128), W=[cbr])
            cw = P.sb([128, 120], F32, name="cw")
            cbias = P.sb([128, 24], F32, name="cbias")
            pX = [P.ps([128, 512], name="pX%d" % i) for i in range(4)]
            pTt = P.ps([128, 1024], BF16, name="pTt")
            pW_ = P.ps([128, 512], name="pW_")
            P.tr(pW_[:, 0:120], cwr[:], identF[0:120, 0:120], R=[cwr, cst], W=[pW_])
            P.cp(cw[:], pW_[:, 0:120], R=[pW_], W=[cw])
            P.tr(pW_[:, 128:152], cbr[:], identF[0:24, 0:24], R=[cbr, cst], W=[pW_])
            P.cp(cbias[:], pW_[:, 128:152], R=[pW_], W=[cbias])
            xb = P.sb([128, 2052], F32, name="xb")
            acc = P.sb([128, 2048], F32, name="acc")
            ub = P.sb([128, 2048], BF16, name="ub")
            stg = [P.sb([128, 8, 128], BF16, name="stg%d" % i) for i in range(2)]
            P.op("pool", lambda e: e.memset(xb[:], 0.0), W=[xb])
            k = 0
            for cc in range(24):
                for (t0, ntile) in ((0, TCTX), (TCTX, NT - TCTX)):
                    N = ntile * 128
                    for b in range((N + 511) // 512):
                        nb = min(512, N - b * 512)
                        c0 = t0 * 128 + b * 512
                        tr_ = [hT.res[t] for t in range(c0 // 128, (c0 + nb) // 128)]
                        for kc in range(8):
                            P.mm(pX[b][:, 0:nb], Wx[:, kc, cc * 128:(cc + 1) * 128], hT[:, kc, c0:c0 + nb],
                                 start=(kc == 0), stop=(kc == 7), R=[Wx] + tr_, W=[pX[b]])
                        P.cp(xb[:, 2 + b * 512:2 + b * 512 + nb], pX[b][:, 0:nb], R=[pX[b]], W=[xb], E=("act" if b % 2 else "dve"))
                    P.op("pool", lambda e: e.memset(xb[:, N + 2:N + 4], 0.0), W=[xb])
                    P.ts(acc[:, 0:N], xb[:, 0:N], cw[:, cc:cc + 1], None, ALU.mult, R=[xb, cw], W=[acc])
                    for j in range(1, 5):
                        P.stt(acc[:, 0:N], xb[:, j:j + N], cw[:, j * 24 + cc:j * 24 + cc + 1], acc[:, 0:N], ALU.mult, ALU.add,
                              R=[xb, cw, acc], W=[acc])
                    if cc < 16:
                        dst, dres_ = ub[:, 0:N], ub
                    elif cc < 20:
                        dst, dres_ = BT[:, cc - 16, t0 * 128:t0 * 128 + N], BT
                    else:
                        dst, dres_ = CT[:, cc - 20, t0 * 128:t0 * 128 + N], CT
                    P.act(dst, acc[:, 0:N], AF.Silu, R=[acc, cbias], W=[dres_], bias=cbias[:, cc:cc + 1])
                    if cc < 16:
                        for g0 in range(0, ntile, 8):
                            ng = min(8, ntile - g0)
                            for j in range(ng):
                                P.tr(pTt[:, j * 128:(j + 1) * 128], ub[:, (g0 + j) * 128:(g0 + j + 1) * 128], identB, R=[ub, cstb], W=[pTt])
                            st = stg[k % 2]
                            k += 1
                            P.cp(st[:, 0:ng, :], pTt[:, 0:ng * 128].rearrange("p (j c) -> p j c", j=ng), R=[pTt], W=[st],
                                 E=("act" if k % 2 else "dve"))
                            ta = t0 + g0
                            P.dma(mb_xs[ta:ta + ng, :, cc * 128:(cc + 1) * 128].rearrange("t p c -> p t c"), st[:, 0:ng, :],
                                  R=[st], W=[P.dres("mb_xs", t) for t in range(ta, ta + ng)])
            P.barrier()
        with ExitStack() as sub:
            P.stack = sub
            Wz = P.sb([128, 8, 2112], BF16, nres=8, name="Wz")
            for i, kc in enumerate(range(0, 8, 2)):
                P.dma(Wz[:, kc:kc + 2, 2048:2112], Wl[:, kc:kc + 2, O_MDT:O_MDT + 64], W=[Wz.res[4 + i]], q="pool")
            for i, kc in enumerate(range(0, 8, 2)):
                P.dma(Wz[:, kc:kc + 2, 0:2048], Wl[:, kc:kc + 2, O_MZ:O_MZ + 2048], W=[Wz.res[i]], q="pool")
            dtb = P.sb([128, 64], F32, name="dtb")
            abc = P.sb([128, 64], F32, name="abc")
            dbc = P.sb([128, 32], F32, name="dbc")
            nrm = P.sb([128, 2048], F32, name="nrm")
            P.dma(dtb[:], Wd["mb_dt_bias"][l:l + 1].rearrange("o d h -> o (d h)").broadcast_to([128, 64]), W=[dtb])
            P.dma(abc[:], Wd["mb_a_log"][l:l + 1].rearrange("o d h -> o (d h)").broadcast_to([128, 64]), W=[abc])
            P.act(abc[:], abc[:], AF.Exp, R=[abc], W=[abc])
            P.ts(abc[:], abc[:], -1.0, None, ALU.mult, R=[abc], W=[abc])
            P.dma(dbc[:], Wd["mb_d"][l:l + 1, :].broadcast_to([128, 32]), W=[dbc])
            P.dma(nrm[:], Wd["mb_norm"][l:l + 1, :].broadcast_to([128, 2048]), W=[nrm])
            xs = [P.sb([128, 2048], BF16, name="xs%d" % i) for i in range(2)]
            xdt = [P.sb([128, 2048], BF16, name="xdt%d" % i) for i in range(2)]
            xdtw = P.sb([128, 2048], BF16, name="xdtw")
            oT = xdtw
            dtv = [P.sb([128, 64], F32, name="dtv%d" % i) for i in range(2)]
            dA = [P.sb([128, 32], F32, name="dA%d" % i) for i in range(2)]
            e3 = [P.sb([128, 96], F32, name="e3%d" % i) for i in range(2)]
            cbm = [P.sb([128, 512], F32, name="cbm%d" % i) for i in range(2)]
            Btk = [P.sb([128, 512], BF16, name="Btk%d" % i) for i in range(2)]
            rh = [P.sb([128, 512], F32, name="rh%d" % i) for i in range(2)]
            es = [P.sb([128, 512], F32, name="es%d" % i) for i in range(2)]
            MT = [P.sb([128, 32, 128], BF16, nres=8, name="MT%d" % i) for i in range(2)]
            tmp = [P.sb([128, 512], F32, name="tmpy%d" % i) for i in range(2)]
            ysb = P.sb([128, 2048], F32, nres=4, name="ysb")
            ST = P.sb([128, 2048], F32, nres=4, name="ST")
            STb = P.sb([128, 2048], BF16, nres=4, name="STb")
            yfl = P.sb([128, 2048], F32, name="yfl")
            ss4 = P.sb([128, 4], F32, name="ss4")
            pM = P.ps([128, 512], name="pM")
            pCB = P.ps([128, 512], name="pCB")
            pCBb = pCB[:].bitcast(BF16)
            pSG = [P.ps([128, 512], name="pSG%d" % i) for i in range(2)]
            pY = [P.ps([128, 512], name="pY%d" % i) for i in range(2)]
            pI = [P.ps([128, 512], name="pI%d" % i) for i in range(2)]
            p4 = [pY[0], pY[1], pI[0], pI[1]]
            v3 = lambda ap, h: ap.rearrange("p (h q) -> p h q", h=h)
            col = lambda ap: ap.rearrange("p (h o) -> p h o", o=1)

            def prologue(t, d, b):
                tc = slice(t * 128, (t + 1) * 128)
                TRI = cst[:, (C_LE if d == 0 else C_GE):(C_LE if d == 0 else C_GE) + 128]
                STR = cst[:, (C_GT if d == 0 else C_LT):(C_GT if d == 0 else C_LT) + 128]
                xs_, xdt_, dtv_, dA_, e3_, cbm_, Btk_, MT_ = xs[b], xdt[b], dtv[b], dA[b], e3[b], cbm[b], Btk[b], MT[b]
                P.dma(xs_[:], mb_xs[t], R=[P.dres("mb_xs", t)], W=[xs_])
                for kc in range(8):
                    P.mm(pM[:, 0:64], hT[:, kc, tc], Wz[:, kc, 2048:2112], start=(kc == 0), stop=(kc == 7), R=[hT.res[t], Wz.res[4:8]], W=[pM])
                P.tt(dtv_[:], pM[:, 0:64], dtb[:], ALU.add, R=[pM, dtb], W=[dtv_])
                P.act(dtv_[:], dtv_[:], AF.Exp, R=[dtv_], W=[dtv_])
                P.act(dtv_[:], dtv_[:], AF.Ln, R=[dtv_], W=[dtv_], bias=1.0)
                P.tt(dA_[:], dtv_[:, d * 32:(d + 1) * 32], abc[:, d * 32:(d + 1) * 32], ALU.mult, R=[dtv_, abc], W=[dA_])
                for g in range(4):
                    P.mm(pCB[:, g * 128:(g + 1) * 128], BT[:, g, tc], CT[:, g, tc], R=[BT, CT], W=[pCB])
                yield
                P.mm(pM[:, 64:96], TRI, dA_[:], R=[cst, dA_], W=[pM])
                P.mm(pM[:, 96:128], STR, dA_[:], R=[cst, dA_], W=[pM])
                P.mm(pM[:, 128:160], onesF, dA_[:], R=[cst, dA_], W=[pM])
                P.act(e3_[:], pM[:, 64:160], AF.Exp, R=[pM], W=[e3_])
                P.tt(v3(cbm_[:], 4), v3(pCB[:], 4), TRI.rearrange("p (o t) -> p o t", o=1).broadcast_to([128, 4, 128]), ALU.mult,
                     R=[pCB, cst], W=[cbm_])
                for g in range(4):
                    P.tr(pCBb[:, g * 128:(g + 1) * 128], BT[:, g, tc], identB, R=[BT, cstb], W=[pCB])
                P.cp(Btk_[:], pCBb[:, 0:512], R=[pCB], W=[Btk_], E="act")
                P.tt(v3(xdt_[:], 32), v3(xs_[:], 32), col(dtv_[:, d * 32:(d + 1) * 32]).broadcast_to([128, 32, 64]), ALU.mult,
                     R=[xs_, dtv_], W=[xdt_])
                yield
                for hb in range(8):
                    h0 = hb * 4
                    g = hb // 2
                    i = hb % 2
                    P.tt(v3(rh[i][:], 4), TRI.rearrange("p (o t) -> p o t", o=1).broadcast_to([128, 4, 128]),
                         col(dA_[:, h0:h0 + 4]).broadcast_to([128, 4, 128]), ALU.mult, R=[cst, dA_], W=[rh[i]])
                    P.mm(pSG[i][:], STR, rh[i][:], R=[cst, rh[i]], W=[pSG[i]])
                    P.act(es[i][:], pSG[i][:], AF.Exp, R=[pSG[i]], W=[es[i]])
                    P.tt(MT_[:, h0:h0 + 4, :], v3(es[i][:], 4),
                         cbm_[:, g * 128:(g + 1) * 128].rearrange("p (o t) -> p o t", o=1).broadcast_to([128, 4, 128]), ALU.mult,
                         R=[es[i], cbm_], W=[MT_.res[hb]])
                    if hb % 2 == 1:
                        yield

            def scan(t, d, b):
                tc = slice(t * 128, (t + 1) * 128)
                xs_, xdt_, e3_, Btk_, MT_ = xs[b], xdt[b], e3[b], Btk[b], MT[b]
                if d == 1:
                    P.dma(yfl[:], mb_yf[t], R=[P.dres("mb_yf", t)], W=[yfl])
                P.tt(v3(xdtw[:], 32), v3(xdt_[:], 32), col(e3_[:, 32:64]).broadcast_to([128, 32, 64]), ALU.mult,
                     R=[xdt_, e3_], W=[xdtw])
                for g in range(4):
                    gs = slice(g * 512, (g + 1) * 512)
                    pY_, pI_, tmp_ = pY[g % 2], pI[g % 2], tmp[g % 2]
                    for hh in range(8):
                        h = g * 8 + hh
                        P.mm(pY_[:, hh * 64:(hh + 1) * 64], MT_[:, h, :], xdt_[:, h * 64:(h + 1) * 64], R=[MT_.res[h // 4], xdt_], W=[pY_])
                    P.mm(pI_[:], CT[:, g, tc], STb[:, gs], R=[CT, STb.res[g]], W=[pI_])
                    P.tt(v3(tmp_[:], 8), v3(pI_[:], 8), col(e3_[:, g * 8:(g + 1) * 8]).broadcast_to([128, 8, 64]), ALU.mult,
                         R=[pI_, e3_], W=[tmp_])
                    P.tt(ysb[:, gs], tmp_[:], pY_[:], ALU.add, R=[tmp_, pY_], W=[ysb.res[g]])
                    if g % 2 == 1:
                        yield
                for g in range(4):
                    gs = slice(g * 512, (g + 1) * 512)
                    pI_ = pI[g % 2]
                    P.mm(pI_[:], Btk_[:, g * 128:(g + 1) * 128], xdtw[:, gs], R=[Btk_, xdtw], W=[pI_])
                    P.tt(v3(ST[:, gs], 8), v3(ST[:, gs], 8), col(e3_[:, 64 + g * 8:64 + (g + 1) * 8]).broadcast_to([128, 8, 64]), ALU.mult,
                         R=[ST.res[g], e3_], W=[ST.res[g]])
                    P.tt(ST[:, gs], ST[:, gs], pI_[:], ALU.add, R=[ST.res[g], pI_], W=[ST.res[g]])
                    P.cp(STb[:, gs], ST[:, gs], R=[ST.res[g]], W=[STb.res[g]], E="act")
                    if g % 2 == 1:
                        yield
                if d == 0:
                    P.dma(mb_yf[t], ysb[:], R=[ysb], W=[P.dres("mb_yf", t)])
                    return
                P.tt(ysb[:], ysb[:], yfl[:], ALU.add, R=[ysb, yfl], W=[ysb])
                P.tt(v3(yfl[:], 32), v3(xs_[:], 32), col(dbc[:]).broadcast_to([128, 32, 64]), ALU.mult, R=[xs_, dbc, yfl], W=[yfl])
                P.tt(ysb[:], ysb[:], yfl[:], ALU.add, R=[ysb, yfl], W=[ysb])
                for j in range(4):
                    for kc in range(8):
                        P.mm(p4[j][:], hT[:, kc, tc], Wz[:, kc, j * 512:(j + 1) * 512], start=(kc == 0), stop=(kc == 7),
                             R=[hT.res[t], Wz.res[0:4]], W=[p4[j]])
                    P.act(yfl[:, j * 512:(j + 1) * 512], p4[j][:], AF.Silu, R=[p4[j]], W=[yfl])
                    if j % 2 == 1:
                        yield
                P.tt(ysb[:], ysb[:], yfl[:], ALU.mult, R=[ysb, yfl], W=[ysb])
                P.act(yfl[:], ysb[:], AF.Square, R=[ysb], W=[yfl])
                P.op("dve", lambda e: e.tensor_reduce(ss4[:], v3(yfl[:], 4), AX.X, ALU.add), R=[yfl], W=[ss4])
                P.act(ss4[:], ss4[:], AF.Sqrt, R=[ss4], W=[ss4], scale=1.0 / 512.0, bias=RMS_EPS)
                P.op("dve", lambda e: e.reciprocal(ss4[:], ss4[:]), R=[ss4], W=[ss4])
                yield
                P.tt(v3(ysb[:], 4), v3(ysb[:], 4), col(ss4[:]).broadcast_to([128, 4, 512]), ALU.mult, R=[ysb, ss4], W=[ysb])
                P.tt(ysb[:], ysb[:], nrm[:], ALU.mult, R=[ysb, nrm], W=[ysb])
                for cc in range(16):
                    P.tr(p4[cc // 4][:, (cc % 4) * 128:(cc % 4 + 1) * 128], ysb[:, cc * 128:(cc + 1) * 128], identF, R=[ysb, cst], W=[p4[cc // 4]])
                for j in range(4):
                    P.cp(oT[:, j * 512:(j + 1) * 512], p4[j][:], R=[p4[j]], W=[oT], E=("act" if j % 2 else "dve"))
                P.dma(ombT[t], oT[:], R=[oT], W=[P.dres("ombT", t)])
                yield

            def reset_state():
                P.op("dve", lambda e: e.memset(ST[:], 0.0), W=[ST])
                P.op("pool", lambda e: e.memset(STb[:], 0.0), W=[STb])

            order = [(t, 0) for t in range(NT)] + [(t, 1) for t in list(range(TCTX - 1, -1, -1)) + list(range(NT - 1, TCTX - 1, -1))]
            reset_state()
            _interleave([prologue(order[0][0], order[0][1], 0)])
            for n, (t, d) in enumerate(order):
                nxt = None
                if n + 1 < len(order):
                    nxt = prologue(order[n + 1][0], order[n + 1][1], (n + 1) % 2)
                _interleave([scan(t, d, n % 2), nxt])
                if n + 1 < len(order) and order[n + 1][1] == 1 and d == 0:
                    reset_state()
            P.barrier()


def _layernorm(P, out, z, zr, lng, lnb, st, mv):
    for c in range(2):
        P.op("dve", lambda e: e.bn_stats(st[:, c * 6:(c + 1) * 6], z[:, c * 512:(c + 1) * 512]), R=[zr], W=[st])
    P.op("dve", lambda e: e.bn_aggr(mv[:, 0:2], st[:]), R=[st], W=[mv])
    P.act(mv[:, 2:3], mv[:, 1:2], AF.Sqrt, R=[mv], W=[mv], bias=LN_EPS)
    P.op("dve", lambda e: e.reciprocal(mv[:, 2:3], mv[:, 2:3]), R=[mv], W=[mv])
    P.ts(out, z, mv[:, 0:1], mv[:, 2:3], ALU.subtract, ALU.mult, R=[zr, mv], W=[zr])
    P.tt(out, out, lng[:], ALU.mult, R=[zr, lng], W=[zr])
    P.tt(out, out, lnb[:], ALU.add, R=[zr, lnb], W=[zr])


def _gate_bcast(P, cst, modrow_d, c0, s, pG, gbc):
    mr = P.sb([3, 1024], F32, name="mr")
    P.dma(mr[:], modrow_d[:, c0:c0 + 1024], R=[P.dres("modrow")], W=[mr])
    for ri, r in enumerate((s, 2)):
        for half in range(2):
            P.mm(pG[:, half * 512:(half + 1) * 512], cst[0:3, C_SEL + r * 128:C_SEL + (r + 1) * 128], mr[0:3, half * 512:(half + 1) * 512],
                 R=[cst, mr], W=[pG])
        P.cp(gbc[:, ri, :], pG[:], R=[pG], W=[gbc])


def phase_merge(P, nc, l, s, last, Wd, cst, cstb, hT, modA, modB, modrow_d, ohgT, oatT, ombT, ymTd, x1s, xsrc_fn, sq):
    identB = cstb[:, 0:128]
    identF = cst[:, C_ID:C_ID + 128]
    tiles = list(range(TCTX, NT)) if last else list(range(NT))
    Wl = Wd["w_in"][l].rearrange("(kc p) n -> p kc n", p=128)
    with ExitStack() as ph:
        P.stack = ph
        Wgt = P.sb([128, 8, 3072], BF16, nres=4, name="Wgt")
        Wbh = P.sb([128, 8, 1024], BF16, nres=4, name="Wbh")
        Wba = P.sb([128, 8, 1024], BF16, nres=4, name="Wba")
        Wbm = P.sb([128, 16, 1024], BF16, nres=8, name="Wbm")
        _loadw(P, Wgt, Wl[:, :, O_GT:O_GT + 3072], 3072)
        _loadw(P, Wbh, Wd["w_br_hg"][l].rearrange("(kc p) n -> p kc n", p=128), 1024)
        _loadw(P, Wba, Wd["w_br_at"][l].rearrange("(kc p) n -> p kc n", p=128), 1024)
        _loadw(P, Wbm, Wd["w_br_mb"][l].rearrange("(kc p) n -> p kc n", p=128), 1024)
        oh = [P.sb([128, 1024], BF16, name="oh%d" % i) for i in range(2)]
        oa = [P.sb([128, 8, 128], BF16, name="oam%d" % i) for i in range(2)]
        om = [P.sb([128, 2048], BF16, name="om%d" % i) for i in range(2)]
        sgt = P.sb([128, 1024], F32, name="sgt")
        ym = P.sb([128, 1024], F32, name="ym")
        tmp = P.sb([128, 1024], F32, name="tmpm")
        ymb = P.sb([128, 1024], BF16, name="ymb")
        ymT = [P.sb([128, 1024], BF16, name="ymT%d" % i) for i in range(2)]
        pG = P.ps([128, 1024], name="pGm")
        pB = P.ps([128, 1024], name="pBm")
        pT = P.ps([128, 1024], BF16, name="pTm")
        for i, t in enumerate(tiles):
            tc = slice(t * 128, (t + 1) * 128)
            oh_, oa_, om_ = oh[i % 2], oa[i % 2], om[i % 2]
            P.dma(oh_[:], ohgT[t], R=[P.dres("ohgT", t)], W=[oh_])
            P.dma(oa_[:], oatT[:, :, tc], R=[P.dres("oatT", (h, q0)) for h in range(8) for q0 in (0, 256, 768, 1280, 1792)], W=[oa_])
            P.dma(om_[:], ombT[t], R=[P.dres("ombT", t)], W=[om_])
            srcs = [(lambda kc, o=oh_: o[:, kc * 128:(kc + 1) * 128], Wbh, 8, oh_),
                    (lambda kc, o=oa_: o[:, kc, :], Wba, 8, oa_),
                    (lambda kc, o=om_: o[:, kc * 128:(kc + 1) * 128], Wbm, 16, om_)]
            for b, (of, Wb, nk, ot) in enumerate(srcs):
                for half in range(2):
                    for kc in range(8):
                        P.mm(pG[:, half * 512:(half + 1) * 512], hT[:, kc, tc], Wgt[:, kc, b * 1024 + half * 512:b * 1024 + (half + 1) * 512],
                             start=(kc == 0), stop=(kc == 7), R=[hT.res[t], Wgt], W=[pG])
                P.act(sgt[:], pG[:], AF.Sigmoid, R=[pG], W=[sgt])
                for half in range(2):
                    for kc in range(nk):
                        P.mm(pB[:, half * 512:(half + 1) * 512], of(kc), Wb[:, kc, half * 512:(half + 1) * 512],
                             start=(kc == 0), stop=(kc == nk - 1), R=[ot, Wb], W=[pB])
                if b == 0:
                    P.tt(ym[:], pB[:], sgt[:], ALU.mult, R=[pB, sgt], W=[ym])
                else:
                    P.tt(tmp[:], pB[:], sgt[:], ALU.mult, R=[pB, sgt], W=[tmp])
                    if b == 1:
                        P.tt(ym[:], ym[:], tmp[:], ALU.add, R=[ym, tmp], W=[ym])
                    else:
                        P.tt(ymb[:], ym[:], tmp[:], ALU.add, R=[ym, tmp], W=[ymb])
            for kc in range(8):
                P.tr(pT[:, kc * 128:(kc + 1) * 128], ymb[:, kc * 128:(kc + 1) * 128], identB, R=[ymb, cstb], W=[pT])
            yT = ymT[i % 2]
            P.cp(yT[:], pT[:], R=[pT], W=[yT], E="act")
            P.dma(ymTd[t], yT[:], R=[yT], W=[P.dres("ymT", t)])
        P.barrier()
    with ExitStack() as ph:
        P.stack = ph
        Wo = P.sb([128, 8, 1024], BF16, nres=4, name="Wo")
        _loadw(P, Wo, Wd["w_out"][l].rearrange("(kc p) n -> p kc n", p=128), 1024)
        pW = P.ps([128, 1024], name="pWm")
        pp = [P.ps([128, 512], name="pp%d" % i) for i in range(2)]
        gbc = P.sb([128, 2, 1024], F32, name="gbc")
        _gate_bcast(P, cst, modrow_d, 2048, s, pW, gbc)
        lng = P.sb([128, 1024], F32, name="lng")
        lnb = P.sb([128, 1024], F32, name="lnb")
        P.dma(lng[:], Wd["ln1_g"][l:l + 1, :].broadcast_to([128, 1024]), W=[lng])
        P.dma(lnb[:], Wd["ln1_b"][l:l + 1, :].broadcast_to([128, 1024]), W=[lnb])
        yT = [P.sb([128, 1024], BF16, name="yTb%d" % i) for i in range(2)]
        xt = [P.sb([128, 1024], F32, name="xtb%d" % i) for i in range(2)]
        z = [P.sb([128, 1024], F32, name="zb%d" % i) for i in range(2)]
        st = P.sb([128, 12], F32, name="st")
        mv = P.sb([128, 4], F32, name="mv")
        for i, t in enumerate(tiles):
            tc = slice(t * 128, (t + 1) * 128)
            r = 2 if t < TCTX else s
            ri = 1 if t < TCTX else 0
            yT_, xt_, z_ = yT[i % 2], xt[i % 2], z[i % 2]
            P.dma(yT_[:], ymTd[t], R=[P.dres("ymT", t)], W=[yT_])
            src, sres = xsrc_fn(t)
            P.dma(xt_[:], src, R=[sres], W=[xt_])
            for half in range(2):
                for kc in range(8):
                    P.mm(pW[:, half * 512:(half + 1) * 512], yT_[:, kc * 128:(kc + 1) * 128], Wo[:, kc, half * 512:(half + 1) * 512],
                         start=(kc == 0), stop=(kc == 7), R=[yT_, Wo], W=[pW])
            P.tt(z_[:], pW[:], gbc[:, ri, :], ALU.mult, R=[pW, gbc], W=[z_])
            P.stt(z_[:], xt_[:], float(DN_ALPHA), z_[:], ALU.mult, ALU.add, R=[xt_, z_], W=[z_])
            _layernorm(P, z_[:], z_[:], z_, lng, lnb, st, mv)
            P.dma(x1s[t], z_[:], R=[z_], W=[P.dres("x1s", t)])
            for half in range(2):
                pb = pp[half]
                for q in range(4):
                    kc = half * 4 + q
                    P.tr(pb[:, q * 128:(q + 1) * 128], z_[:, kc * 128:(kc + 1) * 128], identF, R=[z_, cst], W=[pb])
                for q in range(4):
                    kc = half * 4 + q
                    o = hT[:, kc, tc]
                    i_ = pb[:, q * 128:(q + 1) * 128]
                    sc = modB[:, 32 + kc, r:r + 1]
                    sh = modA[:, 24 + kc, r:r + 1]
                    if q % 2 == 0:
                        P.ts(o, i_, sc, sh, ALU.mult, ALU.add, R=[pb, modA, modB], W=[hT.res[t]])
                    else:
                        P.act(o, i_, AF.Identity, R=[pb, modA, modB], W=[hT.res[t]], bias=sh, scale=sc)
        P.barrier()


def phase_ffn(P, nc, l, s, last, Wd, cst, cstb, hT, modrow_d, x1s, xres, out_d, ffn_act):
    blocks = ([] if last else [(0, TCTX)]) + [(TCTX + 4 * i, 4) for i in range(4)]
    with ExitStack() as ph:
        P.stack = ph
        W1 = P.sb([128, 8, 2 * FFH], BF16, nres=4, name="W1")
        _loadw(P, W1, Wd["w_ffn_in"][l].rearrange("(kc p) n -> p kc n", p=128), 2 * FFH)
        pGt = [P.ps([128, 512], name="pGt%d" % i) for i in range(2)]
        pUp = [P.ps([128, 512], name="pUp%d" % i) for i in range(2)]
        sgu = [P.sb([128, 512], F32, name="sgu%d" % i) for i in range(2)]
        actT = [P.sb([128, 22, 512], BF16, name="actT%d" % i) for i in range(2)]
        for bi, (t0, ntile) in enumerate(blocks):
            N = ntile * 128
            cols = slice(t0 * 128, t0 * 128 + N)
            tr_ = [hT.res[t] for t in range(t0, t0 + ntile)]
            aT = actT[bi % 2]
            for j in range(22):
                pg, pu, sg_ = pGt[j % 2], pUp[j % 2], sgu[j % 2]
                for kc in range(8):
                    P.mm(pg[:, 0:N], W1[:, kc, j * 128:(j + 1) * 128], hT[:, kc, cols], start=(kc == 0), stop=(kc == 7), R=[W1] + tr_, W=[pg])
                for kc in range(8):
                    P.mm(pu[:, 0:N], W1[:, kc, FFH + j * 128:FFH + (j + 1) * 128], hT[:, kc, cols], start=(kc == 0), stop=(kc == 7),
                         R=[W1] + tr_, W=[pu])
                P.act(sg_[:, 0:N], pg[:, 0:N], AF.Silu, R=[pg], W=[sg_])
                P.tt(aT[:, j, 0:N], sg_[:, 0:N], pu[:, 0:N], ALU.mult, R=[sg_, pu], W=[aT])
            for ti in range(ntile):
                t = t0 + ti
                P.dma(ffn_act[t].rearrange("p (j k) -> p j k", j=22), aT[:, :, ti * 128:(ti + 1) * 128], R=[aT], W=[P.dres("ffn_act", t)])
        P.barrier()
    with ExitStack() as ph:
        P.stack = ph
        W2 = P.sb([128, 22, 1024], BF16, nres=11, name="W2")
        _loadw(P, W2, Wd["w_ffn_out"][l].rearrange("(j p) n -> p j n", p=128), 1024)
        pF = [P.ps([128, 1024], name="pF%d" % i) for i in range(2)]
        gbc = P.sb([128, 2, 1024], F32, name="gbc2")
        _gate_bcast(P, cst, modrow_d, 5120, s, pF[0], gbc)
        lng = P.sb([128, 1024], F32, name="lng2")
        lnb = P.sb([128, 1024], F32, name="lnb2")
        P.dma(lng[:], Wd["ln2_g"][l:l + 1, :].broadcast_to([128, 1024]), W=[lng])
        P.dma(lnb[:], Wd["ln2_b"][l:l + 1, :].broadcast_to([128, 1024]), W=[lnb])
        aT = [P.sb([128, 22, 128], BF16, name="aTl%d" % i) for i in range(2)]
        xt = [P.sb([128, 1024], F32, name="x1l%d" % i) for i in range(2)]
        z = [P.sb([128, 1024], F32, name="z2%d" % i) for i in range(2)]
        st = P.sb([128, 12], F32, name="st2")
        mv = P.sb([128, 4], F32, name="mv2")
        tiles = [t0 + ti for (t0, ntile) in blocks for ti in range(ntile)]
        for i, t in enumerate(tiles):
            ri = 1 if t < TCTX else 0
            a_, xt_, z_, pF_ = aT[i % 2], xt[i % 2], z[i % 2], pF[i % 2]
            P.dma(a_[:], ffn_act[t].rearrange("p (j k) -> p j k", j=22), R=[P.dres("ffn_act", t)], W=[a_])
            P.dma(xt_[:], x1s[t], R=[P.dres("x1s", t)], W=[xt_])
            for j in range(22):
                for half in range(2):
                    P.mm(pF_[:, half * 512:(half + 1) * 512], a_[:, j, :], W2[:, j, half * 512:(half + 1) * 512],
                         start=(j == 0), stop=(j == 21), R=[a_, W2], W=[pF_])
            P.tt(z_[:], pF_[:], gbc[:, ri, :], ALU.mult, R=[pF_, gbc], W=[z_])
            P.stt(z_[:], xt_[:], float(DN_ALPHA), z_[:], ALU.mult, ALU.add, R=[xt_, z_], W=[z_])
            _layernorm(P, z_[:], z_[:], z_, lng, lnb, st, mv)
            if last:
                P.dma(out_d[s, (t - TCTX) * 128:(t - TCTX + 1) * 128, :], z_[:], R=[z_], W=[P.dres("out", (s, t))])
            else:
                P.dma(xres[s, t], z_[:], R=[z_], W=[P.dres("xres", (s, t))])
        P.barrier()


def kernel(**inputs):
    inp = {k: np.asarray(v) for k, v in inputs.items()}
    nc = build(DEPTH)
    consts = make_consts()
    rope = make_rope()
    maps = []
    for core in range(8):
        b0 = 2 * core
        c3 = np.stack([inp["c"][b0], inp["c"][b0 + 1], inp["c_ctx"]])
        c3T = np.ascontiguousarray(c3.reshape(3, 8, 128).transpose(2, 1, 0))
        m = {"x": np.ascontiguousarray(inp["x"][b0:b0 + 2]), "ctx": np.ascontiguousarray(inp["ctx"][b0:b0 + 2]),
             "c3T": c3T, "consts": consts, "rope": rope, "hg_lb": inp["hg_lb"]}
        for n, _ in WSHAPES:
            m[n] = inp[n]
        maps.append(m)
    res = run_bass_kernel_spmd(nc, maps, core_ids=list(range(8)))
    out = np.concatenate([np.asarray(r["out"]) for r in res.results], axis=0)
    return out.astype(np.float32)
```

```python
from contextlib import ExitStack
import numpy as np
import concourse.bass as bass
import concourse.mybir as mybir
from concourse.bass_utils import run_bass_kernel_spmd

F32 = mybir.dt.float32
BF16 = mybir.dt.bfloat16
AF = mybir.ActivationFunctionType
ALU = mybir.AluOpType
AX = mybir.AxisListType

SAME_ENGINE_SYNC = True
N_DMA_SEMS = 16

DEPTH = 4
DM = 1024
NT = 18
TCTX = 2
TOK = NT * 128
FFH = 2816
IN_W = 14912
O_HQ, O_HFF, O_HFB, O_HI, O_HG = 0, 1024, 2048, 3072, 4096
O_AQ, O_AK, O_AV = 5120, 6144, 6400
O_MZ, O_MX, O_MDT, O_GT = 6656, 8704, 11776, 11840
DN_ALPHA = (2 * DEPTH) ** 0.25
LN_EPS = 1e-5
RMS_EPS = 1e-6

C_ID, C_ONES, C_LE, C_GE, C_GT, C_LT, C_HDF, C_HDB, C_HMF, C_HMB, C_SEL, C_CI = (
    0, 128, 256, 384, 512, 640, 768, 896, 1024, 1152, 1280, 1664)
NCONST = 1664 + 4


def make_consts():
    a = np.arange(128)[:, None]
    b = np.arange(128)[None, :]
    bd = (a // 32) == (b // 32)
    blocks = [a == b, np.ones((128, 128), bool), a <= b, a >= b, a > b, a < b,
              bd & (a > b), bd & (a < b), bd & (a <= b), bd & (a >= b)]
    sel = np.zeros((128, 3 * 128), bool)
    for r in range(3):
        sel[r, r * 128:(r + 1) * 128] = True
    ci = (a // 32) == np.arange(4)[None, :]
    return np.concatenate(blocks + [sel, ci], axis=1).astype(np.float32)


def make_rope():
    rows = 2048 // 64
    row, col = np.meshgrid(np.arange(rows, dtype=np.float32), np.arange(64, dtype=np.float32), indexing="ij")
    n_pairs = 32
    inv_freq = (np.float32(10000.0) ** (-np.arange(n_pairs, dtype=np.float32) / np.float32(n_pairs))).astype(np.float32)
    ang = np.concatenate([row.reshape(-1, 1) * inv_freq, col.reshape(-1, 1) * inv_freq], axis=-1).astype(np.float32)
    return np.stack([np.cos(ang), np.sin(ang)]).astype(np.float32)


class Res:
    __slots__ = ("w", "r")

    def __init__(self):
        self.w = None
        self.r = {}


class Tile:
    def __init__(self, t, nres=1):
        self.t = t
        self.res = [Res() for _ in range(nres)]

    def __getitem__(self, k):
        return self.t[k]

    @property
    def r(self):
        return self.res[0]


def _res(items):
    out = []
    for it in items:
        if isinstance(it, Tile):
            out.extend(it.res)
        elif isinstance(it, Res):
            out.append(it)
        elif it is None:
            pass
        else:
            out.extend(_res(it))
    return out


class Prog:
    def __init__(self, nc, stack):
        self.nc = nc
        self.eng = {"pe": nc.tensor, "act": nc.scalar, "dve": nc.vector, "pool": nc.gpsimd, "sp": nc.sync}
        self.sem = {}
        for e in self.eng:
            self.sem[e] = stack.enter_context(nc.semaphore("s_" + e))
        self.dma_sems = {}
        for q in ("sp", "act", "pool"):
            self.dma_sems[q] = [("d", q, i) for i in range(N_DMA_SEMS)]
            for k in self.dma_sems[q]:
                self.sem[k] = stack.enter_context(nc.semaphore("d_%s_%d" % (q, k[2])))
        self.count = {k: 0 for k in self.sem}
        self.known = {e: {} for e in self.eng}
        self.vc = {}
        self.dma_rr = {q: 0 for q in self.dma_sems}
        self.n_instr = 0
        self.n_wait = 0
        self.stack = None
        self._dres = {}
        self._uid = 0

    def sb(self, shape, dtype, nres=1, name=None):
        self._uid += 1
        t = self.stack.enter_context(self.nc.sbuf_tensor("%s_%d" % (name or "t", self._uid), list(shape), dtype))
        return Tile(t, nres)

    def ps(self, shape, dtype=F32, name=None):
        self._uid += 1
        t = self.stack.enter_context(self.nc.psum_tensor("%s_%d" % (name or "p", self._uid), list(shape), dtype))
        return Tile(t)

    def dres(self, name, idx=0):
        k = (name, idx)
        if k not in self._dres:
            self._dres[k] = Res()
        return self._dres[k]

    def _deps(self, E, reads, writes):
        deps = {}
        for r in reads:
            if r.w is not None and deps.get(r.w[0], 0) < r.w[1]:
                deps[r.w[0]] = r.w[1]
        for w in writes:
            if w.w is not None and deps.get(w.w[0], 0) < w.w[1]:
                deps[w.w[0]] = w.w[1]
            for k, v in w.r.items():
                if deps.get(k, 0) < v:
                    deps[k] = v
        kn = self.known[E]
        out = []
        for k, v in deps.items():
            if k == E and (E == "pe" or E == "sp" or not SAME_ENGINE_SYNC):
                continue
            if kn.get(k, 0) >= v:
                continue
            out.append((k, v))
        return out

    def _wait(self, E, waits):
        eng = self.eng[E]
        kn = self.known[E]
        for k, v in waits:
            eng.wait_ge(self.sem[k], v)
            self.n_wait += 1
            snap = self.vc.get((k, v))
            if snap is not None:
                for kk, vv in snap.items():
                    if kn.get(kk, 0) < vv:
                        kn[kk] = vv
            if kn.get(k, 0) < v:
                kn[k] = v

    def _finish(self, E, ev, ins, inc, reads, writes):
        ins.then_inc(self.sem[ev[0]], inc)
        snap = dict(self.known[E])
        snap[ev[0]] = ev[1]
        self.vc[ev] = snap
        k, v = ev
        for r in reads:
            if r.r.get(k, 0) < v:
                r.r[k] = v
        for w in writes:
            w.w = ev
            w.r = {}
        self.n_instr += 1

    def op(self, E, fn, R=(), W=()):
        reads = _res(R)
        writes = _res(W)
        self._wait(E, self._deps(E, reads, writes))
        ins = fn(self.eng[E])
        self.count[E] += 1
        ev = (E, self.count[E])
        self._finish(E, ev, ins, 1, reads, writes)

    def dma(self, out, in_, R=(), W=(), q="sp", **kw):
        reads = _res(R)
        writes = _res(W)
        self._wait(q, self._deps(q, reads, writes))
        key = self.dma_sems[q][self.dma_rr[q] % N_DMA_SEMS]
        self.dma_rr[q] += 1
        if self.count[key] > 0 and self.known[q].get(key, 0) < self.count[key]:
            self._wait(q, [(key, self.count[key])])
        ins = self.eng[q].dma_start(out=out, in_=in_, **kw)
        self.count[key] += 16
        ev = (key, self.count[key])
        self._finish(q, ev, ins, 16, reads, writes)

    def barrier(self):
        targets = [(k, v) for k, v in self.count.items() if v > 0]
        for E in self.eng:
            waits = [(k, v) for k, v in targets if self.known[E].get(k, 0) < v and not (k == E and E == "pe")]
            self._wait(E, waits)

    def wait_all_on(self, E="sp"):
        waits = [(k, v) for k, v in self.count.items() if v > 0 and k != E and self.known[E].get(k, 0) < v]
        self._wait(E, waits)

    def mm(self, out, lhsT, rhs, start=True, stop=True, R=(), W=()):
        self.op("pe", lambda e: e.matmul(out, lhsT, rhs, start=start, stop=stop), R, W)

    def tr(self, out, in_, ident, R=(), W=()):
        self.op("pe", lambda e: e.transpose(out, in_, ident), R, W)

    def act(self, out, in_, func, R=(), W=(), bias=0.0, scale=1.0, accum_out=None):
        if accum_out is None:
            self.op("act", lambda e: e.activation(out, in_, func, bias=bias, scale=scale), R, W)
        else:
            self.op("act", lambda e: e.activation(out, in_, func, bias=bias, scale=scale, accum_out=accum_out), R, W)

    def tt(self, out, in0, in1, op, R=(), W=(), E="dve"):
        self.op(E, lambda e: e.tensor_tensor(out, in0, in1, op), R, W)

    def ts(self, out, in0, s1, s2, op0, op1=None, R=(), W=(), E="dve"):
        if op1 is None:
            self.op(E, lambda e: e.tensor_scalar(out, in0, s1, None, op0), R, W)
        else:
            self.op(E, lambda e: e.tensor_scalar(out, in0, s1, s2, op0, op1), R, W)

    def stt(self, out, in0, scalar, in1, op0, op1, R=(), W=()):
        self.op("dve", lambda e: e.scalar_tensor_tensor(out, in0, scalar, in1, op0, op1), R, W)

    def cp(self, out, in_, R=(), W=(), E="dve"):
        if E == "act":
            self.op("act", lambda e: e.activation(out, in_, AF.Copy), R, W)
        else:
            self.op(E, lambda e: e.tensor_copy(out, in_), R, W)


WSHAPES = [
    ("w_mod", (DM, 6 * DM)), ("b_mod", (6 * DM,)), ("w_in", (DM, IN_W)), ("hg_gnorm", (128,)),
    ("at_qnorm", (128,)), ("at_knorm", (128,)), ("mb_conv_w", (5, 3072)), ("mb_conv_b", (3072,)),
    ("mb_dt_bias", (2, 32)), ("mb_a_log", (2, 32)), ("mb_d", (32,)), ("mb_norm", (2048,)),
    ("w_br_hg", (1024, DM)), ("w_br_at", (1024, DM)), ("w_br_mb", (2048, DM)), ("w_out", (DM, DM)),
    ("ln1_g", (DM,)), ("ln1_b", (DM,)), ("w_ffn_in", (DM, 2 * FFH)), ("w_ffn_out", (FFH, DM)),
    ("ln2_g", (DM,)), ("ln2_b", (DM,)),
]


def build(depth=DEPTH, dbg=None):
    dbg = dbg or {}
    dump = dbg.get("dump", set())
    phases = dbg.get("phases", {"p0", "hg", "at", "mb", "merge", "ffn"})
    seqs = dbg.get("seqs", [0, 1])
    nc = bass.Bass("TRN2", target_bir_lowering=False)

    def din(name, shape):
        return nc.dram_tensor(name, list(shape), F32, kind="ExternalInput").ap()

    x_in = din("x", [2, 2048, DM])
    ctx_in = din("ctx", [2, 256, DM])
    c3T = din("c3T", [128, 8, 3])
    consts_d = din("consts", [128, NCONST])
    rope_d = din("rope", [2, 2048, 64])
    hg_lb_d = din("hg_lb", [4, 2, 1024])
    Wd = {n: din(n, (depth,) + s) for n, s in WSHAPES}
    out_d = nc.dram_tensor("out", [2, 2048, DM], F32, kind="ExternalOutput").ap()

    def dscr(name, shape, dt):
        kind = "ExternalOutput" if name in dump else "Internal"
        return nc.dram_tensor(name, list(shape), dt, kind=kind).ap()

    xres = dscr("xres", [2, NT, 128, DM], F32)
    x1s = dscr("x1s", [NT, 128, DM], F32)
    hg_of = dscr("hg_of", [NT, 128, 1024], F32)
    ohgT = dscr("ohgT", [NT, 128, 1024], BF16)
    oatT = dscr("oatT", [128, 8, TOK], BF16)
    ombT = dscr("ombT", [NT, 128, 2048], BF16)
    mb_xs = dscr("mb_xs", [NT, 128, 2048], BF16)
    mb_yf = dscr("mb_yf", [NT, 128, 2048], F32)
    ymTd = dscr("ymT", [NT, 128, 1024], BF16)
    lbs_d = dscr("lbs", [4, 2, 1024], F32)
    modrow_d = dscr("modrow", [3, 6 * DM], F32)
    ffn_act = dscr("ffn_act", [NT, 128, 22 * 128], BF16)

    with ExitStack() as top:
        P = Prog(nc, top)
        P.stack = top
        cst = P.sb([128, NCONST], F32, name="cst")
        P.dma(cst[:], consts_d, W=[cst])
        cstb = P.sb([128, 256], BF16, name="cstb")
        P.cp(cstb[:], cst[:, 0:256], R=[cst], W=[cstb])
        identF = cst[:, C_ID:C_ID + 128]
        onesF = cst[:, C_ONES:C_ONES + 128]
        identB = cstb[:, 0:128]
        onesB = cstb[:, 128:256]
        scT = P.sb([128, 8, 3], F32, name="scT")
        P.dma(scT[:], c3T, W=[scT])
        P.act(scT[:], scT[:], AF.Silu, R=[scT], W=[scT])
        hT = P.sb([128, 8, TOK], BF16, nres=NT, name="hT")

        def xsrc(l, s, t):
            if l == 0:
                return (ctx_in[s, t * 128:(t + 1) * 128, :] if t < TCTX else x_in[s, (t - TCTX) * 128:(t - TCTX + 1) * 128, :]), None
            return xres[s, t], P.dres("xres", (s, t))

        with ExitStack() as ph:
            P.stack = ph
            e = P.sb([128, 4, 16], F32, name="lb_e")
            P.dma(e[:].rearrange("p l (d j) -> p l d j", d=2),
                  hg_lb_d.rearrange("l d (p j) -> p l d j", p=128), W=[e])
            P.act(e[:], e[:], AF.Exp, R=[e], W=[e])
            s_ = P.sb([128, 16], F32, name="lb_s")
            P.tt(s_[:], e[:, 0, :], e[:, 1, :], ALU.add, R=[e], W=[s_])
            P.tt(s_[:], s_[:], e[:, 2, :], ALU.add, R=[e, s_], W=[s_])
            P.tt(s_[:], s_[:], e[:, 3, :], ALU.add, R=[e, s_], W=[s_])
            P.op("dve", lambda en: en.reciprocal(s_[:], s_[:]), R=[s_], W=[s_])
            P.tt(e[:], e[:], s_[:].rearrange("p (o j) -> p o j", o=1).broadcast_to([128, 4, 16]), ALU.mult, R=[e, s_], W=[e])
            lbt = P.sb([128, 4, 16], F32, name="lb_t")
            P.op("dve", lambda en: en.memset(lbt[:, 0, :], 0.0), W=[lbt])
            P.cp(lbt[:, 1, :], e[:, 1, :], R=[e], W=[lbt])
            P.tt(lbt[:, 2, :], lbt[:, 1, :], e[:, 2, :], ALU.add, R=[e, lbt], W=[lbt])
            P.tt(lbt[:, 3, :], lbt[:, 2, :], e[:, 3, :], ALU.add, R=[e, lbt], W=[lbt])
            P.dma(lbs_d.rearrange("l d (p j) -> p l d j", p=128),
                  lbt[:].rearrange("p l (d j) -> p l d j", d=2), R=[lbt], W=[P.dres("lbs")])
            P.barrier()
        P.stack = top

        for l in range(depth):
            last = (l == depth - 1) and not dbg.get('nolast', False)
            with ExitStack() as lay:
                P.stack = lay
                modA = P.sb([128, 48, 3], F32, name="modA")
                modB = P.sb([128, 48, 3], F32, name="modB")
                with ExitStack() as ph:
                    P.stack = ph
                    mrow = P.sb([3, 6 * DM], F32, name="mrow")
                    brow = P.sb([3, 6 * DM], F32, name="brow")
                    P.dma(brow[:], Wd["b_mod"][l:l + 1, :].broadcast_to([3, 6 * DM]), W=[brow])
                    wm = [P.sb([128, 8, 1536], F32, name="wm%d" % i) for i in range(2)]
                    pm = [P.ps([128, 512], name="pm%d" % i) for i in range(2)]
                    for blk in range(4):
                        w_ = wm[blk % 2]
                        P.dma(w_[:], Wd["w_mod"][l].rearrange("(kc p) n -> p kc n", p=128)[:, :, blk * 1536:(blk + 1) * 1536], W=[w_])
                        for j in range(3):
                            pj = pm[j % 2]
                            for kc in range(8):
                                P.mm(pj[0:3, :], scT[:, kc, :], w_[:, kc, j * 512:(j + 1) * 512],
                                     start=(kc == 0), stop=(kc == 7), R=[scT, w_], W=[pj])
                            c0 = blk * 1536 + j * 512
                            P.tt(mrow[:, c0:c0 + 512], pj[0:3, :], brow[:, c0:c0 + 512], ALU.add, R=[pj, brow], W=[mrow])
                    P.dma(modrow_d, mrow[:], R=[mrow], W=[P.dres("modrow")])
                    pT = P.ps([128, 512], name="pmT")
                    for j in range(48):
                        P.tr(pT[:, j * 3:(j + 1) * 3], mrow[:, j * 128:(j + 1) * 128], identF[0:3, 0:3], R=[mrow, cst], W=[pT])
                    P.cp(modA[:].rearrange("p j r -> p (j r)"), pT[:, 0:144], R=[pT], W=[modA])
                    P.ts(modB[:].rearrange("p j r -> p (j r)"), pT[:, 0:144], 1.0, None, ALU.add, R=[pT], W=[modB])
                    P.barrier()
                P.stack = lay

                for s in seqs:
                    with ExitStack() as sq:
                        P.stack = sq
                        def featT(l_, s_, tiles, src_fn, sc_chunk, sh_chunk, ph):
                            xt = [P.sb([128, DM], F32, name="p0x%d" % i) for i in range(2)]
                            pp = [P.ps([128, 512], name="p0p%d" % i) for i in range(2)]
                            for i, t in enumerate(tiles):
                                r = 2 if t < TCTX else s_
                                xb = xt[i % 2]
                                src, sres = src_fn(t)
                                P.dma(xb[:], src, R=[sres], W=[xb])
                                for half in range(2):
                                    pb = pp[half]
                                    for q in range(4):
                                        kc = half * 4 + q
                                        P.tr(pb[:, q * 128:(q + 1) * 128], xb[:, kc * 128:(kc + 1) * 128], identF, R=[xb, cst], W=[pb])
                                    for q in range(4):
                                        kc = half * 4 + q
                                        o = hT[:, kc, t * 128:(t + 1) * 128]
                                        i_ = pb[:, q * 128:(q + 1) * 128]
                                        sc = modB[:, sc_chunk + kc, r:r + 1]
                                        sh = modA[:, sh_chunk + kc, r:r + 1]
                                        if q % 2 == 0:
                                            P.ts(o, i_, sc, sh, ALU.mult, ALU.add, R=[pb, modA, modB], W=[hT.res[t]])
                                        else:
                                            P.act(o, i_, AF.Identity, R=[pb, modA, modB], W=[hT.res[t]], bias=sh, scale=sc)

                        if "p0" in phases:
                            with ExitStack() as ph:
                                P.stack = ph
                                featT(l, s, list(range(NT)), lambda t: xsrc(l, s, t), 8, 0, ph)
                                P.barrier()
                            P.stack = sq
                        if "hg" in phases:
                            with ExitStack() as ph:
                                P.stack = ph
                                phase_hg(P, nc, l, s, Wd, cst, cstb, hT, lbs_d, hg_of, ohgT)
                                P.barrier()
                            P.stack = sq
                        if "at" in phases:
                            with ExitStack() as ph:
                                P.stack = ph
                                phase_at(P, nc, l, s, Wd, cst, cstb, hT, rope_d, oatT)
                                P.barrier()
                            P.stack = sq
                        if "mb" in phases:
                            phase_mb(P, nc, l, s, Wd, cst, cstb, hT, mb_xs, mb_yf, ombT, sq)
                            P.stack = sq
                        if "merge" in phases:
                            phase_merge(P, nc, l, s, last, Wd, cst, cstb, hT, modA, modB, modrow_d, ohgT, oatT, ombT,
                                        ymTd, x1s, lambda t: xsrc(l, s, t), sq)
                            P.stack = sq
                        if "ffn" in phases:
                            phase_ffn(P, nc, l, s, last, Wd, cst, cstb, hT, modrow_d, x1s, xres, out_d, ffn_act)
                            P.stack = sq
                    P.stack = lay
            P.stack = top
        P.wait_all_on("sp")
        build.stats = (P.n_instr, P.n_wait)
    return nc


def _loadw(P, dst, src3, n, q="pool"):
    KC = src3.shape[1]
    step = 2
    for i, kc in enumerate(range(0, KC, step)):
        P.dma(dst[:, kc:kc + step, 0:n], src3[:, kc:kc + step, :], W=[dst.res[i % len(dst.res)]], q=q)


def _loadw_cols(P, dst, src3, n, cb, order=None, q="pool"):
    npc = (n + cb - 1) // cb
    KC = src3.shape[1]
    assert len(dst.res) >= 2 * npc
    dst.cb = cb
    for i in (order or range(npc)):
        c0, c1 = i * cb, min(n, (i + 1) * cb)
        for hf in range(2):
            k0, k1 = hf * KC // 2, (hf + 1) * KC // 2
            P.dma(dst[:, k0:k1, c0:c1], src3[:, k0:k1, c0:c1], W=[dst.res[2 * i + hf]], q=q)


def _cr(W, c0, c1):
    return [W.res[j] for i in range(c0 // W.cb, (c1 - 1) // W.cb + 1) for j in (2 * i, 2 * i + 1)]


def _interleave(gens):
    gens = [g for g in gens if g is not None]
    while gens:
        for g in list(gens):
            try:
                next(g)
            except StopIteration:
                gens.remove(g)


def phase_hg(P, nc, l, s, Wd, cst, cstb, hT, lbs_d, hg_of, ohgT):
    identB = cstb[:, 0:128]
    onesF = cst[:, C_ONES:C_ONES + 128]
    CI = cst[:, C_CI:C_CI + 4]
    Wl = Wd["w_in"][l].rearrange("(kc p) n -> p kc n", p=128)
    Wq = P.sb([128, 8, 1024], BF16, nres=4, name="Wq")
    Wf = P.sb([128, 8, 1024], BF16, nres=4, name="Wf")
    Wi = P.sb([128, 8, 1024], BF16, nres=4, name="Wi")
    Wg = P.sb([128, 8, 1024], BF16, nres=4, name="Wg")
    _loadw(P, Wf, Wl[:, :, O_HFF:O_HFF + 1024], 1024)
    _loadw(P, Wq, Wl[:, :, O_HQ:O_HQ + 1024], 1024)
    _loadw(P, Wi, Wl[:, :, O_HI:O_HI + 1024], 1024)
    lbb = P.sb([128, 1024], F32, name="lbb")
    oml = P.sb([128, 1024], F32, name="oml")

    def load_lb(d):
        P.dma(lbb[:], lbs_d[l, d:d + 1, :].broadcast_to([128, 1024]), R=[P.dres("lbs")], W=[lbb])
        P.ts(oml[:], lbb[:], -1.0, 1.0, ALU.mult, ALU.add, R=[lbb], W=[oml])

    load_lb(0)
    gn = P.sb([128, 1], F32, name="gn")
    P.dma(gn[:], Wd["hg_gnorm"][l].rearrange("(p o) -> p o", o=1), W=[gn])
    lf = P.sb([128, 1024], F32, name="lf")
    kk = P.sb([128, 1024], F32, name="kk")
    ex = P.sb([128, 1024], F32, name="ex")
    exn = P.sb([128, 1024], F32, name="exn")
    qs = P.sb([128, 1024], F32, name="qs")
    Kt = [P.sb([128, 1024], BF16, name="Kt%d" % i) for i in range(2)]
    Qt = P.sb([128, 1024], BF16, name="Qt")
    Vt = [P.sb([128, 1024], BF16, name="Vt%d" % i) for i in range(2)]
    Ktm = [P.sb([128, 1024], BF16, name="Ktm%d" % i) for i in range(2)]
    KT = [P.sb([128, 1024], BF16, name="KT%d" % i) for i in range(2)]
    QT = [P.sb([128, 1024], BF16, name="QT%d" % i) for i in range(2)]
    ec = [P.sb([128, 32], F32, name="ec%d" % i) for i in range(2)]
    sg = [P.sb([128, 1024], F32, name="sg%d" % i) for i in range(2)]
    S = P.sb([128, 8, 128], F32, nres=8, name="S")
    Sb = P.sb([128, 8, 128], BF16, nres=8, name="Sb")
    ATm = P.sb([128, 8, 128], BF16, nres=8, name="ATm")
    ost = P.sb([128, 1024], F32, name="ost")
    ofl = P.sb([128, 1024], F32, name="ofl")
    osq = P.sb([128, 1024], F32, name="osq")
    rs = ofl
    ob = P.sb([128, 1024], BF16, name="ob")
    pA = P.ps([128, 1024], name="pA")
    pT = P.ps([128, 1024], BF16, name="pT")
    pC = P.ps([128, 512], name="pC")
    pCb = pC[:].bitcast(BF16)
    pO = P.ps([128, 1024], name="pO")
    pSl = P.ps([128, 1024], name="pSl")
    _r0, _r1 = Res(), Res()
    pSl.res = [_r0] * 4 + [_r1] * 4

    def proj(Wt, t):
        for half in range(2):
            for kc in range(8):
                P.mm(pA[:, half * 512:(half + 1) * 512], hT[:, kc, t * 128:(t + 1) * 128],
                     Wt[:, kc, half * 512:(half + 1) * 512], start=(kc == 0), stop=(kc == 7),
                     R=[hT.res[t], Wt], W=[pA])

    def prologue(t, d, b):
        D = cst[:, (C_HDF if d == 0 else C_HDB):(C_HDF if d == 0 else C_HDB) + 128]
        proj(Wf, t)
        P.act(lf[:], pA[:], AF.Sigmoid, R=[pA], W=[lf])
        P.act(kk[:], pA[:], AF.Sigmoid, R=[pA], W=[kk], scale=-1.0)
        yield
        proj(Wq, t)
        P.tt(lf[:], lf[:], oml[:], ALU.mult, R=[lf, oml], W=[lf])
        P.tt(lf[:], lf[:], lbb[:], ALU.add, R=[lf, lbb], W=[lf])
        P.act(lf[:], lf[:], AF.Ln, R=[lf], W=[lf])
        P.act(qs[:], pA[:], AF.Silu, R=[pA], W=[qs])
        P.tt(kk[:], kk[:], oml[:], ALU.mult, R=[kk, oml], W=[kk])
        yield
        proj(Wi, t)
        P.cp(Vt[b][:], pA[:], R=[pA], W=[Vt[b]], E="act")
        yield
        for half in range(2):
            P.mm(pA[:, half * 512:(half + 1) * 512], D, lf[:, half * 512:(half + 1) * 512], R=[cst, lf], W=[pA])
        for h in range(8):
            P.mm(pC[:, h * 4:(h + 1) * 4], lf[:, h * 128:(h + 1) * 128], CI, R=[lf, cst], W=[pC])
        P.act(ex[:], pA[:], AF.Exp, R=[pA], W=[ex])
        P.act(exn[:], pA[:], AF.Exp, R=[pA], W=[exn], scale=-1.0)
        P.act(ec[b][:], pC[:, 0:32], AF.Exp, R=[pC], W=[ec[b]])
        yield
        P.stt(Qt[:], qs[:], float(128 ** -0.5), exn[:], ALU.mult, ALU.mult, R=[qs, exn], W=[Qt])
        P.tt(Kt[b][:], kk[:], ex[:], ALU.mult, R=[kk, ex], W=[Kt[b]])
        P.ts(Ktm[b][:], Kt[b][:], CI[:, 3:4], None, ALU.mult, R=[Kt[b], cst], W=[Ktm[b]])
        yield
        for h in range(8):
            P.tr(pT[:, h * 128:(h + 1) * 128], Kt[b][:, h * 128:(h + 1) * 128], identB, R=[Kt[b], cstb], W=[pT])
        P.cp(KT[b][:], pT[:], R=[pT], W=[KT[b]])
        for h in range(8):
            P.tr(pCb[:, h * 128:(h + 1) * 128], Qt[:, h * 128:(h + 1) * 128], identB, R=[Qt, cstb], W=[pC])
        P.cp(QT[b][:], pCb, R=[pC], W=[QT[b]], E="act")
        yield
        if d == 1:
            for h in range(8):
                for kc in range(8):
                    P.mm(pA[:, h * 128:(h + 1) * 128], Wg[:, kc, h * 128:(h + 1) * 128], hT[:, kc, t * 128:(t + 1) * 128],
                         start=(kc == 0), stop=(kc == 7), R=[Wg, hT.res[t]], W=[pA])
            P.act(sg[b][:], pA[:], AF.Silu, R=[pA], W=[sg[b]])
            yield

    def scan(t, d, b):
        M = cst[:, (C_HMF if d == 0 else C_HMB):(C_HMF if d == 0 else C_HMB) + 128]
        if d == 1:
            P.dma(ofl[:], hg_of[t], R=[P.dres("hg_of", t)], W=[ofl])
        for h in range(8):
            hs = slice(h * 128, (h + 1) * 128)
            P.mm(pSl[:, hs], KT[b][:, hs], QT[b][:, hs], R=[KT[b], QT[b]], W=[pSl.res[h]])
        for h in range(8):
            hs = slice(h * 128, (h + 1) * 128)
            P.tt(ATm[:, h, :], pSl[:, hs], M, ALU.mult, R=[pSl.res[h], cst], W=[ATm.res[h]])
        yield
        for h in range(8):
            hs = slice(h * 128, (h + 1) * 128)
            P.op("pe", lambda e, hs=hs, h=h: e.matmul(pO[:, hs], Vt[b][:, hs], ATm[:, h, :], start=(h % 4 == 0), stop=False,
                                                       skip_group_check=True), R=[Vt[b], ATm.res[h]], W=[pO])
        chunks = [0, 1, 2, 3] if d == 0 else [3, 2, 1, 0]
        for ci, c in enumerate(chunks):
            for h in range(8):
                e_c = ec[b][:, h * 4 + c:h * 4 + c + 1]
                P.op("act", lambda e, h=h, e_c=e_c: e.activation(Sb[:, h, :], S[:, h, :], AF.Identity, scale=e_c),
                     R=[S.res[h], ec[b]], W=[Sb.res[h]])
            for h in range(8):
                hs = slice(h * 128, (h + 1) * 128)
                if c == 3:
                    P.mm(pSl[:, hs], Ktm[b][:, hs], Vt[b][:, hs], R=[Ktm[b], Vt[b]], W=[pSl.res[h]])
                else:
                    P.mm(pSl[:, hs], Kt[b][c * 32:(c + 1) * 32, hs], Vt[b][c * 32:(c + 1) * 32, hs], R=[Kt[b], Vt[b]], W=[pSl.res[h]])
            for h in range(8):
                cs = slice(h * 128 + c * 32, h * 128 + (c + 1) * 32)
                P.op("pe", lambda e, cs=cs, h=h: e.matmul(pO[:, cs], Sb[:, h, :], QT[b][:, cs], start=False, stop=(ci == 3),
                                                           skip_group_check=True), R=[Sb.res[h], QT[b]], W=[pO])
            for h in range(8):
                hs = slice(h * 128, (h + 1) * 128)
                e_c = ec[b][:, h * 4 + c:h * 4 + c + 1]
                P.stt(S[:, h, :], S[:, h, :], e_c, pSl[:, hs], ALU.mult, ALU.add, R=[S.res[h], ec[b], pSl.res[h]], W=[S.res[h]])
            yield
        if d == 0:
            P.cp(ost[:], pO[:], R=[pO], W=[ost], E="act")
            P.dma(hg_of[t], ost[:], R=[ost], W=[P.dres("hg_of", t)])
        else:
            P.tt(ost[:], pO[:], ofl[:], ALU.add, R=[pO, ofl], W=[ost])
            P.act(osq[:], ost[:], AF.Square, R=[ost], W=[osq])
            for half in range(2):
                P.mm(pSl[:, half * 512:(half + 1) * 512], onesF, osq[:, half * 512:(half + 1) * 512], R=[cst, osq], W=[pSl])
            yield
            P.act(rs[:], pSl[:], AF.Sqrt, R=[pSl], W=[rs], scale=1.0 / 128.0, bias=RMS_EPS)
            P.op("dve", lambda e: e.reciprocal(rs[:], rs[:]), R=[rs], W=[rs])
            P.stt(osq[:], ost[:], gn[:, 0:1], rs[:], ALU.mult, ALU.mult, R=[ost, gn, rs], W=[osq])
            P.tt(ob[:], osq[:], sg[b][:], ALU.mult, R=[osq, sg[b]], W=[ob])
            P.dma(ohgT[t], ob[:], R=[ob], W=[P.dres("ohgT", t)])
        yield

    order = [(t, 0) for t in range(NT)] + [(t, 1) for t in list(range(TCTX - 1, -1, -1)) + list(range(NT - 1, TCTX - 1, -1))]
    P.op("dve", lambda e: e.memset(S[:], 0.0), W=[S])
    _interleave([prologue(order[0][0], order[0][1], 0)])
    for n, (t, d) in enumerate(order):
        nxt = None
        if n + 1 < len(order):
            t2, d2 = order[n + 1]
            if d2 == 1 and d == 0:
                _loadw(P, Wf, Wl[:, :, O_HFB:O_HFB + 1024], 1024)
                _loadw(P, Wg, Wl[:, :, O_HG:O_HG + 1024], 1024)
                load_lb(1)
            nxt = prologue(t2, d2, (n + 1) % 2)
        _interleave([scan(t, d, n % 2), nxt])
        if n + 1 < len(order) and order[n + 1][1] == 1 and d == 0:
            P.op("dve", lambda e: e.memset(S[:], 0.0), W=[S])


def _pipeline(items, gen_fn):
    active = []

    def rnd():
        for g in list(active):
            try:
                next(g)
            except StopIteration:
                active.remove(g)

    for it in items:
        active.append(gen_fn(it))
        rnd()
    while active:
        rnd()


def phase_at(P, nc, l, s, Wd, cst, cstb, hT, rope_d, oatT):
    identB = cstb[:, 0:128]
    onesB = cstb[:, 128:256]
    Wl = Wd["w_in"][l].rearrange("(kc p) n -> p kc n", p=128)
    Wa = P.sb([128, 8, 1536], BF16, nres=4, name="Wa")
    _loadw(P, Wa, Wl[:, :, O_AQ:O_AQ + 1536], 1536)
    QT = P.sb([128, 8, TOK], BF16, nres=NT, name="QTa")
    KTa = P.sb([128, 2, TOK], BF16, nres=NT, name="KTa")
    Va = P.sb([128, NT, 256], BF16, nres=NT, name="Va")
    with ExitStack() as sub:
        P.stack = sub
        cosT = P.sb([128, 16, 64], F32, name="cosT")
        sinT = P.sb([128, 16, 64], F32, name="sinT")
        P.dma(cosT[:], rope_d[0].rearrange("(t p) j -> p t j", p=128), W=[cosT])
        P.dma(sinT[:], rope_d[1].rearrange("(t p) j -> p t j", p=128), W=[sinT])
        wqk = P.sb([128, 10, 128], F32, name="wqk")
        P.dma(wqk[:, 0:8, :], Wd["at_qnorm"][l:l + 1, :].rearrange("o (h d) -> o h d", h=1).broadcast_to([128, 8, 128]), W=[wqk])
        P.dma(wqk[:, 8:10, :], Wd["at_knorm"][l:l + 1, :].rearrange("o (h d) -> o h d", h=1).broadcast_to([128, 2, 128]), W=[wqk])
        NB = 2
        sqt = [P.sb([128, 1280], F32, name="sqt%d" % i) for i in range(NB)]
        xn = [P.sb([128, 1280], F32, name="xn%d" % i) for i in range(NB)]
        ss = [P.sb([128, 10], F32, name="ss%d" % i) for i in range(NB)]
        t1 = [P.sb([128, 10, 64], F32, name="t1%d" % i) for i in range(NB)]
        t2 = [P.sb([128, 10, 64], F32, name="t2%d" % i) for i in range(NB)]
        xr = [P.sb([128, 1280], BF16, name="xr%d" % i) for i in range(NB)]
        pQ = [P.ps([128, 1536], name="pQ%d" % i) for i in range(NB)]
        pT = P.ps([128, 2048], BF16, name="pTa")

        def prep(t):
            b = t % NB
            tc = slice(t * 128, (t + 1) * 128)
            pQ_, sqt_, xn_, ss_, t1_, t2_, xr_ = pQ[b], sqt[b], xn[b], ss[b], t1[b], t2[b], xr[b]
            for j in range(3):
                for kc in range(8):
                    P.mm(pQ_[:, j * 512:(j + 1) * 512], hT[:, kc, tc], Wa[:, kc, j * 512:(j + 1) * 512],
                         start=(kc == 0), stop=(kc == 7), R=[hT.res[t], Wa], W=[pQ_])
            P.cp(Va[:, t, :], pQ_[:, 1280:1536], R=[pQ_], W=[Va.res[t]], E="act")
            P.act(sqt_[:], pQ_[:, 0:1280], AF.Square, R=[pQ_], W=[sqt_])
            P.op("dve", lambda e: e.tensor_reduce(ss_[:], sqt_[:].rearrange("p (h d) -> p h d", h=10), AX.X, ALU.add), R=[sqt_], W=[ss_])
            P.act(ss_[:], ss_[:], AF.Sqrt, R=[ss_], W=[ss_], scale=1.0 / 128.0, bias=RMS_EPS)
            P.op("dve", lambda e: e.reciprocal(ss_[:], ss_[:]), R=[ss_], W=[ss_])
            yield
            P.tt(xn_[:].rearrange("p (h d) -> p h d", h=10), pQ_[:, 0:1280].rearrange("p (h d) -> p h d", h=10),
                 ss_[:].rearrange("p (h o) -> p h o", o=1).broadcast_to([128, 10, 128]), ALU.mult, R=[pQ_, ss_], W=[xn_])
            P.tt(xn_[:], xn_[:], wqk[:].rearrange("p h d -> p (h d)"), ALU.mult, R=[xn_, wqk], W=[xn_])
            if t >= TCTX:
                tl = t - TCTX
                xv = xn_[:].rearrange("p (h j two) -> p h j two", h=10, two=2)
                xo = xr_[:].rearrange("p (h j two) -> p h j two", h=10, two=2)
                cb = cosT[:, tl:tl + 1, :].broadcast_to([128, 10, 64])
                sb_ = sinT[:, tl:tl + 1, :].broadcast_to([128, 10, 64])
                P.tt(t1_[:], xv[:, :, :, 0], cb, ALU.mult, R=[xn_, cosT], W=[t1_])
                P.tt(t2_[:], xv[:, :, :, 1], sb_, ALU.mult, R=[xn_, sinT], W=[t2_])
                P.tt(xo[:, :, :, 0], t1_[:], t2_[:], ALU.subtract, R=[t1_, t2_], W=[xr_])
                P.tt(t1_[:], xv[:, :, :, 0], sb_, ALU.mult, R=[xn_, sinT], W=[t1_])
                P.tt(t2_[:], xv[:, :, :, 1], cb, ALU.mult, R=[xn_, cosT], W=[t2_])
                P.tt(xo[:, :, :, 1], t1_[:], t2_[:], ALU.add, R=[t1_, t2_], W=[xr_])
            else:
                P.cp(xr_[:], xn_[:], R=[xn_], W=[xr_], E="act")
            yield
            for j in range(10):
                P.tr(pT[:, j * 128:(j + 1) * 128], xr_[:, j * 128:(j + 1) * 128], identB, R=[xr_, cstb], W=[pT])
            P.cp(QT[:, :, tc], pT[:, 0:1024].rearrange("p (h k) -> p h k", h=8), R=[pT], W=[QT.res[t]])
            P.cp(KTa[:, :, tc], pT[:, 1024:1280].rearrange("p (h k) -> p h k", h=2), R=[pT], W=[KTa.res[t]], E="act")
            yield

        _pipeline(list(range(NT)), prep)
        P.barrier()
    with ExitStack() as sub:
        P.stack = sub
        NS = 3
        pS = [P.ps([128, 512], name="pSa%d" % i) for i in range(NS)]
        pO = [P.ps([128, 512], name="pOa%d" % i) for i in range(2)]
        pL = [P.ps([128, 512], name="pLa%d" % i) for i in range(2)]
        Pt = [P.sb([128, 512], BF16, name="Pt%d" % i) for i in range(NS)]
        rl = [P.sb([128, 512], F32, name="rl%d" % i) for i in range(2)]
        oa = [P.sb([128, 512], BF16, name="oa%d" % i) for i in range(2)]
        blocks = [(0, 256, [0, 1])] + [(256 + qb * 512, 512, list(range(NT))) for qb in range(4)]
        sc = float(128 ** -0.5)
        steps = []
        for h in range(8):
            for bi, (q0, nq, kts) in enumerate(blocks):
                for i, kt in enumerate(kts):
                    steps.append((h, bi, q0, nq, kt, i == 0, i == len(kts) - 1))
        jobn = {}
        for (h, bi, *_r) in steps:
            jobn.setdefault((h, bi), len(jobn))

        def s_mm(n):
            h, bi, q0, nq, kt, first, lastk = steps[n]
            kv = h // 4
            qres = [QT.res[t] for t in range(q0 // 128, (q0 + nq) // 128)]
            P.mm(pS[n % NS][:, 0:nq], KTa[:, kv, kt * 128:(kt + 1) * 128], QT[:, h, q0:q0 + nq], R=[KTa.res[kt]] + qres, W=[pS[n % NS]])

        s_mm(0)
        for n in range(len(steps)):
            h, bi, q0, nq, kt, first, lastk = steps[n]
            kv = h // 4
            jb = jobn[(h, bi)] % 2
            if n + 1 < len(steps):
                s_mm(n + 1)
            ps, pt = pS[n % NS], Pt[n % NS]
            P.act(pt[:, 0:nq], ps[:, 0:nq], AF.Exp, R=[ps], W=[pt], scale=sc)
            P.mm(pO[jb][:, 0:nq], Va[:, kt, kv * 128:(kv + 1) * 128], pt[:, 0:nq], start=first, stop=lastk, R=[Va.res[kt], pt], W=[pO[jb]])
            P.mm(pL[jb][:, 0:nq], onesB, pt[:, 0:nq], start=first, stop=lastk, R=[cstb, pt], W=[pL[jb]])
            if lastk:
                P.op("dve", lambda e: e.reciprocal(rl[jb][:, 0:nq], pL[jb][:, 0:nq]), R=[pL[jb]], W=[rl[jb]])
                P.tt(oa[jb][:, 0:nq], pO[jb][:, 0:nq], rl[jb][:, 0:nq], ALU.mult, R=[pO[jb], rl[jb]], W=[oa[jb]])
                P.dma(oatT[:, h, q0:q0 + nq], oa[jb][:, 0:nq], R=[oa[jb]], W=[P.dres("oatT", (h, q0))])
        P.barrier()


def phase_mb(P, nc, l, s, Wd, cst, cstb, hT, mb_xs, mb_yf, ombT, sq):
    identB = cstb[:, 0:128]
    identF = cst[:, C_ID:C_ID + 128]
    onesF = cst[:, C_ONES:C_ONES + 128]
    Wl = Wd["w_in"][l].rearrange("(kc p) n -> p kc n", p=128)
    with ExitStack() as ph:
        P.stack = ph
        BT = P.sb([128, 4, TOK], BF16, name="BT")
        CT = P.sb([128, 4, TOK], BF16, name="CT")
        with ExitStack() as sub:
            P.stack = sub
            Wx = P.sb([128, 8, 3072], BF16, nres=24, name="Wx")
            _loadw_cols(P, Wx, Wl[:, :, O_MX:O_MX + 3072], 3072, 256)
            cwr = P.sb([120, 128], F32, name="cwr")
            P.dma(cwr[:], Wd["mb_conv_w"][l].rearrange("j (cc p) -> (j cc) p", p=128), W=[cwr])
            cbr = P.sb([24, 128], F32, name="cbr")
            P.dma(cbr[:], Wd["mb_conv_b"][l].rearrange("(cc p) -> cc p", p=128), W=[cbr])
            cw = P.sb([128, 120], F32, name="cw")
            cbias = P.sb([128, 24], F32, name="cbias")
            pX = [P.ps([128, 512], name="pX%d" % i) for i in range(4)]
            pTt = P.ps([128, 1024], BF16, name="pTt")
            pW_ = P.ps([128, 512], name="pW_")
            P.tr(pW_[:, 0:120], cwr[:], identF[0:120, 0:120], R=[cwr, cst], W=[pW_])
            P.cp(cw[:], pW_[:, 0:120], R=[pW_], W=[cw])
            P.tr(pW_[:, 128:152], cbr[:], identF[0:24, 0:24], R=[cbr, cst], W=[pW_])
            P.cp(cbias[:], pW_[:, 128:152], R=[pW_], W=[cbias])
            xb = P.sb([128, 2052], F32, name="xb")
            acc = P.sb([128, 2048], F32, name="acc")
            ub = P.sb([128, 2048], BF16, name="ub")
            stg = [P.sb([128, 8, 128], BF16, name="stg%d" % i) for i in range(2)]
            P.op("pool", lambda e: e.memset(xb[:], 0.0), W=[xb])
            k = 0
            for cc in range(24):
                for (t0, ntile) in ((0, TCTX), (TCTX, NT - TCTX)):
                    N = ntile * 128
                    for b in range((N + 511) // 512):
                        nb = min(512, N - b * 512)
                        c0 = t0 * 128 + b * 512
                        tr_ = [hT.res[t] for t in range(c0 // 128, (c0 + nb) // 128)]
                        for kc in range(8):
                            P.mm(pX[b][:, 0:nb], Wx[:, kc, cc * 128:(cc + 1) * 128], hT[:, kc, c0:c0 + nb],
                                 start=(kc == 0), stop=(kc == 7), R=_cr(Wx, cc * 128, (cc + 1) * 128) + tr_, W=[pX[b]])
                        P.cp(xb[:, 2 + b * 512:2 + b * 512 + nb], pX[b][:, 0:nb], R=[pX[b]], W=[xb], E=("act" if b % 2 else "dve"))
                    P.op("pool", lambda e: e.memset(xb[:, N + 2:N + 4], 0.0), W=[xb])
                    P.ts(acc[:, 0:N], xb[:, 0:N], cw[:, cc:cc + 1], None, ALU.mult, R=[xb, cw], W=[acc])
                    for j in range(1, 5):
                        P.stt(acc[:, 0:N], xb[:, j:j + N], cw[:, j * 24 + cc:j * 24 + cc + 1], acc[:, 0:N], ALU.mult, ALU.add,
                              R=[xb, cw, acc], W=[acc])
                    if cc < 16:
                        dst, dres_ = ub[:, 0:N], ub
                    elif cc < 20:
                        dst, dres_ = BT[:, cc - 16, t0 * 128:t0 * 128 + N], BT
                    else:
                        dst, dres_ = CT[:, cc - 20, t0 * 128:t0 * 128 + N], CT
                    P.act(dst, acc[:, 0:N], AF.Silu, R=[acc, cbias], W=[dres_], bias=cbias[:, cc:cc + 1])
                    if cc < 16:
                        for g0 in range(0, ntile, 8):
                            ng = min(8, ntile - g0)
                            for j in range(ng):
                                P.tr(pTt[:, j * 128:(j + 1) * 128], ub[:, (g0 + j) * 128:(g0 + j + 1) * 128], identB, R=[ub, cstb], W=[pTt])
                            st = stg[k % 2]
                            k += 1
                            P.cp(st[:, 0:ng, :], pTt[:, 0:ng * 128].rearrange("p (j c) -> p j c", j=ng), R=[pTt], W=[st],
                                 E=("act" if k % 2 else "dve"))
                            ta = t0 + g0
                            P.dma(mb_xs[ta:ta + ng, :, cc * 128:(cc + 1) * 128].rearrange("t p c -> p t c"), st[:, 0:ng, :],
                                  R=[st], W=[P.dres("mb_xs", t) for t in range(ta, ta + ng)])
            P.barrier()
        with ExitStack() as sub:
            P.stack = sub
            Wz = P.sb([128, 8, 2112], BF16, nres=8, name="Wz")
            for i, kc in enumerate(range(0, 8, 2)):
                P.dma(Wz[:, kc:kc + 2, 2048:2112], Wl[:, kc:kc + 2, O_MDT:O_MDT + 64], W=[Wz.res[4 + i]], q="pool")
            for i, kc in enumerate(range(0, 8, 2)):
                P.dma(Wz[:, kc:kc + 2, 0:2048], Wl[:, kc:kc + 2, O_MZ:O_MZ + 2048], W=[Wz.res[i]], q="pool")
            dtb = P.sb([128, 64], F32, name="dtb")
            abc = P.sb([128, 64], F32, name="abc")
            dbc = P.sb([128, 32], F32, name="dbc")
            nrm = P.sb([128, 2048], F32, name="nrm")
            P.dma(dtb[:], Wd["mb_dt_bias"][l:l + 1].rearrange("o d h -> o (d h)").broadcast_to([128, 64]), W=[dtb])
            P.dma(abc[:], Wd["mb_a_log"][l:l + 1].rearrange("o d h -> o (d h)").broadcast_to([128, 64]), W=[abc])
            P.act(abc[:], abc[:], AF.Exp, R=[abc], W=[abc])
            P.ts(abc[:], abc[:], -1.0, None, ALU.mult, R=[abc], W=[abc])
            P.dma(dbc[:], Wd["mb_d"][l:l + 1, :].broadcast_to([128, 32]), W=[dbc])
            P.dma(nrm[:], Wd["mb_norm"][l:l + 1, :].broadcast_to([128, 2048]), W=[nrm])
            xs = [P.sb([128, 2048], BF16, name="xs%d" % i) for i in range(2)]
            xdt = [P.sb([128, 2048], BF16, name="xdt%d" % i) for i in range(2)]
            xdtw = P.sb([128, 2048], BF16, name="xdtw")
            oT = xdtw
            dtv = [P.sb([128, 64], F32, name="dtv%d" % i) for i in range(2)]
            dA = [P.sb([128, 32], F32, name="dA%d" % i) for i in range(2)]
            e3 = [P.sb([128, 96], F32, name="e3%d" % i) for i in range(2)]
            cbm = [P.sb([128, 512], F32, name="cbm%d" % i) for i in range(2)]
            Btk = [P.sb([128, 512], BF16, name="Btk%d" % i) for i in range(2)]
            rh = [P.sb([128, 512], F32, name="rh%d" % i) for i in range(2)]
            es = [P.sb([128, 512], F32, name="es%d" % i) for i in range(2)]
            MT = [P.sb([128, 32, 128], BF16, nres=8, name="MT%d" % i) for i in range(2)]
            tmp = [P.sb([128, 512], F32, name="tmpy%d" % i) for i in range(2)]
            ysb = P.sb([128, 2048], F32, nres=4, name="ysb")
            ST = P.sb([128, 2048], F32, nres=4, name="ST")
            STb = P.sb([128, 2048], BF16, nres=4, name="STb")
            yfl = P.sb([128, 2048], F32, name="yfl")
            ss4 = P.sb([128, 4], F32, name="ss4")
            pM = P.ps([128, 512], name="pM")
            pCB = P.ps([128, 512], name="pCB")
            pCBb = pCB[:].bitcast(BF16)
            pSG = [P.ps([128, 512], name="pSG%d" % i) for i in range(2)]
            pY = [P.ps([128, 512], name="pY%d" % i) for i in range(2)]
            pI = [P.ps([128, 512], name="pI%d" % i) for i in range(2)]
            p4 = [pY[0], pY[1], pI[0], pI[1]]
            v3 = lambda ap, h: ap.rearrange("p (h q) -> p h q", h=h)
            col = lambda ap: ap.rearrange("p (h o) -> p h o", o=1)

            def prologue(t, d, b):
                tc = slice(t * 128, (t + 1) * 128)
                TRI = cst[:, (C_LE if d == 0 else C_GE):(C_LE if d == 0 else C_GE) + 128]
                STR = cst[:, (C_GT if d == 0 else C_LT):(C_GT if d == 0 else C_LT) + 128]
                xs_, xdt_, dtv_, dA_, e3_, cbm_, Btk_, MT_ = xs[b], xdt[b], dtv[b], dA[b], e3[b], cbm[b], Btk[b], MT[b]
                P.dma(xs_[:], mb_xs[t], R=[P.dres("mb_xs", t)], W=[xs_])
                for kc in range(8):
                    P.mm(pM[:, 0:64], hT[:, kc, tc], Wz[:, kc, 2048:2112], start=(kc == 0), stop=(kc == 7), R=[hT.res[t], Wz.res[4:8]], W=[pM])
                P.tt(dtv_[:], pM[:, 0:64], dtb[:], ALU.add, R=[pM, dtb], W=[dtv_])
                P.act(dtv_[:], dtv_[:], AF.Exp, R=[dtv_], W=[dtv_])
                P.act(dtv_[:], dtv_[:], AF.Ln, R=[dtv_], W=[dtv_], bias=1.0)
                P.tt(dA_[:], dtv_[:, d * 32:(d + 1) * 32], abc[:, d * 32:(d + 1) * 32], ALU.mult, R=[dtv_, abc], W=[dA_])
                for g in range(4):
                    P.mm(pCB[:, g * 128:(g + 1) * 128], BT[:, g, tc], CT[:, g, tc], R=[BT, CT], W=[pCB])
                yield
                P.mm(pM[:, 64:96], TRI, dA_[:], R=[cst, dA_], W=[pM])
                P.mm(pM[:, 96:128], STR, dA_[:], R=[cst, dA_], W=[pM])
                P.mm(pM[:, 128:160], onesF, dA_[:], R=[cst, dA_], W=[pM])
                P.act(e3_[:], pM[:, 64:160], AF.Exp, R=[pM], W=[e3_])
                P.tt(v3(cbm_[:], 4), v3(pCB[:], 4), TRI.rearrange("p (o t) -> p o t", o=1).broadcast_to([128, 4, 128]), ALU.mult,
                     R=[pCB, cst], W=[cbm_])
                for g in range(4):
                    P.tr(pCBb[:, g * 128:(g + 1) * 128], BT[:, g, tc], identB, R=[BT, cstb], W=[pCB])
                P.cp(Btk_[:], pCBb[:, 0:512], R=[pCB], W=[Btk_], E="act")
                P.tt(v3(xdt_[:], 32), v3(xs_[:], 32), col(dtv_[:, d * 32:(d + 1) * 32]).broadcast_to([128, 32, 64]), ALU.mult,
                     R=[xs_, dtv_], W=[xdt_])
                yield
                TRI3 = TRI.rearrange("p (o t) -> p o t", o=1)
                for step in range(10):
                    if step < 8:
                        hb = step
                        i = hb % 2
                        P.tt(v3(rh[i][:], 4), TRI3.broadcast_to([128, 4, 128]),
                             col(dA_[:, hb * 4:hb * 4 + 4]).broadcast_to([128, 4, 128]), ALU.mult, R=[cst, dA_], W=[rh[i]])
                        P.mm(pSG[i][:], STR, rh[i][:], R=[cst, rh[i]], W=[pSG[i]])
                    if 1 <= step <= 8:
                        i = (step - 1) % 2
                        P.act(es[i][:], pSG[i][:], AF.Exp, R=[pSG[i]], W=[es[i]])
                    if 2 <= step <= 9:
                        hb = step - 2
                        i = hb % 2
                        g = hb // 2
                        P.tt(MT_[:, hb * 4:hb * 4 + 4, :], v3(es[i][:], 4),
                             cbm_[:, g * 128:(g + 1) * 128].rearrange("p (o t) -> p o t", o=1).broadcast_to([128, 4, 128]), ALU.mult,
                             R=[es[i], cbm_], W=[MT_.res[hb]])
                    if step % 2 == 1:
                        yield

            def scan(t, d, b):
                tc = slice(t * 128, (t + 1) * 128)
                xs_, xdt_, e3_, Btk_, MT_ = xs[b], xdt[b], e3[b], Btk[b], MT[b]
                if d == 1:
                    P.dma(yfl[:], mb_yf[t], R=[P.dres("mb_yf", t)], W=[yfl])
                P.tt(v3(xdtw[:], 32), v3(xdt_[:], 32), col(e3_[:, 32:64]).broadcast_to([128, 32, 64]), ALU.mult,
                     R=[xdt_, e3_], W=[xdtw])
                for g in range(4):
                    gs = slice(g * 512, (g + 1) * 512)
                    pY_, pI_, tmp_ = pY[g % 2], pI[g % 2], tmp[g % 2]
                    for hh in range(8):
                        h = g * 8 + hh
                        P.mm(pY_[:, hh * 64:(hh + 1) * 64], MT_[:, h, :], xdt_[:, h * 64:(h + 1) * 64], R=[MT_.res[h // 4], xdt_], W=[pY_])
                    P.mm(pI_[:], CT[:, g, tc], STb[:, gs], R=[CT, STb.res[g]], W=[pI_])
                    P.tt(v3(tmp_[:], 8), v3(pI_[:], 8), col(e3_[:, g * 8:(g + 1) * 8]).broadcast_to([128, 8, 64]), ALU.mult,
                         R=[pI_, e3_], W=[tmp_])
                    P.tt(ysb[:, gs], tmp_[:], pY_[:], ALU.add, R=[tmp_, pY_], W=[ysb.res[g]])
                    if g % 2 == 1:
                        yield
                for g in range(4):
                    gs = slice(g * 512, (g + 1) * 512)
                    pI_ = pI[g % 2]
                    P.mm(pI_[:], Btk_[:, g * 128:(g + 1) * 128], xdtw[:, gs], R=[Btk_, xdtw], W=[pI_])
                    P.tt(v3(ST[:, gs], 8), v3(ST[:, gs], 8), col(e3_[:, 64 + g * 8:64 + (g + 1) * 8]).broadcast_to([128, 8, 64]), ALU.mult,
                         R=[ST.res[g], e3_], W=[ST.res[g]])
                    P.tt(ST[:, gs], ST[:, gs], pI_[:], ALU.add, R=[ST.res[g], pI_], W=[ST.res[g]])
                    P.cp(STb[:, gs], ST[:, gs], R=[ST.res[g]], W=[STb.res[g]], E="act")
                    if g % 2 == 1:
                        yield
                if d == 0:
                    P.dma(mb_yf[t], ysb[:], R=[ysb], W=[P.dres("mb_yf", t)])
                    return
                P.tt(ysb[:], ysb[:], yfl[:], ALU.add, R=[ysb, yfl], W=[ysb])
                P.tt(v3(yfl[:], 32), v3(xs_[:], 32), col(dbc[:]).broadcast_to([128, 32, 64]), ALU.mult, R=[xs_, dbc, yfl], W=[yfl])
                P.tt(ysb[:], ysb[:], yfl[:], ALU.add, R=[ysb, yfl], W=[ysb])
                for j in range(4):
                    for kc in range(8):
                        P.mm(p4[j][:], hT[:, kc, tc], Wz[:, kc, j * 512:(j + 1) * 512], start=(kc == 0), stop=(kc == 7),
                             R=[hT.res[t], Wz.res[0:4]], W=[p4[j]])
                    P.act(yfl[:, j * 512:(j + 1) * 512], p4[j][:], AF.Silu, R=[p4[j]], W=[yfl])
                    if j % 2 == 1:
                        yield
                P.tt(ysb[:], ysb[:], yfl[:], ALU.mult, R=[ysb, yfl], W=[ysb])
                P.act(yfl[:], ysb[:], AF.Square, R=[ysb], W=[yfl])
                P.op("dve", lambda e: e.tensor_reduce(ss4[:], v3(yfl[:], 4), AX.X, ALU.add), R=[yfl], W=[ss4])
                P.act(ss4[:], ss4[:], AF.Sqrt, R=[ss4], W=[ss4], scale=1.0 / 512.0, bias=RMS_EPS)
                P.op("dve", lambda e: e.reciprocal(ss4[:], ss4[:]), R=[ss4], W=[ss4])
                yield
                for g in range(4):
                    gs = slice(g * 512, (g + 1) * 512)
                    P.stt(ysb[:, gs], ysb[:, gs], ss4[:, g:g + 1], nrm[:, gs], ALU.mult, ALU.mult, R=[ysb.res[g], ss4, nrm], W=[ysb.res[g]])
                for cc in range(16):
                    P.tr(p4[cc // 4][:, (cc % 4) * 128:(cc % 4 + 1) * 128], ysb[:, cc * 128:(cc + 1) * 128], identF, R=[ysb, cst], W=[p4[cc // 4]])
                for j in range(4):
                    P.cp(oT[:, j * 512:(j + 1) * 512], p4[j][:], R=[p4[j]], W=[oT], E=("act" if j % 2 else "dve"))
                P.dma(ombT[t], oT[:], R=[oT], W=[P.dres("ombT", t)])
                yield

            def reset_state():
                P.op("dve", lambda e: e.memset(ST[:], 0.0), W=[ST])
                P.op("pool", lambda e: e.memset(STb[:], 0.0), W=[STb])

            order = [(t, 0) for t in range(NT)] + [(t, 1) for t in list(range(TCTX - 1, -1, -1)) + list(range(NT - 1, TCTX - 1, -1))]
            reset_state()
            _interleave([prologue(order[0][0], order[0][1], 0)])
            for n, (t, d) in enumerate(order):
                nxt = None
                if n + 1 < len(order):
                    nxt = prologue(order[n + 1][0], order[n + 1][1], (n + 1) % 2)
                _interleave([scan(t, d, n % 2), nxt])
                if n + 1 < len(order) and order[n + 1][1] == 1 and d == 0:
                    reset_state()
            P.barrier()


def _layernorm(P, out, z, zr, lng, lnb, st, mv):
    for c in range(2):
        P.op("dve", lambda e: e.bn_stats(st[:, c * 6:(c + 1) * 6], z[:, c * 512:(c + 1) * 512]), R=[zr], W=[st])
    P.op("dve", lambda e: e.bn_aggr(mv[:, 0:2], st[:]), R=[st], W=[mv])
    P.act(mv[:, 2:3], mv[:, 1:2], AF.Sqrt, R=[mv], W=[mv], bias=LN_EPS)
    P.op("dve", lambda e: e.reciprocal(mv[:, 2:3], mv[:, 2:3]), R=[mv], W=[mv])
    P.ts(out, z, mv[:, 0:1], mv[:, 2:3], ALU.subtract, ALU.mult, R=[zr, mv], W=[zr])
    P.tt(out, out, lng[:], ALU.mult, R=[zr, lng], W=[zr])
    P.tt(out, out, lnb[:], ALU.add, R=[zr, lnb], W=[zr])


def _gate_bcast(P, cst, modrow_d, c0, s, pG, gbc):
    mr = P.sb([3, 1024], F32, name="mr")
    P.dma(mr[:], modrow_d[:, c0:c0 + 1024], R=[P.dres("modrow")], W=[mr])
    for ri, r in enumerate((s, 2)):
        for half in range(2):
            P.mm(pG[:, half * 512:(half + 1) * 512], cst[0:3, C_SEL + r * 128:C_SEL + (r + 1) * 128], mr[0:3, half * 512:(half + 1) * 512],
                 R=[cst, mr], W=[pG])
        P.cp(gbc[:, ri, :], pG[:], R=[pG], W=[gbc])


def phase_merge(P, nc, l, s, last, Wd, cst, cstb, hT, modA, modB, modrow_d, ohgT, oatT, ombT, ymTd, x1s, xsrc_fn, sq):
    identB = cstb[:, 0:128]
    identF = cst[:, C_ID:C_ID + 128]
    tiles = list(range(TCTX, NT)) if last else list(range(NT))
    Wl = Wd["w_in"][l].rearrange("(kc p) n -> p kc n", p=128)
    with ExitStack() as ph:
        P.stack = ph
        Wgt = P.sb([128, 8, 3072], BF16, nres=12, name="Wgt")
        Wbh = P.sb([128, 8, 1024], BF16, nres=4, name="Wbh")
        Wba = P.sb([128, 8, 1024], BF16, nres=4, name="Wba")
        Wbm = P.sb([128, 16, 1024], BF16, nres=8, name="Wbm")
        Wgs = Wl[:, :, O_GT:O_GT + 3072]

        def ldg(b):
            for i, kc in enumerate(range(0, 8, 2)):
                P.dma(Wgt[:, kc:kc + 2, b * 1024:(b + 1) * 1024], Wgs[:, kc:kc + 2, b * 1024:(b + 1) * 1024], W=[Wgt.res[b * 4 + i]], q="pool")

        ldg(0)
        _loadw(P, Wbh, Wd["w_br_hg"][l].rearrange("(kc p) n -> p kc n", p=128), 1024)
        ldg(1)
        _loadw(P, Wba, Wd["w_br_at"][l].rearrange("(kc p) n -> p kc n", p=128), 1024)
        ldg(2)
        _loadw(P, Wbm, Wd["w_br_mb"][l].rearrange("(kc p) n -> p kc n", p=128), 1024)
        oh = [P.sb([128, 1024], BF16, name="oh%d" % i) for i in range(2)]
        oa = [P.sb([128, 8, 128], BF16, name="oam%d" % i) for i in range(2)]
        om = [P.sb([128, 2048], BF16, name="om%d" % i) for i in range(2)]
        sgt = P.sb([128, 1024], F32, name="sgt")
        ym = P.sb([128, 1024], F32, name="ym")
        tmp = P.sb([128, 1024], F32, name="tmpm")
        ymb = P.sb([128, 1024], BF16, name="ymb")
        ymT = [P.sb([128, 1024], BF16, name="ymT%d" % i) for i in range(2)]
        pG = P.ps([128, 1024], name="pGm")
        pB = P.ps([128, 1024], name="pBm")
        pT = P.ps([128, 1024], BF16, name="pTm")
        for i, t in enumerate(tiles):
            tc = slice(t * 128, (t + 1) * 128)
            oh_, oa_, om_ = oh[i % 2], oa[i % 2], om[i % 2]
            P.dma(oh_[:], ohgT[t], R=[P.dres("ohgT", t)], W=[oh_])
            P.dma(oa_[:], oatT[:, :, tc], R=[P.dres("oatT", (h, q0)) for h in range(8) for q0 in (0, 256, 768, 1280, 1792)], W=[oa_])
            P.dma(om_[:], ombT[t], R=[P.dres("ombT", t)], W=[om_])
            srcs = [(lambda kc, o=oh_: o[:, kc * 128:(kc + 1) * 128], Wbh, 8, oh_),
                    (lambda kc, o=oa_: o[:, kc, :], Wba, 8, oa_),
                    (lambda kc, o=om_: o[:, kc * 128:(kc + 1) * 128], Wbm, 16, om_)]
            for b, (of, Wb, nk, ot) in enumerate(srcs):
                for half in range(2):
                    for kc in range(8):
                        P.mm(pG[:, half * 512:(half + 1) * 512], hT[:, kc, tc], Wgt[:, kc, b * 1024 + half * 512:b * 1024 + (half + 1) * 512],
                             start=(kc == 0), stop=(kc == 7), R=[hT.res[t]] + Wgt.res[b * 4:(b + 1) * 4], W=[pG])
                P.act(sgt[:], pG[:], AF.Sigmoid, R=[pG], W=[sgt])
                for half in range(2):
                    for kc in range(nk):
                        P.mm(pB[:, half * 512:(half + 1) * 512], of(kc), Wb[:, kc, half * 512:(half + 1) * 512],
                             start=(kc == 0), stop=(kc == nk - 1), R=[ot, Wb], W=[pB])
                if b == 0:
                    P.tt(ym[:], pB[:], sgt[:], ALU.mult, R=[pB, sgt], W=[ym])
                else:
                    P.tt(tmp[:], pB[:], sgt[:], ALU.mult, R=[pB, sgt], W=[tmp])
                    if b == 1:
                        P.tt(ym[:], ym[:], tmp[:], ALU.add, R=[ym, tmp], W=[ym])
                    else:
                        P.tt(ymb[:], ym[:], tmp[:], ALU.add, R=[ym, tmp], W=[ymb])
            for kc in range(8):
                P.tr(pT[:, kc * 128:(kc + 1) * 128], ymb[:, kc * 128:(kc + 1) * 128], identB, R=[ymb, cstb], W=[pT])
            yT = ymT[i % 2]
            P.cp(yT[:], pT[:], R=[pT], W=[yT], E="act")
            P.dma(ymTd[t], yT[:], R=[yT], W=[P.dres("ymT", t)])
        P.barrier()
    with ExitStack() as ph:
        P.stack = ph
        Wo = P.sb([128, 8, 1024], BF16, nres=4, name="Wo")
        _loadw(P, Wo, Wd["w_out"][l].rearrange("(kc p) n -> p kc n", p=128), 1024)
        pW = P.ps([128, 1024], name="pWm")
        pp = [P.ps([128, 512], name="pp%d" % i) for i in range(2)]
        gbc = P.sb([128, 2, 1024], F32, name="gbc")
        _gate_bcast(P, cst, modrow_d, 2048, s, pW, gbc)
        lng = P.sb([128, 1024], F32, name="lng")
        lnb = P.sb([128, 1024], F32, name="lnb")
        P.dma(lng[:], Wd["ln1_g"][l:l + 1, :].broadcast_to([128, 1024]), W=[lng])
        P.dma(lnb[:], Wd["ln1_b"][l:l + 1, :].broadcast_to([128, 1024]), W=[lnb])
        yT = [P.sb([128, 1024], BF16, name="yTb%d" % i) for i in range(2)]
        xt = [P.sb([128, 1024], F32, name="xtb%d" % i) for i in range(2)]
        z = [P.sb([128, 1024], F32, name="zb%d" % i) for i in range(2)]
        st = P.sb([128, 12], F32, name="st")
        mv = P.sb([128, 4], F32, name="mv")
        for i, t in enumerate(tiles):
            tc = slice(t * 128, (t + 1) * 128)
            r = 2 if t < TCTX else s
            ri = 1 if t < TCTX else 0
            yT_, xt_, z_ = yT[i % 2], xt[i % 2], z[i % 2]
            P.dma(yT_[:], ymTd[t], R=[P.dres("ymT", t)], W=[yT_])
            src, sres = xsrc_fn(t)
            P.dma(xt_[:], src, R=[sres], W=[xt_])
            for half in range(2):
                for kc in range(8):
                    P.mm(pW[:, half * 512:(half + 1) * 512], yT_[:, kc * 128:(kc + 1) * 128], Wo[:, kc, half * 512:(half + 1) * 512],
                         start=(kc == 0), stop=(kc == 7), R=[yT_, Wo], W=[pW])
            P.tt(z_[:], pW[:], gbc[:, ri, :], ALU.mult, R=[pW, gbc], W=[z_])
            P.stt(z_[:], xt_[:], float(DN_ALPHA), z_[:], ALU.mult, ALU.add, R=[xt_, z_], W=[z_])
            _layernorm(P, z_[:], z_[:], z_, lng, lnb, st, mv)
            P.dma(x1s[t], z_[:], R=[z_], W=[P.dres("x1s", t)])
            for half in range(2):
                pb = pp[half]
                for q in range(4):
                    kc = half * 4 + q
                    P.tr(pb[:, q * 128:(q + 1) * 128], z_[:, kc * 128:(kc + 1) * 128], identF, R=[z_, cst], W=[pb])
                for q in range(4):
                    kc = half * 4 + q
                    o = hT[:, kc, tc]
                    i_ = pb[:, q * 128:(q + 1) * 128]
                    sc = modB[:, 32 + kc, r:r + 1]
                    sh = modA[:, 24 + kc, r:r + 1]
                    if q % 2 == 0:
                        P.ts(o, i_, sc, sh, ALU.mult, ALU.add, R=[pb, modA, modB], W=[hT.res[t]])
                    else:
                        P.act(o, i_, AF.Identity, R=[pb, modA, modB], W=[hT.res[t]], bias=sh, scale=sc)
        P.barrier()


def phase_ffn(P, nc, l, s, last, Wd, cst, cstb, hT, modrow_d, x1s, xres, out_d, ffn_act):
    blocks = ([] if last else [(0, TCTX)]) + [(TCTX + 4 * i, 4) for i in range(4)]
    with ExitStack() as ph0:
        P.stack = ph0
        W2 = P.sb([128, 22, 1024], BF16, nres=11, name="W2")
        with ExitStack() as ph:
            P.stack = ph
            W1 = P.sb([128, 8, 2 * FFH], BF16, nres=44, name="W1")
            _loadw_cols(P, W1, Wd["w_ffn_in"][l].rearrange("(kc p) n -> p kc n", p=128), 2 * FFH, 256,
                        order=[x for j in range(11) for x in (j, 11 + j)])
            _loadw(P, W2, Wd["w_ffn_out"][l].rearrange("(j p) n -> p j n", p=128), 1024)
            pGt = [P.ps([128, 512], name="pGt%d" % i) for i in range(2)]
            pUp = [P.ps([128, 512], name="pUp%d" % i) for i in range(2)]
            sgu = [P.sb([128, 512], F32, name="sgu%d" % i) for i in range(2)]
            aT = P.sb([128, 22, 512], BF16, name="actT")
            for bi, (t0, ntile) in enumerate(blocks):
                N = ntile * 128
                cols = slice(t0 * 128, t0 * 128 + N)
                tr_ = [hT.res[t] for t in range(t0, t0 + ntile)]
                for j in range(22):
                    pg, pu, sg_ = pGt[j % 2], pUp[j % 2], sgu[j % 2]
                    for kc in range(8):
                        P.mm(pg[:, 0:N], W1[:, kc, j * 128:(j + 1) * 128], hT[:, kc, cols], start=(kc == 0), stop=(kc == 7),
                             R=_cr(W1, j * 128, (j + 1) * 128) + tr_, W=[pg])
                    for kc in range(8):
                        P.mm(pu[:, 0:N], W1[:, kc, FFH + j * 128:FFH + (j + 1) * 128], hT[:, kc, cols], start=(kc == 0), stop=(kc == 7),
                             R=_cr(W1, FFH + j * 128, FFH + (j + 1) * 128) + tr_, W=[pu])
                    P.act(sg_[:, 0:N], pg[:, 0:N], AF.Silu, R=[pg], W=[sg_])
                    P.tt(aT[:, j, 0:N], sg_[:, 0:N], pu[:, 0:N], ALU.mult, R=[sg_, pu], W=[aT])
                for ti in range(ntile):
                    t = t0 + ti
                    P.dma(ffn_act[t].rearrange("p (j k) -> p j k", j=22), aT[:, :, ti * 128:(ti + 1) * 128], R=[aT], W=[P.dres("ffn_act", t)])
            P.barrier()
        with ExitStack() as ph:
            P.stack = ph
            pF = [P.ps([128, 1024], name="pF%d" % i) for i in range(2)]
            gbc = P.sb([128, 2, 1024], F32, name="gbc2")
            _gate_bcast(P, cst, modrow_d, 5120, s, pF[0], gbc)
            lng = P.sb([128, 1024], F32, name="lng2")
            lnb = P.sb([128, 1024], F32, name="lnb2")
            P.dma(lng[:], Wd["ln2_g"][l:l + 1, :].broadcast_to([128, 1024]), W=[lng])
            P.dma(lnb[:], Wd["ln2_b"][l:l + 1, :].broadcast_to([128, 1024]), W=[lnb])
            aTl = [P.sb([128, 22, 128], BF16, name="aTl%d" % i) for i in range(2)]
            xt = [P.sb([128, 1024], F32, name="x1l%d" % i) for i in range(2)]
            z = [P.sb([128, 1024], F32, name="z2%d" % i) for i in range(2)]
            st = P.sb([128, 12], F32, name="st2")
            mv = P.sb([128, 4], F32, name="mv2")
            tiles = [t0 + ti for (t0, ntile) in blocks for ti in range(ntile)]
            for i, t in enumerate(tiles):
                ri = 1 if t < TCTX else 0
                a_, xt_, z_, pF_ = aTl[i % 2], xt[i % 2], z[i % 2], pF[i % 2]
                P.dma(a_[:], ffn_act[t].rearrange("p (j k) -> p j k", j=22), R=[P.dres("ffn_act", t)], W=[a_])
                P.dma(xt_[:], x1s[t], R=[P.dres("x1s", t)], W=[xt_])
                for j in range(22):
                    for half in range(2):
                        P.mm(pF_[:, half * 512:(half + 1) * 512], a_[:, j, :], W2[:, j, half * 512:(half + 1) * 512],
                             start=(j == 0), stop=(j == 21), R=[a_, W2], W=[pF_])
                P.tt(z_[:], pF_[:], gbc[:, ri, :], ALU.mult, R=[pF_, gbc], W=[z_])
                P.stt(z_[:], xt_[:], float(DN_ALPHA), z_[:], ALU.mult, ALU.add, R=[xt_, z_], W=[z_])
                _layernorm(P, z_[:], z_[:], z_, lng, lnb, st, mv)
                if last:
                    P.dma(out_d[s, (t - TCTX) * 128:(t - TCTX + 1) * 128, :], z_[:], R=[z_], W=[P.dres("out", (s, t))])
                else:
                    P.dma(xres[s, t], z_[:], R=[z_], W=[P.dres("xres", (s, t))])
            P.barrier()


def kernel(**inputs):
    inp = {k: np.asarray(v) for k, v in inputs.items()}
    nc = build(DEPTH)
    consts = make_consts()
    rope = make_rope()
    maps = []
    for core in range(8):
        b0 = 2 * core
        c3 = np.stack([inp["c"][b0], inp["c"][b0 + 1], inp["c_ctx"]])
        c3T = np.ascontiguousarray(c3.reshape(3, 8, 128).transpose(2, 1, 0))
        m = {"x": np.ascontiguousarray(inp["x"][b0:b0 + 2]), "ctx": np.ascontiguousarray(inp["ctx"][b0:b0 + 2]),
             "c3T": c3T, "consts": consts, "rope": rope, "hg_lb": inp["hg_lb"]}
        for n, _ in WSHAPES:
            m[n] = inp[n]
        maps.append(m)
    res = run_bass_kernel_spmd(nc, maps, core_ids=list(range(8)))
    out = np.concatenate([np.asarray(r["out"]) for r in res.results], axis=0)
    return out.astype(np.float32)
```

```python
from contextlib import ExitStack
import numpy as np
import concourse.bass as bass
import concourse.mybir as mybir
from concourse.bass_utils import run_bass_kernel_spmd

F32 = mybir.dt.float32
BF16 = mybir.dt.bfloat16
AF = mybir.ActivationFunctionType
ALU = mybir.AluOpType
AX = mybir.AxisListType

SAME_ENGINE_SYNC = True
N_DMA_SEMS = 16

DEPTH = 4
DM = 1024
NT = 18
TCTX = 2
TOK = NT * 128
FFH = 2816
IN_W = 14912
O_HQ, O_HFF, O_HFB, O_HI, O_HG = 0, 1024, 2048, 3072, 4096
O_AQ, O_AK, O_AV = 5120, 6144, 6400
O_MZ, O_MX, O_MDT, O_GT = 6656, 8704, 11776, 11840
DN_ALPHA = (2 * DEPTH) ** 0.25
LN_EPS = 1e-5
RMS_EPS = 1e-6

C_ID, C_ONES, C_LE, C_GE, C_GT, C_LT, C_HDF, C_HDB, C_HMF, C_HMB, C_SEL, C_CI = (
    0, 128, 256, 384, 512, 640, 768, 896, 1024, 1152, 1280, 1664)
NCONST = 1664 + 4


def make_consts():
    a = np.arange(128)[:, None]
    b = np.arange(128)[None, :]
    bd = (a // 32) == (b // 32)
    blocks = [a == b, np.ones((128, 128), bool), a <= b, a >= b, a > b, a < b,
              bd & (a > b), bd & (a < b), bd & (a <= b), bd & (a >= b)]
    sel = np.zeros((128, 3 * 128), bool)
    for r in range(3):
        sel[r, r * 128:(r + 1) * 128] = True
    ci = (a // 32) == np.arange(4)[None, :]
    return np.concatenate(blocks + [sel, ci], axis=1).astype(np.float32)


def make_rope():
    rows = 2048 // 64
    row, col = np.meshgrid(np.arange(rows, dtype=np.float32), np.arange(64, dtype=np.float32), indexing="ij")
    n_pairs = 32
    inv_freq = (np.float32(10000.0) ** (-np.arange(n_pairs, dtype=np.float32) / np.float32(n_pairs))).astype(np.float32)
    ang = np.concatenate([row.reshape(-1, 1) * inv_freq, col.reshape(-1, 1) * inv_freq], axis=-1).astype(np.float32)
    return np.stack([np.cos(ang), np.sin(ang)]).astype(np.float32)


class Res:
    __slots__ = ("w", "r")

    def __init__(self):
        self.w = None
        self.r = {}


class Tile:
    def __init__(self, t, nres=1):
        self.t = t
        self.res = [Res() for _ in range(nres)]

    def __getitem__(self, k):
        return self.t[k]

    @property
    def r(self):
        return self.res[0]


def _res(items):
    out = []
    for it in items:
        if isinstance(it, Tile):
            out.extend(it.res)
        elif isinstance(it, Res):
            out.append(it)
        elif it is None:
            pass
        else:
            out.extend(_res(it))
    return out


class Prog:
    def __init__(self, nc, stack):
        self.nc = nc
        self.eng = {"pe": nc.tensor, "act": nc.scalar, "dve": nc.vector, "pool": nc.gpsimd, "sp": nc.sync}
        self.sem = {}
        for e in self.eng:
            self.sem[e] = stack.enter_context(nc.semaphore("s_" + e))
        self.dma_sems = {}
        for q in ("sp", "act", "pool"):
            self.dma_sems[q] = [("d", q, i) for i in range(N_DMA_SEMS)]
            for k in self.dma_sems[q]:
                self.sem[k] = stack.enter_context(nc.semaphore("d_%s_%d" % (q, k[2])))
        self.count = {k: 0 for k in self.sem}
        self.known = {e: {} for e in self.eng}
        self.vc = {}
        self.dma_rr = {q: 0 for q in self.dma_sems}
        self.n_instr = 0
        self.n_wait = 0
        self.stack = None
        self._dres = {}
        self._uid = 0

    def sb(self, shape, dtype, nres=1, name=None):
        self._uid += 1
        t = self.stack.enter_context(self.nc.sbuf_tensor("%s_%d" % (name or "t", self._uid), list(shape), dtype))
        return Tile(t, nres)

    def ps(self, shape, dtype=F32, name=None):
        self._uid += 1
        t = self.stack.enter_context(self.nc.psum_tensor("%s_%d" % (name or "p", self._uid), list(shape), dtype))
        return Tile(t)

    def dres(self, name, idx=0):
        k = (name, idx)
        if k not in self._dres:
            self._dres[k] = Res()
        return self._dres[k]

    def _deps(self, E, reads, writes):
        deps = {}
        for r in reads:
            if r.w is not None and deps.get(r.w[0], 0) < r.w[1]:
                deps[r.w[0]] = r.w[1]
        for w in writes:
            if w.w is not None and deps.get(w.w[0], 0) < w.w[1]:
                deps[w.w[0]] = w.w[1]
            for k, v in w.r.items():
                if deps.get(k, 0) < v:
                    deps[k] = v
        kn = self.known[E]
        out = []
        for k, v in deps.items():
            if k == E and (E == "pe" or E == "sp" or not SAME_ENGINE_SYNC):
                continue
            if kn.get(k, 0) >= v:
                continue
            out.append((k, v))
        return out

    def _wait(self, E, waits):
        eng = self.eng[E]
        kn = self.known[E]
        for k, v in waits:
            eng.wait_ge(self.sem[k], v)
            self.n_wait += 1
            snap = self.vc.get((k, v))
            if snap is not None:
                for kk, vv in snap.items():
                    if kn.get(kk, 0) < vv:
                        kn[kk] = vv
            if kn.get(k, 0) < v:
                kn[k] = v

    def _finish(self, E, ev, ins, inc, reads, writes):
        ins.then_inc(self.sem[ev[0]], inc)
        snap = dict(self.known[E])
        snap[ev[0]] = ev[1]
        self.vc[ev] = snap
        k, v = ev
        for r in reads:
            if r.r.get(k, 0) < v:
                r.r[k] = v
        for w in writes:
            w.w = ev
            w.r = {}
        self.n_instr += 1

    def op(self, E, fn, R=(), W=()):
        reads = _res(R)
        writes = _res(W)
        self._wait(E, self._deps(E, reads, writes))
        ins = fn(self.eng[E])
        self.count[E] += 1
        ev = (E, self.count[E])
        self._finish(E, ev, ins, 1, reads, writes)

    def dma(self, out, in_, R=(), W=(), q="sp", **kw):
        reads = _res(R)
        writes = _res(W)
        self._wait(q, self._deps(q, reads, writes))
        key = self.dma_sems[q][self.dma_rr[q] % N_DMA_SEMS]
        self.dma_rr[q] += 1
        if self.count[key] > 0 and self.known[q].get(key, 0) < self.count[key]:
            self._wait(q, [(key, self.count[key])])
        ins = self.eng[q].dma_start(out=out, in_=in_, **kw)
        self.count[key] += 16
        ev = (key, self.count[key])
        self._finish(q, ev, ins, 16, reads, writes)

    def barrier(self):
        targets = [(k, v) for k, v in self.count.items() if v > 0]
        for E in self.eng:
            waits = [(k, v) for k, v in targets if self.known[E].get(k, 0) < v and not (k == E and E == "pe")]
            self._wait(E, waits)

    def wait_all_on(self, E="sp"):
        waits = [(k, v) for k, v in self.count.items() if v > 0 and k != E and self.known[E].get(k, 0) < v]
        self._wait(E, waits)

    def mm(self, out, lhsT, rhs, start=True, stop=True, R=(), W=()):
        self.op("pe", lambda e: e.matmul(out, lhsT, rhs, start=start, stop=stop), R, W)

    def tr(self, out, in_, ident, R=(), W=()):
        self.op("pe", lambda e: e.transpose(out, in_, ident), R, W)

    def act(self, out, in_, func, R=(), W=(), bias=0.0, scale=1.0, accum_out=None):
        if accum_out is None:
            self.op("act", lambda e: e.activation(out, in_, func, bias=bias, scale=scale), R, W)
        else:
            self.op("act", lambda e: e.activation(out, in_, func, bias=bias, scale=scale, accum_out=accum_out), R, W)

    def tt(self, out, in0, in1, op, R=(), W=(), E="dve"):
        self.op(E, lambda e: e.tensor_tensor(out, in0, in1, op), R, W)

    def ts(self, out, in0, s1, s2, op0, op1=None, R=(), W=(), E="dve"):
        if op1 is None:
            self.op(E, lambda e: e.tensor_scalar(out, in0, s1, None, op0), R, W)
        else:
            self.op(E, lambda e: e.tensor_scalar(out, in0, s1, s2, op0, op1), R, W)

    def stt(self, out, in0, scalar, in1, op0, op1, R=(), W=()):
        self.op("dve", lambda e: e.scalar_tensor_tensor(out, in0, scalar, in1, op0, op1), R, W)

    def cp(self, out, in_, R=(), W=(), E="dve"):
        if E == "act":
            self.op("act", lambda e: e.activation(out, in_, AF.Copy), R, W)
        else:
            self.op(E, lambda e: e.tensor_copy(out, in_), R, W)


WSHAPES = [
    ("w_mod", (DM, 6 * DM)), ("b_mod", (6 * DM,)), ("w_in", (DM, IN_W)), ("hg_gnorm", (128,)),
    ("at_qnorm", (128,)), ("at_knorm", (128,)), ("mb_conv_w", (5, 3072)), ("mb_conv_b", (3072,)),
    ("mb_dt_bias", (2, 32)), ("mb_a_log", (2, 32)), ("mb_d", (32,)), ("mb_norm", (2048,)),
    ("w_br_hg", (1024, DM)), ("w_br_at", (1024, DM)), ("w_br_mb", (2048, DM)), ("w_out", (DM, DM)),
    ("ln1_g", (DM,)), ("ln1_b", (DM,)), ("w_ffn_in", (DM, 2 * FFH)), ("w_ffn_out", (FFH, DM)),
    ("ln2_g", (DM,)), ("ln2_b", (DM,)),
]


def build(depth=DEPTH, dbg=None):
    dbg = dbg or {}
    dump = dbg.get("dump", set())
    phases = dbg.get("phases", {"p0", "hg", "at", "mb", "merge", "ffn"})
    seqs = dbg.get("seqs", [0, 1])
    nc = bass.Bass("TRN2", target_bir_lowering=False)

    def din(name, shape):
        return nc.dram_tensor(name, list(shape), F32, kind="ExternalInput").ap()

    x_in = din("x", [2, 2048, DM])
    ctx_in = din("ctx", [2, 256, DM])
    c3T = din("c3T", [128, 8, 3])
    consts_d = din("consts", [128, NCONST])
    rope_d = din("rope", [2, 2048, 64])
    hg_lb_d = din("hg_lb", [4, 2, 1024])
    Wd = {n: din(n, (depth,) + s) for n, s in WSHAPES}
    out_d = nc.dram_tensor("out", [2, 2048, DM], F32, kind="ExternalOutput").ap()

    def dscr(name, shape, dt):
        kind = "ExternalOutput" if name in dump else "Internal"
        return nc.dram_tensor(name, list(shape), dt, kind=kind).ap()

    xres = dscr("xres", [2, NT, 128, DM], F32)
    x1s = dscr("x1s", [NT, 128, DM], F32)
    hg_of = dscr("hg_of", [NT, 128, 1024], F32)
    ohgT = dscr("ohgT", [NT, 128, 1024], BF16)
    oatT = dscr("oatT", [128, 8, TOK], BF16)
    ombT = dscr("ombT", [NT, 128, 2048], BF16)
    mb_xs = dscr("mb_xs", [NT, 128, 2048], BF16)
    mb_yf = dscr("mb_yf", [NT, 128, 2048], F32)
    ymTd = dscr("ymT", [NT, 128, 1024], BF16)
    lbs_d = dscr("lbs", [4, 2, 1024], F32)
    modrow_d = dscr("modrow", [3, 6 * DM], F32)
    ffn_act = dscr("ffn_act", [NT, 128, 22 * 128], BF16)

    with ExitStack() as top:
        P = Prog(nc, top)
        P.stack = top
        cst = P.sb([128, NCONST], F32, name="cst")
        P.dma(cst[:], consts_d, W=[cst])
        cstb = P.sb([128, 256], BF16, name="cstb")
        P.cp(cstb[:], cst[:, 0:256], R=[cst], W=[cstb])
        identF = cst[:, C_ID:C_ID + 128]
        onesF = cst[:, C_ONES:C_ONES + 128]
        identB = cstb[:, 0:128]
        onesB = cstb[:, 128:256]
        scT = P.sb([128, 8, 3], F32, name="scT")
        P.dma(scT[:], c3T, W=[scT])
        P.act(scT[:], scT[:], AF.Silu, R=[scT], W=[scT])
        hT = P.sb([128, 8, TOK], BF16, nres=NT, name="hT")

        def xsrc(l, s, t):
            if l == 0:
                return (ctx_in[s, t * 128:(t + 1) * 128, :] if t < TCTX else x_in[s, (t - TCTX) * 128:(t - TCTX + 1) * 128, :]), None
            return xres[s, t], P.dres("xres", (s, t))

        with ExitStack() as ph:
            P.stack = ph
            e = P.sb([128, 4, 16], F32, name="lb_e")
            P.dma(e[:].rearrange("p l (d j) -> p l d j", d=2),
                  hg_lb_d.rearrange("l d (p j) -> p l d j", p=128), W=[e])
            P.act(e[:], e[:], AF.Exp, R=[e], W=[e])
            s_ = P.sb([128, 16], F32, name="lb_s")
            P.tt(s_[:], e[:, 0, :], e[:, 1, :], ALU.add, R=[e], W=[s_])
            P.tt(s_[:], s_[:], e[:, 2, :], ALU.add, R=[e, s_], W=[s_])
            P.tt(s_[:], s_[:], e[:, 3, :], ALU.add, R=[e, s_], W=[s_])
            P.op("dve", lambda en: en.reciprocal(s_[:], s_[:]), R=[s_], W=[s_])
            P.tt(e[:], e[:], s_[:].rearrange("p (o j) -> p o j", o=1).broadcast_to([128, 4, 16]), ALU.mult, R=[e, s_], W=[e])
            lbt = P.sb([128, 4, 16], F32, name="lb_t")
            P.op("dve", lambda en: en.memset(lbt[:, 0, :], 0.0), W=[lbt])
            P.cp(lbt[:, 1, :], e[:, 1, :], R=[e], W=[lbt])
            P.tt(lbt[:, 2, :], lbt[:, 1, :], e[:, 2, :], ALU.add, R=[e, lbt], W=[lbt])
            P.tt(lbt[:, 3, :], lbt[:, 2, :], e[:, 3, :], ALU.add, R=[e, lbt], W=[lbt])
            P.dma(lbs_d.rearrange("l d (p j) -> p l d j", p=128),
                  lbt[:].rearrange("p l (d j) -> p l d j", d=2), R=[lbt], W=[P.dres("lbs")])
            P.barrier()
        P.stack = top

        for l in range(depth):
            last = (l == depth - 1) and not dbg.get('nolast', False)
            with ExitStack() as lay:
                P.stack = lay
                modA = P.sb([128, 48, 3], F32, name="modA")
                modB = P.sb([128, 48, 3], F32, name="modB")
                with ExitStack() as ph:
                    P.stack = ph
                    mrow = P.sb([3, 6 * DM], F32, name="mrow")
                    brow = P.sb([3, 6 * DM], F32, name="brow")
                    P.dma(brow[:], Wd["b_mod"][l:l + 1, :].broadcast_to([3, 6 * DM]), W=[brow])
                    wm = [P.sb([128, 8, 1536], F32, name="wm%d" % i) for i in range(2)]
                    pm = [P.ps([128, 512], name="pm%d" % i) for i in range(2)]
                    for blk in range(4):
                        w_ = wm[blk % 2]
                        P.dma(w_[:], Wd["w_mod"][l].rearrange("(kc p) n -> p kc n", p=128)[:, :, blk * 1536:(blk + 1) * 1536], W=[w_])
                        for j in range(3):
                            pj = pm[j % 2]
                            for kc in range(8):
                                P.mm(pj[0:3, :], scT[:, kc, :], w_[:, kc, j * 512:(j + 1) * 512],
                                     start=(kc == 0), stop=(kc == 7), R=[scT, w_], W=[pj])
                            c0 = blk * 1536 + j * 512
                            P.tt(mrow[:, c0:c0 + 512], pj[0:3, :], brow[:, c0:c0 + 512], ALU.add, R=[pj, brow], W=[mrow])
                    P.dma(modrow_d, mrow[:], R=[mrow], W=[P.dres("modrow")])
                    pT = P.ps([128, 512], name="pmT")
                    for j in range(48):
                        P.tr(pT[:, j * 3:(j + 1) * 3], mrow[:, j * 128:(j + 1) * 128], identF[0:3, 0:3], R=[mrow, cst], W=[pT])
                    P.cp(modA[:].rearrange("p j r -> p (j r)"), pT[:, 0:144], R=[pT], W=[modA])
                    P.ts(modB[:].rearrange("p j r -> p (j r)"), pT[:, 0:144], 1.0, None, ALU.add, R=[pT], W=[modB])
                    P.barrier()
                P.stack = lay

                for s in seqs:
                    with ExitStack() as sq:
                        P.stack = sq
                        def featT(l_, s_, tiles, src_fn, sc_chunk, sh_chunk, ph):
                            xt = [P.sb([128, DM], F32, name="p0x%d" % i) for i in range(2)]
                            pp = [P.ps([128, 512], name="p0p%d" % i) for i in range(2)]
                            for i, t in enumerate(tiles):
                                r = 2 if t < TCTX else s_
                                xb = xt[i % 2]
                                src, sres = src_fn(t)
                                P.dma(xb[:], src, R=[sres], W=[xb])
                                for half in range(2):
                                    pb = pp[half]
                                    for q in range(4):
                                        kc = half * 4 + q
                                        P.tr(pb[:, q * 128:(q + 1) * 128], xb[:, kc * 128:(kc + 1) * 128], identF, R=[xb, cst], W=[pb])
                                    for q in range(4):
                                        kc = half * 4 + q
                                        o = hT[:, kc, t * 128:(t + 1) * 128]
                                        i_ = pb[:, q * 128:(q + 1) * 128]
                                        sc = modB[:, sc_chunk + kc, r:r + 1]
                                        sh = modA[:, sh_chunk + kc, r:r + 1]
                                        if q % 2 == 0:
                                            P.ts(o, i_, sc, sh, ALU.mult, ALU.add, R=[pb, modA, modB], W=[hT.res[t]])
                                        else:
                                            P.act(o, i_, AF.Identity, R=[pb, modA, modB], W=[hT.res[t]], bias=sh, scale=sc)

                        if "p0" in phases:
                            with ExitStack() as ph:
                                P.stack = ph
                                featT(l, s, list(range(NT)), lambda t: xsrc(l, s, t), 8, 0, ph)
                                P.barrier()
                            P.stack = sq
                        if "hg" in phases:
                            with ExitStack() as ph:
                                P.stack = ph
                                phase_hg(P, nc, l, s, Wd, cst, cstb, hT, lbs_d, hg_of, ohgT)
                                P.barrier()
                            P.stack = sq
                        if "at" in phases:
                            with ExitStack() as ph:
                                P.stack = ph
                                phase_at(P, nc, l, s, Wd, cst, cstb, hT, rope_d, oatT)
                                P.barrier()
                            P.stack = sq
                        if "mb" in phases:
                            phase_mb(P, nc, l, s, Wd, cst, cstb, hT, mb_xs, mb_yf, ombT, sq)
                            P.stack = sq
                        if "merge" in phases:
                            phase_merge(P, nc, l, s, last, Wd, cst, cstb, hT, modA, modB, modrow_d, ohgT, oatT, ombT,
                                        ymTd, x1s, lambda t: xsrc(l, s, t), sq)
                            P.stack = sq
                        if "ffn" in phases:
                            phase_ffn(P, nc, l, s, last, Wd, cst, cstb, hT, modrow_d, x1s, xres, out_d, ffn_act)
                            P.stack = sq
                    P.stack = lay
            P.stack = top
        P.wait_all_on("sp")
        build.stats = (P.n_instr, P.n_wait)
    return nc


def _loadw(P, dst, src3, n, q="pool"):
    KC = src3.shape[1]
    step = 2
    for i, kc in enumerate(range(0, KC, step)):
        P.dma(dst[:, kc:kc + step, 0:n], src3[:, kc:kc + step, :], W=[dst.res[i % len(dst.res)]], q=q)


def _loadw_cols(P, dst, src3, n, cb, order=None, q="pool"):
    npc = (n + cb - 1) // cb
    KC = src3.shape[1]
    assert len(dst.res) >= 2 * npc
    dst.cb = cb
    for i in (order or range(npc)):
        c0, c1 = i * cb, min(n, (i + 1) * cb)
        for hf in range(2):
            k0, k1 = hf * KC // 2, (hf + 1) * KC // 2
            P.dma(dst[:, k0:k1, c0:c1], src3[:, k0:k1, c0:c1], W=[dst.res[2 * i + hf]], q=q)


def _cr(W, c0, c1):
    return [W.res[j] for i in range(c0 // W.cb, (c1 - 1) // W.cb + 1) for j in (2 * i, 2 * i + 1)]


def _interleave(gens):
    gens = [g for g in gens if g is not None]
    while gens:
        for g in list(gens):
            try:
                next(g)
            except StopIteration:
                gens.remove(g)


def phase_hg(P, nc, l, s, Wd, cst, cstb, hT, lbs_d, hg_of, ohgT):
    identB = cstb[:, 0:128]
    onesF = cst[:, C_ONES:C_ONES + 128]
    CI = cst[:, C_CI:C_CI + 4]
    Wl = Wd["w_in"][l].rearrange("(kc p) n -> p kc n", p=128)
    Wq = P.sb([128, 8, 1024], BF16, nres=4, name="Wq")
    Wf = P.sb([128, 8, 1024], BF16, nres=4, name="Wf")
    Wi = P.sb([128, 8, 1024], BF16, nres=4, name="Wi")
    Wg = P.sb([128, 8, 1024], BF16, nres=4, name="Wg")
    _loadw(P, Wf, Wl[:, :, O_HFF:O_HFF + 1024], 1024)
    _loadw(P, Wq, Wl[:, :, O_HQ:O_HQ + 1024], 1024)
    _loadw(P, Wi, Wl[:, :, O_HI:O_HI + 1024], 1024)
    lbb = P.sb([128, 1024], F32, name="lbb")
    oml = P.sb([128, 1024], F32, name="oml")

    def load_lb(d):
        P.dma(lbb[:], lbs_d[l, d:d + 1, :].broadcast_to([128, 1024]), R=[P.dres("lbs")], W=[lbb])
        P.ts(oml[:], lbb[:], -1.0, 1.0, ALU.mult, ALU.add, R=[lbb], W=[oml])

    load_lb(0)
    gn = P.sb([128, 1], F32, name="gn")
    P.dma(gn[:], Wd["hg_gnorm"][l].rearrange("(p o) -> p o", o=1), W=[gn])
    lf = P.sb([128, 1024], F32, name="lf")
    kk = P.sb([128, 1024], F32, name="kk")
    ex = P.sb([128, 1024], F32, name="ex")
    exn = P.sb([128, 1024], F32, name="exn")
    qs = P.sb([128, 1024], F32, name="qs")
    Kt = [P.sb([128, 1024], BF16, name="Kt%d" % i) for i in range(2)]
    Qt = P.sb([128, 1024], BF16, name="Qt")
    Vt = [P.sb([128, 1024], BF16, name="Vt%d" % i) for i in range(2)]
    Ktm = [P.sb([128, 1024], BF16, name="Ktm%d" % i) for i in range(2)]
    KT = [P.sb([128, 1024], BF16, name="KT%d" % i) for i in range(2)]
    QT = [P.sb([128, 1024], BF16, name="QT%d" % i) for i in range(2)]
    ec = [P.sb([128, 32], F32, name="ec%d" % i) for i in range(2)]
    sg = [P.sb([128, 1024], F32, name="sg%d" % i) for i in range(2)]
    S = P.sb([128, 8, 128], F32, nres=8, name="S")
    Sb = P.sb([128, 8, 128], BF16, nres=8, name="Sb")
    ATm = P.sb([128, 8, 128], BF16, nres=8, name="ATm")
    ost = P.sb([128, 1024], F32, name="ost")
    ofl = P.sb([128, 1024], F32, name="ofl")
    osq = P.sb([128, 1024], F32, name="osq")
    rs = ofl
    ob = P.sb([128, 1024], BF16, name="ob")
    pA = P.ps([128, 1024], name="pA")
    pT = P.ps([128, 1024], BF16, name="pT")
    pC = P.ps([128, 512], name="pC")
    pCb = pC[:].bitcast(BF16)
    pO = P.ps([128, 1024], name="pO")
    pSl = P.ps([128, 1024], name="pSl")
    _r0, _r1 = Res(), Res()
    pSl.res = [_r0] * 4 + [_r1] * 4

    def proj(Wt, t):
        for half in range(2):
            for kc in range(8):
                P.mm(pA[:, half * 512:(half + 1) * 512], hT[:, kc, t * 128:(t + 1) * 128],
                     Wt[:, kc, half * 512:(half + 1) * 512], start=(kc == 0), stop=(kc == 7),
                     R=[hT.res[t], Wt], W=[pA])

    def prologue(t, d, b):
        D = cst[:, (C_HDF if d == 0 else C_HDB):(C_HDF if d == 0 else C_HDB) + 128]
        proj(Wf, t)
        P.act(lf[:], pA[:], AF.Sigmoid, R=[pA], W=[lf])
        P.act(kk[:], pA[:], AF.Sigmoid, R=[pA], W=[kk], scale=-1.0)
        yield
        proj(Wq, t)
        P.tt(lf[:], lf[:], oml[:], ALU.mult, R=[lf, oml], W=[lf])
        P.tt(lf[:], lf[:], lbb[:], ALU.add, R=[lf, lbb], W=[lf])
        P.act(lf[:], lf[:], AF.Ln, R=[lf], W=[lf])
        P.act(qs[:], pA[:], AF.Silu, R=[pA], W=[qs])
        P.tt(kk[:], kk[:], oml[:], ALU.mult, R=[kk, oml], W=[kk])
        yield
        proj(Wi, t)
        P.cp(Vt[b][:], pA[:], R=[pA], W=[Vt[b]], E="act")
        yield
        for half in range(2):
            P.mm(pA[:, half * 512:(half + 1) * 512], D, lf[:, half * 512:(half + 1) * 512], R=[cst, lf], W=[pA])
        for h in range(8):
            P.mm(pC[:, h * 4:(h + 1) * 4], lf[:, h * 128:(h + 1) * 128], CI, R=[lf, cst], W=[pC])
        P.act(ex[:], pA[:], AF.Exp, R=[pA], W=[ex])
        P.act(exn[:], pA[:], AF.Exp, R=[pA], W=[exn], scale=-1.0)
        P.act(ec[b][:], pC[:, 0:32], AF.Exp, R=[pC], W=[ec[b]])
        yield
        P.stt(Qt[:], qs[:], float(128 ** -0.5), exn[:], ALU.mult, ALU.mult, R=[qs, exn], W=[Qt])
        P.tt(Kt[b][:], kk[:], ex[:], ALU.mult, R=[kk, ex], W=[Kt[b]])
        P.ts(Ktm[b][:], Kt[b][:], CI[:, 3:4], None, ALU.mult, R=[Kt[b], cst], W=[Ktm[b]])
        yield
        for h in range(8):
            P.tr(pT[:, h * 128:(h + 1) * 128], Kt[b][:, h * 128:(h + 1) * 128], identB, R=[Kt[b], cstb], W=[pT])
        P.cp(KT[b][:], pT[:], R=[pT], W=[KT[b]])
        for h in range(8):
            P.tr(pCb[:, h * 128:(h + 1) * 128], Qt[:, h * 128:(h + 1) * 128], identB, R=[Qt, cstb], W=[pC])
        P.cp(QT[b][:], pCb, R=[pC], W=[QT[b]], E="act")
        yield
        if d == 1:
            for h in range(8):
                for kc in range(8):
                    P.mm(pA[:, h * 128:(h + 1) * 128], Wg[:, kc, h * 128:(h + 1) * 128], hT[:, kc, t * 128:(t + 1) * 128],
                         start=(kc == 0), stop=(kc == 7), R=[Wg, hT.res[t]], W=[pA])
            P.act(sg[b][:], pA[:], AF.Silu, R=[pA], W=[sg[b]])
            yield

    def scan(t, d, b):
        M = cst[:, (C_HMF if d == 0 else C_HMB):(C_HMF if d == 0 else C_HMB) + 128]
        if d == 1:
            P.dma(ofl[:], hg_of[t], R=[P.dres("hg_of", t)], W=[ofl])
        for h in range(8):
            hs = slice(h * 128, (h + 1) * 128)
            P.mm(pSl[:, hs], KT[b][:, hs], QT[b][:, hs], R=[KT[b], QT[b]], W=[pSl.res[h]])
        for h in range(8):
            hs = slice(h * 128, (h + 1) * 128)
            P.tt(ATm[:, h, :], pSl[:, hs], M, ALU.mult, R=[pSl.res[h], cst], W=[ATm.res[h]])
        yield
        for h in range(8):
            hs = slice(h * 128, (h + 1) * 128)
            P.op("pe", lambda e, hs=hs, h=h: e.matmul(pO[:, hs], Vt[b][:, hs], ATm[:, h, :], start=(h % 4 == 0), stop=False,
                                                       skip_group_check=True), R=[Vt[b], ATm.res[h]], W=[pO])
        chunks = [0, 1, 2, 3] if d == 0 else [3, 2, 1, 0]
        for ci, c in enumerate(chunks):
            for h in range(8):
                e_c = ec[b][:, h * 4 + c:h * 4 + c + 1]
                P.op("act", lambda e, h=h, e_c=e_c: e.activation(Sb[:, h, :], S[:, h, :], AF.Identity, scale=e_c),
                     R=[S.res[h], ec[b]], W=[Sb.res[h]])
            for h in range(8):
                hs = slice(h * 128, (h + 1) * 128)
                if c == 3:
                    P.mm(pSl[:, hs], Ktm[b][:, hs], Vt[b][:, hs], R=[Ktm[b], Vt[b]], W=[pSl.res[h]])
                else:
                    P.mm(pSl[:, hs], Kt[b][c * 32:(c + 1) * 32, hs], Vt[b][c * 32:(c + 1) * 32, hs], R=[Kt[b], Vt[b]], W=[pSl.res[h]])
            for h in range(8):
                cs = slice(h * 128 + c * 32, h * 128 + (c + 1) * 32)
                P.op("pe", lambda e, cs=cs, h=h: e.matmul(pO[:, cs], Sb[:, h, :], QT[b][:, cs], start=False, stop=(ci == 3),
                                                           skip_group_check=True), R=[Sb.res[h], QT[b]], W=[pO])
            for h in range(8):
                hs = slice(h * 128, (h + 1) * 128)
                e_c = ec[b][:, h * 4 + c:h * 4 + c + 1]
                P.stt(S[:, h, :], S[:, h, :], e_c, pSl[:, hs], ALU.mult, ALU.add, R=[S.res[h], ec[b], pSl.res[h]], W=[S.res[h]])
            yield
        if d == 0:
            P.cp(ost[:], pO[:], R=[pO], W=[ost], E="act")
            P.dma(hg_of[t], ost[:], R=[ost], W=[P.dres("hg_of", t)])
        else:
            P.tt(ost[:], pO[:], ofl[:], ALU.add, R=[pO, ofl], W=[ost])
            P.act(osq[:], ost[:], AF.Square, R=[ost], W=[osq])
            for half in range(2):
                P.mm(pSl[:, half * 512:(half + 1) * 512], onesF, osq[:, half * 512:(half + 1) * 512], R=[cst, osq], W=[pSl])
            yield
            P.act(rs[:], pSl[:], AF.Sqrt, R=[pSl], W=[rs], scale=1.0 / 128.0, bias=RMS_EPS)
            P.op("dve", lambda e: e.reciprocal(rs[:], rs[:]), R=[rs], W=[rs])
            P.stt(osq[:], ost[:], gn[:, 0:1], rs[:], ALU.mult, ALU.mult, R=[ost, gn, rs], W=[osq])
            P.tt(ob[:], osq[:], sg[b][:], ALU.mult, R=[osq, sg[b]], W=[ob])
            P.dma(ohgT[t], ob[:], R=[ob], W=[P.dres("ohgT", t)])
        yield

    order = [(t, 0) for t in range(NT)] + [(t, 1) for t in list(range(TCTX - 1, -1, -1)) + list(range(NT - 1, TCTX - 1, -1))]
    P.op("dve", lambda e: e.memset(S[:], 0.0), W=[S])
    _interleave([prologue(order[0][0], order[0][1], 0)])
    for n, (t, d) in enumerate(order):
        nxt = None
        if n + 1 < len(order):
            t2, d2 = order[n + 1]
            if d2 == 1 and d == 0:
                _loadw(P, Wf, Wl[:, :, O_HFB:O_HFB + 1024], 1024)
                _loadw(P, Wg, Wl[:, :, O_HG:O_HG + 1024], 1024)
                load_lb(1)
            nxt = prologue(t2, d2, (n + 1) % 2)
        _interleave([scan(t, d, n % 2), nxt])
        if n + 1 < len(order) and order[n + 1][1] == 1 and d == 0:
            P.op("dve", lambda e: e.memset(S[:], 0.0), W=[S])


def _pipeline(items, gen_fn):
    active = []

    def rnd():
        for g in list(active):
            try:
                next(g)
            except StopIteration:
                active.remove(g)

    for it in items:
        active.append(gen_fn(it))
        rnd()
    while active:
        rnd()


def phase_at(P, nc, l, s, Wd, cst, cstb, hT, rope_d, oatT):
    identB = cstb[:, 0:128]
    onesB = cstb[:, 128:256]
    Wl = Wd["w_in"][l].rearrange("(kc p) n -> p kc n", p=128)
    Wa = P.sb([128, 8, 1536], BF16, nres=4, name="Wa")
    _loadw(P, Wa, Wl[:, :, O_AQ:O_AQ + 1536], 1536)
    QT = P.sb([128, 8, TOK], BF16, nres=NT, name="QTa")
    KTa = P.sb([128, 2, TOK], BF16, nres=NT, name="KTa")
    Va = P.sb([128, NT, 256], BF16, nres=NT, name="Va")
    with ExitStack() as sub:
        P.stack = sub
        cosT = P.sb([128, 16, 64], F32, name="cosT")
        sinT = P.sb([128, 16, 64], F32, name="sinT")
        P.dma(cosT[:], rope_d[0].rearrange("(t p) j -> p t j", p=128), W=[cosT])
        P.dma(sinT[:], rope_d[1].rearrange("(t p) j -> p t j", p=128), W=[sinT])
        wqk = P.sb([128, 10, 128], F32, name="wqk")
        P.dma(wqk[:, 0:8, :], Wd["at_qnorm"][l:l + 1, :].rearrange("o (h d) -> o h d", h=1).broadcast_to([128, 8, 128]), W=[wqk])
        P.dma(wqk[:, 8:10, :], Wd["at_knorm"][l:l + 1, :].rearrange("o (h d) -> o h d", h=1).broadcast_to([128, 2, 128]), W=[wqk])
        NB = 2
        sqt = [P.sb([128, 1280], F32, name="sqt%d" % i) for i in range(NB)]
        xn = [P.sb([128, 1280], F32, name="xn%d" % i) for i in range(NB)]
        ss = [P.sb([128, 10], F32, name="ss%d" % i) for i in range(NB)]
        t1 = [P.sb([128, 10, 64], F32, name="t1%d" % i) for i in range(NB)]
        t2 = [P.sb([128, 10, 64], F32, name="t2%d" % i) for i in range(NB)]
        xr = [P.sb([128, 1280], BF16, name="xr%d" % i) for i in range(NB)]
        pQ = [P.ps([128, 1536], name="pQ%d" % i) for i in range(NB)]
        pT = P.ps([128, 2048], BF16, name="pTa")

        def prep(t):
            b = t % NB
            tc = slice(t * 128, (t + 1) * 128)
            pQ_, sqt_, xn_, ss_, t1_, t2_, xr_ = pQ[b], sqt[b], xn[b], ss[b], t1[b], t2[b], xr[b]
            for j in range(3):
                for kc in range(8):
                    P.mm(pQ_[:, j * 512:(j + 1) * 512], hT[:, kc, tc], Wa[:, kc, j * 512:(j + 1) * 512],
                         start=(kc == 0), stop=(kc == 7), R=[hT.res[t], Wa], W=[pQ_])
            P.cp(Va[:, t, :], pQ_[:, 1280:1536], R=[pQ_], W=[Va.res[t]], E="act")
            P.act(sqt_[:], pQ_[:, 0:1280], AF.Square, R=[pQ_], W=[sqt_])
            P.op("dve", lambda e: e.tensor_reduce(ss_[:], sqt_[:].rearrange("p (h d) -> p h d", h=10), AX.X, ALU.add), R=[sqt_], W=[ss_])
            P.act(ss_[:], ss_[:], AF.Sqrt, R=[ss_], W=[ss_], scale=1.0 / 128.0, bias=RMS_EPS)
            P.op("dve", lambda e: e.reciprocal(ss_[:], ss_[:]), R=[ss_], W=[ss_])
            yield
            P.tt(xn_[:].rearrange("p (h d) -> p h d", h=10), pQ_[:, 0:1280].rearrange("p (h d) -> p h d", h=10),
                 ss_[:].rearrange("p (h o) -> p h o", o=1).broadcast_to([128, 10, 128]), ALU.mult, R=[pQ_, ss_], W=[xn_])
            P.tt(xn_[:], xn_[:], wqk[:].rearrange("p h d -> p (h d)"), ALU.mult, R=[xn_, wqk], W=[xn_])
            if t >= TCTX:
                tl = t - TCTX
                xv = xn_[:].rearrange("p (h j two) -> p h j two", h=10, two=2)
                xo = xr_[:].rearrange("p (h j two) -> p h j two", h=10, two=2)
                cb = cosT[:, tl:tl + 1, :].broadcast_to([128, 10, 64])
                sb_ = sinT[:, tl:tl + 1, :].broadcast_to([128, 10, 64])
                P.tt(t1_[:], xv[:, :, :, 0], cb, ALU.mult, R=[xn_, cosT], W=[t1_])
                P.tt(t2_[:], xv[:, :, :, 1], sb_, ALU.mult, R=[xn_, sinT], W=[t2_])
                P.tt(xo[:, :, :, 0], t1_[:], t2_[:], ALU.subtract, R=[t1_, t2_], W=[xr_])
                P.tt(t1_[:], xv[:, :, :, 0], sb_, ALU.mult, R=[xn_, sinT], W=[t1_])
                P.tt(t2_[:], xv[:, :, :, 1], cb, ALU.mult, R=[xn_, cosT], W=[t2_])
                P.tt(xo[:, :, :, 1], t1_[:], t2_[:], ALU.add, R=[t1_, t2_], W=[xr_])
            else:
                P.cp(xr_[:], xn_[:], R=[xn_], W=[xr_], E="act")
            yield
            for j in range(10):
                P.tr(pT[:, j * 128:(j + 1) * 128], xr_[:, j * 128:(j + 1) * 128], identB, R=[xr_, cstb], W=[pT])
            P.cp(QT[:, :, tc], pT[:, 0:1024].rearrange("p (h k) -> p h k", h=8), R=[pT], W=[QT.res[t]])
            P.cp(KTa[:, :, tc], pT[:, 1024:1280].rearrange("p (h k) -> p h k", h=2), R=[pT], W=[KTa.res[t]], E="act")
            yield

        _pipeline(list(range(NT)), prep)
        P.barrier()
    with ExitStack() as sub:
        P.stack = sub
        NS = 3
        pS = [P.ps([128, 512], name="pSa%d" % i) for i in range(NS)]
        pO = [P.ps([128, 512], name="pOa%d" % i) for i in range(2)]
        pL = [P.ps([128, 512], name="pLa%d" % i) for i in range(2)]
        Pt = [P.sb([128, 512], BF16, name="Pt%d" % i) for i in range(NS)]
        rl = [P.sb([128, 512], F32, name="rl%d" % i) for i in range(2)]
        oa = [P.sb([128, 512], BF16, name="oa%d" % i) for i in range(2)]
        blocks = [(0, 256, [0, 1])] + [(256 + qb * 512, 512, list(range(NT))) for qb in range(4)]
        sc = float(128 ** -0.5)
        steps = []
        for h in range(8):
            for bi, (q0, nq, kts) in enumerate(blocks):
                for i, kt in enumerate(kts):
                    steps.append((h, bi, q0, nq, kt, i == 0, i == len(kts) - 1))
        jobn = {}
        for (h, bi, *_r) in steps:
            jobn.setdefault((h, bi), len(jobn))

        def s_mm(n):
            h, bi, q0, nq, kt, first, lastk = steps[n]
            kv = h // 4
            qres = [QT.res[t] for t in range(q0 // 128, (q0 + nq) // 128)]
            P.mm(pS[n % NS][:, 0:nq], KTa[:, kv, kt * 128:(kt + 1) * 128], QT[:, h, q0:q0 + nq], R=[KTa.res[kt]] + qres, W=[pS[n % NS]])

        s_mm(0)
        for n in range(len(steps)):
            h, bi, q0, nq, kt, first, lastk = steps[n]
            kv = h // 4
            jb = jobn[(h, bi)] % 2
            if n + 1 < len(steps):
                s_mm(n + 1)
            ps, pt = pS[n % NS], Pt[n % NS]
            P.act(pt[:, 0:nq], ps[:, 0:nq], AF.Exp, R=[ps], W=[pt], scale=sc)
            P.mm(pO[jb][:, 0:nq], Va[:, kt, kv * 128:(kv + 1) * 128], pt[:, 0:nq], start=first, stop=lastk, R=[Va.res[kt], pt], W=[pO[jb]])
            P.mm(pL[jb][:, 0:nq], onesB, pt[:, 0:nq], start=first, stop=lastk, R=[cstb, pt], W=[pL[jb]])
            if lastk:
                P.op("dve", lambda e: e.reciprocal(rl[jb][:, 0:nq], pL[jb][:, 0:nq]), R=[pL[jb]], W=[rl[jb]])
                P.tt(oa[jb][:, 0:nq], pO[jb][:, 0:nq], rl[jb][:, 0:nq], ALU.mult, R=[pO[jb], rl[jb]], W=[oa[jb]])
                P.dma(oatT[:, h, q0:q0 + nq], oa[jb][:, 0:nq], R=[oa[jb]], W=[P.dres("oatT", (h, q0))])
        P.barrier()


def phase_mb(P, nc, l, s, Wd, cst, cstb, hT, mb_xs, mb_yf, ombT, sq):
    identB = cstb[:, 0:128]
    identF = cst[:, C_ID:C_ID + 128]
    onesF = cst[:, C_ONES:C_ONES + 128]
    Wl = Wd["w_in"][l].rearrange("(kc p) n -> p kc n", p=128)
    with ExitStack() as ph:
        P.stack = ph
        BT = P.sb([128, 4, TOK], BF16, name="BT")
        CT = P.sb([128, 4, TOK], BF16, name="CT")
        with ExitStack() as sub:
            P.stack = sub
            Wx = P.sb([128, 8, 3072], BF16, nres=24, name="Wx")
            _loadw_cols(P, Wx, Wl[:, :, O_MX:O_MX + 3072], 3072, 256)
            cwr = P.sb([120, 128], F32, name="cwr")
            P.dma(cwr[:], Wd["mb_conv_w"][l].rearrange("j (cc p) -> (j cc) p", p=128), W=[cwr])
            cbr = P.sb([24, 128], F32, name="cbr")
            P.dma(cbr[:], Wd["mb_conv_b"][l].rearrange("(cc p) -> cc p", p=128), W=[cbr])
            cw = P.sb([128, 120], F32, name="cw")
            cbias = P.sb([128, 24], F32, name="cbias")
            pX = [P.ps([128, 512], name="pX%d" % i) for i in range(2)]
            pTt = P.ps([128, 1024], BF16, name="pTt")
            pW_ = P.ps([128, 512], name="pW_")
            P.tr(pW_[:, 0:120], cwr[:], identF[0:120, 0:120], R=[cwr, cst], W=[pW_])
            P.cp(cw[:], pW_[:, 0:120], R=[pW_], W=[cw])
            P.tr(pW_[:, 128:152], cbr[:], identF[0:24, 0:24], R=[cbr, cst], W=[pW_])
            P.cp(cbias[:], pW_[:, 128:152], R=[pW_], W=[cbias])
            xb = [P.sb([128, 2052], BF16, name="xb%d" % i) for i in range(2)]
            dgs = [P.sb([128, 5, 128], BF16, name="dg%d" % i) for i in range(2)]
            pCv = [P.ps([128, 512], name="pCv%d" % i) for i in range(2)]
            ub = P.sb([128, 2048], BF16, name="ub")
            stg = [P.sb([128, 8, 128], BF16, name="stg%d" % i) for i in range(2)]
            for xb_ in xb:
                P.op("pool", lambda e, xb_=xb_: e.memset(xb_[:], 0.0), W=[xb_])
            k = 0
            kx = 0
            cw3 = cw[:].rearrange("p (j c) -> p j c", j=5)
            for cc in range(24):
                dg = dgs[cc % 2]
                P.tt(dg[:], identB.rearrange("p (o c) -> p o c", o=1).broadcast_to([128, 5, 128]),
                     cw3[:, :, cc:cc + 1].broadcast_to([128, 5, 128]), ALU.mult, R=[cstb, cw], W=[dg])
                for (t0, ntile) in ((0, TCTX), (TCTX, NT - TCTX)):
                    N = ntile * 128
                    xb_ = xb[kx % 2]
                    kx += 1
                    nblk = (N + 511) // 512
                    for b in range(nblk):
                        nb = min(512, N - b * 512)
                        c0 = t0 * 128 + b * 512
                        tr_ = [hT.res[t] for t in range(c0 // 128, (c0 + nb) // 128)]
                        for kc in range(8):
                            P.mm(pX[b % 2][:, 0:nb], Wx[:, kc, cc * 128:(cc + 1) * 128], hT[:, kc, c0:c0 + nb],
                                 start=(kc == 0), stop=(kc == 7), R=_cr(Wx, cc * 128, (cc + 1) * 128) + tr_, W=[pX[b % 2]])
                        P.cp(xb_[:, 2 + b * 512:2 + b * 512 + nb], pX[b % 2][:, 0:nb], R=[pX[b % 2]], W=[xb_], E=("act" if b % 2 else "dve"))
                    P.op("pool", lambda e, xb_=xb_, N=N: e.memset(xb_[:, N + 2:N + 4], 0.0), W=[xb_])
                    if cc < 16:
                        dst, dres_ = ub, ub
                        off = 0
                    elif cc < 20:
                        dst, dres_ = BT[:, cc - 16, :], BT
                        off = t0 * 128
                    else:
                        dst, dres_ = CT[:, cc - 20, :], CT
                        off = t0 * 128
                    for b in range(nblk):
                        nb = min(512, N - b * 512)
                        for j in range(5):
                            P.mm(pCv[b % 2][:, 0:nb], dg[:, j, :], xb_[:, b * 512 + j:b * 512 + j + nb], start=(j == 0), stop=(j == 4),
                                 R=[dg, xb_], W=[pCv[b % 2]])
                        P.act(dst[:, off + b * 512:off + b * 512 + nb], pCv[b % 2][:, 0:nb], AF.Silu, R=[pCv[b % 2], cbias], W=[dres_],
                              bias=cbias[:, cc:cc + 1])
                    if cc < 16:
                        for g0 in range(0, ntile, 8):
                            ng = min(8, ntile - g0)
                            for j in range(ng):
                                P.tr(pTt[:, j * 128:(j + 1) * 128], ub[:, (g0 + j) * 128:(g0 + j + 1) * 128], identB, R=[ub, cstb], W=[pTt])
                            st = stg[k % 2]
                            k += 1
                            P.cp(st[:, 0:ng, :], pTt[:, 0:ng * 128].rearrange("p (j c) -> p j c", j=ng), R=[pTt], W=[st],
                                 E=("act" if k % 2 else "dve"))
                            ta = t0 + g0
                            P.dma(mb_xs[ta:ta + ng, :, cc * 128:(cc + 1) * 128].rearrange("t p c -> p t c"), st[:, 0:ng, :],
                                  R=[st], W=[P.dres("mb_xs", t) for t in range(ta, ta + ng)])
            P.barrier()
        with ExitStack() as sub:
            P.stack = sub
            Wz = P.sb([128, 8, 2112], BF16, nres=8, name="Wz")
            for i, kc in enumerate(range(0, 8, 2)):
                P.dma(Wz[:, kc:kc + 2, 2048:2112], Wl[:, kc:kc + 2, O_MDT:O_MDT + 64], W=[Wz.res[4 + i]], q="pool")
            for i, kc in enumerate(range(0, 8, 2)):
                P.dma(Wz[:, kc:kc + 2, 0:2048], Wl[:, kc:kc + 2, O_MZ:O_MZ + 2048], W=[Wz.res[i]], q="pool")
            dtb = P.sb([128, 64], F32, name="dtb")
            abc = P.sb([128, 64], F32, name="abc")
            dbc = P.sb([128, 32], F32, name="dbc")
            nrm = P.sb([128, 2048], F32, name="nrm")
            P.dma(dtb[:], Wd["mb_dt_bias"][l:l + 1].rearrange("o d h -> o (d h)").broadcast_to([128, 64]), W=[dtb])
            P.dma(abc[:], Wd["mb_a_log"][l:l + 1].rearrange("o d h -> o (d h)").broadcast_to([128, 64]), W=[abc])
            P.act(abc[:], abc[:], AF.Exp, R=[abc], W=[abc])
            P.ts(abc[:], abc[:], -1.0, None, ALU.mult, R=[abc], W=[abc])
            P.dma(dbc[:], Wd["mb_d"][l:l + 1, :].broadcast_to([128, 32]), W=[dbc])
            P.dma(nrm[:], Wd["mb_norm"][l:l + 1, :].broadcast_to([128, 2048]), W=[nrm])
            xs = [P.sb([128, 2048], BF16, name="xs%d" % i) for i in range(2)]
            xdt = [P.sb([128, 2048], BF16, name="xdt%d" % i) for i in range(2)]
            xdtw = P.sb([128, 2048], BF16, name="xdtw")
            oT = xdtw
            dtv = [P.sb([128, 64], F32, name="dtv%d" % i) for i in range(2)]
            dA = [P.sb([128, 32], F32, name="dA%d" % i) for i in range(2)]
            e3 = [P.sb([128, 96], F32, name="e3%d" % i) for i in range(2)]
            cbm = [P.sb([128, 512], F32, name="cbm%d" % i) for i in range(2)]
            Btk = [P.sb([128, 512], BF16, name="Btk%d" % i) for i in range(2)]
            rh = [P.sb([128, 512], F32, name="rh%d" % i) for i in range(2)]
            es = [P.sb([128, 512], F32, name="es%d" % i) for i in range(2)]
            MT = [P.sb([128, 32, 128], BF16, nres=8, name="MT%d" % i) for i in range(2)]
            tmp = [P.sb([128, 512], F32, name="tmpy%d" % i) for i in range(2)]
            ysb = P.sb([128, 2048], F32, nres=4, name="ysb")
            ST = P.sb([128, 2048], F32, nres=4, name="ST")
            STb = P.sb([128, 2048], BF16, nres=4, name="STb")
            yfl = P.sb([128, 2048], F32, name="yfl")
            ss4 = P.sb([128, 4], F32, name="ss4")
            pM = P.ps([128, 512], name="pM")
            pCB = P.ps([128, 512], name="pCB")
            pCBb = pCB[:].bitcast(BF16)
            pSG = [P.ps([128, 512], name="pSG%d" % i) for i in range(2)]
            pY = [P.ps([128, 512], name="pY%d" % i) for i in range(2)]
            pI = [P.ps([128, 512], name="pI%d" % i) for i in range(2)]
            p4 = [pY[0], pY[1], pI[0], pI[1]]
            v3 = lambda ap, h: ap.rearrange("p (h q) -> p h q", h=h)
            col = lambda ap: ap.rearrange("p (h o) -> p h o", o=1)

            def prologue(t, d, b):
                tc = slice(t * 128, (t + 1) * 128)
                TRI = cst[:, (C_LE if d == 0 else C_GE):(C_LE if d == 0 else C_GE) + 128]
                STR = cst[:, (C_GT if d == 0 else C_LT):(C_GT if d == 0 else C_LT) + 128]
                xs_, xdt_, dtv_, dA_, e3_, cbm_, Btk_, MT_ = xs[b], xdt[b], dtv[b], dA[b], e3[b], cbm[b], Btk[b], MT[b]
                P.dma(xs_[:], mb_xs[t], R=[P.dres("mb_xs", t)], W=[xs_])
                for kc in range(8):
                    P.mm(pM[:, 0:64], hT[:, kc, tc], Wz[:, kc, 2048:2112], start=(kc == 0), stop=(kc == 7), R=[hT.res[t], Wz.res[4:8]], W=[pM])
                P.tt(dtv_[:], pM[:, 0:64], dtb[:], ALU.add, R=[pM, dtb], W=[dtv_])
                P.act(dtv_[:], dtv_[:], AF.Exp, R=[dtv_], W=[dtv_])
                P.act(dtv_[:], dtv_[:], AF.Ln, R=[dtv_], W=[dtv_], bias=1.0)
                P.tt(dA_[:], dtv_[:, d * 32:(d + 1) * 32], abc[:, d * 32:(d + 1) * 32], ALU.mult, R=[dtv_, abc], W=[dA_])
                for g in range(4):
                    P.mm(pCB[:, g * 128:(g + 1) * 128], BT[:, g, tc], CT[:, g, tc], R=[BT, CT], W=[pCB])
                yield
                P.mm(pM[:, 64:96], TRI, dA_[:], R=[cst, dA_], W=[pM])
                P.mm(pM[:, 96:128], STR, dA_[:], R=[cst, dA_], W=[pM])
                P.mm(pM[:, 128:160], onesF, dA_[:], R=[cst, dA_], W=[pM])
                P.act(e3_[:], pM[:, 64:160], AF.Exp, R=[pM], W=[e3_])
                P.tt(v3(cbm_[:], 4), v3(pCB[:], 4), TRI.rearrange("p (o t) -> p o t", o=1).broadcast_to([128, 4, 128]), ALU.mult,
                     R=[pCB, cst], W=[cbm_])
                for g in range(4):
                    P.tr(pCBb[:, g * 128:(g + 1) * 128], BT[:, g, tc], identB, R=[BT, cstb], W=[pCB])
                P.cp(Btk_[:], pCBb[:, 0:512], R=[pCB], W=[Btk_], E="act")
                P.tt(v3(xdt_[:], 32), v3(xs_[:], 32), col(dtv_[:, d * 32:(d + 1) * 32]).broadcast_to([128, 32, 64]), ALU.mult,
                     R=[xs_, dtv_], W=[xdt_])
                yield
                TRI3 = TRI.rearrange("p (o t) -> p o t", o=1)
                for step in range(10):
                    if step < 8:
                        hb = step
                        i = hb % 2
                        P.tt(v3(rh[i][:], 4), TRI3.broadcast_to([128, 4, 128]),
                             col(dA_[:, hb * 4:hb * 4 + 4]).broadcast_to([128, 4, 128]), ALU.mult, R=[cst, dA_], W=[rh[i]])
                        P.mm(pSG[i][:], STR, rh[i][:], R=[cst, rh[i]], W=[pSG[i]])
                    if 1 <= step <= 8:
                        i = (step - 1) % 2
                        P.act(es[i][:], pSG[i][:], AF.Exp, R=[pSG[i]], W=[es[i]])
                    if 2 <= step <= 9:
                        hb = step - 2
                        i = hb % 2
                        g = hb // 2
                        P.tt(MT_[:, hb * 4:hb * 4 + 4, :], v3(es[i][:], 4),
                             cbm_[:, g * 128:(g + 1) * 128].rearrange("p (o t) -> p o t", o=1).broadcast_to([128, 4, 128]), ALU.mult,
                             R=[es[i], cbm_], W=[MT_.res[hb]])
                    if step % 2 == 1:
                        yield

            def scan(t, d, b):
                tc = slice(t * 128, (t + 1) * 128)
                xs_, xdt_, e3_, Btk_, MT_ = xs[b], xdt[b], e3[b], Btk[b], MT[b]
                if d == 1:
                    P.dma(yfl[:], mb_yf[t], R=[P.dres("mb_yf", t)], W=[yfl])
                P.tt(v3(xdtw[:], 32), v3(xdt_[:], 32), col(e3_[:, 32:64]).broadcast_to([128, 32, 64]), ALU.mult,
                     R=[xdt_, e3_], W=[xdtw])
                for g in range(4):
                    gs = slice(g * 512, (g + 1) * 512)
                    pY_, pI_, tmp_ = pY[g % 2], pI[g % 2], tmp[g % 2]
                    for hh in range(8):
                        h = g * 8 + hh
                        P.mm(pY_[:, hh * 64:(hh + 1) * 64], MT_[:, h, :], xdt_[:, h * 64:(h + 1) * 64], R=[MT_.res[h // 4], xdt_], W=[pY_])
                    P.mm(pI_[:], CT[:, g, tc], STb[:, gs], R=[CT, STb.res[g]], W=[pI_])
                    P.tt(v3(tmp_[:], 8), v3(pI_[:], 8), col(e3_[:, g * 8:(g + 1) * 8]).broadcast_to([128, 8, 64]), ALU.mult,
                         R=[pI_, e3_], W=[tmp_])
                    P.tt(ysb[:, gs], tmp_[:], pY_[:], ALU.add, R=[tmp_, pY_], W=[ysb.res[g]])
                    if g % 2 == 1:
                        yield
                for g in range(4):
                    gs = slice(g * 512, (g + 1) * 512)
                    pI_ = pI[g % 2]
                    P.mm(pI_[:], Btk_[:, g * 128:(g + 1) * 128], xdtw[:, gs], R=[Btk_, xdtw], W=[pI_])
                    P.tt(v3(ST[:, gs], 8), v3(ST[:, gs], 8), col(e3_[:, 64 + g * 8:64 + (g + 1) * 8]).broadcast_to([128, 8, 64]), ALU.mult,
                         R=[ST.res[g], e3_], W=[ST.res[g]])
                    P.tt(ST[:, gs], ST[:, gs], pI_[:], ALU.add, R=[ST.res[g], pI_], W=[ST.res[g]])
                    P.cp(STb[:, gs], ST[:, gs], R=[ST.res[g]], W=[STb.res[g]], E="act")
                    if g % 2 == 1:
                        yield
                if d == 0:
                    P.dma(mb_yf[t], ysb[:], R=[ysb], W=[P.dres("mb_yf", t)])
                    return
                P.tt(ysb[:], ysb[:], yfl[:], ALU.add, R=[ysb, yfl], W=[ysb])
                P.tt(v3(yfl[:], 32), v3(xs_[:], 32), col(dbc[:]).broadcast_to([128, 32, 64]), ALU.mult, R=[xs_, dbc, yfl], W=[yfl])
                P.tt(ysb[:], ysb[:], yfl[:], ALU.add, R=[ysb, yfl], W=[ysb])
                for j in range(4):
                    for kc in range(8):
                        P.mm(p4[j][:], hT[:, kc, tc], Wz[:, kc, j * 512:(j + 1) * 512], start=(kc == 0), stop=(kc == 7),
                             R=[hT.res[t], Wz.res[0:4]], W=[p4[j]])
                    P.act(yfl[:, j * 512:(j + 1) * 512], p4[j][:], AF.Silu, R=[p4[j]], W=[yfl])
                    if j % 2 == 1:
                        yield
                P.tt(ysb[:], ysb[:], yfl[:], ALU.mult, R=[ysb, yfl], W=[ysb])
                P.act(yfl[:], ysb[:], AF.Square, R=[ysb], W=[yfl])
                P.op("dve", lambda e: e.tensor_reduce(ss4[:], v3(yfl[:], 4), AX.X, ALU.add), R=[yfl], W=[ss4])
                P.act(ss4[:], ss4[:], AF.Sqrt, R=[ss4], W=[ss4], scale=1.0 / 512.0, bias=RMS_EPS)
                P.op("dve", lambda e: e.reciprocal(ss4[:], ss4[:]), R=[ss4], W=[ss4])
                yield
                for g in range(4):
                    gs = slice(g * 512, (g + 1) * 512)
                    P.stt(ysb[:, gs], ysb[:, gs], ss4[:, g:g + 1], nrm[:, gs], ALU.mult, ALU.mult, R=[ysb.res[g], ss4, nrm], W=[ysb.res[g]])
                for cc in range(16):
                    P.tr(p4[cc // 4][:, (cc % 4) * 128:(cc % 4 + 1) * 128], ysb[:, cc * 128:(cc + 1) * 128], identF, R=[ysb, cst], W=[p4[cc // 4]])
                for j in range(4):
                    P.cp(oT[:, j * 512:(j + 1) * 512], p4[j][:], R=[p4[j]], W=[oT], E=("act" if j % 2 else "dve"))
                P.dma(ombT[t], oT[:], R=[oT], W=[P.dres("ombT", t)])
                yield

            def reset_state():
                P.op("dve", lambda e: e.memset(ST[:], 0.0), W=[ST])
                P.op("pool", lambda e: e.memset(STb[:], 0.0), W=[STb])

            order = [(t, 0) for t in range(NT)] + [(t, 1) for t in list(range(TCTX - 1, -1, -1)) + list(range(NT - 1, TCTX - 1, -1))]
            reset_state()
            _interleave([prologue(order[0][0], order[0][1], 0)])
            for n, (t, d) in enumerate(order):
                nxt = None
                if n + 1 < len(order):
                    nxt = prologue(order[n + 1][0], order[n + 1][1], (n + 1) % 2)
                _interleave([scan(t, d, n % 2), nxt])
                if n + 1 < len(order) and order[n + 1][1] == 1 and d == 0:
                    reset_state()
            P.barrier()


def _layernorm(P, out, z, zr, lng, lnb, st, mv):
    for c in range(2):
        P.op("dve", lambda e: e.bn_stats(st[:, c * 6:(c + 1) * 6], z[:, c * 512:(c + 1) * 512]), R=[zr], W=[st])
    P.op("dve", lambda e: e.bn_aggr(mv[:, 0:2], st[:]), R=[st], W=[mv])
    P.act(mv[:, 2:3], mv[:, 1:2], AF.Sqrt, R=[mv], W=[mv], bias=LN_EPS)
    P.op("dve", lambda e: e.reciprocal(mv[:, 2:3], mv[:, 2:3]), R=[mv], W=[mv])
    P.ts(out, z, mv[:, 0:1], mv[:, 2:3], ALU.subtract, ALU.mult, R=[zr, mv], W=[zr])
    P.tt(out, out, lng[:], ALU.mult, R=[zr, lng], W=[zr])
    P.tt(out, out, lnb[:], ALU.add, R=[zr, lnb], W=[zr])


def _gate_bcast(P, cst, modrow_d, c0, s, pG, gbc):
    mr = P.sb([3, 1024], F32, name="mr")
    P.dma(mr[:], modrow_d[:, c0:c0 + 1024], R=[P.dres("modrow")], W=[mr])
    for ri, r in enumerate((s, 2)):
        for half in range(2):
            P.mm(pG[:, half * 512:(half + 1) * 512], cst[0:3, C_SEL + r * 128:C_SEL + (r + 1) * 128], mr[0:3, half * 512:(half + 1) * 512],
                 R=[cst, mr], W=[pG])
        P.cp(gbc[:, ri, :], pG[:], R=[pG], W=[gbc])


def phase_merge(P, nc, l, s, last, Wd, cst, cstb, hT, modA, modB, modrow_d, ohgT, oatT, ombT, ymTd, x1s, xsrc_fn, sq):
    identB = cstb[:, 0:128]
    identF = cst[:, C_ID:C_ID + 128]
    tiles = list(range(TCTX, NT)) if last else list(range(NT))
    Wl = Wd["w_in"][l].rearrange("(kc p) n -> p kc n", p=128)
    with ExitStack() as ph:
        P.stack = ph
        Wgt = P.sb([128, 8, 3072], BF16, nres=12, name="Wgt")
        Wbh = P.sb([128, 8, 1024], BF16, nres=4, name="Wbh")
        Wba = P.sb([128, 8, 1024], BF16, nres=4, name="Wba")
        Wbm = P.sb([128, 16, 1024], BF16, nres=8, name="Wbm")
        Wgs = Wl[:, :, O_GT:O_GT + 3072]

        def ldg(b):
            for i, kc in enumerate(range(0, 8, 2)):
                P.dma(Wgt[:, kc:kc + 2, b * 1024:(b + 1) * 1024], Wgs[:, kc:kc + 2, b * 1024:(b + 1) * 1024], W=[Wgt.res[b * 4 + i]], q="pool")

        ldg(0)
        _loadw(P, Wbh, Wd["w_br_hg"][l].rearrange("(kc p) n -> p kc n", p=128), 1024)
        ldg(1)
        _loadw(P, Wba, Wd["w_br_at"][l].rearrange("(kc p) n -> p kc n", p=128), 1024)
        ldg(2)
        _loadw(P, Wbm, Wd["w_br_mb"][l].rearrange("(kc p) n -> p kc n", p=128), 1024)
        oh = [P.sb([128, 1024], BF16, name="oh%d" % i) for i in range(2)]
        oa = [P.sb([128, 8, 128], BF16, name="oam%d" % i) for i in range(2)]
        om = [P.sb([128, 2048], BF16, name="om%d" % i) for i in range(2)]
        sgt = P.sb([128, 1024], F32, name="sgt")
        ym = P.sb([128, 1024], F32, name="ym")
        tmp = P.sb([128, 1024], F32, name="tmpm")
        ymb = P.sb([128, 1024], BF16, name="ymb")
        ymT = [P.sb([128, 1024], BF16, name="ymT%d" % i) for i in range(2)]
        pG = P.ps([128, 1024], name="pGm")
        pB = P.ps([128, 1024], name="pBm")
        pT = P.ps([128, 1024], BF16, name="pTm")
        for i, t in enumerate(tiles):
            tc = slice(t * 128, (t + 1) * 128)
            oh_, oa_, om_ = oh[i % 2], oa[i % 2], om[i % 2]
            P.dma(oh_[:], ohgT[t], R=[P.dres("ohgT", t)], W=[oh_])
            P.dma(oa_[:], oatT[:, :, tc], R=[P.dres("oatT", (h, q0)) for h in range(8) for q0 in (0, 256, 768, 1280, 1792)], W=[oa_])
            P.dma(om_[:], ombT[t], R=[P.dres("ombT", t)], W=[om_])
            srcs = [(lambda kc, o=oh_: o[:, kc * 128:(kc + 1) * 128], Wbh, 8, oh_),
                    (lambda kc, o=oa_: o[:, kc, :], Wba, 8, oa_),
                    (lambda kc, o=om_: o[:, kc * 128:(kc + 1) * 128], Wbm, 16, om_)]
            for b, (of, Wb, nk, ot) in enumerate(srcs):
                for half in range(2):
                    for kc in range(8):
                        P.mm(pG[:, half * 512:(half + 1) * 512], hT[:, kc, tc], Wgt[:, kc, b * 1024 + half * 512:b * 1024 + (half + 1) * 512],
                             start=(kc == 0), stop=(kc == 7), R=[hT.res[t]] + Wgt.res[b * 4:(b + 1) * 4], W=[pG])
                P.act(sgt[:], pG[:], AF.Sigmoid, R=[pG], W=[sgt])
                for half in range(2):
                    for kc in range(nk):
                        P.mm(pB[:, half * 512:(half + 1) * 512], of(kc), Wb[:, kc, half * 512:(half + 1) * 512],
                             start=(kc == 0), stop=(kc == nk - 1), R=[ot, Wb], W=[pB])
                if b == 0:
                    P.tt(ym[:], pB[:], sgt[:], ALU.mult, R=[pB, sgt], W=[ym])
                else:
                    P.tt(tmp[:], pB[:], sgt[:], ALU.mult, R=[pB, sgt], W=[tmp])
                    if b == 1:
                        P.tt(ym[:], ym[:], tmp[:], ALU.add, R=[ym, tmp], W=[ym])
                    else:
                        P.tt(ymb[:], ym[:], tmp[:], ALU.add, R=[ym, tmp], W=[ymb])
            for kc in range(8):
                P.tr(pT[:, kc * 128:(kc + 1) * 128], ymb[:, kc * 128:(kc + 1) * 128], identB, R=[ymb, cstb], W=[pT])
            yT = ymT[i % 2]
            P.cp(yT[:], pT[:], R=[pT], W=[yT], E="act")
            P.dma(ymTd[t], yT[:], R=[yT], W=[P.dres("ymT", t)])
        P.barrier()
    with ExitStack() as ph:
        P.stack = ph
        Wo = P.sb([128, 8, 1024], BF16, nres=4, name="Wo")
        _loadw(P, Wo, Wd["w_out"][l].rearrange("(kc p) n -> p kc n", p=128), 1024)
        pW = P.ps([128, 1024], name="pWm")
        pp = [P.ps([128, 512], name="pp%d" % i) for i in range(2)]
        gbc = P.sb([128, 2, 1024], F32, name="gbc")
        _gate_bcast(P, cst, modrow_d, 2048, s, pW, gbc)
        lng = P.sb([128, 1024], F32, name="lng")
        lnb = P.sb([128, 1024], F32, name="lnb")
        P.dma(lng[:], Wd["ln1_g"][l:l + 1, :].broadcast_to([128, 1024]), W=[lng])
        P.dma(lnb[:], Wd["ln1_b"][l:l + 1, :].broadcast_to([128, 1024]), W=[lnb])
        yT = [P.sb([128, 1024], BF16, name="yTb%d" % i) for i in range(2)]
        xt = [P.sb([128, 1024], F32, name="xtb%d" % i) for i in range(2)]
        z = [P.sb([128, 1024], F32, name="zb%d" % i) for i in range(2)]
        st = P.sb([128, 12], F32, name="st")
        mv = P.sb([128, 4], F32, name="mv")
        for i, t in enumerate(tiles):
            tc = slice(t * 128, (t + 1) * 128)
            r = 2 if t < TCTX else s
            ri = 1 if t < TCTX else 0
            yT_, xt_, z_ = yT[i % 2], xt[i % 2], z[i % 2]
            P.dma(yT_[:], ymTd[t], R=[P.dres("ymT", t)], W=[yT_])
            src, sres = xsrc_fn(t)
            P.dma(xt_[:], src, R=[sres], W=[xt_])
            for half in range(2):
                for kc in range(8):
                    P.mm(pW[:, half * 512:(half + 1) * 512], yT_[:, kc * 128:(kc + 1) * 128], Wo[:, kc, half * 512:(half + 1) * 512],
                         start=(kc == 0), stop=(kc == 7), R=[yT_, Wo], W=[pW])
            P.tt(z_[:], pW[:], gbc[:, ri, :], ALU.mult, R=[pW, gbc], W=[z_])
            P.stt(z_[:], xt_[:], float(DN_ALPHA), z_[:], ALU.mult, ALU.add, R=[xt_, z_], W=[z_])
            _layernorm(P, z_[:], z_[:], z_, lng, lnb, st, mv)
            P.dma(x1s[t], z_[:], R=[z_], W=[P.dres("x1s", t)])
            for half in range(2):
                pb = pp[half]
                for q in range(4):
                    kc = half * 4 + q
                    P.tr(pb[:, q * 128:(q + 1) * 128], z_[:, kc * 128:(kc + 1) * 128], identF, R=[z_, cst], W=[pb])
                for q in range(4):
                    kc = half * 4 + q
                    o = hT[:, kc, tc]
                    i_ = pb[:, q * 128:(q + 1) * 128]
                    sc = modB[:, 32 + kc, r:r + 1]
                    sh = modA[:, 24 + kc, r:r + 1]
                    if q % 2 == 0:
                        P.ts(o, i_, sc, sh, ALU.mult, ALU.add, R=[pb, modA, modB], W=[hT.res[t]])
                    else:
                        P.act(o, i_, AF.Identity, R=[pb, modA, modB], W=[hT.res[t]], bias=sh, scale=sc)
        P.barrier()


def phase_ffn(P, nc, l, s, last, Wd, cst, cstb, hT, modrow_d, x1s, xres, out_d, ffn_act):
    blocks = ([] if last else [(0, TCTX)]) + [(TCTX + 4 * i, 4) for i in range(4)]
    with ExitStack() as ph0:
        P.stack = ph0
        W2 = P.sb([128, 22, 1024], BF16, nres=11, name="W2")
        with ExitStack() as ph:
            P.stack = ph
            W1 = P.sb([128, 8, 2 * FFH], BF16, nres=44, name="W1")
            _loadw_cols(P, W1, Wd["w_ffn_in"][l].rearrange("(kc p) n -> p kc n", p=128), 2 * FFH, 256,
                        order=[x for j in range(11) for x in (j, 11 + j)])
            _loadw(P, W2, Wd["w_ffn_out"][l].rearrange("(j p) n -> p j n", p=128), 1024)
            pGt = [P.ps([128, 512], name="pGt%d" % i) for i in range(2)]
            pUp = [P.ps([128, 512], name="pUp%d" % i) for i in range(2)]
            sgu = [P.sb([128, 512], F32, name="sgu%d" % i) for i in range(2)]
            aT = P.sb([128, 22, 512], BF16, name="actT")
            for bi, (t0, ntile) in enumerate(blocks):
                N = ntile * 128
                cols = slice(t0 * 128, t0 * 128 + N)
                tr_ = [hT.res[t] for t in range(t0, t0 + ntile)]
                for j in range(22):
                    pg, pu, sg_ = pGt[j % 2], pUp[j % 2], sgu[j % 2]
                    for kc in range(8):
                        P.mm(pg[:, 0:N], W1[:, kc, j * 128:(j + 1) * 128], hT[:, kc, cols], start=(kc == 0), stop=(kc == 7),
                             R=_cr(W1, j * 128, (j + 1) * 128) + tr_, W=[pg])
                    for kc in range(8):
                        P.mm(pu[:, 0:N], W1[:, kc, FFH + j * 128:FFH + (j + 1) * 128], hT[:, kc, cols], start=(kc == 0), stop=(kc == 7),
                             R=_cr(W1, FFH + j * 128, FFH + (j + 1) * 128) + tr_, W=[pu])
                    P.act(sg_[:, 0:N], pg[:, 0:N], AF.Silu, R=[pg], W=[sg_])
                    P.tt(aT[:, j, 0:N], sg_[:, 0:N], pu[:, 0:N], ALU.mult, R=[sg_, pu], W=[aT])
                for ti in range(ntile):
                    t = t0 + ti
                    P.dma(ffn_act[t].rearrange("p (j k) -> p j k", j=22), aT[:, :, ti * 128:(ti + 1) * 128], R=[aT], W=[P.dres("ffn_act", t)])
            P.barrier()
        with ExitStack() as ph:
            P.stack = ph
            pF = [P.ps([128, 1024], name="pF%d" % i) for i in range(2)]
            gbc = P.sb([128, 2, 1024], F32, name="gbc2")
            _gate_bcast(P, cst, modrow_d, 5120, s, pF[0], gbc)
            lng = P.sb([128, 1024], F32, name="lng2")
            lnb = P.sb([128, 1024], F32, name="lnb2")
            P.dma(lng[:], Wd["ln2_g"][l:l + 1, :].broadcast_to([128, 1024]), W=[lng])
            P.dma(lnb[:], Wd["ln2_b"][l:l + 1, :].broadcast_to([128, 1024]), W=[lnb])
            aTl = [P.sb([128, 22, 128], BF16, name="aTl%d" % i) for i in range(2)]
            xt = [P.sb([128, 1024], F32, name="x1l%d" % i) for i in range(2)]
            z = [P.sb([128, 1024], F32, name="z2%d" % i) for i in range(2)]
            st = P.sb([128, 12], F32, name="st2")
            mv = P.sb([128, 4], F32, name="mv2")
            tiles = [t0 + ti for (t0, ntile) in blocks for ti in range(ntile)]
            for i, t in enumerate(tiles):
                ri = 1 if t < TCTX else 0
                a_, xt_, z_, pF_ = aTl[i % 2], xt[i % 2], z[i % 2], pF[i % 2]
                P.dma(a_[:], ffn_act[t].rearrange("p (j k) -> p j k", j=22), R=[P.dres("ffn_act", t)], W=[a_])
                P.dma(xt_[:], x1s[t], R=[P.dres("x1s", t)], W=[xt_])
                for j in range(22):
                    for half in range(2):
                        P.mm(pF_[:, half * 512:(half + 1) * 512], a_[:, j, :], W2[:, j, half * 512:(half + 1) * 512],
                             start=(j == 0), stop=(j == 21), R=[a_, W2], W=[pF_])
                P.tt(z_[:], pF_[:], gbc[:, ri, :], ALU.mult, R=[pF_, gbc], W=[z_])
                P.stt(z_[:], xt_[:], float(DN_ALPHA), z_[:], ALU.mult, ALU.add, R=[xt_, z_], W=[z_])
                _layernorm(P, z_[:], z_[:], z_, lng, lnb, st, mv)
                if last:
                    P.dma(out_d[s, (t - TCTX) * 128:(t - TCTX + 1) * 128, :], z_[:], R=[z_], W=[P.dres("out", (s, t))])
                else:
                    P.dma(xres[s, t], z_[:], R=[z_], W=[P.dres("xres", (s, t))])
            P.barrier()


def kernel(**inputs):
    inp = {k: np.asarray(v) for k, v in inputs.items()}
    nc = build(DEPTH)
    consts = make_consts()
    rope = make_rope()
    maps = []
    for core in range(8):
        b0 = 2 * core
        c3 = np.stack([inp["c"][b0], inp["c"][b0 + 1], inp["c_ctx"]])
        c3T = np.ascontiguousarray(c3.reshape(3, 8, 128).transpose(2, 1, 0))
        m = {"x": np.ascontiguousarray(inp["x"][b0:b0 + 2]), "ctx": np.ascontiguousarray(inp["ctx"][b0:b0 + 2]),
             "c3T": c3T, "consts": consts, "rope": rope, "hg_lb": inp["hg_lb"]}
        for n, _ in WSHAPES:
            m[n] = inp[n]
        maps.append(m)
    res = run_bass_kernel_spmd(nc, maps, core_ids=list(range(8)))
    out = np.concatenate([np.asarray(r["out"]) for r in res.results], axis=0)
    return out.astype(np.float32)
```

```python
from contextlib import ExitStack
import numpy as np
import concourse.bass as bass
import concourse.mybir as mybir
from concourse.bass_utils import run_bass_kernel_spmd

F32 = mybir.dt.float32
BF16 = mybir.dt.bfloat16
AF = mybir.ActivationFunctionType
ALU = mybir.AluOpType
AX = mybir.AxisListType

SAME_ENGINE_SYNC = True
N_DMA_SEMS = 16

DEPTH = 4
DM = 1024
NT = 18
TCTX = 2
TOK = NT * 128
FFH = 2816
IN_W = 14912
O_HQ, O_HFF, O_HFB, O_HI, O_HG = 0, 1024, 2048, 3072, 4096
O_AQ, O_AK, O_AV = 5120, 6144, 6400
O_MZ, O_MX, O_MDT, O_GT = 6656, 8704, 11776, 11840
DN_ALPHA = (2 * DEPTH) ** 0.25
LN_EPS = 1e-5
RMS_EPS = 1e-6

C_ID, C_ONES, C_LE, C_GE, C_GT, C_LT, C_HDF, C_HDB, C_HMF, C_HMB, C_SEL, C_CI = (
    0, 128, 256, 384, 512, 640, 768, 896, 1024, 1152, 1280, 1664)
NCONST = 1664 + 4


def make_consts():
    a = np.arange(128)[:, None]
    b = np.arange(128)[None, :]
    bd = (a // 32) == (b // 32)
    blocks = [a == b, np.ones((128, 128), bool), a <= b, a >= b, a > b, a < b,
              bd & (a > b), bd & (a < b), bd & (a <= b), bd & (a >= b)]
    sel = np.zeros((128, 3 * 128), bool)
    for r in range(3):
        sel[r, r * 128:(r + 1) * 128] = True
    ci = (a // 32) == np.arange(4)[None, :]
    return np.concatenate(blocks + [sel, ci], axis=1).astype(np.float32)


def make_rope():
    rows = 2048 // 64
    row, col = np.meshgrid(np.arange(rows, dtype=np.float32), np.arange(64, dtype=np.float32), indexing="ij")
    n_pairs = 32
    inv_freq = (np.float32(10000.0) ** (-np.arange(n_pairs, dtype=np.float32) / np.float32(n_pairs))).astype(np.float32)
    ang = np.concatenate([row.reshape(-1, 1) * inv_freq, col.reshape(-1, 1) * inv_freq], axis=-1).astype(np.float32)
    return np.stack([np.cos(ang), np.sin(ang)]).astype(np.float32)


class Res:
    __slots__ = ("w", "r")

    def __init__(self):
        self.w = None
        self.r = {}


class Tile:
    def __init__(self, t, nres=1):
        self.t = t
        self.res = [Res() for _ in range(nres)]

    def __getitem__(self, k):
        return self.t[k]

    @property
    def r(self):
        return self.res[0]


def _res(items):
    out = []
    for it in items:
        if isinstance(it, Tile):
            out.extend(it.res)
        elif isinstance(it, Res):
            out.append(it)
        elif it is None:
            pass
        else:
            out.extend(_res(it))
    return out


class Prog:
    def __init__(self, nc, stack):
        self.nc = nc
        self.eng = {"pe": nc.tensor, "act": nc.scalar, "dve": nc.vector, "pool": nc.gpsimd, "sp": nc.sync}
        self.sem = {}
        for e in self.eng:
            self.sem[e] = stack.enter_context(nc.semaphore("s_" + e))
        self.dma_sems = {}
        for q in ("sp", "act", "pool"):
            self.dma_sems[q] = [("d", q, i) for i in range(N_DMA_SEMS)]
            for k in self.dma_sems[q]:
                self.sem[k] = stack.enter_context(nc.semaphore("d_%s_%d" % (q, k[2])))
        self.count = {k: 0 for k in self.sem}
        self.known = {e: {} for e in self.eng}
        self.vc = {}
        self.dma_rr = {q: 0 for q in self.dma_sems}
        self.n_instr = 0
        self.n_wait = 0
        self.stack = None
        self._dres = {}
        self._uid = 0

    def sb(self, shape, dtype, nres=1, name=None):
        self._uid += 1
        t = self.stack.enter_context(self.nc.sbuf_tensor("%s_%d" % (name or "t", self._uid), list(shape), dtype))
        return Tile(t, nres)

    def ps(self, shape, dtype=F32, name=None):
        self._uid += 1
        t = self.stack.enter_context(self.nc.psum_tensor("%s_%d" % (name or "p", self._uid), list(shape), dtype))
        return Tile(t)

    def dres(self, name, idx=0):
        k = (name, idx)
        if k not in self._dres:
            self._dres[k] = Res()
        return self._dres[k]

    def _deps(self, E, reads, writes):
        deps = {}
        for r in reads:
            if r.w is not None and deps.get(r.w[0], 0) < r.w[1]:
                deps[r.w[0]] = r.w[1]
        for w in writes:
            if w.w is not None and deps.get(w.w[0], 0) < w.w[1]:
                deps[w.w[0]] = w.w[1]
            for k, v in w.r.items():
                if deps.get(k, 0) < v:
                    deps[k] = v
        kn = self.known[E]
        out = []
        for k, v in deps.items():
            if k == E and (E == "pe" or E == "sp" or not SAME_ENGINE_SYNC):
                continue
            if kn.get(k, 0) >= v:
                continue
            out.append((k, v))
        return out

    def _wait(self, E, waits):
        eng = self.eng[E]
        kn = self.known[E]
        for k, v in waits:
            eng.wait_ge(self.sem[k], v)
            self.n_wait += 1
            snap = self.vc.get((k, v))
            if snap is not None:
                for kk, vv in snap.items():
                    if kn.get(kk, 0) < vv:
                        kn[kk] = vv
            if kn.get(k, 0) < v:
                kn[k] = v

    def _finish(self, E, ev, ins, inc, reads, writes):
        ins.then_inc(self.sem[ev[0]], inc)
        snap = dict(self.known[E])
        snap[ev[0]] = ev[1]
        self.vc[ev] = snap
        k, v = ev
        for r in reads:
            if r.r.get(k, 0) < v:
                r.r[k] = v
        for w in writes:
            w.w = ev
            w.r = {}
        self.n_instr += 1

    def op(self, E, fn, R=(), W=()):
        reads = _res(R)
        writes = _res(W)
        self._wait(E, self._deps(E, reads, writes))
        ins = fn(self.eng[E])
        self.count[E] += 1
        ev = (E, self.count[E])
        self._finish(E, ev, ins, 1, reads, writes)

    def dma(self, out, in_, R=(), W=(), q="sp", **kw):
        reads = _res(R)
        writes = _res(W)
        self._wait(q, self._deps(q, reads, writes))
        key = self.dma_sems[q][self.dma_rr[q] % N_DMA_SEMS]
        self.dma_rr[q] += 1
        if self.count[key] > 0 and self.known[q].get(key, 0) < self.count[key]:
            self._wait(q, [(key, self.count[key])])
        ins = self.eng[q].dma_start(out=out, in_=in_, **kw)
        self.count[key] += 16
        ev = (key, self.count[key])
        self._finish(q, ev, ins, 16, reads, writes)

    def barrier(self):
        targets = [(k, v) for k, v in self.count.items() if v > 0]
        for E in self.eng:
            waits = [(k, v) for k, v in targets if self.known[E].get(k, 0) < v and not (k == E and E == "pe")]
            self._wait(E, waits)

    def wait_all_on(self, E="sp"):
        waits = [(k, v) for k, v in self.count.items() if v > 0 and k != E and self.known[E].get(k, 0) < v]
        self._wait(E, waits)

    def mm(self, out, lhsT, rhs, start=True, stop=True, R=(), W=()):
        self.op("pe", lambda e: e.matmul(out, lhsT, rhs, start=start, stop=stop), R, W)

    def tr(self, out, in_, ident, R=(), W=()):
        self.op("pe", lambda e: e.transpose(out, in_, ident), R, W)

    def act(self, out, in_, func, R=(), W=(), bias=0.0, scale=1.0, accum_out=None):
        if accum_out is None:
            self.op("act", lambda e: e.activation(out, in_, func, bias=bias, scale=scale), R, W)
        else:
            self.op("act", lambda e: e.activation(out, in_, func, bias=bias, scale=scale, accum_out=accum_out), R, W)

    def tt(self, out, in0, in1, op, R=(), W=(), E="dve"):
        self.op(E, lambda e: e.tensor_tensor(out, in0, in1, op), R, W)

    def ts(self, out, in0, s1, s2, op0, op1=None, R=(), W=(), E="dve"):
        if op1 is None:
            self.op(E, lambda e: e.tensor_scalar(out, in0, s1, None, op0), R, W)
        else:
            self.op(E, lambda e: e.tensor_scalar(out, in0, s1, s2, op0, op1), R, W)

    def stt(self, out, in0, scalar, in1, op0, op1, R=(), W=()):
        self.op("dve", lambda e: e.scalar_tensor_tensor(out, in0, scalar, in1, op0, op1), R, W)

    def cp(self, out, in_, R=(), W=(), E="dve"):
        if E == "act":
            self.op("act", lambda e: e.activation(out, in_, AF.Copy), R, W)
        else:
            self.op(E, lambda e: e.tensor_copy(out, in_), R, W)


WSHAPES = [
    ("w_mod", (DM, 6 * DM)), ("b_mod", (6 * DM,)), ("w_in", (DM, IN_W)), ("hg_gnorm", (128,)),
    ("at_qnorm", (128,)), ("at_knorm", (128,)), ("mb_conv_w", (5, 3072)), ("mb_conv_b", (3072,)),
    ("mb_dt_bias", (2, 32)), ("mb_a_log", (2, 32)), ("mb_d", (32,)), ("mb_norm", (2048,)),
    ("w_br_hg", (1024, DM)), ("w_br_at", (1024, DM)), ("w_br_mb", (2048, DM)), ("w_out", (DM, DM)),
    ("ln1_g", (DM,)), ("ln1_b", (DM,)), ("w_ffn_in", (DM, 2 * FFH)), ("w_ffn_out", (FFH, DM)),
    ("ln2_g", (DM,)), ("ln2_b", (DM,)),
]


def build(depth=DEPTH, dbg=None):
    dbg = dbg or {}
    dump = dbg.get("dump", set())
    phases = dbg.get("phases", {"p0", "hg", "at", "mb", "merge", "ffn"})
    seqs = dbg.get("seqs", [0, 1])
    nc = bass.Bass("TRN2", target_bir_lowering=False)

    def din(name, shape):
        return nc.dram_tensor(name, list(shape), F32, kind="ExternalInput").ap()

    x_in = din("x", [2, 2048, DM])
    ctx_in = din("ctx", [2, 256, DM])
    c3T = din("c3T", [128, 8, 3])
    consts_d = din("consts", [128, NCONST])
    rope_d = din("rope", [2, 2048, 64])
    hg_lb_d = din("hg_lb", [4, 2, 1024])
    Wd = {n: din(n, (depth,) + s) for n, s in WSHAPES}
    out_d = nc.dram_tensor("out", [2, 2048, DM], F32, kind="ExternalOutput").ap()

    def dscr(name, shape, dt):
        kind = "ExternalOutput" if name in dump else "Internal"
        return nc.dram_tensor(name, list(shape), dt, kind=kind).ap()

    xres = dscr("xres", [2, NT, 128, DM], F32)
    x1s = dscr("x1s", [NT, 128, DM], F32)
    hg_of = dscr("hg_of", [NT, 128, 1024], F32)
    ohgT = dscr("ohgT", [NT, 128, 1024], BF16)
    oatT = dscr("oatT", [128, 8, TOK], BF16)
    ombT = dscr("ombT", [NT, 128, 2048], BF16)
    mb_xs = dscr("mb_xs", [NT, 128, 2048], BF16)
    mb_yf = dscr("mb_yf", [NT, 128, 2048], F32)
    ymTd = dscr("ymT", [NT, 128, 1024], BF16)
    lbs_d = dscr("lbs", [4, 2, 1024], F32)
    modrow_d = dscr("modrow", [3, 6 * DM], F32)
    ffn_act = dscr("ffn_act", [NT, 128, 22 * 128], BF16)

    with ExitStack() as top:
        P = Prog(nc, top)
        P.stack = top
        cst = P.sb([128, NCONST], F32, name="cst")
        P.dma(cst[:], consts_d, W=[cst])
        cstb = P.sb([128, 256], BF16, name="cstb")
        P.cp(cstb[:], cst[:, 0:256], R=[cst], W=[cstb])
        identF = cst[:, C_ID:C_ID + 128]
        onesF = cst[:, C_ONES:C_ONES + 128]
        identB = cstb[:, 0:128]
        onesB = cstb[:, 128:256]
        scT = P.sb([128, 8, 3], F32, name="scT")
        P.dma(scT[:], c3T, W=[scT])
        P.act(scT[:], scT[:], AF.Silu, R=[scT], W=[scT])
        hT = P.sb([128, 8, TOK], BF16, nres=NT, name="hT")

        def xsrc(l, s, t):
            if l == 0:
                return (ctx_in[s, t * 128:(t + 1) * 128, :] if t < TCTX else x_in[s, (t - TCTX) * 128:(t - TCTX + 1) * 128, :]), None
            return xres[s, t], P.dres("xres", (s, t))

        with ExitStack() as ph:
            P.stack = ph
            e = P.sb([128, 4, 16], F32, name="lb_e")
            P.dma(e[:].rearrange("p l (d j) -> p l d j", d=2),
                  hg_lb_d.rearrange("l d (p j) -> p l d j", p=128), W=[e])
            P.act(e[:], e[:], AF.Exp, R=[e], W=[e])
            s_ = P.sb([128, 16], F32, name="lb_s")
            P.tt(s_[:], e[:, 0, :], e[:, 1, :], ALU.add, R=[e], W=[s_])
            P.tt(s_[:], s_[:], e[:, 2, :], ALU.add, R=[e, s_], W=[s_])
            P.tt(s_[:], s_[:], e[:, 3, :], ALU.add, R=[e, s_], W=[s_])
            P.op("dve", lambda en: en.reciprocal(s_[:], s_[:]), R=[s_], W=[s_])
            P.tt(e[:], e[:], s_[:].rearrange("p (o j) -> p o j", o=1).broadcast_to([128, 4, 16]), ALU.mult, R=[e, s_], W=[e])
            lbt = P.sb([128, 4, 16], F32, name="lb_t")
            P.op("dve", lambda en: en.memset(lbt[:, 0, :], 0.0), W=[lbt])
            P.cp(lbt[:, 1, :], e[:, 1, :], R=[e], W=[lbt])
            P.tt(lbt[:, 2, :], lbt[:, 1, :], e[:, 2, :], ALU.add, R=[e, lbt], W=[lbt])
            P.tt(lbt[:, 3, :], lbt[:, 2, :], e[:, 3, :], ALU.add, R=[e, lbt], W=[lbt])
            P.dma(lbs_d.rearrange("l d (p j) -> p l d j", p=128),
                  lbt[:].rearrange("p l (d j) -> p l d j", d=2), R=[lbt], W=[P.dres("lbs")])
            P.barrier()
        P.stack = top

        for l in range(depth):
            last = (l == depth - 1) and not dbg.get('nolast', False)
            with ExitStack() as lay:
                P.stack = lay
                modA = P.sb([128, 48, 3], F32, name="modA")
                modB = P.sb([128, 48, 3], F32, name="modB")
                with ExitStack() as ph:
                    P.stack = ph
                    mrow = P.sb([3, 6 * DM], F32, name="mrow")
                    brow = P.sb([3, 6 * DM], F32, name="brow")
                    P.dma(brow[:], Wd["b_mod"][l:l + 1, :].broadcast_to([3, 6 * DM]), W=[brow])
                    wm = [P.sb([128, 8, 1536], F32, name="wm%d" % i) for i in range(2)]
                    pm = [P.ps([128, 512], name="pm%d" % i) for i in range(2)]
                    for blk in range(4):
                        w_ = wm[blk % 2]
                        P.dma(w_[:], Wd["w_mod"][l].rearrange("(kc p) n -> p kc n", p=128)[:, :, blk * 1536:(blk + 1) * 1536], W=[w_])
                        for j in range(3):
                            pj = pm[j % 2]
                            for kc in range(8):
                                P.mm(pj[0:3, :], scT[:, kc, :], w_[:, kc, j * 512:(j + 1) * 512],
                                     start=(kc == 0), stop=(kc == 7), R=[scT, w_], W=[pj])
                            c0 = blk * 1536 + j * 512
                            P.tt(mrow[:, c0:c0 + 512], pj[0:3, :], brow[:, c0:c0 + 512], ALU.add, R=[pj, brow], W=[mrow])
                    P.dma(modrow_d, mrow[:], R=[mrow], W=[P.dres("modrow")])
                    pT = P.ps([128, 512], name="pmT")
                    for j in range(48):
                        P.tr(pT[:, j * 3:(j + 1) * 3], mrow[:, j * 128:(j + 1) * 128], identF[0:3, 0:3], R=[mrow, cst], W=[pT])
                    P.cp(modA[:].rearrange("p j r -> p (j r)"), pT[:, 0:144], R=[pT], W=[modA])
                    P.ts(modB[:].rearrange("p j r -> p (j r)"), pT[:, 0:144], 1.0, None, ALU.add, R=[pT], W=[modB])
                    P.barrier()
                P.stack = lay

                for s in seqs:
                    with ExitStack() as sq:
                        P.stack = sq
                        def featT(l_, s_, tiles, src_fn, sc_chunk, sh_chunk, ph):
                            xt = [P.sb([128, DM], F32, name="p0x%d" % i) for i in range(2)]
                            pp = [P.ps([128, 512], name="p0p%d" % i) for i in range(2)]
                            for i, t in enumerate(tiles):
                                r = 2 if t < TCTX else s_
                                xb = xt[i % 2]
                                src, sres = src_fn(t)
                                P.dma(xb[:], src, R=[sres], W=[xb])
                                for half in range(2):
                                    pb = pp[half]
                                    for q in range(4):
                                        kc = half * 4 + q
                                        P.tr(pb[:, q * 128:(q + 1) * 128], xb[:, kc * 128:(kc + 1) * 128], identF, R=[xb, cst], W=[pb])
                                    for q in range(4):
                                        kc = half * 4 + q
                                        o = hT[:, kc, t * 128:(t + 1) * 128]
                                        i_ = pb[:, q * 128:(q + 1) * 128]
                                        sc = modB[:, sc_chunk + kc, r:r + 1]
                                        sh = modA[:, sh_chunk + kc, r:r + 1]
                                        if q % 2 == 0:
                                            P.ts(o, i_, sc, sh, ALU.mult, ALU.add, R=[pb, modA, modB], W=[hT.res[t]])
                                        else:
                                            P.act(o, i_, AF.Identity, R=[pb, modA, modB], W=[hT.res[t]], bias=sh, scale=sc)

                        if "p0" in phases:
                            with ExitStack() as ph:
                                P.stack = ph
                                featT(l, s, list(range(NT)), lambda t: xsrc(l, s, t), 8, 0, ph)
                                P.barrier()
                            P.stack = sq
                        if "hg" in phases:
                            with ExitStack() as ph:
                                P.stack = ph
                                phase_hg(P, nc, l, s, Wd, cst, cstb, hT, lbs_d, hg_of, ohgT)
                                P.barrier()
                            P.stack = sq
                        if "at" in phases:
                            with ExitStack() as ph:
                                P.stack = ph
                                phase_at(P, nc, l, s, Wd, cst, cstb, hT, rope_d, oatT)
                                P.barrier()
                            P.stack = sq
                        if "mb" in phases:
                            phase_mb(P, nc, l, s, Wd, cst, cstb, hT, mb_xs, mb_yf, ombT, sq)
                            P.stack = sq
                        if "merge" in phases:
                            phase_merge(P, nc, l, s, last, Wd, cst, cstb, hT, modA, modB, modrow_d, ohgT, oatT, ombT,
                                        ymTd, x1s, lambda t: xsrc(l, s, t), sq)
                            P.stack = sq
                        if "ffn" in phases:
                            phase_ffn(P, nc, l, s, last, Wd, cst, cstb, hT, modrow_d, x1s, xres, out_d, ffn_act)
                            P.stack = sq
                    P.stack = lay
            P.stack = top
        P.wait_all_on("sp")
        build.stats = (P.n_instr, P.n_wait)
    return nc


def _loadw(P, dst, src3, n, q="pool"):
    KC = src3.shape[1]
    step = 2
    for i, kc in enumerate(range(0, KC, step)):
        P.dma(dst[:, kc:kc + step, 0:n], src3[:, kc:kc + step, :], W=[dst.res[i % len(dst.res)]], q=q)


def _loadw_cols(P, dst, src3, n, cb, order=None, q="pool"):
    npc = (n + cb - 1) // cb
    KC = src3.shape[1]
    assert len(dst.res) >= 2 * npc
    dst.cb = cb
    for i in (order or range(npc)):
        c0, c1 = i * cb, min(n, (i + 1) * cb)
        for hf in range(2):
            k0, k1 = hf * KC // 2, (hf + 1) * KC // 2
            P.dma(dst[:, k0:k1, c0:c1], src3[:, k0:k1, c0:c1], W=[dst.res[2 * i + hf]], q=q)


def _cr(W, c0, c1):
    return [W.res[j] for i in range(c0 // W.cb, (c1 - 1) // W.cb + 1) for j in (2 * i, 2 * i + 1)]


def _interleave(gens):
    gens = [g for g in gens if g is not None]
    while gens:
        for g in list(gens):
            try:
                next(g)
            except StopIteration:
                gens.remove(g)


def phase_hg(P, nc, l, s, Wd, cst, cstb, hT, lbs_d, hg_of, ohgT):
    identB = cstb[:, 0:128]
    onesF = cst[:, C_ONES:C_ONES + 128]
    CI = cst[:, C_CI:C_CI + 4]
    Wl = Wd["w_in"][l].rearrange("(kc p) n -> p kc n", p=128)
    Wq = P.sb([128, 8, 1024], BF16, nres=4, name="Wq")
    Wf = P.sb([128, 8, 1024], BF16, nres=4, name="Wf")
    Wi = P.sb([128, 8, 1024], BF16, nres=4, name="Wi")
    Wg = P.sb([128, 8, 1024], BF16, nres=4, name="Wg")
    _loadw(P, Wf, Wl[:, :, O_HFF:O_HFF + 1024], 1024)
    _loadw(P, Wq, Wl[:, :, O_HQ:O_HQ + 1024], 1024)
    _loadw(P, Wi, Wl[:, :, O_HI:O_HI + 1024], 1024)
    lbb = P.sb([128, 1024], F32, name="lbb")
    oml = P.sb([128, 1024], F32, name="oml")

    def load_lb(d):
        P.dma(lbb[:], lbs_d[l, d:d + 1, :].broadcast_to([128, 1024]), R=[P.dres("lbs")], W=[lbb])
        P.ts(oml[:], lbb[:], -1.0, 1.0, ALU.mult, ALU.add, R=[lbb], W=[oml])

    load_lb(0)
    gn = P.sb([128, 1], F32, name="gn")
    P.dma(gn[:], Wd["hg_gnorm"][l].rearrange("(p o) -> p o", o=1), W=[gn])
    lf = P.sb([128, 1024], F32, name="lf")
    kk = P.sb([128, 1024], F32, name="kk")
    ex = P.sb([128, 1024], F32, name="ex")
    exn = P.sb([128, 1024], F32, name="exn")
    qs = P.sb([128, 1024], F32, name="qs")
    Kt = [P.sb([128, 1024], BF16, name="Kt%d" % i) for i in range(2)]
    Qt = P.sb([128, 1024], BF16, name="Qt")
    Vt = [P.sb([128, 1024], BF16, name="Vt%d" % i) for i in range(2)]
    Ktm = [P.sb([128, 1024], BF16, name="Ktm%d" % i) for i in range(2)]
    KT = [P.sb([128, 1024], BF16, name="KT%d" % i) for i in range(2)]
    QT = [P.sb([128, 1024], BF16, name="QT%d" % i) for i in range(2)]
    ec = [P.sb([128, 32], F32, name="ec%d" % i) for i in range(2)]
    sg = [P.sb([128, 1024], F32, name="sg%d" % i) for i in range(2)]
    S = P.sb([128, 8, 128], F32, nres=8, name="S")
    Sb = P.sb([128, 8, 128], BF16, nres=8, name="Sb")
    ATm = P.sb([128, 8, 128], BF16, nres=8, name="ATm")
    ost = P.sb([128, 1024], F32, name="ost")
    ofl = P.sb([128, 1024], F32, name="ofl")
    osq = P.sb([128, 1024], F32, name="osq")
    rs = ofl
    ob = P.sb([128, 1024], BF16, name="ob")
    pA = P.ps([128, 1024], name="pA")
    pT = P.ps([128, 1024], BF16, name="pT")
    pC = P.ps([128, 512], name="pC")
    pCb = pC[:].bitcast(BF16)
    pO = P.ps([128, 1024], name="pO")
    pSl = P.ps([128, 1024], name="pSl")
    _r0, _r1 = Res(), Res()
    pSl.res = [_r0] * 4 + [_r1] * 4

    def proj(Wt, t):
        for half in range(2):
            for kc in range(8):
                P.mm(pA[:, half * 512:(half + 1) * 512], hT[:, kc, t * 128:(t + 1) * 128],
                     Wt[:, kc, half * 512:(half + 1) * 512], start=(kc == 0), stop=(kc == 7),
                     R=[hT.res[t], Wt], W=[pA])

    def prologue(t, d, b):
        D = cst[:, (C_HDF if d == 0 else C_HDB):(C_HDF if d == 0 else C_HDB) + 128]
        proj(Wf, t)
        P.act(lf[:], pA[:], AF.Sigmoid, R=[pA], W=[lf])
        P.act(kk[:], pA[:], AF.Sigmoid, R=[pA], W=[kk], scale=-1.0)
        yield
        proj(Wq, t)
        P.tt(lf[:], lf[:], oml[:], ALU.mult, R=[lf, oml], W=[lf])
        P.tt(lf[:], lf[:], lbb[:], ALU.add, R=[lf, lbb], W=[lf])
        P.act(lf[:], lf[:], AF.Ln, R=[lf], W=[lf])
        P.act(qs[:], pA[:], AF.Silu, R=[pA], W=[qs])
        P.tt(kk[:], kk[:], oml[:], ALU.mult, R=[kk, oml], W=[kk])
        yield
        proj(Wi, t)
        P.cp(Vt[b][:], pA[:], R=[pA], W=[Vt[b]], E="act")
        yield
        for half in range(2):
            P.mm(pA[:, half * 512:(half + 1) * 512], D, lf[:, half * 512:(half + 1) * 512], R=[cst, lf], W=[pA])
        for h in range(8):
            P.mm(pC[:, h * 4:(h + 1) * 4], lf[:, h * 128:(h + 1) * 128], CI, R=[lf, cst], W=[pC])
        P.act(ex[:], pA[:], AF.Exp, R=[pA], W=[ex])
        P.act(exn[:], pA[:], AF.Exp, R=[pA], W=[exn], scale=-1.0)
        P.act(ec[b][:], pC[:, 0:32], AF.Exp, R=[pC], W=[ec[b]])
        yield
        P.stt(Qt[:], qs[:], float(128 ** -0.5), exn[:], ALU.mult, ALU.mult, R=[qs, exn], W=[Qt])
        P.tt(Kt[b][:], kk[:], ex[:], ALU.mult, R=[kk, ex], W=[Kt[b]])
        P.ts(Ktm[b][:], Kt[b][:], CI[:, 3:4], None, ALU.mult, R=[Kt[b], cst], W=[Ktm[b]])
        yield
        for h in range(8):
            P.tr(pT[:, h * 128:(h + 1) * 128], Kt[b][:, h * 128:(h + 1) * 128], identB, R=[Kt[b], cstb], W=[pT])
        P.cp(KT[b][:], pT[:], R=[pT], W=[KT[b]])
        for h in range(8):
            P.tr(pCb[:, h * 128:(h + 1) * 128], Qt[:, h * 128:(h + 1) * 128], identB, R=[Qt, cstb], W=[pC])
        P.cp(QT[b][:], pCb, R=[pC], W=[QT[b]], E="act")
        yield
        if d == 1:
            for h in range(8):
                for kc in range(8):
                    P.mm(pA[:, h * 128:(h + 1) * 128], Wg[:, kc, h * 128:(h + 1) * 128], hT[:, kc, t * 128:(t + 1) * 128],
                         start=(kc == 0), stop=(kc == 7), R=[Wg, hT.res[t]], W=[pA])
            P.act(sg[b][:], pA[:], AF.Silu, R=[pA], W=[sg[b]])
            yield

    def scan(t, d, b):
        M = cst[:, (C_HMF if d == 0 else C_HMB):(C_HMF if d == 0 else C_HMB) + 128]
        if d == 1:
            P.dma(ofl[:], hg_of[t], R=[P.dres("hg_of", t)], W=[ofl])
        for h in range(8):
            hs = slice(h * 128, (h + 1) * 128)
            P.mm(pSl[:, hs], KT[b][:, hs], QT[b][:, hs], R=[KT[b], QT[b]], W=[pSl.res[h]])
        for h in range(8):
            hs = slice(h * 128, (h + 1) * 128)
            P.tt(ATm[:, h, :], pSl[:, hs], M, ALU.mult, R=[pSl.res[h], cst], W=[ATm.res[h]])
        yield
        for h in range(8):
            hs = slice(h * 128, (h + 1) * 128)
            P.op("pe", lambda e, hs=hs, h=h: e.matmul(pO[:, hs], Vt[b][:, hs], ATm[:, h, :], start=(h % 4 == 0), stop=False,
                                                       skip_group_check=True), R=[Vt[b], ATm.res[h]], W=[pO])
        chunks = [0, 1, 2, 3] if d == 0 else [3, 2, 1, 0]
        for ci, c in enumerate(chunks):
            for h in range(8):
                e_c = ec[b][:, h * 4 + c:h * 4 + c + 1]
                P.op("act", lambda e, h=h, e_c=e_c: e.activation(Sb[:, h, :], S[:, h, :], AF.Identity, scale=e_c),
                     R=[S.res[h], ec[b]], W=[Sb.res[h]])
            for h in range(8):
                hs = slice(h * 128, (h + 1) * 128)
                if c == 3:
                    P.mm(pSl[:, hs], Ktm[b][:, hs], Vt[b][:, hs], R=[Ktm[b], Vt[b]], W=[pSl.res[h]])
                else:
                    P.mm(pSl[:, hs], Kt[b][c * 32:(c + 1) * 32, hs], Vt[b][c * 32:(c + 1) * 32, hs], R=[Kt[b], Vt[b]], W=[pSl.res[h]])
            for h in range(8):
                cs = slice(h * 128 + c * 32, h * 128 + (c + 1) * 32)
                P.op("pe", lambda e, cs=cs, h=h: e.matmul(pO[:, cs], Sb[:, h, :], QT[b][:, cs], start=False, stop=(ci == 3),
                                                           skip_group_check=True), R=[Sb.res[h], QT[b]], W=[pO])
            for h in range(8):
                hs = slice(h * 128, (h + 1) * 128)
                e_c = ec[b][:, h * 4 + c:h * 4 + c + 1]
                P.stt(S[:, h, :], S[:, h, :], e_c, pSl[:, hs], ALU.mult, ALU.add, R=[S.res[h], ec[b], pSl.res[h]], W=[S.res[h]])
            yield
        if d == 0:
            P.cp(ost[:], pO[:], R=[pO], W=[ost], E="act")
            P.dma(hg_of[t], ost[:], R=[ost], W=[P.dres("hg_of", t)])
        else:
            P.tt(ost[:], pO[:], ofl[:], ALU.add, R=[pO, ofl], W=[ost])
            P.act(osq[:], ost[:], AF.Square, R=[ost], W=[osq])
            for half in range(2):
                P.mm(pSl[:, half * 512:(half + 1) * 512], onesF, osq[:, half * 512:(half + 1) * 512], R=[cst, osq], W=[pSl])
            yield
            P.act(rs[:], pSl[:], AF.Sqrt, R=[pSl], W=[rs], scale=1.0 / 128.0, bias=RMS_EPS)
            P.op("dve", lambda e: e.reciprocal(rs[:], rs[:]), R=[rs], W=[rs])
            P.stt(osq[:], ost[:], gn[:, 0:1], rs[:], ALU.mult, ALU.mult, R=[ost, gn, rs], W=[osq])
            P.tt(ob[:], osq[:], sg[b][:], ALU.mult, R=[osq, sg[b]], W=[ob])
            P.dma(ohgT[t], ob[:], R=[ob], W=[P.dres("ohgT", t)])
        yield

    order = [(t, 0) for t in range(NT)] + [(t, 1) for t in list(range(TCTX - 1, -1, -1)) + list(range(NT - 1, TCTX - 1, -1))]
    P.op("dve", lambda e: e.memset(S[:], 0.0), W=[S])
    _interleave([prologue(order[0][0], order[0][1], 0)])
    for n, (t, d) in enumerate(order):
        nxt = None
        if n + 1 < len(order):
            t2, d2 = order[n + 1]
            if d2 == 1 and d == 0:
                _loadw(P, Wf, Wl[:, :, O_HFB:O_HFB + 1024], 1024)
                _loadw(P, Wg, Wl[:, :, O_HG:O_HG + 1024], 1024)
                load_lb(1)
            nxt = prologue(t2, d2, (n + 1) % 2)
        _interleave([scan(t, d, n % 2), nxt])
        if n + 1 < len(order) and order[n + 1][1] == 1 and d == 0:
            P.op("dve", lambda e: e.memset(S[:], 0.0), W=[S])


def _pipeline(items, gen_fn):
    active = []

    def rnd():
        for g in list(active):
            try:
                next(g)
            except StopIteration:
                active.remove(g)

    for it in items:
        active.append(gen_fn(it))
        rnd()
    while active:
        rnd()


def phase_at(P, nc, l, s, Wd, cst, cstb, hT, rope_d, oatT):
    identB = cstb[:, 0:128]
    onesB = cstb[:, 128:256]
    Wl = Wd["w_in"][l].rearrange("(kc p) n -> p kc n", p=128)
    Wa = P.sb([128, 8, 1536], BF16, nres=4, name="Wa")
    _loadw(P, Wa, Wl[:, :, O_AQ:O_AQ + 1536], 1536)
    QT = P.sb([128, 8, TOK], BF16, nres=NT, name="QTa")
    KTa = P.sb([128, 2, TOK], BF16, nres=NT, name="KTa")
    Va = P.sb([128, NT, 256], BF16, nres=NT, name="Va")
    with ExitStack() as sub:
        P.stack = sub
        cosT = P.sb([128, 16, 64], F32, name="cosT")
        sinT = P.sb([128, 16, 64], F32, name="sinT")
        P.dma(cosT[:], rope_d[0].rearrange("(t p) j -> p t j", p=128), W=[cosT])
        P.dma(sinT[:], rope_d[1].rearrange("(t p) j -> p t j", p=128), W=[sinT])
        wqk = P.sb([128, 10, 128], F32, name="wqk")
        P.dma(wqk[:, 0:8, :], Wd["at_qnorm"][l:l + 1, :].rearrange("o (h d) -> o h d", h=1).broadcast_to([128, 8, 128]), W=[wqk])
        P.dma(wqk[:, 8:10, :], Wd["at_knorm"][l:l + 1, :].rearrange("o (h d) -> o h d", h=1).broadcast_to([128, 2, 128]), W=[wqk])
        NB = 2
        sqt = [P.sb([128, 1280], F32, name="sqt%d" % i) for i in range(NB)]
        xn = [P.sb([128, 1280], F32, name="xn%d" % i) for i in range(NB)]
        ss = [P.sb([128, 10], F32, name="ss%d" % i) for i in range(NB)]
        t1 = [P.sb([128, 10, 64], F32, name="t1%d" % i) for i in range(NB)]
        t2 = [P.sb([128, 10, 64], F32, name="t2%d" % i) for i in range(NB)]
        xr = [P.sb([128, 1280], BF16, name="xr%d" % i) for i in range(NB)]
        pQ = [P.ps([128, 1536], name="pQ%d" % i) for i in range(NB)]
        pT = P.ps([128, 2048], BF16, name="pTa")

        def prep(t):
            b = t % NB
            tc = slice(t * 128, (t + 1) * 128)
            pQ_, sqt_, xn_, ss_, t1_, t2_, xr_ = pQ[b], sqt[b], xn[b], ss[b], t1[b], t2[b], xr[b]
            for j in range(3):
                for kc in range(8):
                    P.mm(pQ_[:, j * 512:(j + 1) * 512], hT[:, kc, tc], Wa[:, kc, j * 512:(j + 1) * 512],
                         start=(kc == 0), stop=(kc == 7), R=[hT.res[t], Wa], W=[pQ_])
            P.cp(Va[:, t, :], pQ_[:, 1280:1536], R=[pQ_], W=[Va.res[t]], E="act")
            P.act(sqt_[:], pQ_[:, 0:1280], AF.Square, R=[pQ_], W=[sqt_])
            P.op("dve", lambda e: e.tensor_reduce(ss_[:], sqt_[:].rearrange("p (h d) -> p h d", h=10), AX.X, ALU.add), R=[sqt_], W=[ss_])
            P.act(ss_[:], ss_[:], AF.Sqrt, R=[ss_], W=[ss_], scale=1.0 / 128.0, bias=RMS_EPS)
            P.op("dve", lambda e: e.reciprocal(ss_[:], ss_[:]), R=[ss_], W=[ss_])
            yield
            P.tt(xn_[:].rearrange("p (h d) -> p h d", h=10), pQ_[:, 0:1280].rearrange("p (h d) -> p h d", h=10),
                 ss_[:].rearrange("p (h o) -> p h o", o=1).broadcast_to([128, 10, 128]), ALU.mult, R=[pQ_, ss_], W=[xn_])
            P.tt(xn_[:], xn_[:], wqk[:].rearrange("p h d -> p (h d)"), ALU.mult, R=[xn_, wqk], W=[xn_])
            if t >= TCTX:
                tl = t - TCTX
                xv = xn_[:].rearrange("p (h j two) -> p h j two", h=10, two=2)
                xo = xr_[:].rearrange("p (h j two) -> p h j two", h=10, two=2)
                cb = cosT[:, tl:tl + 1, :].broadcast_to([128, 10, 64])
                sb_ = sinT[:, tl:tl + 1, :].broadcast_to([128, 10, 64])
                P.tt(t1_[:], xv[:, :, :, 0], cb, ALU.mult, R=[xn_, cosT], W=[t1_])
                P.tt(t2_[:], xv[:, :, :, 1], sb_, ALU.mult, R=[xn_, sinT], W=[t2_])
                P.tt(xo[:, :, :, 0], t1_[:], t2_[:], ALU.subtract, R=[t1_, t2_], W=[xr_])
                P.tt(t1_[:], xv[:, :, :, 0], sb_, ALU.mult, R=[xn_, sinT], W=[t1_])
                P.tt(t2_[:], xv[:, :, :, 1], cb, ALU.mult, R=[xn_, cosT], W=[t2_])
                P.tt(xo[:, :, :, 1], t1_[:], t2_[:], ALU.add, R=[t1_, t2_], W=[xr_])
            else:
                P.cp(xr_[:], xn_[:], R=[xn_], W=[xr_], E="act")
            yield
            for j in range(10):
                P.tr(pT[:, j * 128:(j + 1) * 128], xr_[:, j * 128:(j + 1) * 128], identB, R=[xr_, cstb], W=[pT])
            P.cp(QT[:, :, tc], pT[:, 0:1024].rearrange("p (h k) -> p h k", h=8), R=[pT], W=[QT.res[t]])
            P.cp(KTa[:, :, tc], pT[:, 1024:1280].rearrange("p (h k) -> p h k", h=2), R=[pT], W=[KTa.res[t]], E="act")
            yield

        _pipeline(list(range(NT)), prep)
        P.barrier()
    with ExitStack() as sub:
        P.stack = sub
        NS = 3
        pS = [P.ps([128, 512], name="pSa%d" % i) for i in range(NS)]
        pO = [P.ps([128, 512], name="pOa%d" % i) for i in range(2)]
        pL = [P.ps([128, 512], name="pLa%d" % i) for i in range(2)]
        Pt = [P.sb([128, 512], BF16, name="Pt%d" % i) for i in range(NS)]
        rl = [P.sb([128, 512], F32, name="rl%d" % i) for i in range(2)]
        oa = [P.sb([128, 512], BF16, name="oa%d" % i) for i in range(2)]
        blocks = [(0, 256, [0, 1])] + [(256 + qb * 512, 512, list(range(NT))) for qb in range(4)]
        sc = float(128 ** -0.5)
        steps = []
        for h in range(8):
            for bi, (q0, nq, kts) in enumerate(blocks):
                for i, kt in enumerate(kts):
                    steps.append((h, bi, q0, nq, kt, i == 0, i == len(kts) - 1))
        jobn = {}
        for (h, bi, *_r) in steps:
            jobn.setdefault((h, bi), len(jobn))

        def s_mm(n):
            h, bi, q0, nq, kt, first, lastk = steps[n]
            kv = h // 4
            qres = [QT.res[t] for t in range(q0 // 128, (q0 + nq) // 128)]
            P.mm(pS[n % NS][:, 0:nq], KTa[:, kv, kt * 128:(kt + 1) * 128], QT[:, h, q0:q0 + nq], R=[KTa.res[kt]] + qres, W=[pS[n % NS]])

        s_mm(0)
        for n in range(len(steps)):
            h, bi, q0, nq, kt, first, lastk = steps[n]
            kv = h // 4
            jb = jobn[(h, bi)] % 2
            if n + 1 < len(steps):
                s_mm(n + 1)
            ps, pt = pS[n % NS], Pt[n % NS]
            P.act(pt[:, 0:nq], ps[:, 0:nq], AF.Exp, R=[ps], W=[pt], scale=sc)
            P.mm(pO[jb][:, 0:nq], Va[:, kt, kv * 128:(kv + 1) * 128], pt[:, 0:nq], start=first, stop=lastk, R=[Va.res[kt], pt], W=[pO[jb]])
            P.mm(pL[jb][:, 0:nq], onesB, pt[:, 0:nq], start=first, stop=lastk, R=[cstb, pt], W=[pL[jb]])
            if lastk:
                P.op("dve", lambda e: e.reciprocal(rl[jb][:, 0:nq], pL[jb][:, 0:nq]), R=[pL[jb]], W=[rl[jb]])
                P.tt(oa[jb][:, 0:nq], pO[jb][:, 0:nq], rl[jb][:, 0:nq], ALU.mult, R=[pO[jb], rl[jb]], W=[oa[jb]])
                P.dma(oatT[:, h, q0:q0 + nq], oa[jb][:, 0:nq], R=[oa[jb]], W=[P.dres("oatT", (h, q0))])
        P.barrier()


def phase_mb(P, nc, l, s, Wd, cst, cstb, hT, mb_xs, mb_yf, ombT, sq):
    identB = cstb[:, 0:128]
    identF = cst[:, C_ID:C_ID + 128]
    onesF = cst[:, C_ONES:C_ONES + 128]
    Wl = Wd["w_in"][l].rearrange("(kc p) n -> p kc n", p=128)
    with ExitStack() as ph:
        P.stack = ph
        BT = P.sb([128, 4, TOK], BF16, name="BT")
        CT = P.sb([128, 4, TOK], BF16, name="CT")
        with ExitStack() as sub:
            P.stack = sub
            Wx = P.sb([128, 8, 3072], BF16, nres=24, name="Wx")
            _loadw_cols(P, Wx, Wl[:, :, O_MX:O_MX + 3072], 3072, 256)
            cwr = P.sb([120, 128], F32, name="cwr")
            P.dma(cwr[:], Wd["mb_conv_w"][l].rearrange("j (cc p) -> (j cc) p", p=128), W=[cwr])
            cbr = P.sb([24, 128], F32, name="cbr")
            P.dma(cbr[:], Wd["mb_conv_b"][l].rearrange("(cc p) -> cc p", p=128), W=[cbr])
            cw = P.sb([128, 120], F32, name="cw")
            cbias = P.sb([128, 24], F32, name="cbias")
            pX = [P.ps([128, 512], name="pX%d" % i) for i in range(2)]
            pTt = P.ps([128, 1024], BF16, name="pTt")
            pW_ = P.ps([128, 512], name="pW_")
            P.tr(pW_[:, 0:120], cwr[:], identF[0:120, 0:120], R=[cwr, cst], W=[pW_])
            P.cp(cw[:], pW_[:, 0:120], R=[pW_], W=[cw])
            P.tr(pW_[:, 128:152], cbr[:], identF[0:24, 0:24], R=[cbr, cst], W=[pW_])
            P.cp(cbias[:], pW_[:, 128:152], R=[pW_], W=[cbias])
            xb = [P.sb([128, 2052], BF16, name="xb%d" % i) for i in range(2)]
            dgs = [P.sb([128, 5, 128], BF16, name="dg%d" % i) for i in range(2)]
            pCv = [P.ps([128, 512], name="pCv%d" % i) for i in range(2)]
            ub = P.sb([128, 2048], BF16, name="ub")
            stg = [P.sb([128, 8, 128], BF16, name="stg%d" % i) for i in range(2)]
            for xb_ in xb:
                P.op("pool", lambda e, xb_=xb_: e.memset(xb_[:], 0.0), W=[xb_])
            k = 0
            kx = 0
            cw3 = cw[:].rearrange("p (j c) -> p j c", j=5)
            for cc in range(24):
                dg = dgs[cc % 2]
                P.tt(dg[:], identB.rearrange("p (o c) -> p o c", o=1).broadcast_to([128, 5, 128]),
                     cw3[:, :, cc:cc + 1].broadcast_to([128, 5, 128]), ALU.mult, R=[cstb, cw], W=[dg])
                for (t0, ntile) in ((0, TCTX), (TCTX, NT - TCTX)):
                    N = ntile * 128
                    xb_ = xb[kx % 2]
                    kx += 1
                    nblk = (N + 511) // 512
                    for b in range(nblk):
                        nb = min(512, N - b * 512)
                        c0 = t0 * 128 + b * 512
                        tr_ = [hT.res[t] for t in range(c0 // 128, (c0 + nb) // 128)]
                        for kc in range(8):
                            P.mm(pX[b % 2][:, 0:nb], Wx[:, kc, cc * 128:(cc + 1) * 128], hT[:, kc, c0:c0 + nb],
                                 start=(kc == 0), stop=(kc == 7), R=_cr(Wx, cc * 128, (cc + 1) * 128) + tr_, W=[pX[b % 2]])
                        P.cp(xb_[:, 2 + b * 512:2 + b * 512 + nb], pX[b % 2][:, 0:nb], R=[pX[b % 2]], W=[xb_], E=("act" if b % 2 else "dve"))
                    P.op("pool", lambda e, xb_=xb_, N=N: e.memset(xb_[:, N + 2:N + 4], 0.0), W=[xb_])
                    if cc < 16:
                        dst, dres_ = ub, ub
                        off = 0
                    elif cc < 20:
                        dst, dres_ = BT[:, cc - 16, :], BT
                        off = t0 * 128
                    else:
                        dst, dres_ = CT[:, cc - 20, :], CT
                        off = t0 * 128
                    for b in range(nblk):
                        nb = min(512, N - b * 512)
                        for j in range(5):
                            P.mm(pCv[b % 2][:, 0:nb], dg[:, j, :], xb_[:, b * 512 + j:b * 512 + j + nb], start=(j == 0), stop=(j == 4),
                                 R=[dg, xb_], W=[pCv[b % 2]])
                        P.act(dst[:, off + b * 512:off + b * 512 + nb], pCv[b % 2][:, 0:nb], AF.Silu, R=[pCv[b % 2], cbias], W=[dres_],
                              bias=cbias[:, cc:cc + 1])
                    if cc < 16:
                        for g0 in range(0, ntile, 8):
                            ng = min(8, ntile - g0)
                            for j in range(ng):
                                P.tr(pTt[:, j * 128:(j + 1) * 128], ub[:, (g0 + j) * 128:(g0 + j + 1) * 128], identB, R=[ub, cstb], W=[pTt])
                            st = stg[k % 2]
                            k += 1
                            P.cp(st[:, 0:ng, :], pTt[:, 0:ng * 128].rearrange("p (j c) -> p j c", j=ng), R=[pTt], W=[st],
                                 E=("act" if k % 2 else "dve"))
                            ta = t0 + g0
                            P.dma(mb_xs[ta:ta + ng, :, cc * 128:(cc + 1) * 128].rearrange("t p c -> p t c"), st[:, 0:ng, :],
                                  R=[st], W=[P.dres("mb_xs", t) for t in range(ta, ta + ng)])
            P.barrier()
        with ExitStack() as sub:
            P.stack = sub
            Wz = P.sb([128, 8, 2112], BF16, nres=8, name="Wz")
            for i, kc in enumerate(range(0, 8, 2)):
                P.dma(Wz[:, kc:kc + 2, 2048:2112], Wl[:, kc:kc + 2, O_MDT:O_MDT + 64], W=[Wz.res[4 + i]], q="pool")
            for i, kc in enumerate(range(0, 8, 2)):
                P.dma(Wz[:, kc:kc + 2, 0:2048], Wl[:, kc:kc + 2, O_MZ:O_MZ + 2048], W=[Wz.res[i]], q="pool")
            dtb = P.sb([128, 64], F32, name="dtb")
            abc = P.sb([128, 64], F32, name="abc")
            dbc = P.sb([128, 32], F32, name="dbc")
            nrm = P.sb([128, 2048], F32, name="nrm")
            P.dma(dtb[:], Wd["mb_dt_bias"][l:l + 1].rearrange("o d h -> o (d h)").broadcast_to([128, 64]), W=[dtb])
            P.dma(abc[:], Wd["mb_a_log"][l:l + 1].rearrange("o d h -> o (d h)").broadcast_to([128, 64]), W=[abc])
            P.act(abc[:], abc[:], AF.Exp, R=[abc], W=[abc])
            P.ts(abc[:], abc[:], -1.0, None, ALU.mult, R=[abc], W=[abc])
            P.dma(dbc[:], Wd["mb_d"][l:l + 1, :].broadcast_to([128, 32]), W=[dbc])
            P.dma(nrm[:], Wd["mb_norm"][l:l + 1, :].broadcast_to([128, 2048]), W=[nrm])
            xs = [P.sb([128, 2048], BF16, name="xs%d" % i) for i in range(2)]
            xdt = [P.sb([128, 2048], BF16, name="xdt%d" % i) for i in range(2)]
            xdtw = P.sb([128, 2048], BF16, name="xdtw")
            oT = xdtw
            dtv = [P.sb([128, 64], F32, name="dtv%d" % i) for i in range(2)]
            dA = [P.sb([128, 32], F32, name="dA%d" % i) for i in range(2)]
            e3 = [P.sb([128, 96], F32, name="e3%d" % i) for i in range(2)]
            cbm = [P.sb([128, 512], F32, name="cbm%d" % i) for i in range(2)]
            Btk = [P.sb([128, 512], BF16, name="Btk%d" % i) for i in range(2)]
            rh = [P.sb([128, 512], F32, name="rh%d" % i) for i in range(2)]
            es = [P.sb([128, 512], F32, name="es%d" % i) for i in range(2)]
            MT = [P.sb([128, 32, 128], BF16, nres=8, name="MT%d" % i) for i in range(2)]
            tmp = [P.sb([128, 512], F32, name="tmpy%d" % i) for i in range(2)]
            ysb = P.sb([128, 2048], F32, nres=4, name="ysb")
            ST = P.sb([128, 2048], F32, nres=4, name="ST")
            STb = P.sb([128, 2048], BF16, nres=4, name="STb")
            yfl = P.sb([128, 2048], F32, name="yfl")
            ss4 = P.sb([128, 4], F32, name="ss4")
            pM = P.ps([128, 512], name="pM")
            pCB = P.ps([128, 512], name="pCB")
            pCBb = pCB[:].bitcast(BF16)
            pSG = [P.ps([128, 512], name="pSG%d" % i) for i in range(2)]
            pY = [P.ps([128, 512], name="pY%d" % i) for i in range(2)]
            pI = [P.ps([128, 512], name="pI%d" % i) for i in range(2)]
            p4 = [pY[0], pY[1], pI[0], pI[1]]
            v3 = lambda ap, h: ap.rearrange("p (h q) -> p h q", h=h)
            col = lambda ap: ap.rearrange("p (h o) -> p h o", o=1)

            def prologue(t, d, b):
                tc = slice(t * 128, (t + 1) * 128)
                TRI = cst[:, (C_LE if d == 0 else C_GE):(C_LE if d == 0 else C_GE) + 128]
                STR = cst[:, (C_GT if d == 0 else C_LT):(C_GT if d == 0 else C_LT) + 128]
                xs_, xdt_, dtv_, dA_, e3_, cbm_, Btk_, MT_ = xs[b], xdt[b], dtv[b], dA[b], e3[b], cbm[b], Btk[b], MT[b]
                P.dma(xs_[:], mb_xs[t], R=[P.dres("mb_xs", t)], W=[xs_])
                for kc in range(8):
                    P.mm(pM[:, 0:64], hT[:, kc, tc], Wz[:, kc, 2048:2112], start=(kc == 0), stop=(kc == 7), R=[hT.res[t], Wz.res[4:8]], W=[pM])
                P.tt(dtv_[:], pM[:, 0:64], dtb[:], ALU.add, R=[pM, dtb], W=[dtv_])
                P.act(dtv_[:], dtv_[:], AF.Exp, R=[dtv_], W=[dtv_])
                P.act(dtv_[:], dtv_[:], AF.Ln, R=[dtv_], W=[dtv_], bias=1.0)
                P.tt(dA_[:], dtv_[:, d * 32:(d + 1) * 32], abc[:, d * 32:(d + 1) * 32], ALU.mult, R=[dtv_, abc], W=[dA_])
                for g in range(4):
                    P.mm(pCB[:, g * 128:(g + 1) * 128], BT[:, g, tc], CT[:, g, tc], R=[BT, CT], W=[pCB])
                yield
                P.mm(pM[:, 64:96], TRI, dA_[:], R=[cst, dA_], W=[pM])
                P.mm(pM[:, 96:128], STR, dA_[:], R=[cst, dA_], W=[pM])
                P.mm(pM[:, 128:160], onesF, dA_[:], R=[cst, dA_], W=[pM])
                P.act(e3_[:], pM[:, 64:160], AF.Exp, R=[pM], W=[e3_])
                P.tt(v3(cbm_[:], 4), v3(pCB[:], 4), TRI.rearrange("p (o t) -> p o t", o=1).broadcast_to([128, 4, 128]), ALU.mult,
                     R=[pCB, cst], W=[cbm_])
                for g in range(4):
                    P.tr(pCBb[:, g * 128:(g + 1) * 128], BT[:, g, tc], identB, R=[BT, cstb], W=[pCB])
                P.cp(Btk_[:], pCBb[:, 0:512], R=[pCB], W=[Btk_], E="act")
                P.tt(v3(xdt_[:], 32), v3(xs_[:], 32), col(dtv_[:, d * 32:(d + 1) * 32]).broadcast_to([128, 32, 64]), ALU.mult,
                     R=[xs_, dtv_], W=[xdt_])
                yield
                TRI3 = TRI.rearrange("p (o t) -> p o t", o=1)
                for step in range(10):
                    if step < 8:
                        hb = step
                        i = hb % 2
                        P.tt(v3(rh[i][:], 4), TRI3.broadcast_to([128, 4, 128]),
                             col(dA_[:, hb * 4:hb * 4 + 4]).broadcast_to([128, 4, 128]), ALU.mult, R=[cst, dA_], W=[rh[i]])
                        P.mm(pSG[i][:], STR, rh[i][:], R=[cst, rh[i]], W=[pSG[i]])
                    if 1 <= step <= 8:
                        i = (step - 1) % 2
                        P.act(es[i][:], pSG[i][:], AF.Exp, R=[pSG[i]], W=[es[i]])
                    if 2 <= step <= 9:
                        hb = step - 2
                        i = hb % 2
                        g = hb // 2
                        P.tt(MT_[:, hb * 4:hb * 4 + 4, :], v3(es[i][:], 4),
                             cbm_[:, g * 128:(g + 1) * 128].rearrange("p (o t) -> p o t", o=1).broadcast_to([128, 4, 128]), ALU.mult,
                             R=[es[i], cbm_], W=[MT_.res[hb]])
                    if step % 2 == 1:
                        yield

            def scan(t, d, b):
                tc = slice(t * 128, (t + 1) * 128)
                xs_, xdt_, e3_, Btk_, MT_ = xs[b], xdt[b], e3[b], Btk[b], MT[b]
                if d == 1:
                    P.dma(yfl[:], mb_yf[t], R=[P.dres("mb_yf", t)], W=[yfl])
                P.tt(v3(xdtw[:], 32), v3(xdt_[:], 32), col(e3_[:, 32:64]).broadcast_to([128, 32, 64]), ALU.mult,
                     R=[xdt_, e3_], W=[xdtw])
                for g in range(4):
                    gs = slice(g * 512, (g + 1) * 512)
                    pY_, pI_, tmp_ = pY[g % 2], pI[g % 2], tmp[g % 2]
                    for hh in range(8):
                        h = g * 8 + hh
                        P.mm(pY_[:, hh * 64:(hh + 1) * 64], MT_[:, h, :], xdt_[:, h * 64:(h + 1) * 64], R=[MT_.res[h // 4], xdt_], W=[pY_])
                    P.mm(pI_[:], CT[:, g, tc], STb[:, gs], R=[CT, STb.res[g]], W=[pI_])
                    P.tt(v3(tmp_[:], 8), v3(pI_[:], 8), col(e3_[:, g * 8:(g + 1) * 8]).broadcast_to([128, 8, 64]), ALU.mult,
                         R=[pI_, e3_], W=[tmp_])
                    P.tt(ysb[:, gs], tmp_[:], pY_[:], ALU.add, R=[tmp_, pY_], W=[ysb.res[g]])
                    if g % 2 == 1:
                        yield
                for g in range(4):
                    gs = slice(g * 512, (g + 1) * 512)
                    pI_ = pI[g % 2]
                    P.mm(pI_[:], Btk_[:, g * 128:(g + 1) * 128], xdtw[:, gs], R=[Btk_, xdtw], W=[pI_])
                    P.tt(v3(ST[:, gs], 8), v3(ST[:, gs], 8), col(e3_[:, 64 + g * 8:64 + (g + 1) * 8]).broadcast_to([128, 8, 64]), ALU.mult,
                         R=[ST.res[g], e3_], W=[ST.res[g]])
                    P.tt(ST[:, gs], ST[:, gs], pI_[:], ALU.add, R=[ST.res[g], pI_], W=[ST.res[g]])
                    P.cp(STb[:, gs], ST[:, gs], R=[ST.res[g]], W=[STb.res[g]], E="act")
                    if g % 2 == 1:
                        yield
                if d == 0:
                    P.dma(mb_yf[t], ysb[:], R=[ysb], W=[P.dres("mb_yf", t)])
                    return
                P.tt(ysb[:], ysb[:], yfl[:], ALU.add, R=[ysb, yfl], W=[ysb])
                P.tt(v3(yfl[:], 32), v3(xs_[:], 32), col(dbc[:]).broadcast_to([128, 32, 64]), ALU.mult, R=[xs_, dbc, yfl], W=[yfl])
                P.tt(ysb[:], ysb[:], yfl[:], ALU.add, R=[ysb, yfl], W=[ysb])
                for j in range(4):
                    for kc in range(8):
                        P.mm(p4[j][:], hT[:, kc, tc], Wz[:, kc, j * 512:(j + 1) * 512], start=(kc == 0), stop=(kc == 7),
                             R=[hT.res[t], Wz.res[0:4]], W=[p4[j]])
                    P.act(yfl[:, j * 512:(j + 1) * 512], p4[j][:], AF.Silu, R=[p4[j]], W=[yfl])
                    if j % 2 == 1:
                        yield
                P.tt(ysb[:], ysb[:], yfl[:], ALU.mult, R=[ysb, yfl], W=[ysb])
                P.act(yfl[:], ysb[:], AF.Square, R=[ysb], W=[yfl])
                P.op("dve", lambda e: e.tensor_reduce(ss4[:], v3(yfl[:], 4), AX.X, ALU.add), R=[yfl], W=[ss4])
                P.act(ss4[:], ss4[:], AF.Sqrt, R=[ss4], W=[ss4], scale=1.0 / 512.0, bias=RMS_EPS)
                P.op("dve", lambda e: e.reciprocal(ss4[:], ss4[:]), R=[ss4], W=[ss4])
                yield
                for g in range(4):
                    gs = slice(g * 512, (g + 1) * 512)
                    P.stt(ysb[:, gs], ysb[:, gs], ss4[:, g:g + 1], nrm[:, gs], ALU.mult, ALU.mult, R=[ysb.res[g], ss4, nrm], W=[ysb.res[g]])
                for cc in range(16):
                    P.tr(p4[cc // 4][:, (cc % 4) * 128:(cc % 4 + 1) * 128], ysb[:, cc * 128:(cc + 1) * 128], identF, R=[ysb, cst], W=[p4[cc // 4]])
                for j in range(4):
                    P.cp(oT[:, j * 512:(j + 1) * 512], p4[j][:], R=[p4[j]], W=[oT], E=("act" if j % 2 else "dve"))
                P.dma(ombT[t], oT[:], R=[oT], W=[P.dres("ombT", t)])
                yield

            def reset_state():
                P.op("dve", lambda e: e.memset(ST[:], 0.0), W=[ST])
                P.op("pool", lambda e: e.memset(STb[:], 0.0), W=[STb])

            order = [(t, 0) for t in range(NT)] + [(t, 1) for t in list(range(TCTX - 1, -1, -1)) + list(range(NT - 1, TCTX - 1, -1))]
            reset_state()
            _interleave([prologue(order[0][0], order[0][1], 0)])
            for n, (t, d) in enumerate(order):
                nxt = None
                if n + 1 < len(order):
                    nxt = prologue(order[n + 1][0], order[n + 1][1], (n + 1) % 2)
                _interleave([scan(t, d, n % 2), nxt])
                if n + 1 < len(order) and order[n + 1][1] == 1 and d == 0:
                    reset_state()
            P.barrier()


def _layernorm(P, out, z, zr, lng, lnb, st, mv):
    for c in range(2):
        P.op("dve", lambda e: e.bn_stats(st[:, c * 6:(c + 1) * 6], z[:, c * 512:(c + 1) * 512]), R=[zr], W=[st])
    P.op("dve", lambda e: e.bn_aggr(mv[:, 0:2], st[:]), R=[st], W=[mv])
    P.act(mv[:, 2:3], mv[:, 1:2], AF.Sqrt, R=[mv], W=[mv], bias=LN_EPS)
    P.op("dve", lambda e: e.reciprocal(mv[:, 2:3], mv[:, 2:3]), R=[mv], W=[mv])
    P.ts(out, z, mv[:, 0:1], mv[:, 2:3], ALU.subtract, ALU.mult, R=[zr, mv], W=[zr])
    P.tt(out, out, lng[:], ALU.mult, R=[zr, lng], W=[zr])
    P.tt(out, out, lnb[:], ALU.add, R=[zr, lnb], W=[zr])


def _gate_bcast(P, cst, modrow_d, c0, s, pG, gbc):
    mr = P.sb([3, 1024], F32, name="mr")
    P.dma(mr[:], modrow_d[:, c0:c0 + 1024], R=[P.dres("modrow")], W=[mr])
    for ri, r in enumerate((s, 2)):
        for half in range(2):
            P.mm(pG[:, half * 512:(half + 1) * 512], cst[0:3, C_SEL + r * 128:C_SEL + (r + 1) * 128], mr[0:3, half * 512:(half + 1) * 512],
                 R=[cst, mr], W=[pG])
        P.cp(gbc[:, ri, :], pG[:], R=[pG], W=[gbc])


def phase_merge(P, nc, l, s, last, Wd, cst, cstb, hT, modA, modB, modrow_d, ohgT, oatT, ombT, ymTd, x1s, xsrc_fn, sq):
    identB = cstb[:, 0:128]
    identF = cst[:, C_ID:C_ID + 128]
    tiles = list(range(TCTX, NT)) if last else list(range(NT))
    Wl = Wd["w_in"][l].rearrange("(kc p) n -> p kc n", p=128)
    with ExitStack() as ph:
        P.stack = ph
        Wgt = P.sb([128, 8, 3072], BF16, nres=12, name="Wgt")
        Wbh = P.sb([128, 8, 1024], BF16, nres=4, name="Wbh")
        Wba = P.sb([128, 8, 1024], BF16, nres=4, name="Wba")
        Wbm = P.sb([128, 16, 1024], BF16, nres=8, name="Wbm")
        Wgs = Wl[:, :, O_GT:O_GT + 3072]

        def ldg(b):
            for i, kc in enumerate(range(0, 8, 2)):
                P.dma(Wgt[:, kc:kc + 2, b * 1024:(b + 1) * 1024], Wgs[:, kc:kc + 2, b * 1024:(b + 1) * 1024], W=[Wgt.res[b * 4 + i]], q="pool")

        ldg(0)
        _loadw(P, Wbh, Wd["w_br_hg"][l].rearrange("(kc p) n -> p kc n", p=128), 1024)
        ldg(1)
        _loadw(P, Wba, Wd["w_br_at"][l].rearrange("(kc p) n -> p kc n", p=128), 1024)
        ldg(2)
        _loadw(P, Wbm, Wd["w_br_mb"][l].rearrange("(kc p) n -> p kc n", p=128), 1024)
        oh = [P.sb([128, 1024], BF16, name="oh%d" % i) for i in range(2)]
        oa = [P.sb([128, 8, 128], BF16, name="oam%d" % i) for i in range(2)]
        om = [P.sb([128, 2048], BF16, name="om%d" % i) for i in range(2)]
        sgt = P.sb([128, 1024], F32, name="sgt")
        ym = P.sb([128, 1024], F32, name="ym")
        tmp = P.sb([128, 1024], F32, name="tmpm")
        ymb = P.sb([128, 1024], BF16, name="ymb")
        ymT = [P.sb([128, 1024], BF16, name="ymT%d" % i) for i in range(2)]
        pG = P.ps([128, 1024], name="pGm")
        pB = P.ps([128, 1024], name="pBm")
        pT = P.ps([128, 1024], BF16, name="pTm")

        def ld_ma(i):
            t = tiles[i]
            P.dma(oh[i % 2][:], ohgT[t], R=[P.dres("ohgT", t)], W=[oh[i % 2]])
            P.dma(oa[i % 2][:], oatT[:, :, t * 128:(t + 1) * 128],
                  R=[P.dres("oatT", (h, q0)) for h in range(8) for q0 in (0, 256, 768, 1280, 1792)], W=[oa[i % 2]])
            P.dma(om[i % 2][:], ombT[t], R=[P.dres("ombT", t)], W=[om[i % 2]])

        for i, t in enumerate(tiles):
            tc = slice(t * 128, (t + 1) * 128)
            oh_, oa_, om_ = oh[i % 2], oa[i % 2], om[i % 2]
            if i == 0:
                ld_ma(0)
            if i + 1 < len(tiles):
                ld_ma(i + 1)
            srcs = [(lambda kc, o=oh_: o[:, kc * 128:(kc + 1) * 128], Wbh, 8, oh_),
                    (lambda kc, o=oa_: o[:, kc, :], Wba, 8, oa_),
                    (lambda kc, o=om_: o[:, kc * 128:(kc + 1) * 128], Wbm, 16, om_)]
            for b, (of, Wb, nk, ot) in enumerate(srcs):
                for half in range(2):
                    for kc in range(8):
                        P.mm(pG[:, half * 512:(half + 1) * 512], hT[:, kc, tc], Wgt[:, kc, b * 1024 + half * 512:b * 1024 + (half + 1) * 512],
                             start=(kc == 0), stop=(kc == 7), R=[hT.res[t]] + Wgt.res[b * 4:(b + 1) * 4], W=[pG])
                P.act(sgt[:], pG[:], AF.Sigmoid, R=[pG], W=[sgt])
                for half in range(2):
                    for kc in range(nk):
                        P.mm(pB[:, half * 512:(half + 1) * 512], of(kc), Wb[:, kc, half * 512:(half + 1) * 512],
                             start=(kc == 0), stop=(kc == nk - 1), R=[ot, Wb], W=[pB])
                if b == 0:
                    P.tt(ym[:], pB[:], sgt[:], ALU.mult, R=[pB, sgt], W=[ym])
                else:
                    P.tt(tmp[:], pB[:], sgt[:], ALU.mult, R=[pB, sgt], W=[tmp])
                    if b == 1:
                        P.tt(ym[:], ym[:], tmp[:], ALU.add, R=[ym, tmp], W=[ym])
                    else:
                        P.tt(ymb[:], ym[:], tmp[:], ALU.add, R=[ym, tmp], W=[ymb])
            for kc in range(8):
                P.tr(pT[:, kc * 128:(kc + 1) * 128], ymb[:, kc * 128:(kc + 1) * 128], identB, R=[ymb, cstb], W=[pT])
            yT = ymT[i % 2]
            P.cp(yT[:], pT[:], R=[pT], W=[yT], E="act")
            P.dma(ymTd[t], yT[:], R=[yT], W=[P.dres("ymT", t)])
        P.barrier()
    with ExitStack() as ph:
        P.stack = ph
        Wo = P.sb([128, 8, 1024], BF16, nres=4, name="Wo")
        _loadw(P, Wo, Wd["w_out"][l].rearrange("(kc p) n -> p kc n", p=128), 1024)
        pW = P.ps([128, 1024], name="pWm")
        pp = [P.ps([128, 512], name="pp%d" % i) for i in range(2)]
        gbc = P.sb([128, 2, 1024], F32, name="gbc")
        _gate_bcast(P, cst, modrow_d, 2048, s, pW, gbc)
        lng = P.sb([128, 1024], F32, name="lng")
        lnb = P.sb([128, 1024], F32, name="lnb")
        P.dma(lng[:], Wd["ln1_g"][l:l + 1, :].broadcast_to([128, 1024]), W=[lng])
        P.dma(lnb[:], Wd["ln1_b"][l:l + 1, :].broadcast_to([128, 1024]), W=[lnb])
        yT = [P.sb([128, 1024], BF16, name="yTb%d" % i) for i in range(2)]
        xt = [P.sb([128, 1024], F32, name="xtb%d" % i) for i in range(2)]
        z = [P.sb([128, 1024], F32, name="zb%d" % i) for i in range(2)]
        st = P.sb([128, 12], F32, name="st")
        mv = P.sb([128, 4], F32, name="mv")

        def ld_mb(i):
            t = tiles[i]
            P.dma(yT[i % 2][:], ymTd[t], R=[P.dres("ymT", t)], W=[yT[i % 2]])
            src, sres = xsrc_fn(t)
            P.dma(xt[i % 2][:], src, R=[sres], W=[xt[i % 2]])

        for i, t in enumerate(tiles):
            tc = slice(t * 128, (t + 1) * 128)
            r = 2 if t < TCTX else s
            ri = 1 if t < TCTX else 0
            yT_, xt_, z_ = yT[i % 2], xt[i % 2], z[i % 2]
            if i == 0:
                ld_mb(0)
            if i + 1 < len(tiles):
                ld_mb(i + 1)
            for half in range(2):
                for kc in range(8):
                    P.mm(pW[:, half * 512:(half + 1) * 512], yT_[:, kc * 128:(kc + 1) * 128], Wo[:, kc, half * 512:(half + 1) * 512],
                         start=(kc == 0), stop=(kc == 7), R=[yT_, Wo], W=[pW])
            P.tt(z_[:], pW[:], gbc[:, ri, :], ALU.mult, R=[pW, gbc], W=[z_])
            P.stt(z_[:], xt_[:], float(DN_ALPHA), z_[:], ALU.mult, ALU.add, R=[xt_, z_], W=[z_])
            _layernorm(P, z_[:], z_[:], z_, lng, lnb, st, mv)
            P.dma(x1s[t], z_[:], R=[z_], W=[P.dres("x1s", t)])
            for half in range(2):
                pb = pp[half]
                for q in range(4):
                    kc = half * 4 + q
                    P.tr(pb[:, q * 128:(q + 1) * 128], z_[:, kc * 128:(kc + 1) * 128], identF, R=[z_, cst], W=[pb])
                for q in range(4):
                    kc = half * 4 + q
                    o = hT[:, kc, tc]
                    i_ = pb[:, q * 128:(q + 1) * 128]
                    sc = modB[:, 32 + kc, r:r + 1]
                    sh = modA[:, 24 + kc, r:r + 1]
                    if q % 2 == 0:
                        P.ts(o, i_, sc, sh, ALU.mult, ALU.add, R=[pb, modA, modB], W=[hT.res[t]])
                    else:
                        P.act(o, i_, AF.Identity, R=[pb, modA, modB], W=[hT.res[t]], bias=sh, scale=sc)
        P.barrier()


def phase_ffn(P, nc, l, s, last, Wd, cst, cstb, hT, modrow_d, x1s, xres, out_d, ffn_act):
    blocks = ([] if last else [(0, TCTX)]) + [(TCTX + 4 * i, 4) for i in range(4)]
    with ExitStack() as ph0:
        P.stack = ph0
        W2 = P.sb([128, 22, 1024], BF16, nres=11, name="W2")
        with ExitStack() as ph:
            P.stack = ph
            W1 = P.sb([128, 8, 2 * FFH], BF16, nres=44, name="W1")
            _loadw_cols(P, W1, Wd["w_ffn_in"][l].rearrange("(kc p) n -> p kc n", p=128), 2 * FFH, 256,
                        order=[x for j in range(11) for x in (j, 11 + j)])
            _loadw(P, W2, Wd["w_ffn_out"][l].rearrange("(j p) n -> p j n", p=128), 1024)
            pGt = [P.ps([128, 512], name="pGt%d" % i) for i in range(2)]
            pUp = [P.ps([128, 512], name="pUp%d" % i) for i in range(2)]
            sgu = [P.sb([128, 512], F32, name="sgu%d" % i) for i in range(2)]
            aT = P.sb([128, 22, 512], BF16, name="actT")
            for bi, (t0, ntile) in enumerate(blocks):
                N = ntile * 128
                cols = slice(t0 * 128, t0 * 128 + N)
                tr_ = [hT.res[t] for t in range(t0, t0 + ntile)]
                for j in range(22):
                    pg, pu, sg_ = pGt[j % 2], pUp[j % 2], sgu[j % 2]
                    for kc in range(8):
                        P.mm(pg[:, 0:N], W1[:, kc, j * 128:(j + 1) * 128], hT[:, kc, cols], start=(kc == 0), stop=(kc == 7),
                             R=_cr(W1, j * 128, (j + 1) * 128) + tr_, W=[pg])
                    for kc in range(8):
                        P.mm(pu[:, 0:N], W1[:, kc, FFH + j * 128:FFH + (j + 1) * 128], hT[:, kc, cols], start=(kc == 0), stop=(kc == 7),
                             R=_cr(W1, FFH + j * 128, FFH + (j + 1) * 128) + tr_, W=[pu])
                    P.act(sg_[:, 0:N], pg[:, 0:N], AF.Silu, R=[pg], W=[sg_])
                    P.tt(aT[:, j, 0:N], sg_[:, 0:N], pu[:, 0:N], ALU.mult, R=[sg_, pu], W=[aT])
                for ti in range(ntile):
                    t = t0 + ti
                    P.dma(ffn_act[t].rearrange("p (j k) -> p j k", j=22), aT[:, :, ti * 128:(ti + 1) * 128], R=[aT], W=[P.dres("ffn_act", t)])
            P.barrier()
        with ExitStack() as ph:
            P.stack = ph
            pF = [P.ps([128, 1024], name="pF%d" % i) for i in range(2)]
            gbc = P.sb([128, 2, 1024], F32, name="gbc2")
            _gate_bcast(P, cst, modrow_d, 5120, s, pF[0], gbc)
            lng = P.sb([128, 1024], F32, name="lng2")
            lnb = P.sb([128, 1024], F32, name="lnb2")
            P.dma(lng[:], Wd["ln2_g"][l:l + 1, :].broadcast_to([128, 1024]), W=[lng])
            P.dma(lnb[:], Wd["ln2_b"][l:l + 1, :].broadcast_to([128, 1024]), W=[lnb])
            aTl = [P.sb([128, 22, 128], BF16, name="aTl%d" % i) for i in range(2)]
            xt = [P.sb([128, 1024], F32, name="x1l%d" % i) for i in range(2)]
            z = [P.sb([128, 1024], F32, name="z2%d" % i) for i in range(2)]
            st = P.sb([128, 12], F32, name="st2")
            mv = P.sb([128, 4], F32, name="mv2")
            tiles = [t0 + ti for (t0, ntile) in blocks for ti in range(ntile)]
            def ld_b(i):
                t = tiles[i]
                P.dma(aTl[i % 2][:], ffn_act[t].rearrange("p (j k) -> p j k", j=22), R=[P.dres("ffn_act", t)], W=[aTl[i % 2]])
                P.dma(xt[i % 2][:], x1s[t], R=[P.dres("x1s", t)], W=[xt[i % 2]])

            ld_b(0)
            for i, t in enumerate(tiles):
                ri = 1 if t < TCTX else 0
                a_, xt_, z_, pF_ = aTl[i % 2], xt[i % 2], z[i % 2], pF[i % 2]
                if i + 1 < len(tiles):
                    ld_b(i + 1)
                for j in range(22):
                    for half in range(2):
                        P.mm(pF_[:, half * 512:(half + 1) * 512], a_[:, j, :], W2[:, j, half * 512:(half + 1) * 512],
                             start=(j == 0), stop=(j == 21), R=[a_, W2], W=[pF_])
                P.tt(z_[:], pF_[:], gbc[:, ri, :], ALU.mult, R=[pF_, gbc], W=[z_])
                P.stt(z_[:], xt_[:], float(DN_ALPHA), z_[:], ALU.mult, ALU.add, R=[xt_, z_], W=[z_])
                _layernorm(P, z_[:], z_[:], z_, lng, lnb, st, mv)
                if last:
                    P.dma(out_d[s, (t - TCTX) * 128:(t - TCTX + 1) * 128, :], z_[:], R=[z_], W=[P.dres("out", (s, t))])
                else:
                    P.dma(xres[s, t], z_[:], R=[z_], W=[P.dres("xres", (s, t))])
            P.barrier()


def kernel(**inputs):
    inp = {k: np.asarray(v) for k, v in inputs.items()}
    nc = build(DEPTH)
    consts = make_consts()
    rope = make_rope()
    maps = []
    for core in range(8):
        b0 = 2 * core
        c3 = np.stack([inp["c"][b0], inp["c"][b0 + 1], inp["c_ctx"]])
        c3T = np.ascontiguousarray(c3.reshape(3, 8, 128).transpose(2, 1, 0))
        m = {"x": np.ascontiguousarray(inp["x"][b0:b0 + 2]), "ctx": np.ascontiguousarray(inp["ctx"][b0:b0 + 2]),
             "c3T": c3T, "consts": consts, "rope": rope, "hg_lb": inp["hg_lb"]}
        for n, _ in WSHAPES:
            m[n] = inp[n]
        maps.append(m)
    res = run_bass_kernel_spmd(nc, maps, core_ids=list(range(8)))
    out = np.concatenate([np.asarray(r["out"]) for r in res.results], axis=0)
    return out.astype(np.float32)
```

```python
from contextlib import ExitStack
import numpy as np
import concourse.bass as bass
import concourse.mybir as mybir
from concourse.bass_utils import run_bass_kernel_spmd

F32 = mybir.dt.float32
BF16 = mybir.dt.bfloat16
AF = mybir.ActivationFunctionType
ALU = mybir.AluOpType
AX = mybir.AxisListType

SAME_ENGINE_SYNC = True
N_DMA_SEMS = 16

DEPTH = 4
DM = 1024
NT = 18
TCTX = 2
TOK = NT * 128
FFH = 2816
IN_W = 14912
O_HQ, O_HFF, O_HFB, O_HI, O_HG = 0, 1024, 2048, 3072, 4096
O_AQ, O_AK, O_AV = 5120, 6144, 6400
O_MZ, O_MX, O_MDT, O_GT = 6656, 8704, 11776, 11840
DN_ALPHA = (2 * DEPTH) ** 0.25
LN_EPS = 1e-5
RMS_EPS = 1e-6

C_ID, C_ONES, C_LE, C_GE, C_GT, C_LT, C_HDF, C_HDB, C_HMF, C_HMB, C_SEL, C_CI = (
    0, 128, 256, 384, 512, 640, 768, 896, 1024, 1152, 1280, 1664)
NCONST = 1664 + 4


def make_consts():
    a = np.arange(128)[:, None]
    b = np.arange(128)[None, :]
    bd = (a // 32) == (b // 32)
    blocks = [a == b, np.ones((128, 128), bool), a <= b, a >= b, a > b, a < b,
              bd & (a > b), bd & (a < b), bd & (a <= b), bd & (a >= b)]
    sel = np.zeros((128, 3 * 128), bool)
    for r in range(3):
        sel[r, r * 128:(r + 1) * 128] = True
    ci = (a // 32) == np.arange(4)[None, :]
    return np.concatenate(blocks + [sel, ci], axis=1).astype(np.float32)


def make_rope():
    rows = 2048 // 64
    row, col = np.meshgrid(np.arange(rows, dtype=np.float32), np.arange(64, dtype=np.float32), indexing="ij")
    n_pairs = 32
    inv_freq = (np.float32(10000.0) ** (-np.arange(n_pairs, dtype=np.float32) / np.float32(n_pairs))).astype(np.float32)
    ang = np.concatenate([row.reshape(-1, 1) * inv_freq, col.reshape(-1, 1) * inv_freq], axis=-1).astype(np.float32)
    return np.stack([np.cos(ang), np.sin(ang)]).astype(np.float32)


class Res:
    __slots__ = ("w", "r")

    def __init__(self):
        self.w = None
        self.r = {}


class Tile:
    def __init__(self, t, nres=1):
        self.t = t
        self.res = [Res() for _ in range(nres)]

    def __getitem__(self, k):
        return self.t[k]

    @property
    def r(self):
        return self.res[0]


def _res(items):
    out = []
    for it in items:
        if isinstance(it, Tile):
            out.extend(it.res)
        elif isinstance(it, Res):
            out.append(it)
        elif it is None:
            pass
        else:
            out.extend(_res(it))
    return out


class Prog:
    def __init__(self, nc, stack):
        self.nc = nc
        self.eng = {"pe": nc.tensor, "act": nc.scalar, "dve": nc.vector, "pool": nc.gpsimd, "sp": nc.sync}
        self.sem = {}
        for e in self.eng:
            self.sem[e] = stack.enter_context(nc.semaphore("s_" + e))
        self.dma_sems = {}
        for q in ("sp", "act", "pool"):
            self.dma_sems[q] = [("d", q, i) for i in range(N_DMA_SEMS)]
            for k in self.dma_sems[q]:
                self.sem[k] = stack.enter_context(nc.semaphore("d_%s_%d" % (q, k[2])))
        self.count = {k: 0 for k in self.sem}
        self.known = {e: {} for e in self.eng}
        self.vc = {}
        self.dma_rr = {q: 0 for q in self.dma_sems}
        self.n_instr = 0
        self.n_wait = 0
        self.stack = None
        self._dres = {}
        self._uid = 0

    def sb(self, shape, dtype, nres=1, name=None):
        self._uid += 1
        t = self.stack.enter_context(self.nc.sbuf_tensor("%s_%d" % (name or "t", self._uid), list(shape), dtype))
        return Tile(t, nres)

    def ps(self, shape, dtype=F32, name=None):
        self._uid += 1
        t = self.stack.enter_context(self.nc.psum_tensor("%s_%d" % (name or "p", self._uid), list(shape), dtype))
        return Tile(t)

    def dres(self, name, idx=0):
        k = (name, idx)
        if k not in self._dres:
            self._dres[k] = Res()
        return self._dres[k]

    def _deps(self, E, reads, writes):
        deps = {}
        for r in reads:
            if r.w is not None and deps.get(r.w[0], 0) < r.w[1]:
                deps[r.w[0]] = r.w[1]
        for w in writes:
            if w.w is not None and deps.get(w.w[0], 0) < w.w[1]:
                deps[w.w[0]] = w.w[1]
            for k, v in w.r.items():
                if deps.get(k, 0) < v:
                    deps[k] = v
        kn = self.known[E]
        out = []
        for k, v in deps.items():
            if k == E and (E == "pe" or E == "sp" or not SAME_ENGINE_SYNC):
                continue
            if kn.get(k, 0) >= v:
                continue
            out.append((k, v))
        return out

    def _wait(self, E, waits):
        eng = self.eng[E]
        kn = self.known[E]
        for k, v in waits:
            eng.wait_ge(self.sem[k], v)
            self.n_wait += 1
            snap = self.vc.get((k, v))
            if snap is not None:
                for kk, vv in snap.items():
                    if kn.get(kk, 0) < vv:
                        kn[kk] = vv
            if kn.get(k, 0) < v:
                kn[k] = v

    def _finish(self, E, ev, ins, inc, reads, writes):
        ins.then_inc(self.sem[ev[0]], inc)
        snap = dict(self.known[E])
        snap[ev[0]] = ev[1]
        self.vc[ev] = snap
        k, v = ev
        for r in reads:
            if r.r.get(k, 0) < v:
                r.r[k] = v
        for w in writes:
            w.w = ev
            w.r = {}
        self.n_instr += 1

    def op(self, E, fn, R=(), W=()):
        reads = _res(R)
        writes = _res(W)
        self._wait(E, self._deps(E, reads, writes))
        ins = fn(self.eng[E])
        self.count[E] += 1
        ev = (E, self.count[E])
        self._finish(E, ev, ins, 1, reads, writes)

    def dma(self, out, in_, R=(), W=(), q="sp", **kw):
        reads = _res(R)
        writes = _res(W)
        self._wait(q, self._deps(q, reads, writes))
        key = self.dma_sems[q][self.dma_rr[q] % N_DMA_SEMS]
        self.dma_rr[q] += 1
        if self.count[key] > 0 and self.known[q].get(key, 0) < self.count[key]:
            self._wait(q, [(key, self.count[key])])
        ins = self.eng[q].dma_start(out=out, in_=in_, **kw)
        self.count[key] += 16
        ev = (key, self.count[key])
        self._finish(q, ev, ins, 16, reads, writes)

    def barrier(self):
        targets = [(k, v) for k, v in self.count.items() if v > 0]
        for E in self.eng:
            waits = [(k, v) for k, v in targets if self.known[E].get(k, 0) < v and not (k == E and E == "pe")]
            self._wait(E, waits)

    def wait_all_on(self, E="sp"):
        waits = [(k, v) for k, v in self.count.items() if v > 0 and k != E and self.known[E].get(k, 0) < v]
        self._wait(E, waits)

    def mm(self, out, lhsT, rhs, start=True, stop=True, R=(), W=()):
        self.op("pe", lambda e: e.matmul(out, lhsT, rhs, start=start, stop=stop), R, W)

    def tr(self, out, in_, ident, R=(), W=()):
        self.op("pe", lambda e: e.transpose(out, in_, ident), R, W)

    def act(self, out, in_, func, R=(), W=(), bias=0.0, scale=1.0, accum_out=None):
        if accum_out is None:
            self.op("act", lambda e: e.activation(out, in_, func, bias=bias, scale=scale), R, W)
        else:
            self.op("act", lambda e: e.activation(out, in_, func, bias=bias, scale=scale, accum_out=accum_out), R, W)

    def tt(self, out, in0, in1, op, R=(), W=(), E="dve"):
        self.op(E, lambda e: e.tensor_tensor(out, in0, in1, op), R, W)

    def ts(self, out, in0, s1, s2, op0, op1=None, R=(), W=(), E="dve"):
        if op1 is None:
            self.op(E, lambda e: e.tensor_scalar(out, in0, s1, None, op0), R, W)
        else:
            self.op(E, lambda e: e.tensor_scalar(out, in0, s1, s2, op0, op1), R, W)

    def stt(self, out, in0, scalar, in1, op0, op1, R=(), W=()):
        self.op("dve", lambda e: e.scalar_tensor_tensor(out, in0, scalar, in1, op0, op1), R, W)

    def cp(self, out, in_, R=(), W=(), E="dve"):
        if E == "act":
            self.op("act", lambda e: e.activation(out, in_, AF.Copy), R, W)
        else:
            self.op(E, lambda e: e.tensor_copy(out, in_), R, W)


WSHAPES = [
    ("w_mod", (DM, 6 * DM)), ("b_mod", (6 * DM,)), ("w_in", (DM, IN_W)), ("hg_gnorm", (128,)),
    ("at_qnorm", (128,)), ("at_knorm", (128,)), ("mb_conv_w", (5, 3072)), ("mb_conv_b", (3072,)),
    ("mb_dt_bias", (2, 32)), ("mb_a_log", (2, 32)), ("mb_d", (32,)), ("mb_norm", (2048,)),
    ("w_br_hg", (1024, DM)), ("w_br_at", (1024, DM)), ("w_br_mb", (2048, DM)), ("w_out", (DM, DM)),
    ("ln1_g", (DM,)), ("ln1_b", (DM,)), ("w_ffn_in", (DM, 2 * FFH)), ("w_ffn_out", (FFH, DM)),
    ("ln2_g", (DM,)), ("ln2_b", (DM,)),
]


def build(depth=DEPTH, dbg=None):
    dbg = dbg or {}
    dump = dbg.get("dump", set())
    phases = dbg.get("phases", {"p0", "hg", "at", "mb", "merge", "ffn"})
    seqs = dbg.get("seqs", [0, 1])
    nc = bass.Bass("TRN2", target_bir_lowering=False)

    def din(name, shape):
        return nc.dram_tensor(name, list(shape), F32, kind="ExternalInput").ap()

    x_in = din("x", [2, 2048, DM])
    ctx_in = din("ctx", [2, 256, DM])
    c3T = din("c3T", [128, 8, 3])
    consts_d = din("consts", [128, NCONST])
    rope_d = din("rope", [2, 2048, 64])
    hg_lb_d = din("hg_lb", [4, 2, 1024])
    Wd = {n: din(n, (depth,) + s) for n, s in WSHAPES}
    out_d = nc.dram_tensor("out", [2, 2048, DM], F32, kind="ExternalOutput").ap()

    def dscr(name, shape, dt):
        kind = "ExternalOutput" if name in dump else "Internal"
        return nc.dram_tensor(name, list(shape), dt, kind=kind).ap()

    xres = dscr("xres", [2, NT, 128, DM], F32)
    x1s = dscr("x1s", [NT, 128, DM], F32)
    hg_of = dscr("hg_of", [NT, 128, 1024], F32)
    ohgT = dscr("ohgT", [NT, 128, 1024], BF16)
    oatT = dscr("oatT", [128, 8, TOK], BF16)
    ombT = dscr("ombT", [NT, 128, 2048], BF16)
    mb_xs = dscr("mb_xs", [NT, 128, 2048], BF16)
    mb_yf = dscr("mb_yf", [NT, 128, 2048], F32)
    ymTd = dscr("ymT", [NT, 128, 1024], BF16)
    lbs_d = dscr("lbs", [4, 2, 1024], F32)
    modrow_d = dscr("modrow", [3, 6 * DM], F32)
    ffn_act = dscr("ffn_act", [NT, 128, 22 * 128], BF16)

    with ExitStack() as top:
        P = Prog(nc, top)
        P.stack = top
        cst = P.sb([128, NCONST], F32, name="cst")
        P.dma(cst[:], consts_d, W=[cst])
        cstb = P.sb([128, 256], BF16, name="cstb")
        P.cp(cstb[:], cst[:, 0:256], R=[cst], W=[cstb])
        identF = cst[:, C_ID:C_ID + 128]
        onesF = cst[:, C_ONES:C_ONES + 128]
        identB = cstb[:, 0:128]
        onesB = cstb[:, 128:256]
        scT = P.sb([128, 8, 3], F32, name="scT")
        P.dma(scT[:], c3T, W=[scT])
        P.act(scT[:], scT[:], AF.Silu, R=[scT], W=[scT])
        hT = P.sb([128, 8, TOK], BF16, nres=NT, name="hT")
        hT.res = [[Res() for _ in range(8)] for _ in range(NT)]

        def xsrc(l, s, t):
            if l == 0:
                return (ctx_in[s, t * 128:(t + 1) * 128, :] if t < TCTX else x_in[s, (t - TCTX) * 128:(t - TCTX + 1) * 128, :]), None
            return xres[s, t], P.dres("xres", (s, t))

        with ExitStack() as ph:
            P.stack = ph
            e = P.sb([128, 4, 16], F32, name="lb_e")
            P.dma(e[:].rearrange("p l (d j) -> p l d j", d=2),
                  hg_lb_d.rearrange("l d (p j) -> p l d j", p=128), W=[e])
            P.act(e[:], e[:], AF.Exp, R=[e], W=[e])
            s_ = P.sb([128, 16], F32, name="lb_s")
            P.tt(s_[:], e[:, 0, :], e[:, 1, :], ALU.add, R=[e], W=[s_])
            P.tt(s_[:], s_[:], e[:, 2, :], ALU.add, R=[e, s_], W=[s_])
            P.tt(s_[:], s_[:], e[:, 3, :], ALU.add, R=[e, s_], W=[s_])
            P.op("dve", lambda en: en.reciprocal(s_[:], s_[:]), R=[s_], W=[s_])
            P.tt(e[:], e[:], s_[:].rearrange("p (o j) -> p o j", o=1).broadcast_to([128, 4, 16]), ALU.mult, R=[e, s_], W=[e])
            lbt = P.sb([128, 4, 16], F32, name="lb_t")
            P.op("dve", lambda en: en.memset(lbt[:, 0, :], 0.0), W=[lbt])
            P.cp(lbt[:, 1, :], e[:, 1, :], R=[e], W=[lbt])
            P.tt(lbt[:, 2, :], lbt[:, 1, :], e[:, 2, :], ALU.add, R=[e, lbt], W=[lbt])
            P.tt(lbt[:, 3, :], lbt[:, 2, :], e[:, 3, :], ALU.add, R=[e, lbt], W=[lbt])
            P.dma(lbs_d.rearrange("l d (p j) -> p l d j", p=128),
                  lbt[:].rearrange("p l (d j) -> p l d j", d=2), R=[lbt], W=[P.dres("lbs")])
            P.barrier()
        P.stack = top

        for l in range(depth):
            last = (l == depth - 1) and not dbg.get('nolast', False)
            with ExitStack() as lay:
                P.stack = lay
                modA = P.sb([128, 48, 3], F32, name="modA")
                modB = P.sb([128, 48, 3], F32, name="modB")
                with ExitStack() as ph:
                    P.stack = ph
                    mrow = P.sb([3, 6 * DM], F32, name="mrow")
                    brow = P.sb([3, 6 * DM], F32, name="brow")
                    P.dma(brow[:], Wd["b_mod"][l:l + 1, :].broadcast_to([3, 6 * DM]), W=[brow])
                    wm = [P.sb([128, 8, 1536], F32, name="wm%d" % i) for i in range(2)]
                    pm = [P.ps([128, 512], name="pm%d" % i) for i in range(2)]
                    for blk in range(4):
                        w_ = wm[blk % 2]
                        P.dma(w_[:], Wd["w_mod"][l].rearrange("(kc p) n -> p kc n", p=128)[:, :, blk * 1536:(blk + 1) * 1536], W=[w_])
                        for j in range(3):
                            pj = pm[j % 2]
                            for kc in range(8):
                                P.mm(pj[0:3, :], scT[:, kc, :], w_[:, kc, j * 512:(j + 1) * 512],
                                     start=(kc == 0), stop=(kc == 7), R=[scT, w_], W=[pj])
                            c0 = blk * 1536 + j * 512
                            P.tt(mrow[:, c0:c0 + 512], pj[0:3, :], brow[:, c0:c0 + 512], ALU.add, R=[pj, brow], W=[mrow])
                    P.dma(modrow_d, mrow[:], R=[mrow], W=[P.dres("modrow")])
                    pT = P.ps([128, 512], name="pmT")
                    for j in range(48):
                        P.tr(pT[:, j * 3:(j + 1) * 3], mrow[:, j * 128:(j + 1) * 128], identF[0:3, 0:3], R=[mrow, cst], W=[pT])
                    P.cp(modA[:].rearrange("p j r -> p (j r)"), pT[:, 0:144], R=[pT], W=[modA])
                    P.ts(modB[:].rearrange("p j r -> p (j r)"), pT[:, 0:144], 1.0, None, ALU.add, R=[pT], W=[modB])
                    P.barrier()
                P.stack = lay

                for s in seqs:
                    with ExitStack() as sq:
                        P.stack = sq
                        def featT(l_, s_, tiles, src_fn, sc_chunk, sh_chunk, ph):
                            xt = [P.sb([128, DM], F32, name="p0x%d" % i) for i in range(2)]
                            pp = [P.ps([128, 512], name="p0p%d" % i) for i in range(2)]
                            for i, t in enumerate(tiles):
                                r = 2 if t < TCTX else s_
                                xb = xt[i % 2]
                                src, sres = src_fn(t)
                                P.dma(xb[:], src, R=[sres], W=[xb])
                                for half in range(2):
                                    pb = pp[half]
                                    for q in range(4):
                                        kc = half * 4 + q
                                        P.tr(pb[:, q * 128:(q + 1) * 128], xb[:, kc * 128:(kc + 1) * 128], identF, R=[xb, cst], W=[pb])
                                    for q in range(4):
                                        kc = half * 4 + q
                                        o = hT[:, kc, t * 128:(t + 1) * 128]
                                        i_ = pb[:, q * 128:(q + 1) * 128]
                                        sc = modB[:, sc_chunk + kc, r:r + 1]
                                        sh = modA[:, sh_chunk + kc, r:r + 1]
                                        if half == 0:
                                            P.ts(o, i_, sc, sh, ALU.mult, ALU.add, R=[pb, modA, modB], W=[hT.res[t][kc]])
                                        else:
                                            P.act(o, i_, AF.Identity, R=[pb, modA, modB], W=[hT.res[t][kc]], bias=sh, scale=sc)

                        if "p0" in phases:
                            with ExitStack() as ph:
                                P.stack = ph
                                featT(l, s, list(range(NT)), lambda t: xsrc(l, s, t), 8, 0, ph)
                                P.barrier()
                            P.stack = sq
                        if "hg" in phases:
                            with ExitStack() as ph:
                                P.stack = ph
                                phase_hg(P, nc, l, s, Wd, cst, cstb, hT, lbs_d, hg_of, ohgT)
                                P.barrier()
                            P.stack = sq
                        if "at" in phases:
                            with ExitStack() as ph:
                                P.stack = ph
                                phase_at(P, nc, l, s, Wd, cst, cstb, hT, rope_d, oatT)
                                P.barrier()
                            P.stack = sq
                        if "mb" in phases:
                            phase_mb(P, nc, l, s, Wd, cst, cstb, hT, mb_xs, mb_yf, ombT, sq)
                            P.stack = sq
                        if "merge" in phases:
                            phase_merge(P, nc, l, s, last, Wd, cst, cstb, hT, modA, modB, modrow_d, ohgT, oatT, ombT,
                                        ymTd, x1s, lambda t: xsrc(l, s, t), sq)
                            P.stack = sq
                        if "ffn" in phases:
                            phase_ffn(P, nc, l, s, last, Wd, cst, cstb, hT, modrow_d, x1s, xres, out_d, ffn_act)
                            P.stack = sq
                    P.stack = lay
            P.stack = top
        P.wait_all_on("sp")
        build.stats = (P.n_instr, P.n_wait)
    return nc


def _loadw(P, dst, src3, n, q="pool"):
    KC = src3.shape[1]
    step = 2
    for i, kc in enumerate(range(0, KC, step)):
        P.dma(dst[:, kc:kc + step, 0:n], src3[:, kc:kc + step, :], W=[dst.res[i % len(dst.res)]], q=q)


def _loadw_cols(P, dst, src3, n, cb, order=None, q="pool"):
    npc = (n + cb - 1) // cb
    KC = src3.shape[1]
    assert len(dst.res) >= 2 * npc
    dst.cb = cb
    for i in (order or range(npc)):
        c0, c1 = i * cb, min(n, (i + 1) * cb)
        for hf in range(2):
            k0, k1 = hf * KC // 2, (hf + 1) * KC // 2
            P.dma(dst[:, k0:k1, c0:c1], src3[:, k0:k1, c0:c1], W=[dst.res[2 * i + hf]], q=q)


def _cr(W, c0, c1):
    return [W.res[j] for i in range(c0 // W.cb, (c1 - 1) // W.cb + 1) for j in (2 * i, 2 * i + 1)]


def _interleave(gens):
    gens = [g for g in gens if g is not None]
    while gens:
        for g in list(gens):
            try:
                next(g)
            except StopIteration:
                gens.remove(g)


def phase_hg(P, nc, l, s, Wd, cst, cstb, hT, lbs_d, hg_of, ohgT):
    identB = cstb[:, 0:128]
    onesF = cst[:, C_ONES:C_ONES + 128]
    CI = cst[:, C_CI:C_CI + 4]
    Wl = Wd["w_in"][l].rearrange("(kc p) n -> p kc n", p=128)
    Wq = P.sb([128, 8, 1024], BF16, nres=4, name="Wq")
    Wf = P.sb([128, 8, 1024], BF16, nres=4, name="Wf")
    Wi = P.sb([128, 8, 1024], BF16, nres=4, name="Wi")
    Wg = P.sb([128, 8, 1024], BF16, nres=4, name="Wg")
    _loadw(P, Wf, Wl[:, :, O_HFF:O_HFF + 1024], 1024)
    _loadw(P, Wq, Wl[:, :, O_HQ:O_HQ + 1024], 1024)
    _loadw(P, Wi, Wl[:, :, O_HI:O_HI + 1024], 1024)
    lbb = P.sb([128, 1024], F32, name="lbb")
    oml = P.sb([128, 1024], F32, name="oml")

    def load_lb(d):
        P.dma(lbb[:], lbs_d[l, d:d + 1, :].broadcast_to([128, 1024]), R=[P.dres("lbs")], W=[lbb])
        P.ts(oml[:], lbb[:], -1.0, 1.0, ALU.mult, ALU.add, R=[lbb], W=[oml])

    load_lb(0)
    gn = P.sb([128, 1], F32, name="gn")
    P.dma(gn[:], Wd["hg_gnorm"][l].rearrange("(p o) -> p o", o=1), W=[gn])
    lf = P.sb([128, 1024], F32, name="lf")
    kk = P.sb([128, 1024], F32, name="kk")
    ex = P.sb([128, 1024], F32, name="ex")
    exn = P.sb([128, 1024], F32, name="exn")
    qs = P.sb([128, 1024], F32, name="qs")
    Kt = [P.sb([128, 1024], BF16, name="Kt%d" % i) for i in range(2)]
    Qt = P.sb([128, 1024], BF16, name="Qt")
    Vt = [P.sb([128, 1024], BF16, name="Vt%d" % i) for i in range(2)]
    Ktm = [P.sb([128, 1024], BF16, name="Ktm%d" % i) for i in range(2)]
    KT = [P.sb([128, 1024], BF16, name="KT%d" % i) for i in range(2)]
    QT = [P.sb([128, 1024], BF16, name="QT%d" % i) for i in range(2)]
    ec = [P.sb([128, 32], F32, name="ec%d" % i) for i in range(2)]
    sg = [P.sb([128, 1024], F32, name="sg%d" % i) for i in range(2)]
    S = P.sb([128, 8, 128], F32, nres=8, name="S")
    Sb = P.sb([128, 8, 128], BF16, nres=8, name="Sb")
    ATm = P.sb([128, 8, 128], BF16, nres=8, name="ATm")
    ost = P.sb([128, 1024], F32, name="ost")
    ofl = P.sb([128, 1024], F32, name="ofl")
    osq = P.sb([128, 1024], F32, name="osq")
    rs = ofl
    ob = P.sb([128, 1024], BF16, name="ob")
    pA = P.ps([128, 1024], name="pA")
    pT = P.ps([128, 1024], BF16, name="pT")
    pC = P.ps([128, 512], name="pC")
    pCb = pC[:].bitcast(BF16)
    pO = P.ps([128, 1024], name="pO")
    pSl = P.ps([128, 1024], name="pSl")
    _r0, _r1 = Res(), Res()
    pSl.res = [_r0] * 4 + [_r1] * 4

    def proj(Wt, t):
        for half in range(2):
            for kc in range(8):
                P.mm(pA[:, half * 512:(half + 1) * 512], hT[:, kc, t * 128:(t + 1) * 128],
                     Wt[:, kc, half * 512:(half + 1) * 512], start=(kc == 0), stop=(kc == 7),
                     R=[hT.res[t], Wt], W=[pA])

    def prologue(t, d, b):
        D = cst[:, (C_HDF if d == 0 else C_HDB):(C_HDF if d == 0 else C_HDB) + 128]
        proj(Wf, t)
        P.act(lf[:], pA[:], AF.Sigmoid, R=[pA], W=[lf])
        P.act(kk[:], pA[:], AF.Sigmoid, R=[pA], W=[kk], scale=-1.0)
        yield
        proj(Wq, t)
        P.tt(lf[:], lf[:], oml[:], ALU.mult, R=[lf, oml], W=[lf])
        P.tt(lf[:], lf[:], lbb[:], ALU.add, R=[lf, lbb], W=[lf])
        P.act(lf[:], lf[:], AF.Ln, R=[lf], W=[lf])
        P.act(qs[:], pA[:], AF.Silu, R=[pA], W=[qs])
        P.tt(kk[:], kk[:], oml[:], ALU.mult, R=[kk, oml], W=[kk])
        yield
        proj(Wi, t)
        P.cp(Vt[b][:], pA[:], R=[pA], W=[Vt[b]], E="act")
        yield
        for half in range(2):
            P.mm(pA[:, half * 512:(half + 1) * 512], D, lf[:, half * 512:(half + 1) * 512], R=[cst, lf], W=[pA])
        for h in range(8):
            P.mm(pC[:, h * 4:(h + 1) * 4], lf[:, h * 128:(h + 1) * 128], CI, R=[lf, cst], W=[pC])
        P.act(ex[:], pA[:], AF.Exp, R=[pA], W=[ex])
        P.act(exn[:], pA[:], AF.Exp, R=[pA], W=[exn], scale=-1.0)
        P.act(ec[b][:], pC[:, 0:32], AF.Exp, R=[pC], W=[ec[b]])
        yield
        P.stt(Qt[:], qs[:], float(128 ** -0.5), exn[:], ALU.mult, ALU.mult, R=[qs, exn], W=[Qt])
        P.tt(Kt[b][:], kk[:], ex[:], ALU.mult, R=[kk, ex], W=[Kt[b]])
        P.ts(Ktm[b][:], Kt[b][:], CI[:, 3:4], None, ALU.mult, R=[Kt[b], cst], W=[Ktm[b]])
        yield
        for h in range(8):
            P.tr(pT[:, h * 128:(h + 1) * 128], Kt[b][:, h * 128:(h + 1) * 128], identB, R=[Kt[b], cstb], W=[pT])
        P.cp(KT[b][:], pT[:], R=[pT], W=[KT[b]])
        for h in range(8):
            P.tr(pCb[:, h * 128:(h + 1) * 128], Qt[:, h * 128:(h + 1) * 128], identB, R=[Qt, cstb], W=[pC])
        P.cp(QT[b][:], pCb, R=[pC], W=[QT[b]], E="act")
        yield
        if d == 1:
            for h in range(8):
                for kc in range(8):
                    P.mm(pA[:, h * 128:(h + 1) * 128], Wg[:, kc, h * 128:(h + 1) * 128], hT[:, kc, t * 128:(t + 1) * 128],
                         start=(kc == 0), stop=(kc == 7), R=[Wg, hT.res[t]], W=[pA])
            P.act(sg[b][:], pA[:], AF.Silu, R=[pA], W=[sg[b]])
            yield

    def scan(t, d, b):
        M = cst[:, (C_HMF if d == 0 else C_HMB):(C_HMF if d == 0 else C_HMB) + 128]
        if d == 1:
            P.dma(ofl[:], hg_of[t], R=[P.dres("hg_of", t)], W=[ofl])
        for h in range(8):
            hs = slice(h * 128, (h + 1) * 128)
            P.mm(pSl[:, hs], KT[b][:, hs], QT[b][:, hs], R=[KT[b], QT[b]], W=[pSl.res[h]])
        for h in range(8):
            hs = slice(h * 128, (h + 1) * 128)
            P.tt(ATm[:, h, :], pSl[:, hs], M, ALU.mult, R=[pSl.res[h], cst], W=[ATm.res[h]])
        yield
        for h in range(8):
            hs = slice(h * 128, (h + 1) * 128)
            P.op("pe", lambda e, hs=hs, h=h: e.matmul(pO[:, hs], Vt[b][:, hs], ATm[:, h, :], start=(h % 4 == 0), stop=False,
                                                       skip_group_check=True), R=[Vt[b], ATm.res[h]], W=[pO])
        chunks = [0, 1, 2, 3] if d == 0 else [3, 2, 1, 0]
        for ci, c in enumerate(chunks):
            for h in range(8):
                e_c = ec[b][:, h * 4 + c:h * 4 + c + 1]
                P.op("act", lambda e, h=h, e_c=e_c: e.activation(Sb[:, h, :], S[:, h, :], AF.Identity, scale=e_c),
                     R=[S.res[h], ec[b]], W=[Sb.res[h]])
            for h in range(8):
                hs = slice(h * 128, (h + 1) * 128)
                if c == 3:
                    P.mm(pSl[:, hs], Ktm[b][:, hs], Vt[b][:, hs], R=[Ktm[b], Vt[b]], W=[pSl.res[h]])
                else:
                    P.mm(pSl[:, hs], Kt[b][c * 32:(c + 1) * 32, hs], Vt[b][c * 32:(c + 1) * 32, hs], R=[Kt[b], Vt[b]], W=[pSl.res[h]])
            for h in range(8):
                cs = slice(h * 128 + c * 32, h * 128 + (c + 1) * 32)
                P.op("pe", lambda e, cs=cs, h=h: e.matmul(pO[:, cs], Sb[:, h, :], QT[b][:, cs], start=False, stop=(ci == 3),
                                                           skip_group_check=True), R=[Sb.res[h], QT[b]], W=[pO])
            for h in range(8):
                hs = slice(h * 128, (h + 1) * 128)
                e_c = ec[b][:, h * 4 + c:h * 4 + c + 1]
                P.stt(S[:, h, :], S[:, h, :], e_c, pSl[:, hs], ALU.mult, ALU.add, R=[S.res[h], ec[b], pSl.res[h]], W=[S.res[h]])
            yield
        if d == 0:
            P.cp(ost[:], pO[:], R=[pO], W=[ost], E="act")
            P.dma(hg_of[t], ost[:], R=[ost], W=[P.dres("hg_of", t)])
        else:
            P.tt(ost[:], pO[:], ofl[:], ALU.add, R=[pO, ofl], W=[ost])
            P.act(osq[:], ost[:], AF.Square, R=[ost], W=[osq])
            for half in range(2):
                P.mm(pSl[:, half * 512:(half + 1) * 512], onesF, osq[:, half * 512:(half + 1) * 512], R=[cst, osq], W=[pSl])
            yield
            P.act(rs[:], pSl[:], AF.Sqrt, R=[pSl], W=[rs], scale=1.0 / 128.0, bias=RMS_EPS)
            P.op("dve", lambda e: e.reciprocal(rs[:], rs[:]), R=[rs], W=[rs])
            P.stt(osq[:], ost[:], gn[:, 0:1], rs[:], ALU.mult, ALU.mult, R=[ost, gn, rs], W=[osq])
            P.tt(ob[:], osq[:], sg[b][:], ALU.mult, R=[osq, sg[b]], W=[ob])
            P.dma(ohgT[t], ob[:], R=[ob], W=[P.dres("ohgT", t)])
        yield

    order = [(t, 0) for t in range(NT)] + [(t, 1) for t in list(range(TCTX - 1, -1, -1)) + list(range(NT - 1, TCTX - 1, -1))]
    P.op("dve", lambda e: e.memset(S[:], 0.0), W=[S])
    _interleave([prologue(order[0][0], order[0][1], 0)])
    for n, (t, d) in enumerate(order):
        nxt = None
        if n + 1 < len(order):
            t2, d2 = order[n + 1]
            if d2 == 1 and d == 0:
                _loadw(P, Wf, Wl[:, :, O_HFB:O_HFB + 1024], 1024)
                _loadw(P, Wg, Wl[:, :, O_HG:O_HG + 1024], 1024)
                load_lb(1)
            nxt = prologue(t2, d2, (n + 1) % 2)
        _interleave([scan(t, d, n % 2), nxt])
        if n + 1 < len(order) and order[n + 1][1] == 1 and d == 0:
            P.op("dve", lambda e: e.memset(S[:], 0.0), W=[S])


def _pipeline(items, gen_fn):
    active = []

    def rnd():
        for g in list(active):
            try:
                next(g)
            except StopIteration:
                active.remove(g)

    for it in items:
        active.append(gen_fn(it))
        rnd()
    while active:
        rnd()


def phase_at(P, nc, l, s, Wd, cst, cstb, hT, rope_d, oatT):
    identB = cstb[:, 0:128]
    onesB = cstb[:, 128:256]
    Wl = Wd["w_in"][l].rearrange("(kc p) n -> p kc n", p=128)
    Wa = P.sb([128, 8, 1536], BF16, nres=4, name="Wa")
    _loadw(P, Wa, Wl[:, :, O_AQ:O_AQ + 1536], 1536)
    QT = P.sb([128, 8, TOK], BF16, nres=NT, name="QTa")
    KTa = P.sb([128, 2, TOK], BF16, nres=NT, name="KTa")
    Va = P.sb([128, NT, 256], BF16, nres=NT, name="Va")
    with ExitStack() as sub:
        P.stack = sub
        cosT = P.sb([128, 16, 64], F32, name="cosT")
        sinT = P.sb([128, 16, 64], F32, name="sinT")
        P.dma(cosT[:], rope_d[0].rearrange("(t p) j -> p t j", p=128), W=[cosT])
        P.dma(sinT[:], rope_d[1].rearrange("(t p) j -> p t j", p=128), W=[sinT])
        wqk = P.sb([128, 10, 128], F32, name="wqk")
        P.dma(wqk[:, 0:8, :], Wd["at_qnorm"][l:l + 1, :].rearrange("o (h d) -> o h d", h=1).broadcast_to([128, 8, 128]), W=[wqk])
        P.dma(wqk[:, 8:10, :], Wd["at_knorm"][l:l + 1, :].rearrange("o (h d) -> o h d", h=1).broadcast_to([128, 2, 128]), W=[wqk])
        NB = 2
        sqt = [P.sb([128, 1280], F32, name="sqt%d" % i) for i in range(NB)]
        xn = [P.sb([128, 1280], F32, name="xn%d" % i) for i in range(NB)]
        ss = [P.sb([128, 10], F32, name="ss%d" % i) for i in range(NB)]
        t1 = [P.sb([128, 10, 64], F32, name="t1%d" % i) for i in range(NB)]
        t2 = [P.sb([128, 10, 64], F32, name="t2%d" % i) for i in range(NB)]
        xr = [P.sb([128, 1280], BF16, name="xr%d" % i) for i in range(NB)]
        pQ = [P.ps([128, 1536], name="pQ%d" % i) for i in range(NB)]
        pT = P.ps([128, 2048], BF16, name="pTa")

        def prep(t):
            b = t % NB
            tc = slice(t * 128, (t + 1) * 128)
            pQ_, sqt_, xn_, ss_, t1_, t2_, xr_ = pQ[b], sqt[b], xn[b], ss[b], t1[b], t2[b], xr[b]
            for j in range(3):
                for kc in range(8):
                    P.mm(pQ_[:, j * 512:(j + 1) * 512], hT[:, kc, tc], Wa[:, kc, j * 512:(j + 1) * 512],
                         start=(kc == 0), stop=(kc == 7), R=[hT.res[t], Wa], W=[pQ_])
            P.cp(Va[:, t, :], pQ_[:, 1280:1536], R=[pQ_], W=[Va.res[t]], E="act")
            P.act(sqt_[:], pQ_[:, 0:1280], AF.Square, R=[pQ_], W=[sqt_])
            P.op("dve", lambda e: e.tensor_reduce(ss_[:], sqt_[:].rearrange("p (h d) -> p h d", h=10), AX.X, ALU.add), R=[sqt_], W=[ss_])
            P.act(ss_[:], ss_[:], AF.Sqrt, R=[ss_], W=[ss_], scale=1.0 / 128.0, bias=RMS_EPS)
            P.op("dve", lambda e: e.reciprocal(ss_[:], ss_[:]), R=[ss_], W=[ss_])
            yield
            P.tt(xn_[:].rearrange("p (h d) -> p h d", h=10), pQ_[:, 0:1280].rearrange("p (h d) -> p h d", h=10),
                 ss_[:].rearrange("p (h o) -> p h o", o=1).broadcast_to([128, 10, 128]), ALU.mult, R=[pQ_, ss_], W=[xn_])
            P.tt(xn_[:], xn_[:], wqk[:].rearrange("p h d -> p (h d)"), ALU.mult, R=[xn_, wqk], W=[xn_])
            if t >= TCTX:
                tl = t - TCTX
                xv = xn_[:].rearrange("p (h j two) -> p h j two", h=10, two=2)
                xo = xr_[:].rearrange("p (h j two) -> p h j two", h=10, two=2)
                cb = cosT[:, tl:tl + 1, :].broadcast_to([128, 10, 64])
                sb_ = sinT[:, tl:tl + 1, :].broadcast_to([128, 10, 64])
                P.tt(t1_[:], xv[:, :, :, 0], cb, ALU.mult, R=[xn_, cosT], W=[t1_])
                P.tt(t2_[:], xv[:, :, :, 1], sb_, ALU.mult, R=[xn_, sinT], W=[t2_])
                P.tt(xo[:, :, :, 0], t1_[:], t2_[:], ALU.subtract, R=[t1_, t2_], W=[xr_])
                P.tt(t1_[:], xv[:, :, :, 0], sb_, ALU.mult, R=[xn_, sinT], W=[t1_])
                P.tt(t2_[:], xv[:, :, :, 1], cb, ALU.mult, R=[xn_, cosT], W=[t2_])
                P.tt(xo[:, :, :, 1], t1_[:], t2_[:], ALU.add, R=[t1_, t2_], W=[xr_])
            else:
                P.cp(xr_[:], xn_[:], R=[xn_], W=[xr_], E="act")
            yield
            for j in range(10):
                P.tr(pT[:, j * 128:(j + 1) * 128], xr_[:, j * 128:(j + 1) * 128], identB, R=[xr_, cstb], W=[pT])
            P.cp(QT[:, :, tc], pT[:, 0:1024].rearrange("p (h k) -> p h k", h=8), R=[pT], W=[QT.res[t]])
            P.cp(KTa[:, :, tc], pT[:, 1024:1280].rearrange("p (h k) -> p h k", h=2), R=[pT], W=[KTa.res[t]], E="act")
            yield

        _pipeline(list(range(NT)), prep)
        P.barrier()
    with ExitStack() as sub:
        P.stack = sub
        NS = 3
        pS = [P.ps([128, 512], name="pSa%d" % i) for i in range(NS)]
        pO = [P.ps([128, 512], name="pOa%d" % i) for i in range(2)]
        pL = [P.ps([128, 512], name="pLa%d" % i) for i in range(2)]
        Pt = [P.sb([128, 512], BF16, name="Pt%d" % i) for i in range(NS)]
        rl = [P.sb([128, 512], F32, name="rl%d" % i) for i in range(2)]
        oa = [P.sb([128, 512], BF16, name="oa%d" % i) for i in range(2)]
        blocks = [(0, 256, [0, 1])] + [(256 + qb * 512, 512, list(range(NT))) for qb in range(4)]
        sc = float(128 ** -0.5)
        steps = []
        for h in range(8):
            for bi, (q0, nq, kts) in enumerate(blocks):
                for i, kt in enumerate(kts):
                    steps.append((h, bi, q0, nq, kt, i == 0, i == len(kts) - 1))
        jobn = {}
        for (h, bi, *_r) in steps:
            jobn.setdefault((h, bi), len(jobn))

        def s_mm(n):
            h, bi, q0, nq, kt, first, lastk = steps[n]
            kv = h // 4
            qres = [QT.res[t] for t in range(q0 // 128, (q0 + nq) // 128)]
            P.mm(pS[n % NS][:, 0:nq], KTa[:, kv, kt * 128:(kt + 1) * 128], QT[:, h, q0:q0 + nq], R=[KTa.res[kt]] + qres, W=[pS[n % NS]])

        s_mm(0)
        for n in range(len(steps)):
            h, bi, q0, nq, kt, first, lastk = steps[n]
            kv = h // 4
            jb = jobn[(h, bi)] % 2
            if n + 1 < len(steps):
                s_mm(n + 1)
            ps, pt = pS[n % NS], Pt[n % NS]
            P.act(pt[:, 0:nq], ps[:, 0:nq], AF.Exp, R=[ps], W=[pt], scale=sc)
            P.mm(pO[jb][:, 0:nq], Va[:, kt, kv * 128:(kv + 1) * 128], pt[:, 0:nq], start=first, stop=lastk, R=[Va.res[kt], pt], W=[pO[jb]])
            P.mm(pL[jb][:, 0:nq], onesB, pt[:, 0:nq], start=first, stop=lastk, R=[cstb, pt], W=[pL[jb]])
            if lastk:
                P.op("dve", lambda e: e.reciprocal(rl[jb][:, 0:nq], pL[jb][:, 0:nq]), R=[pL[jb]], W=[rl[jb]])
                P.tt(oa[jb][:, 0:nq], pO[jb][:, 0:nq], rl[jb][:, 0:nq], ALU.mult, R=[pO[jb], rl[jb]], W=[oa[jb]])
                P.dma(oatT[:, h, q0:q0 + nq], oa[jb][:, 0:nq], R=[oa[jb]], W=[P.dres("oatT", (h, q0))])
        P.barrier()


def phase_mb(P, nc, l, s, Wd, cst, cstb, hT, mb_xs, mb_yf, ombT, sq):
    identB = cstb[:, 0:128]
    identF = cst[:, C_ID:C_ID + 128]
    onesF = cst[:, C_ONES:C_ONES + 128]
    Wl = Wd["w_in"][l].rearrange("(kc p) n -> p kc n", p=128)
    with ExitStack() as ph:
        P.stack = ph
        BT = P.sb([128, 4, TOK], BF16, name="BT")
        CT = P.sb([128, 4, TOK], BF16, name="CT")
        with ExitStack() as sub:
            P.stack = sub
            Wx = P.sb([128, 8, 3072], BF16, nres=24, name="Wx")
            _loadw_cols(P, Wx, Wl[:, :, O_MX:O_MX + 3072], 3072, 256)
            cwr = P.sb([120, 128], F32, name="cwr")
            P.dma(cwr[:], Wd["mb_conv_w"][l].rearrange("j (cc p) -> (j cc) p", p=128), W=[cwr])
            cbr = P.sb([24, 128], F32, name="cbr")
            P.dma(cbr[:], Wd["mb_conv_b"][l].rearrange("(cc p) -> cc p", p=128), W=[cbr])
            cw = P.sb([128, 120], F32, name="cw")
            cbias = P.sb([128, 24], F32, name="cbias")
            pX = [P.ps([128, 512], name="pX%d" % i) for i in range(2)]
            pTt = P.ps([128, 1024], BF16, name="pTt")
            pW_ = P.ps([128, 512], name="pW_")
            P.tr(pW_[:, 0:120], cwr[:], identF[0:120, 0:120], R=[cwr, cst], W=[pW_])
            P.cp(cw[:], pW_[:, 0:120], R=[pW_], W=[cw])
            P.tr(pW_[:, 128:152], cbr[:], identF[0:24, 0:24], R=[cbr, cst], W=[pW_])
            P.cp(cbias[:], pW_[:, 128:152], R=[pW_], W=[cbias])
            xb = [P.sb([128, 2052], BF16, name="xb%d" % i) for i in range(2)]
            dgs = [P.sb([128, 5, 128], BF16, name="dg%d" % i) for i in range(2)]
            pCv = [P.ps([128, 512], name="pCv%d" % i) for i in range(2)]
            ub = P.sb([128, 2048], BF16, name="ub")
            stg = [P.sb([128, 8, 128], BF16, name="stg%d" % i) for i in range(2)]
            for xb_ in xb:
                P.op("pool", lambda e, xb_=xb_: e.memset(xb_[:], 0.0), W=[xb_])
            k = 0
            kx = 0
            cw3 = cw[:].rearrange("p (j c) -> p j c", j=5)
            for cc in range(24):
                dg = dgs[cc % 2]
                P.tt(dg[:], identB.rearrange("p (o c) -> p o c", o=1).broadcast_to([128, 5, 128]),
                     cw3[:, :, cc:cc + 1].broadcast_to([128, 5, 128]), ALU.mult, R=[cstb, cw], W=[dg])
                for (t0, ntile) in ((0, TCTX), (TCTX, NT - TCTX)):
                    N = ntile * 128
                    xb_ = xb[kx % 2]
                    kx += 1
                    nblk = (N + 511) // 512
                    for b in range(nblk):
                        nb = min(512, N - b * 512)
                        c0 = t0 * 128 + b * 512
                        tr_ = [hT.res[t] for t in range(c0 // 128, (c0 + nb) // 128)]
                        for kc in range(8):
                            P.mm(pX[b % 2][:, 0:nb], Wx[:, kc, cc * 128:(cc + 1) * 128], hT[:, kc, c0:c0 + nb],
                                 start=(kc == 0), stop=(kc == 7), R=_cr(Wx, cc * 128, (cc + 1) * 128) + tr_, W=[pX[b % 2]])
                        P.cp(xb_[:, 2 + b * 512:2 + b * 512 + nb], pX[b % 2][:, 0:nb], R=[pX[b % 2]], W=[xb_], E=("act" if b % 2 else "dve"))
                    P.op("pool", lambda e, xb_=xb_, N=N: e.memset(xb_[:, N + 2:N + 4], 0.0), W=[xb_])
                    if cc < 16:
                        dst, dres_ = ub, ub
                        off = 0
                    elif cc < 20:
                        dst, dres_ = BT[:, cc - 16, :], BT
                        off = t0 * 128
                    else:
                        dst, dres_ = CT[:, cc - 20, :], CT
                        off = t0 * 128
                    for b in range(nblk):
                        nb = min(512, N - b * 512)
                        for j in range(5):
                            P.mm(pCv[b % 2][:, 0:nb], dg[:, j, :], xb_[:, b * 512 + j:b * 512 + j + nb], start=(j == 0), stop=(j == 4),
                                 R=[dg, xb_], W=[pCv[b % 2]])
                        P.act(dst[:, off + b * 512:off + b * 512 + nb], pCv[b % 2][:, 0:nb], AF.Silu, R=[pCv[b % 2], cbias], W=[dres_],
                              bias=cbias[:, cc:cc + 1])
                    if cc < 16:
                        for g0 in range(0, ntile, 8):
                            ng = min(8, ntile - g0)
                            for j in range(ng):
                                P.tr(pTt[:, j * 128:(j + 1) * 128], ub[:, (g0 + j) * 128:(g0 + j + 1) * 128], identB, R=[ub, cstb], W=[pTt])
                            st = stg[k % 2]
                            k += 1
                            P.cp(st[:, 0:ng, :], pTt[:, 0:ng * 128].rearrange("p (j c) -> p j c", j=ng), R=[pTt], W=[st],
                                 E=("act" if k % 2 else "dve"))
                            ta = t0 + g0
                            P.dma(mb_xs[ta:ta + ng, :, cc * 128:(cc + 1) * 128].rearrange("t p c -> p t c"), st[:, 0:ng, :],
                                  R=[st], W=[P.dres("mb_xs", t) for t in range(ta, ta + ng)])
            P.barrier()
        with ExitStack() as sub:
            P.stack = sub
            Wz = P.sb([128, 8, 2112], BF16, nres=8, name="Wz")
            for i, kc in enumerate(range(0, 8, 2)):
                P.dma(Wz[:, kc:kc + 2, 2048:2112], Wl[:, kc:kc + 2, O_MDT:O_MDT + 64], W=[Wz.res[4 + i]], q="pool")
            for i, kc in enumerate(range(0, 8, 2)):
                P.dma(Wz[:, kc:kc + 2, 0:2048], Wl[:, kc:kc + 2, O_MZ:O_MZ + 2048], W=[Wz.res[i]], q="pool")
            dtb = P.sb([128, 64], F32, name="dtb")
            abc = P.sb([128, 64], F32, name="abc")
            dbc = P.sb([128, 32], F32, name="dbc")
            nrm = P.sb([128, 2048], F32, name="nrm")
            P.dma(dtb[:], Wd["mb_dt_bias"][l:l + 1].rearrange("o d h -> o (d h)").broadcast_to([128, 64]), W=[dtb])
            P.dma(abc[:], Wd["mb_a_log"][l:l + 1].rearrange("o d h -> o (d h)").broadcast_to([128, 64]), W=[abc])
            P.act(abc[:], abc[:], AF.Exp, R=[abc], W=[abc])
            P.ts(abc[:], abc[:], -1.0, None, ALU.mult, R=[abc], W=[abc])
            P.dma(dbc[:], Wd["mb_d"][l:l + 1, :].broadcast_to([128, 32]), W=[dbc])
            P.dma(nrm[:], Wd["mb_norm"][l:l + 1, :].broadcast_to([128, 2048]), W=[nrm])
            xs = [P.sb([128, 2048], BF16, name="xs%d" % i) for i in range(2)]
            xdt = [P.sb([128, 2048], BF16, name="xdt%d" % i) for i in range(2)]
            xdtw = P.sb([128, 2048], BF16, name="xdtw")
            oT = xdtw
            dtv = [P.sb([128, 64], F32, name="dtv%d" % i) for i in range(2)]
            dA = [P.sb([128, 32], F32, name="dA%d" % i) for i in range(2)]
            e3 = [P.sb([128, 96], F32, name="e3%d" % i) for i in range(2)]
            cbm = [P.sb([128, 512], F32, name="cbm%d" % i) for i in range(2)]
            Btk = [P.sb([128, 512], BF16, name="Btk%d" % i) for i in range(2)]
            rh = [P.sb([128, 512], F32, name="rh%d" % i) for i in range(2)]
            es = [P.sb([128, 512], F32, name="es%d" % i) for i in range(2)]
            MT = [P.sb([128, 32, 128], BF16, nres=8, name="MT%d" % i) for i in range(2)]
            tmp = [P.sb([128, 512], F32, name="tmpy%d" % i) for i in range(2)]
            ysb = P.sb([128, 2048], F32, nres=4, name="ysb")
            ST = P.sb([128, 2048], F32, nres=4, name="ST")
            STb = P.sb([128, 2048], BF16, nres=4, name="STb")
            yfl = P.sb([128, 2048], F32, name="yfl")
            ss4 = P.sb([128, 4], F32, name="ss4")
            pM = P.ps([128, 512], name="pM")
            pCB = P.ps([128, 512], name="pCB")
            pCBb = pCB[:].bitcast(BF16)
            pSG = [P.ps([128, 512], name="pSG%d" % i) for i in range(2)]
            pY = [P.ps([128, 512], name="pY%d" % i) for i in range(2)]
            pI = [P.ps([128, 512], name="pI%d" % i) for i in range(2)]
            p4 = [pY[0], pY[1], pI[0], pI[1]]
            v3 = lambda ap, h: ap.rearrange("p (h q) -> p h q", h=h)
            col = lambda ap: ap.rearrange("p (h o) -> p h o", o=1)

            def prologue(t, d, b):
                tc = slice(t * 128, (t + 1) * 128)
                TRI = cst[:, (C_LE if d == 0 else C_GE):(C_LE if d == 0 else C_GE) + 128]
                STR = cst[:, (C_GT if d == 0 else C_LT):(C_GT if d == 0 else C_LT) + 128]
                xs_, xdt_, dtv_, dA_, e3_, cbm_, Btk_, MT_ = xs[b], xdt[b], dtv[b], dA[b], e3[b], cbm[b], Btk[b], MT[b]
                P.dma(xs_[:], mb_xs[t], R=[P.dres("mb_xs", t)], W=[xs_])
                for kc in range(8):
                    P.mm(pM[:, 0:64], hT[:, kc, tc], Wz[:, kc, 2048:2112], start=(kc == 0), stop=(kc == 7), R=[hT.res[t], Wz.res[4:8]], W=[pM])
                P.tt(dtv_[:], pM[:, 0:64], dtb[:], ALU.add, R=[pM, dtb], W=[dtv_])
                P.act(dtv_[:], dtv_[:], AF.Exp, R=[dtv_], W=[dtv_])
                P.act(dtv_[:], dtv_[:], AF.Ln, R=[dtv_], W=[dtv_], bias=1.0)
                P.tt(dA_[:], dtv_[:, d * 32:(d + 1) * 32], abc[:, d * 32:(d + 1) * 32], ALU.mult, R=[dtv_, abc], W=[dA_])
                for g in range(4):
                    P.mm(pCB[:, g * 128:(g + 1) * 128], BT[:, g, tc], CT[:, g, tc], R=[BT, CT], W=[pCB])
                yield
                P.mm(pM[:, 64:96], TRI, dA_[:], R=[cst, dA_], W=[pM])
                P.mm(pM[:, 96:128], STR, dA_[:], R=[cst, dA_], W=[pM])
                P.mm(pM[:, 128:160], onesF, dA_[:], R=[cst, dA_], W=[pM])
                P.act(e3_[:], pM[:, 64:160], AF.Exp, R=[pM], W=[e3_])
                P.tt(v3(cbm_[:], 4), v3(pCB[:], 4), TRI.rearrange("p (o t) -> p o t", o=1).broadcast_to([128, 4, 128]), ALU.mult,
                     R=[pCB, cst], W=[cbm_])
                for g in range(4):
                    P.tr(pCBb[:, g * 128:(g + 1) * 128], BT[:, g, tc], identB, R=[BT, cstb], W=[pCB])
                P.cp(Btk_[:], pCBb[:, 0:512], R=[pCB], W=[Btk_], E="act")
                P.tt(v3(xdt_[:], 32), v3(xs_[:], 32), col(dtv_[:, d * 32:(d + 1) * 32]).broadcast_to([128, 32, 64]), ALU.mult,
                     R=[xs_, dtv_], W=[xdt_])
                yield
                TRI3 = TRI.rearrange("p (o t) -> p o t", o=1)
                for step in range(10):
                    if step < 8:
                        hb = step
                        i = hb % 2
                        P.tt(v3(rh[i][:], 4), TRI3.broadcast_to([128, 4, 128]),
                             col(dA_[:, hb * 4:hb * 4 + 4]).broadcast_to([128, 4, 128]), ALU.mult, R=[cst, dA_], W=[rh[i]])
                        P.mm(pSG[i][:], STR, rh[i][:], R=[cst, rh[i]], W=[pSG[i]])
                    if 1 <= step <= 8:
                        i = (step - 1) % 2
                        P.act(es[i][:], pSG[i][:], AF.Exp, R=[pSG[i]], W=[es[i]])
                    if 2 <= step <= 9:
                        hb = step - 2
                        i = hb % 2
                        g = hb // 2
                        P.tt(MT_[:, hb * 4:hb * 4 + 4, :], v3(es[i][:], 4),
                             cbm_[:, g * 128:(g + 1) * 128].rearrange("p (o t) -> p o t", o=1).broadcast_to([128, 4, 128]), ALU.mult,
                             R=[es[i], cbm_], W=[MT_.res[hb]])
                    if step % 2 == 1:
                        yield

            def scan(t, d, b):
                tc = slice(t * 128, (t + 1) * 128)
                xs_, xdt_, e3_, Btk_, MT_ = xs[b], xdt[b], e3[b], Btk[b], MT[b]
                if d == 1:
                    P.dma(yfl[:], mb_yf[t], R=[P.dres("mb_yf", t)], W=[yfl])
                P.tt(v3(xdtw[:], 32), v3(xdt_[:], 32), col(e3_[:, 32:64]).broadcast_to([128, 32, 64]), ALU.mult,
                     R=[xdt_, e3_], W=[xdtw])
                for g in range(4):
                    gs = slice(g * 512, (g + 1) * 512)
                    pY_, pI_, tmp_ = pY[g % 2], pI[g % 2], tmp[g % 2]
                    for hh in range(8):
                        h = g * 8 + hh
                        P.mm(pY_[:, hh * 64:(hh + 1) * 64], MT_[:, h, :], xdt_[:, h * 64:(h + 1) * 64], R=[MT_.res[h // 4], xdt_], W=[pY_])
                    P.mm(pI_[:], CT[:, g, tc], STb[:, gs], R=[CT, STb.res[g]], W=[pI_])
                    P.tt(v3(tmp_[:], 8), v3(pI_[:], 8), col(e3_[:, g * 8:(g + 1) * 8]).broadcast_to([128, 8, 64]), ALU.mult,
                         R=[pI_, e3_], W=[tmp_])
                    P.tt(ysb[:, gs], tmp_[:], pY_[:], ALU.add, R=[tmp_, pY_], W=[ysb.res[g]])
                    if g % 2 == 1:
                        yield
                for g in range(4):
                    gs = slice(g * 512, (g + 1) * 512)
                    pI_ = pI[g % 2]
                    P.mm(pI_[:], Btk_[:, g * 128:(g + 1) * 128], xdtw[:, gs], R=[Btk_, xdtw], W=[pI_])
                    P.tt(v3(ST[:, gs], 8), v3(ST[:, gs], 8), col(e3_[:, 64 + g * 8:64 + (g + 1) * 8]).broadcast_to([128, 8, 64]), ALU.mult,
                         R=[ST.res[g], e3_], W=[ST.res[g]])
                    P.tt(ST[:, gs], ST[:, gs], pI_[:], ALU.add, R=[ST.res[g], pI_], W=[ST.res[g]])
                    P.cp(STb[:, gs], ST[:, gs], R=[ST.res[g]], W=[STb.res[g]], E="act")
                    if g % 2 == 1:
                        yield
                if d == 0:
                    P.dma(mb_yf[t], ysb[:], R=[ysb], W=[P.dres("mb_yf", t)])
                    return
                P.tt(ysb[:], ysb[:], yfl[:], ALU.add, R=[ysb, yfl], W=[ysb])
                P.tt(v3(yfl[:], 32), v3(xs_[:], 32), col(dbc[:]).broadcast_to([128, 32, 64]), ALU.mult, R=[xs_, dbc, yfl], W=[yfl])
                P.tt(ysb[:], ysb[:], yfl[:], ALU.add, R=[ysb, yfl], W=[ysb])
                for j in range(4):
                    for kc in range(8):
                        P.mm(p4[j][:], hT[:, kc, tc], Wz[:, kc, j * 512:(j + 1) * 512], start=(kc == 0), stop=(kc == 7),
                             R=[hT.res[t], Wz.res[0:4]], W=[p4[j]])
                    P.act(yfl[:, j * 512:(j + 1) * 512], p4[j][:], AF.Silu, R=[p4[j]], W=[yfl])
                    if j % 2 == 1:
                        yield
                P.tt(ysb[:], ysb[:], yfl[:], ALU.mult, R=[ysb, yfl], W=[ysb])
                P.act(yfl[:], ysb[:], AF.Square, R=[ysb], W=[yfl])
                P.op("dve", lambda e: e.tensor_reduce(ss4[:], v3(yfl[:], 4), AX.X, ALU.add), R=[yfl], W=[ss4])
                P.act(ss4[:], ss4[:], AF.Sqrt, R=[ss4], W=[ss4], scale=1.0 / 512.0, bias=RMS_EPS)
                P.op("dve", lambda e: e.reciprocal(ss4[:], ss4[:]), R=[ss4], W=[ss4])
                yield
                for g in range(4):
                    gs = slice(g * 512, (g + 1) * 512)
                    P.stt(ysb[:, gs], ysb[:, gs], ss4[:, g:g + 1], nrm[:, gs], ALU.mult, ALU.mult, R=[ysb.res[g], ss4, nrm], W=[ysb.res[g]])
                for cc in range(16):
                    P.tr(p4[cc // 4][:, (cc % 4) * 128:(cc % 4 + 1) * 128], ysb[:, cc * 128:(cc + 1) * 128], identF, R=[ysb, cst], W=[p4[cc // 4]])
                for j in range(4):
                    P.cp(oT[:, j * 512:(j + 1) * 512], p4[j][:], R=[p4[j]], W=[oT], E=("act" if j % 2 else "dve"))
                P.dma(ombT[t], oT[:], R=[oT], W=[P.dres("ombT", t)])
                yield

            def reset_state():
                P.op("dve", lambda e: e.memset(ST[:], 0.0), W=[ST])
                P.op("pool", lambda e: e.memset(STb[:], 0.0), W=[STb])

            order = [(t, 0) for t in range(NT)] + [(t, 1) for t in list(range(TCTX - 1, -1, -1)) + list(range(NT - 1, TCTX - 1, -1))]
            reset_state()
            _interleave([prologue(order[0][0], order[0][1], 0)])
            for n, (t, d) in enumerate(order):
                nxt = None
                if n + 1 < len(order):
                    nxt = prologue(order[n + 1][0], order[n + 1][1], (n + 1) % 2)
                _interleave([scan(t, d, n % 2), nxt])
                if n + 1 < len(order) and order[n + 1][1] == 1 and d == 0:
                    reset_state()
            P.barrier()


def _layernorm(P, out, z, zr, lng, lnb, st, mv):
    for c in range(2):
        P.op("dve", lambda e: e.bn_stats(st[:, c * 6:(c + 1) * 6], z[:, c * 512:(c + 1) * 512]), R=[zr], W=[st])
    P.op("dve", lambda e: e.bn_aggr(mv[:, 0:2], st[:]), R=[st], W=[mv])
    P.act(mv[:, 2:3], mv[:, 1:2], AF.Sqrt, R=[mv], W=[mv], bias=LN_EPS)
    P.op("dve", lambda e: e.reciprocal(mv[:, 2:3], mv[:, 2:3]), R=[mv], W=[mv])
    P.ts(out, z, mv[:, 0:1], mv[:, 2:3], ALU.subtract, ALU.mult, R=[zr, mv], W=[zr])
    P.tt(out, out, lng[:], ALU.mult, R=[zr, lng], W=[zr])
    P.tt(out, out, lnb[:], ALU.add, R=[zr, lnb], W=[zr])


def _gate_bcast(P, cst, modrow_d, c0, s, pG, gbc):
    mr = P.sb([3, 1024], F32, name="mr")
    P.dma(mr[:], modrow_d[:, c0:c0 + 1024], R=[P.dres("modrow")], W=[mr])
    for ri, r in enumerate((s, 2)):
        for half in range(2):
            P.mm(pG[:, half * 512:(half + 1) * 512], cst[0:3, C_SEL + r * 128:C_SEL + (r + 1) * 128], mr[0:3, half * 512:(half + 1) * 512],
                 R=[cst, mr], W=[pG])
        P.cp(gbc[:, ri, :], pG[:], R=[pG], W=[gbc])


def phase_merge(P, nc, l, s, last, Wd, cst, cstb, hT, modA, modB, modrow_d, ohgT, oatT, ombT, ymTd, x1s, xsrc_fn, sq):
    identB = cstb[:, 0:128]
    identF = cst[:, C_ID:C_ID + 128]
    tiles = list(range(TCTX, NT)) if last else list(range(NT))
    Wl = Wd["w_in"][l].rearrange("(kc p) n -> p kc n", p=128)
    with ExitStack() as ph:
        P.stack = ph
        Wgt = P.sb([128, 8, 3072], BF16, nres=12, name="Wgt")
        Wbh = P.sb([128, 8, 1024], BF16, nres=4, name="Wbh")
        Wba = P.sb([128, 8, 1024], BF16, nres=4, name="Wba")
        Wbm = P.sb([128, 16, 1024], BF16, nres=8, name="Wbm")
        Wgs = Wl[:, :, O_GT:O_GT + 3072]

        def ldg(b):
            for i, kc in enumerate(range(0, 8, 2)):
                P.dma(Wgt[:, kc:kc + 2, b * 1024:(b + 1) * 1024], Wgs[:, kc:kc + 2, b * 1024:(b + 1) * 1024], W=[Wgt.res[b * 4 + i]], q="pool")

        ldg(0)
        _loadw(P, Wbh, Wd["w_br_hg"][l].rearrange("(kc p) n -> p kc n", p=128), 1024)
        ldg(1)
        _loadw(P, Wba, Wd["w_br_at"][l].rearrange("(kc p) n -> p kc n", p=128), 1024)
        ldg(2)
        _loadw(P, Wbm, Wd["w_br_mb"][l].rearrange("(kc p) n -> p kc n", p=128), 1024)
        oh = [P.sb([128, 1024], BF16, name="oh%d" % i) for i in range(2)]
        oa = [P.sb([128, 8, 128], BF16, name="oam%d" % i) for i in range(2)]
        om = [P.sb([128, 2048], BF16, name="om%d" % i) for i in range(2)]
        sgt = P.sb([128, 1024], F32, name="sgt")
        ym = P.sb([128, 1024], F32, name="ym")
        tmp = P.sb([128, 1024], F32, name="tmpm")
        ymb = P.sb([128, 1024], BF16, name="ymb")
        ymT = [P.sb([128, 1024], BF16, name="ymT%d" % i) for i in range(2)]
        pG = P.ps([128, 1024], name="pGm")
        pB = P.ps([128, 1024], name="pBm")
        pT = P.ps([128, 1024], BF16, name="pTm")

        def ld_ma(i):
            t = tiles[i]
            P.dma(oh[i % 2][:], ohgT[t], R=[P.dres("ohgT", t)], W=[oh[i % 2]])
            P.dma(oa[i % 2][:], oatT[:, :, t * 128:(t + 1) * 128],
                  R=[P.dres("oatT", (h, q0)) for h in range(8) for q0 in (0, 256, 768, 1280, 1792)], W=[oa[i % 2]])
            P.dma(om[i % 2][:], ombT[t], R=[P.dres("ombT", t)], W=[om[i % 2]])

        for i, t in enumerate(tiles):
            tc = slice(t * 128, (t + 1) * 128)
            oh_, oa_, om_ = oh[i % 2], oa[i % 2], om[i % 2]
            if i == 0:
                ld_ma(0)
            if i + 1 < len(tiles):
                ld_ma(i + 1)
            srcs = [(lambda kc, o=oh_: o[:, kc * 128:(kc + 1) * 128], Wbh, 8, oh_),
                    (lambda kc, o=oa_: o[:, kc, :], Wba, 8, oa_),
                    (lambda kc, o=om_: o[:, kc * 128:(kc + 1) * 128], Wbm, 16, om_)]
            for b, (of, Wb, nk, ot) in enumerate(srcs):
                for half in range(2):
                    for kc in range(8):
                        P.mm(pG[:, half * 512:(half + 1) * 512], hT[:, kc, tc], Wgt[:, kc, b * 1024 + half * 512:b * 1024 + (half + 1) * 512],
                             start=(kc == 0), stop=(kc == 7), R=[hT.res[t]] + Wgt.res[b * 4:(b + 1) * 4], W=[pG])
                P.act(sgt[:], pG[:], AF.Sigmoid, R=[pG], W=[sgt])
                for half in range(2):
                    for kc in range(nk):
                        P.mm(pB[:, half * 512:(half + 1) * 512], of(kc), Wb[:, kc, half * 512:(half + 1) * 512],
                             start=(kc == 0), stop=(kc == nk - 1), R=[ot, Wb], W=[pB])
                if b == 0:
                    P.tt(ym[:], pB[:], sgt[:], ALU.mult, R=[pB, sgt], W=[ym])
                else:
                    P.tt(tmp[:], pB[:], sgt[:], ALU.mult, R=[pB, sgt], W=[tmp])
                    if b == 1:
                        P.tt(ym[:], ym[:], tmp[:], ALU.add, R=[ym, tmp], W=[ym])
                    else:
                        P.tt(ymb[:], ym[:], tmp[:], ALU.add, R=[ym, tmp], W=[ymb])
            for kc in range(8):
                P.tr(pT[:, kc * 128:(kc + 1) * 128], ymb[:, kc * 128:(kc + 1) * 128], identB, R=[ymb, cstb], W=[pT])
            yT = ymT[i % 2]
            P.cp(yT[:], pT[:], R=[pT], W=[yT], E="act")
            P.dma(ymTd[t], yT[:], R=[yT], W=[P.dres("ymT", t)])
        P.barrier()
    with ExitStack() as ph:
        P.stack = ph
        Wo = P.sb([128, 8, 1024], BF16, nres=4, name="Wo")
        _loadw(P, Wo, Wd["w_out"][l].rearrange("(kc p) n -> p kc n", p=128), 1024)
        pW = P.ps([128, 1024], name="pWm")
        pp = [P.ps([128, 512], name="pp%d" % i) for i in range(2)]
        gbc = P.sb([128, 2, 1024], F32, name="gbc")
        _gate_bcast(P, cst, modrow_d, 2048, s, pW, gbc)
        lng = P.sb([128, 1024], F32, name="lng")
        lnb = P.sb([128, 1024], F32, name="lnb")
        P.dma(lng[:], Wd["ln1_g"][l:l + 1, :].broadcast_to([128, 1024]), W=[lng])
        P.dma(lnb[:], Wd["ln1_b"][l:l + 1, :].broadcast_to([128, 1024]), W=[lnb])
        yT = [P.sb([128, 1024], BF16, name="yTb%d" % i) for i in range(2)]
        xt = [P.sb([128, 1024], F32, name="xtb%d" % i) for i in range(2)]
        z = [P.sb([128, 1024], F32, name="zb%d" % i) for i in range(2)]
        st = P.sb([128, 12], F32, name="st")
        mv = P.sb([128, 4], F32, name="mv")

        def ld_mb(i):
            t = tiles[i]
            P.dma(yT[i % 2][:], ymTd[t], R=[P.dres("ymT", t)], W=[yT[i % 2]])
            src, sres = xsrc_fn(t)
            P.dma(xt[i % 2][:], src, R=[sres], W=[xt[i % 2]])

        for i, t in enumerate(tiles):
            tc = slice(t * 128, (t + 1) * 128)
            r = 2 if t < TCTX else s
            ri = 1 if t < TCTX else 0
            yT_, xt_, z_ = yT[i % 2], xt[i % 2], z[i % 2]
            if i == 0:
                ld_mb(0)
            if i + 1 < len(tiles):
                ld_mb(i + 1)
            for half in range(2):
                for kc in range(8):
                    P.mm(pW[:, half * 512:(half + 1) * 512], yT_[:, kc * 128:(kc + 1) * 128], Wo[:, kc, half * 512:(half + 1) * 512],
                         start=(kc == 0), stop=(kc == 7), R=[yT_, Wo], W=[pW])
            P.tt(z_[:], pW[:], gbc[:, ri, :], ALU.mult, R=[pW, gbc], W=[z_])
            P.stt(z_[:], xt_[:], float(DN_ALPHA), z_[:], ALU.mult, ALU.add, R=[xt_, z_], W=[z_])
            _layernorm(P, z_[:], z_[:], z_, lng, lnb, st, mv)
            P.dma(x1s[t], z_[:], R=[z_], W=[P.dres("x1s", t)])
            for half in range(2):
                pb = pp[half]
                for q in range(4):
                    kc = half * 4 + q
                    P.tr(pb[:, q * 128:(q + 1) * 128], z_[:, kc * 128:(kc + 1) * 128], identF, R=[z_, cst], W=[pb])
                for q in range(4):
                    kc = half * 4 + q
                    o = hT[:, kc, tc]
                    i_ = pb[:, q * 128:(q + 1) * 128]
                    sc = modB[:, 32 + kc, r:r + 1]
                    sh = modA[:, 24 + kc, r:r + 1]
                    if half == 0:
                        P.ts(o, i_, sc, sh, ALU.mult, ALU.add, R=[pb, modA, modB], W=[hT.res[t][kc]])
                    else:
                        P.act(o, i_, AF.Identity, R=[pb, modA, modB], W=[hT.res[t][kc]], bias=sh, scale=sc)
        P.barrier()


def phase_ffn(P, nc, l, s, last, Wd, cst, cstb, hT, modrow_d, x1s, xres, out_d, ffn_act):
    blocks = ([] if last else [(0, TCTX)]) + [(TCTX + 4 * i, 4) for i in range(4)]
    with ExitStack() as ph0:
        P.stack = ph0
        W2 = P.sb([128, 22, 1024], BF16, nres=11, name="W2")
        with ExitStack() as ph:
            P.stack = ph
            W1 = P.sb([128, 8, 2 * FFH], BF16, nres=44, name="W1")
            _loadw_cols(P, W1, Wd["w_ffn_in"][l].rearrange("(kc p) n -> p kc n", p=128), 2 * FFH, 256,
                        order=[x for j in range(11) for x in (j, 11 + j)])
            _loadw(P, W2, Wd["w_ffn_out"][l].rearrange("(j p) n -> p j n", p=128), 1024)
            pGt = [P.ps([128, 512], name="pGt%d" % i) for i in range(2)]
            pUp = [P.ps([128, 512], name="pUp%d" % i) for i in range(2)]
            sgu = [P.sb([128, 512], F32, name="sgu%d" % i) for i in range(2)]
            aT = P.sb([128, 22, 512], BF16, name="actT")
            for bi, (t0, ntile) in enumerate(blocks):
                N = ntile * 128
                cols = slice(t0 * 128, t0 * 128 + N)
                tr_ = [hT.res[t] for t in range(t0, t0 + ntile)]
                for j in range(22):
                    pg, pu, sg_ = pGt[j % 2], pUp[j % 2], sgu[j % 2]
                    for kc in range(8):
                        P.mm(pg[:, 0:N], W1[:, kc, j * 128:(j + 1) * 128], hT[:, kc, cols], start=(kc == 0), stop=(kc == 7),
                             R=_cr(W1, j * 128, (j + 1) * 128) + tr_, W=[pg])
                    for kc in range(8):
                        P.mm(pu[:, 0:N], W1[:, kc, FFH + j * 128:FFH + (j + 1) * 128], hT[:, kc, cols], start=(kc == 0), stop=(kc == 7),
                             R=_cr(W1, FFH + j * 128, FFH + (j + 1) * 128) + tr_, W=[pu])
                    P.act(sg_[:, 0:N], pg[:, 0:N], AF.Silu, R=[pg], W=[sg_])
                    P.tt(aT[:, j, 0:N], sg_[:, 0:N], pu[:, 0:N], ALU.mult, R=[sg_, pu], W=[aT])
                for ti in range(ntile):
                    t = t0 + ti
                    P.dma(ffn_act[t].rearrange("p (j k) -> p j k", j=22), aT[:, :, ti * 128:(ti + 1) * 128], R=[aT], W=[P.dres("ffn_act", t)])
            P.barrier()
        with ExitStack() as ph:
            P.stack = ph
            pF = [P.ps([128, 1024], name="pF%d" % i) for i in range(2)]
            gbc = P.sb([128, 2, 1024], F32, name="gbc2")
            _gate_bcast(P, cst, modrow_d, 5120, s, pF[0], gbc)
            lng = P.sb([128, 1024], F32, name="lng2")
            lnb = P.sb([128, 1024], F32, name="lnb2")
            P.dma(lng[:], Wd["ln2_g"][l:l + 1, :].broadcast_to([128, 1024]), W=[lng])
            P.dma(lnb[:], Wd["ln2_b"][l:l + 1, :].broadcast_to([128, 1024]), W=[lnb])
            aTl = [P.sb([128, 22, 128], BF16, name="aTl%d" % i) for i in range(2)]
            xt = [P.sb([128, 1024], F32, name="x1l%d" % i) for i in range(2)]
            z = [P.sb([128, 1024], F32, name="z2%d" % i) for i in range(2)]
            st = P.sb([128, 12], F32, name="st2")
            mv = P.sb([128, 4], F32, name="mv2")
            tiles = [t0 + ti for (t0, ntile) in blocks for ti in range(ntile)]
            def ld_b(i):
                t = tiles[i]
                P.dma(aTl[i % 2][:], ffn_act[t].rearrange("p (j k) -> p j k", j=22), R=[P.dres("ffn_act", t)], W=[aTl[i % 2]])
                P.dma(xt[i % 2][:], x1s[t], R=[P.dres("x1s", t)], W=[xt[i % 2]])

            ld_b(0)
            for i, t in enumerate(tiles):
                ri = 1 if t < TCTX else 0
                a_, xt_, z_, pF_ = aTl[i % 2], xt[i % 2], z[i % 2], pF[i % 2]
                if i + 1 < len(tiles):
                    ld_b(i + 1)
                for j in range(22):
                    for half in range(2):
                        P.mm(pF_[:, half * 512:(half + 1) * 512], a_[:, j, :], W2[:, j, half * 512:(half + 1) * 512],
                             start=(j == 0), stop=(j == 21), R=[a_, W2], W=[pF_])
                P.tt(z_[:], pF_[:], gbc[:, ri, :], ALU.mult, R=[pF_, gbc], W=[z_])
                P.stt(z_[:], xt_[:], float(DN_ALPHA), z_[:], ALU.mult, ALU.add, R=[xt_, z_], W=[z_])
                _layernorm(P, z_[:], z_[:], z_, lng, lnb, st, mv)
                if last:
                    P.dma(out_d[s, (t - TCTX) * 128:(t - TCTX + 1) * 128, :], z_[:], R=[z_], W=[P.dres("out", (s, t))])
                else:
                    P.dma(xres[s, t], z_[:], R=[z_], W=[P.dres("xres", (s, t))])
            P.barrier()


def kernel(**inputs):
    inp = {k: np.asarray(v) for k, v in inputs.items()}
    nc = build(DEPTH)
    consts = make_consts()
    rope = make_rope()
    maps = []
    for core in range(8):
        b0 = 2 * core
        c3 = np.stack([inp["c"][b0], inp["c"][b0 + 1], inp["c_ctx"]])
        c3T = np.ascontiguousarray(c3.reshape(3, 8, 128).transpose(2, 1, 0))
        m = {"x": np.ascontiguousarray(inp["x"][b0:b0 + 2]), "ctx": np.ascontiguousarray(inp["ctx"][b0:b0 + 2]),
             "c3T": c3T, "consts": consts, "rope": rope, "hg_lb": inp["hg_lb"]}
        for n, _ in WSHAPES:
            m[n] = inp[n]
        maps.append(m)
    res = run_bass_kernel_spmd(nc, maps, core_ids=list(range(8)))
    out = np.concatenate([np.asarray(r["out"]) for r in res.results], axis=0)
    return out.astype(np.float32)
```

```python
from contextlib import ExitStack
import numpy as np
import concourse.bass as bass
import concourse.mybir as mybir
from concourse.bass_utils import run_bass_kernel_spmd

F32 = mybir.dt.float32
BF16 = mybir.dt.bfloat16
AF = mybir.ActivationFunctionType
ALU = mybir.AluOpType
AX = mybir.AxisListType

SAME_ENGINE_SYNC = True
N_DMA_SEMS = 28

DEPTH = 4
DM = 1024
NT = 18
TCTX = 2
TOK = NT * 128
FFH = 2816
IN_W = 14912
O_HQ, O_HFF, O_HFB, O_HI, O_HG = 0, 1024, 2048, 3072, 4096
O_AQ, O_AK, O_AV = 5120, 6144, 6400
O_MZ, O_MX, O_MDT, O_GT = 6656, 8704, 11776, 11840
DN_ALPHA = (2 * DEPTH) ** 0.25
LN_EPS = 1e-5
RMS_EPS = 1e-6

C_ID, C_ONES, C_LE, C_GE, C_GT, C_LT, C_HDF, C_HDB, C_HMF, C_HMB, C_SEL, C_CI = (
    0, 128, 256, 384, 512, 640, 768, 896, 1024, 1152, 1280, 1664)
NCONST = 1664 + 4


def make_consts():
    a = np.arange(128)[:, None]
    b = np.arange(128)[None, :]
    bd = (a // 32) == (b // 32)
    blocks = [a == b, np.ones((128, 128), bool), a <= b, a >= b, a > b, a < b,
              bd & (a > b), bd & (a < b), bd & (a <= b), bd & (a >= b)]
    sel = np.zeros((128, 3 * 128), bool)
    for r in range(3):
        sel[r, r * 128:(r + 1) * 128] = True
    ci = (a // 32) == np.arange(4)[None, :]
    return np.concatenate(blocks + [sel, ci], axis=1).astype(np.float32)


def make_rope():
    rows = 2048 // 64
    row, col = np.meshgrid(np.arange(rows, dtype=np.float32), np.arange(64, dtype=np.float32), indexing="ij")
    n_pairs = 32
    inv_freq = (np.float32(10000.0) ** (-np.arange(n_pairs, dtype=np.float32) / np.float32(n_pairs))).astype(np.float32)
    ang = np.concatenate([row.reshape(-1, 1) * inv_freq, col.reshape(-1, 1) * inv_freq], axis=-1).astype(np.float32)
    return np.stack([np.cos(ang), np.sin(ang)]).astype(np.float32)


class Res:
    __slots__ = ("w", "r")

    def __init__(self):
        self.w = None
        self.r = {}


class Tile:
    def __init__(self, t, nres=1):
        self.t = t
        self.res = [Res() for _ in range(nres)]

    def __getitem__(self, k):
        return self.t[k]

    @property
    def r(self):
        return self.res[0]


def _res(items):
    out = []
    for it in items:
        if isinstance(it, Tile):
            out.extend(it.res)
        elif isinstance(it, Res):
            out.append(it)
        elif it is None:
            pass
        else:
            out.extend(_res(it))
    return out


class Prog:
    def __init__(self, nc, stack):
        self.nc = nc
        self.eng = {"pe": nc.tensor, "act": nc.scalar, "dve": nc.vector, "pool": nc.gpsimd, "sp": nc.sync}
        self.sem = {}
        for e in self.eng:
            self.sem[e] = stack.enter_context(nc.semaphore("s_" + e))
        self.dma_sems = {}
        for q in ("sp", "act", "pool"):
            self.dma_sems[q] = [("d", q, i) for i in range(N_DMA_SEMS)]
            for k in self.dma_sems[q]:
                self.sem[k] = stack.enter_context(nc.semaphore("d_%s_%d" % (q, k[2])))
        self.count = {k: 0 for k in self.sem}
        self.known = {e: {} for e in self.eng}
        self.vc = {}
        self.dma_rr = {q: 0 for q in self.dma_sems}
        self.n_instr = 0
        self.n_wait = 0
        self.stack = None
        self._dres = {}
        self._uid = 0

    def sb(self, shape, dtype, nres=1, name=None):
        self._uid += 1
        t = self.stack.enter_context(self.nc.sbuf_tensor("%s_%d" % (name or "t", self._uid), list(shape), dtype))
        return Tile(t, nres)

    def ps(self, shape, dtype=F32, name=None):
        self._uid += 1
        t = self.stack.enter_context(self.nc.psum_tensor("%s_%d" % (name or "p", self._uid), list(shape), dtype))
        return Tile(t)

    def dres(self, name, idx=0):
        k = (name, idx)
        if k not in self._dres:
            self._dres[k] = Res()
        return self._dres[k]

    def _deps(self, E, reads, writes):
        deps = {}
        for r in reads:
            if r.w is not None and deps.get(r.w[0], 0) < r.w[1]:
                deps[r.w[0]] = r.w[1]
        for w in writes:
            if w.w is not None and deps.get(w.w[0], 0) < w.w[1]:
                deps[w.w[0]] = w.w[1]
            for k, v in w.r.items():
                if deps.get(k, 0) < v:
                    deps[k] = v
        kn = self.known[E]
        out = []
        for k, v in deps.items():
            if k == E and (E == "pe" or E == "sp" or not SAME_ENGINE_SYNC):
                continue
            if kn.get(k, 0) >= v:
                continue
            out.append((k, v))
        return out

    def _wait(self, E, waits):
        eng = self.eng[E]
        kn = self.known[E]
        for k, v in waits:
            eng.wait_ge(self.sem[k], v)
            self.n_wait += 1
            snap = self.vc.get((k, v))
            if snap is not None:
                for kk, vv in snap.items():
                    if kn.get(kk, 0) < vv:
                        kn[kk] = vv
            if kn.get(k, 0) < v:
                kn[k] = v

    def _finish(self, E, ev, ins, inc, reads, writes):
        ins.then_inc(self.sem[ev[0]], inc)
        snap = dict(self.known[E])
        snap[ev[0]] = ev[1]
        self.vc[ev] = snap
        k, v = ev
        for r in reads:
            if r.r.get(k, 0) < v:
                r.r[k] = v
        for w in writes:
            w.w = ev
            w.r = {}
        self.n_instr += 1

    def op(self, E, fn, R=(), W=()):
        reads = _res(R)
        writes = _res(W)
        self._wait(E, self._deps(E, reads, writes))
        ins = fn(self.eng[E])
        self.count[E] += 1
        ev = (E, self.count[E])
        self._finish(E, ev, ins, 1, reads, writes)

    def dma(self, out, in_, R=(), W=(), q="sp", **kw):
        reads = _res(R)
        writes = _res(W)
        self._wait(q, self._deps(q, reads, writes))
        key = self.dma_sems[q][self.dma_rr[q] % N_DMA_SEMS]
        self.dma_rr[q] += 1
        if self.count[key] > 0 and self.known[q].get(key, 0) < self.count[key]:
            self._wait(q, [(key, self.count[key])])
        ins = self.eng[q].dma_start(out=out, in_=in_, **kw)
        self.count[key] += 16
        ev = (key, self.count[key])
        self._finish(q, ev, ins, 16, reads, writes)

    def barrier(self):
        targets = [(k, v) for k, v in self.count.items() if v > 0]
        for E in self.eng:
            waits = [(k, v) for k, v in targets if self.known[E].get(k, 0) < v and not (k == E and E == "pe")]
            self._wait(E, waits)

    def wait_all_on(self, E="sp"):
        waits = [(k, v) for k, v in self.count.items() if v > 0 and k != E and self.known[E].get(k, 0) < v]
        self._wait(E, waits)

    def mm(self, out, lhsT, rhs, start=True, stop=True, R=(), W=()):
        self.op("pe", lambda e: e.matmul(out, lhsT, rhs, start=start, stop=stop), R, W)

    def tr(self, out, in_, ident, R=(), W=()):
        self.op("pe", lambda e: e.transpose(out, in_, ident), R, W)

    def act(self, out, in_, func, R=(), W=(), bias=0.0, scale=1.0, accum_out=None):
        if accum_out is None:
            self.op("act", lambda e: e.activation(out, in_, func, bias=bias, scale=scale), R, W)
        else:
            self.op("act", lambda e: e.activation(out, in_, func, bias=bias, scale=scale, accum_out=accum_out), R, W)

    def tt(self, out, in0, in1, op, R=(), W=(), E="dve"):
        self.op(E, lambda e: e.tensor_tensor(out, in0, in1, op), R, W)

    def ts(self, out, in0, s1, s2, op0, op1=None, R=(), W=(), E="dve"):
        if op1 is None:
            self.op(E, lambda e: e.tensor_scalar(out, in0, s1, None, op0), R, W)
        else:
            self.op(E, lambda e: e.tensor_scalar(out, in0, s1, s2, op0, op1), R, W)

    def stt(self, out, in0, scalar, in1, op0, op1, R=(), W=()):
        self.op("dve", lambda e: e.scalar_tensor_tensor(out, in0, scalar, in1, op0, op1), R, W)

    def cp(self, out, in_, R=(), W=(), E="dve"):
        if E == "act":
            self.op("act", lambda e: e.activation(out, in_, AF.Copy), R, W)
        else:
            self.op(E, lambda e: e.tensor_copy(out, in_), R, W)


WSHAPES = [
    ("w_mod", (DM, 6 * DM)), ("b_mod", (6 * DM,)), ("w_in", (DM, IN_W)), ("hg_gnorm", (128,)),
    ("at_qnorm", (128,)), ("at_knorm", (128,)), ("mb_conv_w", (5, 3072)), ("mb_conv_b", (3072,)),
    ("mb_dt_bias", (2, 32)), ("mb_a_log", (2, 32)), ("mb_d", (32,)), ("mb_norm", (2048,)),
    ("w_br_hg", (1024, DM)), ("w_br_at", (1024, DM)), ("w_br_mb", (2048, DM)), ("w_out", (DM, DM)),
    ("ln1_g", (DM,)), ("ln1_b", (DM,)), ("w_ffn_in", (DM, 2 * FFH)), ("w_ffn_out", (FFH, DM)),
    ("ln2_g", (DM,)), ("ln2_b", (DM,)),
]


def build(depth=DEPTH, dbg=None):
    dbg = dbg or {}
    dump = dbg.get("dump", set())
    phases = dbg.get("phases", {"p0", "hg", "at", "mb", "merge", "ffn"})
    seqs = dbg.get("seqs", [0, 1])
    nc = bass.Bass("TRN2", target_bir_lowering=False)

    def din(name, shape):
        return nc.dram_tensor(name, list(shape), F32, kind="ExternalInput").ap()

    x_in = din("x", [2, 2048, DM])
    ctx_in = din("ctx", [2, 256, DM])
    c3T = din("c3T", [128, 8, 3])
    consts_d = din("consts", [128, NCONST])
    rope_d = din("rope", [2, 2048, 64])
    hg_lb_d = din("hg_lb", [4, 2, 1024])
    Wd = {n: din(n, (depth,) + s) for n, s in WSHAPES}
    out_d = nc.dram_tensor("out", [2, 2048, DM], F32, kind="ExternalOutput").ap()

    def dscr(name, shape, dt):
        kind = "ExternalOutput" if name in dump else "Internal"
        return nc.dram_tensor(name, list(shape), dt, kind=kind).ap()

    xres = dscr("xres", [2, NT, 128, DM], F32)
    x1s = dscr("x1s", [NT, 128, DM], F32)
    hg_of = dscr("hg_of", [NT, 128, 1024], F32)
    ohgT = dscr("ohgT", [NT, 128, 1024], BF16)
    oatT = dscr("oatT", [128, 8, TOK], BF16)
    ombT = dscr("ombT", [NT, 128, 2048], BF16)
    mb_xs = dscr("mb_xs", [NT, 128, 2048], BF16)
    mb_yf = dscr("mb_yf", [NT, 128, 2048], F32)
    ymTd = dscr("ymT", [NT, 128, 1024], BF16)
    lbs_d = dscr("lbs", [4, 2, 1024], F32)
    modrow_d = dscr("modrow", [3, 6 * DM], F32)
    ffn_act = dscr("ffn_act", [NT, 128, 22 * 128], BF16)

    with ExitStack() as top:
        P = Prog(nc, top)
        P.stack = top
        cst = P.sb([128, NCONST], F32, name="cst")
        P.dma(cst[:], consts_d, W=[cst])
        cstb = P.sb([128, 256], BF16, name="cstb")
        P.cp(cstb[:], cst[:, 0:256], R=[cst], W=[cstb])
        identF = cst[:, C_ID:C_ID + 128]
        onesF = cst[:, C_ONES:C_ONES + 128]
        identB = cstb[:, 0:128]
        onesB = cstb[:, 128:256]
        scT = P.sb([128, 8, 3], F32, name="scT")
        P.dma(scT[:], c3T, W=[scT])
        P.act(scT[:], scT[:], AF.Silu, R=[scT], W=[scT])
        hT = P.sb([128, 8, TOK], BF16, nres=NT, name="hT")

        def xsrc(l, s, t):
            if l == 0:
                return (ctx_in[s, t * 128:(t + 1) * 128, :] if t < TCTX else x_in[s, (t - TCTX) * 128:(t - TCTX + 1) * 128, :]), None
            return xres[s, t], P.dres("xres", (s, t))

        with ExitStack() as ph:
            P.stack = ph
            e = P.sb([128, 4, 16], F32, name="lb_e")
            P.dma(e[:].rearrange("p l (d j) -> p l d j", d=2),
                  hg_lb_d.rearrange("l d (p j) -> p l d j", p=128), W=[e])
            P.act(e[:], e[:], AF.Exp, R=[e], W=[e])
            s_ = P.sb([128, 16], F32, name="lb_s")
            P.tt(s_[:], e[:, 0, :], e[:, 1, :], ALU.add, R=[e], W=[s_])
            P.tt(s_[:], s_[:], e[:, 2, :], ALU.add, R=[e, s_], W=[s_])
            P.tt(s_[:], s_[:], e[:, 3, :], ALU.add, R=[e, s_], W=[s_])
            P.op("dve", lambda en: en.reciprocal(s_[:], s_[:]), R=[s_], W=[s_])
            P.tt(e[:], e[:], s_[:].rearrange("p (o j) -> p o j", o=1).broadcast_to([128, 4, 16]), ALU.mult, R=[e, s_], W=[e])
            lbt = P.sb([128, 4, 16], F32, name="lb_t")
            P.op("dve", lambda en: en.memset(lbt[:, 0, :], 0.0), W=[lbt])
            P.cp(lbt[:, 1, :], e[:, 1, :], R=[e], W=[lbt])
            P.tt(lbt[:, 2, :], lbt[:, 1, :], e[:, 2, :], ALU.add, R=[e, lbt], W=[lbt])
            P.tt(lbt[:, 3, :], lbt[:, 2, :], e[:, 3, :], ALU.add, R=[e, lbt], W=[lbt])
            P.dma(lbs_d.rearrange("l d (p j) -> p l d j", p=128),
                  lbt[:].rearrange("p l (d j) -> p l d j", d=2), R=[lbt], W=[P.dres("lbs")])
            P.barrier()
        P.stack = top

        for l in range(depth):
            last = (l == depth - 1) and not dbg.get('nolast', False)
            with ExitStack() as lay:
                P.stack = lay
                modA = P.sb([128, 48, 3], F32, name="modA")
                modB = P.sb([128, 48, 3], F32, name="modB")
                with ExitStack() as ph:
                    P.stack = ph
                    mrow = P.sb([3, 6 * DM], F32, name="mrow")
                    brow = P.sb([3, 6 * DM], F32, name="brow")
                    P.dma(brow[:], Wd["b_mod"][l:l + 1, :].broadcast_to([3, 6 * DM]), W=[brow])
                    wm = [P.sb([128, 8, 1536], F32, name="wm%d" % i) for i in range(2)]
                    pm = [P.ps([128, 512], name="pm%d" % i) for i in range(2)]
                    for blk in range(4):
                        w_ = wm[blk % 2]
                        P.dma(w_[:], Wd["w_mod"][l].rearrange("(kc p) n -> p kc n", p=128)[:, :, blk * 1536:(blk + 1) * 1536], W=[w_])
                        for j in range(3):
                            pj = pm[j % 2]
                            for kc in range(8):
                                P.mm(pj[0:3, :], scT[:, kc, :], w_[:, kc, j * 512:(j + 1) * 512],
                                     start=(kc == 0), stop=(kc == 7), R=[scT, w_], W=[pj])
                            c0 = blk * 1536 + j * 512
                            P.tt(mrow[:, c0:c0 + 512], pj[0:3, :], brow[:, c0:c0 + 512], ALU.add, R=[pj, brow], W=[mrow])
                    P.dma(modrow_d, mrow[:], R=[mrow], W=[P.dres("modrow")])
                    pT = P.ps([128, 512], name="pmT")
                    for j in range(48):
                        P.tr(pT[:, j * 3:(j + 1) * 3], mrow[:, j * 128:(j + 1) * 128], identF[0:3, 0:3], R=[mrow, cst], W=[pT])
                    P.cp(modA[:].rearrange("p j r -> p (j r)"), pT[:, 0:144], R=[pT], W=[modA])
                    P.ts(modB[:].rearrange("p j r -> p (j r)"), pT[:, 0:144], 1.0, None, ALU.add, R=[pT], W=[modB])
                    P.barrier()
                P.stack = lay

                for s in seqs:
                    with ExitStack() as sq:
                        P.stack = sq
                        def featT(l_, s_, tiles, src_fn, sc_chunk, sh_chunk, ph):
                            xt = [P.sb([128, DM], F32, name="p0x%d" % i) for i in range(2)]
                            pp = [P.ps([128, 512], name="p0p%d" % i) for i in range(2)]
                            for i, t in enumerate(tiles):
                                r = 2 if t < TCTX else s_
                                xb = xt[i % 2]
                                src, sres = src_fn(t)
                                P.dma(xb[:], src, R=[sres], W=[xb])
                                for half in range(2):
                                    pb = pp[half]
                                    for q in range(4):
                                        kc = half * 4 + q
                                        P.tr(pb[:, q * 128:(q + 1) * 128], xb[:, kc * 128:(kc + 1) * 128], identF, R=[xb, cst], W=[pb])
                                    for q in range(4):
                                        kc = half * 4 + q
                                        o = hT[:, kc, t * 128:(t + 1) * 128]
                                        i_ = pb[:, q * 128:(q + 1) * 128]
                                        sc = modB[:, sc_chunk + kc, r:r + 1]
                                        sh = modA[:, sh_chunk + kc, r:r + 1]
                                        if q % 2 == 0:
                                            P.ts(o, i_, sc, sh, ALU.mult, ALU.add, R=[pb, modA, modB], W=[hT.res[t]])
                                        else:
                                            P.act(o, i_, AF.Identity, R=[pb, modA, modB], W=[hT.res[t]], bias=sh, scale=sc)

                        if "p0" in phases:
                            with ExitStack() as ph:
                                P.stack = ph
                                featT(l, s, list(range(NT)), lambda t: xsrc(l, s, t), 8, 0, ph)
                                P.barrier()
                            P.stack = sq
                        if "hg" in phases:
                            with ExitStack() as ph:
                                P.stack = ph
                                phase_hg(P, nc, l, s, Wd, cst, cstb, hT, lbs_d, hg_of, ohgT)
                                P.barrier()
                            P.stack = sq
                        if "at" in phases:
                            with ExitStack() as ph:
                                P.stack = ph
                                phase_at(P, nc, l, s, Wd, cst, cstb, hT, rope_d, oatT)
                                P.barrier()
                            P.stack = sq
                        if "mb" in phases:
                            phase_mb(P, nc, l, s, Wd, cst, cstb, hT, mb_xs, mb_yf, ombT, sq)
                            P.stack = sq
                        if "merge" in phases:
                            phase_merge(P, nc, l, s, last, Wd, cst, cstb, hT, modA, modB, modrow_d, ohgT, oatT, ombT,
                                        ymTd, x1s, lambda t: xsrc(l, s, t), sq)
                            P.stack = sq
                        if "ffn" in phases:
                            phase_ffn(P, nc, l, s, last, Wd, cst, cstb, hT, modrow_d, x1s, xres, out_d, ffn_act)
                            P.stack = sq
                    P.stack = lay
            P.stack = top
        P.wait_all_on("sp")
        build.stats = (P.n_instr, P.n_wait)
    return nc


def _loadw(P, dst, src3, n, q="pool"):
    KC = src3.shape[1]
    step = 2
    for i, kc in enumerate(range(0, KC, step)):
        P.dma(dst[:, kc:kc + step, 0:n], src3[:, kc:kc + step, :], W=[dst.res[i % len(dst.res)]], q=q)


def _loadw_cols(P, dst, src3, n, cb, order=None, q="pool"):
    npc = (n + cb - 1) // cb
    KC = src3.shape[1]
    assert len(dst.res) >= 2 * npc
    dst.cb = cb
    for i in (order or range(npc)):
        c0, c1 = i * cb, min(n, (i + 1) * cb)
        for hf in range(2):
            k0, k1 = hf * KC // 2, (hf + 1) * KC // 2
            P.dma(dst[:, k0:k1, c0:c1], src3[:, k0:k1, c0:c1], W=[dst.res[2 * i + hf]], q=q)


def _cr(W, c0, c1):
    return [W.res[j] for i in range(c0 // W.cb, (c1 - 1) // W.cb + 1) for j in (2 * i, 2 * i + 1)]


def _interleave(gens):
    gens = [g for g in gens if g is not None]
    while gens:
        for g in list(gens):
            try:
                next(g)
            except StopIteration:
                gens.remove(g)


def phase_hg(P, nc, l, s, Wd, cst, cstb, hT, lbs_d, hg_of, ohgT):
    identB = cstb[:, 0:128]
    onesF = cst[:, C_ONES:C_ONES + 128]
    CI = cst[:, C_CI:C_CI + 4]
    Wl = Wd["w_in"][l].rearrange("(kc p) n -> p kc n", p=128)
    Wq = P.sb([128, 8, 1024], BF16, nres=4, name="Wq")
    Wf = P.sb([128, 8, 1024], BF16, nres=4, name="Wf")
    Wi = P.sb([128, 8, 1024], BF16, nres=4, name="Wi")
    Wg = P.sb([128, 8, 1024], BF16, nres=4, name="Wg")
    _loadw(P, Wf, Wl[:, :, O_HFF:O_HFF + 1024], 1024)
    _loadw(P, Wq, Wl[:, :, O_HQ:O_HQ + 1024], 1024)
    _loadw(P, Wi, Wl[:, :, O_HI:O_HI + 1024], 1024)
    lbb = P.sb([128, 1024], F32, name="lbb")
    oml = P.sb([128, 1024], F32, name="oml")

    def load_lb(d):
        P.dma(lbb[:], lbs_d[l, d:d + 1, :].broadcast_to([128, 1024]), R=[P.dres("lbs")], W=[lbb])
        P.ts(oml[:], lbb[:], -1.0, 1.0, ALU.mult, ALU.add, R=[lbb], W=[oml])

    load_lb(0)
    gn = P.sb([128, 1], F32, name="gn")
    P.dma(gn[:], Wd["hg_gnorm"][l].rearrange("(p o) -> p o", o=1), W=[gn])
    lf = P.sb([128, 1024], F32, name="lf")
    kk = P.sb([128, 1024], F32, name="kk")
    ex = P.sb([128, 1024], F32, name="ex")
    exn = P.sb([128, 1024], F32, name="exn")
    qs = P.sb([128, 1024], F32, name="qs")
    Kt = [P.sb([128, 1024], BF16, name="Kt%d" % i) for i in range(2)]
    Qt = P.sb([128, 1024], BF16, name="Qt")
    Vt = [P.sb([128, 1024], BF16, name="Vt%d" % i) for i in range(2)]
    Ktm = [P.sb([128, 1024], BF16, name="Ktm%d" % i) for i in range(2)]
    KT = [P.sb([128, 1024], BF16, name="KT%d" % i) for i in range(2)]
    QT = [P.sb([128, 1024], BF16, name="QT%d" % i) for i in range(2)]
    ec = [P.sb([128, 32], F32, name="ec%d" % i) for i in range(2)]
    sg = [P.sb([128, 1024], F32, name="sg%d" % i) for i in range(2)]
    S = P.sb([128, 8, 128], F32, nres=8, name="S")
    Sb = P.sb([128, 8, 128], BF16, nres=8, name="Sb")
    ATm = P.sb([128, 8, 128], BF16, nres=8, name="ATm")
    ost = P.sb([128, 1024], F32, name="ost")
    ofl = P.sb([128, 1024], F32, name="ofl")
    osq = P.sb([128, 1024], F32, name="osq")
    rs = ofl
    ob = P.sb([128, 1024], BF16, name="ob")
    pA = P.ps([128, 1024], name="pA")
    pT = P.ps([128, 1024], BF16, name="pT")
    pC = P.ps([128, 512], name="pC")
    pCb = pC[:].bitcast(BF16)
    pO = P.ps([128, 1024], name="pO")
    pSl = P.ps([128, 1024], name="pSl")
    _r0, _r1 = Res(), Res()
    pSl.res = [_r0] * 4 + [_r1] * 4

    def proj(Wt, t):
        for half in range(2):
            for kc in range(8):
                P.mm(pA[:, half * 512:(half + 1) * 512], hT[:, kc, t * 128:(t + 1) * 128],
                     Wt[:, kc, half * 512:(half + 1) * 512], start=(kc == 0), stop=(kc == 7),
                     R=[hT.res[t], Wt], W=[pA])

    def prologue(t, d, b):
        D = cst[:, (C_HDF if d == 0 else C_HDB):(C_HDF if d == 0 else C_HDB) + 128]
        proj(Wf, t)
        P.act(lf[:], pA[:], AF.Sigmoid, R=[pA], W=[lf])
        P.act(kk[:], pA[:], AF.Sigmoid, R=[pA], W=[kk], scale=-1.0)
        yield
        proj(Wq, t)
        P.tt(lf[:], lf[:], oml[:], ALU.mult, R=[lf, oml], W=[lf])
        P.tt(lf[:], lf[:], lbb[:], ALU.add, R=[lf, lbb], W=[lf])
        P.act(lf[:], lf[:], AF.Ln, R=[lf], W=[lf])
        P.act(qs[:], pA[:], AF.Silu, R=[pA], W=[qs])
        P.tt(kk[:], kk[:], oml[:], ALU.mult, R=[kk, oml], W=[kk])
        yield
        proj(Wi, t)
        P.cp(Vt[b][:], pA[:], R=[pA], W=[Vt[b]], E="act")
        yield
        for half in range(2):
            P.mm(pA[:, half * 512:(half + 1) * 512], D, lf[:, half * 512:(half + 1) * 512], R=[cst, lf], W=[pA])
        for h in range(8):
            P.mm(pC[:, h * 4:(h + 1) * 4], lf[:, h * 128:(h + 1) * 128], CI, R=[lf, cst], W=[pC])
        P.act(ex[:], pA[:], AF.Exp, R=[pA], W=[ex])
        P.act(exn[:], pA[:], AF.Exp, R=[pA], W=[exn], scale=-1.0)
        P.act(ec[b][:], pC[:, 0:32], AF.Exp, R=[pC], W=[ec[b]])
        yield
        P.stt(Qt[:], qs[:], float(128 ** -0.5), exn[:], ALU.mult, ALU.mult, R=[qs, exn], W=[Qt])
        P.tt(Kt[b][:], kk[:], ex[:], ALU.mult, R=[kk, ex], W=[Kt[b]])
        P.ts(Ktm[b][:], Kt[b][:], CI[:, 3:4], None, ALU.mult, R=[Kt[b], cst], W=[Ktm[b]])
        yield
        for h in range(8):
            P.tr(pT[:, h * 128:(h + 1) * 128], Kt[b][:, h * 128:(h + 1) * 128], identB, R=[Kt[b], cstb], W=[pT])
        P.cp(KT[b][:], pT[:], R=[pT], W=[KT[b]])
        for h in range(8):
            P.tr(pCb[:, h * 128:(h + 1) * 128], Qt[:, h * 128:(h + 1) * 128], identB, R=[Qt, cstb], W=[pC])
        P.cp(QT[b][:], pCb, R=[pC], W=[QT[b]], E="act")
        yield
        if d == 1:
            for h in range(8):
                for kc in range(8):
                    P.mm(pA[:, h * 128:(h + 1) * 128], Wg[:, kc, h * 128:(h + 1) * 128], hT[:, kc, t * 128:(t + 1) * 128],
                         start=(kc == 0), stop=(kc == 7), R=[Wg, hT.res[t]], W=[pA])
            P.act(sg[b][:], pA[:], AF.Silu, R=[pA], W=[sg[b]])
            yield

    def scan(t, d, b):
        M = cst[:, (C_HMF if d == 0 else C_HMB):(C_HMF if d == 0 else C_HMB) + 128]
        if d == 1:
            P.dma(ofl[:], hg_of[t], R=[P.dres("hg_of", t)], W=[ofl])
        for h in range(8):
            hs = slice(h * 128, (h + 1) * 128)
            P.mm(pSl[:, hs], KT[b][:, hs], QT[b][:, hs], R=[KT[b], QT[b]], W=[pSl.res[h]])
        for h in range(8):
            hs = slice(h * 128, (h + 1) * 128)
            P.tt(ATm[:, h, :], pSl[:, hs], M, ALU.mult, R=[pSl.res[h], cst], W=[ATm.res[h]])
        yield
        for h in range(8):
            hs = slice(h * 128, (h + 1) * 128)
            P.op("pe", lambda e, hs=hs, h=h: e.matmul(pO[:, hs], Vt[b][:, hs], ATm[:, h, :], start=(h % 4 == 0), stop=False,
                                                       skip_group_check=True), R=[Vt[b], ATm.res[h]], W=[pO])
        chunks = [0, 1, 2, 3] if d == 0 else [3, 2, 1, 0]
        for ci, c in enumerate(chunks):
            for h in range(8):
                e_c = ec[b][:, h * 4 + c:h * 4 + c + 1]
                P.op("act", lambda e, h=h, e_c=e_c: e.activation(Sb[:, h, :], S[:, h, :], AF.Identity, scale=e_c),
                     R=[S.res[h], ec[b]], W=[Sb.res[h]])
            for h in range(8):
                hs = slice(h * 128, (h + 1) * 128)
                if c == 3:
                    P.mm(pSl[:, hs], Ktm[b][:, hs], Vt[b][:, hs], R=[Ktm[b], Vt[b]], W=[pSl.res[h]])
                else:
                    P.mm(pSl[:, hs], Kt[b][c * 32:(c + 1) * 32, hs], Vt[b][c * 32:(c + 1) * 32, hs], R=[Kt[b], Vt[b]], W=[pSl.res[h]])
            for h in range(8):
                cs = slice(h * 128 + c * 32, h * 128 + (c + 1) * 32)
                P.op("pe", lambda e, cs=cs, h=h: e.matmul(pO[:, cs], Sb[:, h, :], QT[b][:, cs], start=False, stop=(ci == 3),
                                                           skip_group_check=True), R=[Sb.res[h], QT[b]], W=[pO])
            for h in range(8):
                hs = slice(h * 128, (h + 1) * 128)
                e_c = ec[b][:, h * 4 + c:h * 4 + c + 1]
                P.stt(S[:, h, :], S[:, h, :], e_c, pSl[:, hs], ALU.mult, ALU.add, R=[S.res[h], ec[b], pSl.res[h]], W=[S.res[h]])
            yield
        if d == 0:
            P.cp(ost[:], pO[:], R=[pO], W=[ost], E="act")
            P.dma(hg_of[t], ost[:], R=[ost], W=[P.dres("hg_of", t)])
        else:
            P.tt(ost[:], pO[:], ofl[:], ALU.add, R=[pO, ofl], W=[ost])
            P.act(osq[:], ost[:], AF.Square, R=[ost], W=[osq])
            for half in range(2):
                P.mm(pSl[:, half * 512:(half + 1) * 512], onesF, osq[:, half * 512:(half + 1) * 512], R=[cst, osq], W=[pSl])
            yield
            P.act(rs[:], pSl[:], AF.Sqrt, R=[pSl], W=[rs], scale=1.0 / 128.0, bias=RMS_EPS)
            P.op("dve", lambda e: e.reciprocal(rs[:], rs[:]), R=[rs], W=[rs])
            P.stt(osq[:], ost[:], gn[:, 0:1], rs[:], ALU.mult, ALU.mult, R=[ost, gn, rs], W=[osq])
            P.tt(ob[:], osq[:], sg[b][:], ALU.mult, R=[osq, sg[b]], W=[ob])
            P.dma(ohgT[t], ob[:], R=[ob], W=[P.dres("ohgT", t)])
        yield

    order = [(t, 0) for t in range(NT)] + [(t, 1) for t in list(range(TCTX - 1, -1, -1)) + list(range(NT - 1, TCTX - 1, -1))]
    P.op("dve", lambda e: e.memset(S[:], 0.0), W=[S])
    _interleave([prologue(order[0][0], order[0][1], 0)])
    for n, (t, d) in enumerate(order):
        nxt = None
        if n + 1 < len(order):
            t2, d2 = order[n + 1]
            if d2 == 1 and d == 0:
                _loadw(P, Wf, Wl[:, :, O_HFB:O_HFB + 1024], 1024)
                _loadw(P, Wg, Wl[:, :, O_HG:O_HG + 1024], 1024)
                load_lb(1)
            nxt = prologue(t2, d2, (n + 1) % 2)
        _interleave([scan(t, d, n % 2), nxt])
        if n + 1 < len(order) and order[n + 1][1] == 1 and d == 0:
            P.op("dve", lambda e: e.memset(S[:], 0.0), W=[S])


def _pipeline(items, gen_fn):
    active = []

    def rnd():
        for g in list(active):
            try:
                next(g)
            except StopIteration:
                active.remove(g)

    for it in items:
        active.append(gen_fn(it))
        rnd()
    while active:
        rnd()


def phase_at(P, nc, l, s, Wd, cst, cstb, hT, rope_d, oatT):
    identB = cstb[:, 0:128]
    onesB = cstb[:, 128:256]
    Wl = Wd["w_in"][l].rearrange("(kc p) n -> p kc n", p=128)
    Wa = P.sb([128, 8, 1536], BF16, nres=4, name="Wa")
    _loadw(P, Wa, Wl[:, :, O_AQ:O_AQ + 1536], 1536)
    QT = P.sb([128, 8, TOK], BF16, nres=NT, name="QTa")
    KTa = P.sb([128, 2, TOK], BF16, nres=NT, name="KTa")
    Va = P.sb([128, NT, 256], BF16, nres=NT, name="Va")
    with ExitStack() as sub:
        P.stack = sub
        cosT = P.sb([128, 16, 64], F32, name="cosT")
        sinT = P.sb([128, 16, 64], F32, name="sinT")
        P.dma(cosT[:], rope_d[0].rearrange("(t p) j -> p t j", p=128), W=[cosT])
        P.dma(sinT[:], rope_d[1].rearrange("(t p) j -> p t j", p=128), W=[sinT])
        wqk = P.sb([128, 10, 128], F32, name="wqk")
        P.dma(wqk[:, 0:8, :], Wd["at_qnorm"][l:l + 1, :].rearrange("o (h d) -> o h d", h=1).broadcast_to([128, 8, 128]), W=[wqk])
        P.dma(wqk[:, 8:10, :], Wd["at_knorm"][l:l + 1, :].rearrange("o (h d) -> o h d", h=1).broadcast_to([128, 2, 128]), W=[wqk])
        NB = 2
        sqt = [P.sb([128, 1280], F32, name="sqt%d" % i) for i in range(NB)]
        xn = [P.sb([128, 1280], F32, name="xn%d" % i) for i in range(NB)]
        ss = [P.sb([128, 10], F32, name="ss%d" % i) for i in range(NB)]
        t1 = [P.sb([128, 10, 64], F32, name="t1%d" % i) for i in range(NB)]
        t2 = [P.sb([128, 10, 64], F32, name="t2%d" % i) for i in range(NB)]
        xr = [P.sb([128, 1280], BF16, name="xr%d" % i) for i in range(NB)]
        pQ = [P.ps([128, 1536], name="pQ%d" % i) for i in range(NB)]
        pT = P.ps([128, 2048], BF16, name="pTa")

        def prep(t):
            b = t % NB
            tc = slice(t * 128, (t + 1) * 128)
            pQ_, sqt_, xn_, ss_, t1_, t2_, xr_ = pQ[b], sqt[b], xn[b], ss[b], t1[b], t2[b], xr[b]
            for j in range(3):
                for kc in range(8):
                    P.mm(pQ_[:, j * 512:(j + 1) * 512], hT[:, kc, tc], Wa[:, kc, j * 512:(j + 1) * 512],
                         start=(kc == 0), stop=(kc == 7), R=[hT.res[t], Wa], W=[pQ_])
            P.cp(Va[:, t, :], pQ_[:, 1280:1536], R=[pQ_], W=[Va.res[t]], E="act")
            P.act(sqt_[:], pQ_[:, 0:1280], AF.Square, R=[pQ_], W=[sqt_])
            P.op("dve", lambda e: e.tensor_reduce(ss_[:], sqt_[:].rearrange("p (h d) -> p h d", h=10), AX.X, ALU.add), R=[sqt_], W=[ss_])
            P.act(ss_[:], ss_[:], AF.Sqrt, R=[ss_], W=[ss_], scale=1.0 / 128.0, bias=RMS_EPS)
            P.op("dve", lambda e: e.reciprocal(ss_[:], ss_[:]), R=[ss_], W=[ss_])
            yield
            P.tt(xn_[:].rearrange("p (h d) -> p h d", h=10), pQ_[:, 0:1280].rearrange("p (h d) -> p h d", h=10),
                 ss_[:].rearrange("p (h o) -> p h o", o=1).broadcast_to([128, 10, 128]), ALU.mult, R=[pQ_, ss_], W=[xn_])
            P.tt(xn_[:], xn_[:], wqk[:].rearrange("p h d -> p (h d)"), ALU.mult, R=[xn_, wqk], W=[xn_])
            if t >= TCTX:
                tl = t - TCTX
                xv = xn_[:].rearrange("p (h j two) -> p h j two", h=10, two=2)
                xo = xr_[:].rearrange("p (h j two) -> p h j two", h=10, two=2)
                cb = cosT[:, tl:tl + 1, :].broadcast_to([128, 10, 64])
                sb_ = sinT[:, tl:tl + 1, :].broadcast_to([128, 10, 64])
                P.tt(t1_[:], xv[:, :, :, 0], cb, ALU.mult, R=[xn_, cosT], W=[t1_])
                P.tt(t2_[:], xv[:, :, :, 1], sb_, ALU.mult, R=[xn_, sinT], W=[t2_])
                P.tt(xo[:, :, :, 0], t1_[:], t2_[:], ALU.subtract, R=[t1_, t2_], W=[xr_])
                P.tt(t1_[:], xv[:, :, :, 0], sb_, ALU.mult, R=[xn_, sinT], W=[t1_])
                P.tt(t2_[:], xv[:, :, :, 1], cb, ALU.mult, R=[xn_, cosT], W=[t2_])
                P.tt(xo[:, :, :, 1], t1_[:], t2_[:], ALU.add, R=[t1_, t2_], W=[xr_])
            else:
                P.cp(xr_[:], xn_[:], R=[xn_], W=[xr_], E="act")
            yield
            for j in range(10):
                P.tr(pT[:, j * 128:(j + 1) * 128], xr_[:, j * 128:(j + 1) * 128], identB, R=[xr_, cstb], W=[pT])
            P.cp(QT[:, :, tc], pT[:, 0:1024].rearrange("p (h k) -> p h k", h=8), R=[pT], W=[QT.res[t]])
            P.cp(KTa[:, :, tc], pT[:, 1024:1280].rearrange("p (h k) -> p h k", h=2), R=[pT], W=[KTa.res[t]], E="act")
            yield

        _pipeline(list(range(NT)), prep)
        P.barrier()
    with ExitStack() as sub:
        P.stack = sub
        NS = 3
        pS = [P.ps([128, 512], name="pSa%d" % i) for i in range(NS)]
        pO = [P.ps([128, 512], name="pOa%d" % i) for i in range(2)]
        pL = [P.ps([128, 512], name="pLa%d" % i) for i in range(2)]
        Pt = [P.sb([128, 512], BF16, name="Pt%d" % i) for i in range(NS)]
        rl = [P.sb([128, 512], F32, name="rl%d" % i) for i in range(2)]
        oa = [P.sb([128, 512], BF16, name="oa%d" % i) for i in range(2)]
        blocks = [(0, 256, [0, 1])] + [(256 + qb * 512, 512, list(range(NT))) for qb in range(4)]
        sc = float(128 ** -0.5)
        steps = []
        for h in range(8):
            for bi, (q0, nq, kts) in enumerate(blocks):
                for i, kt in enumerate(kts):
                    steps.append((h, bi, q0, nq, kt, i == 0, i == len(kts) - 1))
        jobn = {}
        for (h, bi, *_r) in steps:
            jobn.setdefault((h, bi), len(jobn))

        def s_mm(n):
            h, bi, q0, nq, kt, first, lastk = steps[n]
            kv = h // 4
            qres = [QT.res[t] for t in range(q0 // 128, (q0 + nq) // 128)]
            P.mm(pS[n % NS][:, 0:nq], KTa[:, kv, kt * 128:(kt + 1) * 128], QT[:, h, q0:q0 + nq], R=[KTa.res[kt]] + qres, W=[pS[n % NS]])

        s_mm(0)
        for n in range(len(steps)):
            h, bi, q0, nq, kt, first, lastk = steps[n]
            kv = h // 4
            jb = jobn[(h, bi)] % 2
            if n + 1 < len(steps):
                s_mm(n + 1)
            ps, pt = pS[n % NS], Pt[n % NS]
            P.act(pt[:, 0:nq], ps[:, 0:nq], AF.Exp, R=[ps], W=[pt], scale=sc)
            P.mm(pO[jb][:, 0:nq], Va[:, kt, kv * 128:(kv + 1) * 128], pt[:, 0:nq], start=first, stop=lastk, R=[Va.res[kt], pt], W=[pO[jb]])
            P.mm(pL[jb][:, 0:nq], onesB, pt[:, 0:nq], start=first, stop=lastk, R=[cstb, pt], W=[pL[jb]])
            if lastk:
                P.op("dve", lambda e: e.reciprocal(rl[jb][:, 0:nq], pL[jb][:, 0:nq]), R=[pL[jb]], W=[rl[jb]])
                P.tt(oa[jb][:, 0:nq], pO[jb][:, 0:nq], rl[jb][:, 0:nq], ALU.mult, R=[pO[jb], rl[jb]], W=[oa[jb]])
                P.dma(oatT[:, h, q0:q0 + nq], oa[jb][:, 0:nq], R=[oa[jb]], W=[P.dres("oatT", (h, q0))])
        P.barrier()


def phase_mb(P, nc, l, s, Wd, cst, cstb, hT, mb_xs, mb_yf, ombT, sq):
    identB = cstb[:, 0:128]
    identF = cst[:, C_ID:C_ID + 128]
    onesF = cst[:, C_ONES:C_ONES + 128]
    Wl = Wd["w_in"][l].rearrange("(kc p) n -> p kc n", p=128)
    with ExitStack() as ph:
        P.stack = ph
        BT = P.sb([128, 4, TOK], BF16, name="BT")
        CT = P.sb([128, 4, TOK], BF16, name="CT")
        with ExitStack() as sub:
            P.stack = sub
            Wx = P.sb([128, 8, 3072], BF16, nres=24, name="Wx")
            _loadw_cols(P, Wx, Wl[:, :, O_MX:O_MX + 3072], 3072, 256)
            cwr = P.sb([120, 128], F32, name="cwr")
            P.dma(cwr[:], Wd["mb_conv_w"][l].rearrange("j (cc p) -> (j cc) p", p=128), W=[cwr])
            cbr = P.sb([24, 128], F32, name="cbr")
            P.dma(cbr[:], Wd["mb_conv_b"][l].rearrange("(cc p) -> cc p", p=128), W=[cbr])
            cw = P.sb([128, 120], F32, name="cw")
            cbias = P.sb([128, 24], F32, name="cbias")
            pX = [P.ps([128, 512], name="pX%d" % i) for i in range(2)]
            pTt = P.ps([128, 1024], BF16, name="pTt")
            pW_ = P.ps([128, 512], name="pW_")
            P.tr(pW_[:, 0:120], cwr[:], identF[0:120, 0:120], R=[cwr, cst], W=[pW_])
            P.cp(cw[:], pW_[:, 0:120], R=[pW_], W=[cw])
            P.tr(pW_[:, 128:152], cbr[:], identF[0:24, 0:24], R=[cbr, cst], W=[pW_])
            P.cp(cbias[:], pW_[:, 128:152], R=[pW_], W=[cbias])
            xb = [P.sb([128, 2052], BF16, name="xb%d" % i) for i in range(2)]
            dgs = [P.sb([128, 5, 128], BF16, name="dg%d" % i) for i in range(2)]
            pCv = [P.ps([128, 512], name="pCv%d" % i) for i in range(2)]
            ub = P.sb([128, 2048], BF16, name="ub")
            stg = [P.sb([128, 8, 128], BF16, name="stg%d" % i) for i in range(2)]
            for xb_ in xb:
                P.op("pool", lambda e, xb_=xb_: e.memset(xb_[:], 0.0), W=[xb_])
            k = 0
            kx = 0
            cw3 = cw[:].rearrange("p (j c) -> p j c", j=5)
            for cc in range(24):
                dg = dgs[cc % 2]
                P.tt(dg[:], identB.rearrange("p (o c) -> p o c", o=1).broadcast_to([128, 5, 128]),
                     cw3[:, :, cc:cc + 1].broadcast_to([128, 5, 128]), ALU.mult, R=[cstb, cw], W=[dg])
                for (t0, ntile) in ((0, TCTX), (TCTX, NT - TCTX)):
                    N = ntile * 128
                    xb_ = xb[kx % 2]
                    kx += 1
                    nblk = (N + 511) // 512
                    for b in range(nblk):
                        nb = min(512, N - b * 512)
                        c0 = t0 * 128 + b * 512
                        tr_ = [hT.res[t] for t in range(c0 // 128, (c0 + nb) // 128)]
                        for kc in range(8):
                            P.mm(pX[b % 2][:, 0:nb], Wx[:, kc, cc * 128:(cc + 1) * 128], hT[:, kc, c0:c0 + nb],
                                 start=(kc == 0), stop=(kc == 7), R=_cr(Wx, cc * 128, (cc + 1) * 128) + tr_, W=[pX[b % 2]])
                        P.cp(xb_[:, 2 + b * 512:2 + b * 512 + nb], pX[b % 2][:, 0:nb], R=[pX[b % 2]], W=[xb_], E=("act" if b % 2 else "dve"))
                    P.op("pool", lambda e, xb_=xb_, N=N: e.memset(xb_[:, N + 2:N + 4], 0.0), W=[xb_])
                    if cc < 16:
                        dst, dres_ = ub, ub
                        off = 0
                    elif cc < 20:
                        dst, dres_ = BT[:, cc - 16, :], BT
                        off = t0 * 128
                    else:
                        dst, dres_ = CT[:, cc - 20, :], CT
                        off = t0 * 128
                    for b in range(nblk):
                        nb = min(512, N - b * 512)
                        for j in range(5):
                            P.mm(pCv[b % 2][:, 0:nb], dg[:, j, :], xb_[:, b * 512 + j:b * 512 + j + nb], start=(j == 0), stop=(j == 4),
                                 R=[dg, xb_], W=[pCv[b % 2]])
                        P.act(dst[:, off + b * 512:off + b * 512 + nb], pCv[b % 2][:, 0:nb], AF.Silu, R=[pCv[b % 2], cbias], W=[dres_],
                              bias=cbias[:, cc:cc + 1])
                    if cc < 16:
                        for g0 in range(0, ntile, 8):
                            ng = min(8, ntile - g0)
                            for j in range(ng):
                                P.tr(pTt[:, j * 128:(j + 1) * 128], ub[:, (g0 + j) * 128:(g0 + j + 1) * 128], identB, R=[ub, cstb], W=[pTt])
                            st = stg[k % 2]
                            k += 1
                            P.cp(st[:, 0:ng, :], pTt[:, 0:ng * 128].rearrange("p (j c) -> p j c", j=ng), R=[pTt], W=[st],
                                 E=("act" if k % 2 else "dve"))
                            ta = t0 + g0
                            P.dma(mb_xs[ta:ta + ng, :, cc * 128:(cc + 1) * 128].rearrange("t p c -> p t c"), st[:, 0:ng, :],
                                  R=[st], W=[P.dres("mb_xs", t) for t in range(ta, ta + ng)])
            P.barrier()
        with ExitStack() as sub:
            P.stack = sub
            Wz = P.sb([128, 8, 2112], BF16, nres=8, name="Wz")
            for i, kc in enumerate(range(0, 8, 2)):
                P.dma(Wz[:, kc:kc + 2, 2048:2112], Wl[:, kc:kc + 2, O_MDT:O_MDT + 64], W=[Wz.res[4 + i]], q="pool")
            for i, kc in enumerate(range(0, 8, 2)):
                P.dma(Wz[:, kc:kc + 2, 0:2048], Wl[:, kc:kc + 2, O_MZ:O_MZ + 2048], W=[Wz.res[i]], q="pool")
            dtb = P.sb([128, 64], F32, name="dtb")
            abc = P.sb([128, 64], F32, name="abc")
            dbc = P.sb([128, 32], F32, name="dbc")
            nrm = P.sb([128, 2048], F32, name="nrm")
            P.dma(dtb[:], Wd["mb_dt_bias"][l:l + 1].rearrange("o d h -> o (d h)").broadcast_to([128, 64]), W=[dtb])
            P.dma(abc[:], Wd["mb_a_log"][l:l + 1].rearrange("o d h -> o (d h)").broadcast_to([128, 64]), W=[abc])
            P.act(abc[:], abc[:], AF.Exp, R=[abc], W=[abc])
            P.ts(abc[:], abc[:], -1.0, None, ALU.mult, R=[abc], W=[abc])
            P.dma(dbc[:], Wd["mb_d"][l:l + 1, :].broadcast_to([128, 32]), W=[dbc])
            P.dma(nrm[:], Wd["mb_norm"][l:l + 1, :].broadcast_to([128, 2048]), W=[nrm])
            xs = [P.sb([128, 2048], BF16, name="xs%d" % i) for i in range(2)]
            xdt = [P.sb([128, 2048], BF16, name="xdt%d" % i) for i in range(2)]
            xdtw = P.sb([128, 2048], BF16, name="xdtw")
            oT = xdtw
            dtv = [P.sb([128, 64], F32, name="dtv%d" % i) for i in range(2)]
            dA = [P.sb([128, 32], F32, name="dA%d" % i) for i in range(2)]
            e3 = [P.sb([128, 96], F32, name="e3%d" % i) for i in range(2)]
            cbm = [P.sb([128, 512], F32, name="cbm%d" % i) for i in range(2)]
            Btk = [P.sb([128, 512], BF16, name="Btk%d" % i) for i in range(2)]
            rh = [P.sb([128, 512], F32, name="rh%d" % i) for i in range(2)]
            es = [P.sb([128, 512], F32, name="es%d" % i) for i in range(2)]
            MT = [P.sb([128, 32, 128], BF16, nres=8, name="MT%d" % i) for i in range(2)]
            tmp = [P.sb([128, 512], F32, name="tmpy%d" % i) for i in range(2)]
            ysb = P.sb([128, 2048], F32, nres=4, name="ysb")
            ST = P.sb([128, 2048], F32, nres=4, name="ST")
            STb = P.sb([128, 2048], BF16, nres=4, name="STb")
            yfl = P.sb([128, 2048], F32, name="yfl")
            ss4 = P.sb([128, 4], F32, name="ss4")
            pM = P.ps([128, 512], name="pM")
            pCB = P.ps([128, 512], name="pCB")
            pCBb = pCB[:].bitcast(BF16)
            pSG = [P.ps([128, 512], name="pSG%d" % i) for i in range(2)]
            pY = [P.ps([128, 512], name="pY%d" % i) for i in range(2)]
            pI = [P.ps([128, 512], name="pI%d" % i) for i in range(2)]
            p4 = [pY[0], pY[1], pI[0], pI[1]]
            v3 = lambda ap, h: ap.rearrange("p (h q) -> p h q", h=h)
            col = lambda ap: ap.rearrange("p (h o) -> p h o", o=1)

            def prologue(t, d, b):
                tc = slice(t * 128, (t + 1) * 128)
                TRI = cst[:, (C_LE if d == 0 else C_GE):(C_LE if d == 0 else C_GE) + 128]
                STR = cst[:, (C_GT if d == 0 else C_LT):(C_GT if d == 0 else C_LT) + 128]
                xs_, xdt_, dtv_, dA_, e3_, cbm_, Btk_, MT_ = xs[b], xdt[b], dtv[b], dA[b], e3[b], cbm[b], Btk[b], MT[b]
                P.dma(xs_[:], mb_xs[t], R=[P.dres("mb_xs", t)], W=[xs_])
                for kc in range(8):
                    P.mm(pM[:, 0:64], hT[:, kc, tc], Wz[:, kc, 2048:2112], start=(kc == 0), stop=(kc == 7), R=[hT.res[t], Wz.res[4:8]], W=[pM])
                P.tt(dtv_[:], pM[:, 0:64], dtb[:], ALU.add, R=[pM, dtb], W=[dtv_])
                P.act(dtv_[:], dtv_[:], AF.Exp, R=[dtv_], W=[dtv_])
                P.act(dtv_[:], dtv_[:], AF.Ln, R=[dtv_], W=[dtv_], bias=1.0)
                P.tt(dA_[:], dtv_[:, d * 32:(d + 1) * 32], abc[:, d * 32:(d + 1) * 32], ALU.mult, R=[dtv_, abc], W=[dA_])
                for g in range(4):
                    P.mm(pCB[:, g * 128:(g + 1) * 128], BT[:, g, tc], CT[:, g, tc], R=[BT, CT], W=[pCB])
                yield
                P.mm(pM[:, 64:96], TRI, dA_[:], R=[cst, dA_], W=[pM])
                P.mm(pM[:, 96:128], STR, dA_[:], R=[cst, dA_], W=[pM])
                P.mm(pM[:, 128:160], onesF, dA_[:], R=[cst, dA_], W=[pM])
                P.act(e3_[:], pM[:, 64:160], AF.Exp, R=[pM], W=[e3_])
                P.tt(v3(cbm_[:], 4), v3(pCB[:], 4), TRI.rearrange("p (o t) -> p o t", o=1).broadcast_to([128, 4, 128]), ALU.mult,
                     R=[pCB, cst], W=[cbm_])
                for g in range(4):
                    P.tr(pCBb[:, g * 128:(g + 1) * 128], BT[:, g, tc], identB, R=[BT, cstb], W=[pCB])
                P.cp(Btk_[:], pCBb[:, 0:512], R=[pCB], W=[Btk_], E="act")
                P.tt(v3(xdt_[:], 32), v3(xs_[:], 32), col(dtv_[:, d * 32:(d + 1) * 32]).broadcast_to([128, 32, 64]), ALU.mult,
                     R=[xs_, dtv_], W=[xdt_])
                yield
                TRI3 = TRI.rearrange("p (o t) -> p o t", o=1)
                for step in range(10):
                    if step < 8:
                        hb = step
                        i = hb % 2
                        P.tt(v3(rh[i][:], 4), TRI3.broadcast_to([128, 4, 128]),
                             col(dA_[:, hb * 4:hb * 4 + 4]).broadcast_to([128, 4, 128]), ALU.mult, R=[cst, dA_], W=[rh[i]])
                        P.mm(pSG[i][:], STR, rh[i][:], R=[cst, rh[i]], W=[pSG[i]])
                    if 1 <= step <= 8:
                        i = (step - 1) % 2
                        P.act(es[i][:], pSG[i][:], AF.Exp, R=[pSG[i]], W=[es[i]])
                    if 2 <= step <= 9:
                        hb = step - 2
                        i = hb % 2
                        g = hb // 2
                        P.tt(MT_[:, hb * 4:hb * 4 + 4, :], v3(es[i][:], 4),
                             cbm_[:, g * 128:(g + 1) * 128].rearrange("p (o t) -> p o t", o=1).broadcast_to([128, 4, 128]), ALU.mult,
                             R=[es[i], cbm_], W=[MT_.res[hb]])
                    if step % 2 == 1:
                        yield

            def scan(t, d, b):
                tc = slice(t * 128, (t + 1) * 128)
                xs_, xdt_, e3_, Btk_, MT_ = xs[b], xdt[b], e3[b], Btk[b], MT[b]
                if d == 1:
                    P.dma(yfl[:], mb_yf[t], R=[P.dres("mb_yf", t)], W=[yfl])
                P.tt(v3(xdtw[:], 32), v3(xdt_[:], 32), col(e3_[:, 32:64]).broadcast_to([128, 32, 64]), ALU.mult,
                     R=[xdt_, e3_], W=[xdtw])
                for g in range(4):
                    gs = slice(g * 512, (g + 1) * 512)
                    pY_, pI_, tmp_ = pY[g % 2], pI[g % 2], tmp[g % 2]
                    for hh in range(8):
                        h = g * 8 + hh
                        P.mm(pY_[:, hh * 64:(hh + 1) * 64], MT_[:, h, :], xdt_[:, h * 64:(h + 1) * 64], R=[MT_.res[h // 4], xdt_], W=[pY_])
                    P.mm(pI_[:], CT[:, g, tc], STb[:, gs], R=[CT, STb.res[g]], W=[pI_])
                    P.tt(v3(tmp_[:], 8), v3(pI_[:], 8), col(e3_[:, g * 8:(g + 1) * 8]).broadcast_to([128, 8, 64]), ALU.mult,
                         R=[pI_, e3_], W=[tmp_])
                    P.tt(ysb[:, gs], tmp_[:], pY_[:], ALU.add, R=[tmp_, pY_], W=[ysb.res[g]])
                    if g % 2 == 1:
                        yield
                for g in range(4):
                    gs = slice(g * 512, (g + 1) * 512)
                    pI_ = pI[g % 2]
                    P.mm(pI_[:], Btk_[:, g * 128:(g + 1) * 128], xdtw[:, gs], R=[Btk_, xdtw], W=[pI_])
                    P.tt(v3(ST[:, gs], 8), v3(ST[:, gs], 8), col(e3_[:, 64 + g * 8:64 + (g + 1) * 8]).broadcast_to([128, 8, 64]), ALU.mult,
                         R=[ST.res[g], e3_], W=[ST.res[g]])
                    P.tt(ST[:, gs], ST[:, gs], pI_[:], ALU.add, R=[ST.res[g], pI_], W=[ST.res[g]])
                    P.cp(STb[:, gs], ST[:, gs], R=[ST.res[g]], W=[STb.res[g]], E="act")
                    if g % 2 == 1:
                        yield
                if d == 0:
                    P.dma(mb_yf[t], ysb[:], R=[ysb], W=[P.dres("mb_yf", t)])
                    return
                P.tt(ysb[:], ysb[:], yfl[:], ALU.add, R=[ysb, yfl], W=[ysb])
                P.tt(v3(yfl[:], 32), v3(xs_[:], 32), col(dbc[:]).broadcast_to([128, 32, 64]), ALU.mult, R=[xs_, dbc, yfl], W=[yfl])
                P.tt(ysb[:], ysb[:], yfl[:], ALU.add, R=[ysb, yfl], W=[ysb])
                for j in range(4):
                    for kc in range(8):
                        P.mm(p4[j][:], hT[:, kc, tc], Wz[:, kc, j * 512:(j + 1) * 512], start=(kc == 0), stop=(kc == 7),
                             R=[hT.res[t], Wz.res[0:4]], W=[p4[j]])
                    P.act(yfl[:, j * 512:(j + 1) * 512], p4[j][:], AF.Silu, R=[p4[j]], W=[yfl])
                    if j % 2 == 1:
                        yield
                P.tt(ysb[:], ysb[:], yfl[:], ALU.mult, R=[ysb, yfl], W=[ysb])
                P.act(yfl[:], ysb[:], AF.Square, R=[ysb], W=[yfl])
                P.op("dve", lambda e: e.tensor_reduce(ss4[:], v3(yfl[:], 4), AX.X, ALU.add), R=[yfl], W=[ss4])
                P.act(ss4[:], ss4[:], AF.Sqrt, R=[ss4], W=[ss4], scale=1.0 / 512.0, bias=RMS_EPS)
                P.op("dve", lambda e: e.reciprocal(ss4[:], ss4[:]), R=[ss4], W=[ss4])
                yield
                for g in range(4):
                    gs = slice(g * 512, (g + 1) * 512)
                    P.stt(ysb[:, gs], ysb[:, gs], ss4[:, g:g + 1], nrm[:, gs], ALU.mult, ALU.mult, R=[ysb.res[g], ss4, nrm], W=[ysb.res[g]])
                for cc in range(16):
                    P.tr(p4[cc // 4][:, (cc % 4) * 128:(cc % 4 + 1) * 128], ysb[:, cc * 128:(cc + 1) * 128], identF, R=[ysb, cst], W=[p4[cc // 4]])
                for j in range(4):
                    P.cp(oT[:, j * 512:(j + 1) * 512], p4[j][:], R=[p4[j]], W=[oT], E=("act" if j % 2 else "dve"))
                P.dma(ombT[t], oT[:], R=[oT], W=[P.dres("ombT", t)])
                yield

            def reset_state():
                P.op("dve", lambda e: e.memset(ST[:], 0.0), W=[ST])
                P.op("pool", lambda e: e.memset(STb[:], 0.0), W=[STb])

            order = [(t, 0) for t in range(NT)] + [(t, 1) for t in list(range(TCTX - 1, -1, -1)) + list(range(NT - 1, TCTX - 1, -1))]
            reset_state()
            _interleave([prologue(order[0][0], order[0][1], 0)])
            for n, (t, d) in enumerate(order):
                nxt = None
                if n + 1 < len(order):
                    nxt = prologue(order[n + 1][0], order[n + 1][1], (n + 1) % 2)
                _interleave([scan(t, d, n % 2), nxt])
                if n + 1 < len(order) and order[n + 1][1] == 1 and d == 0:
                    reset_state()
            P.barrier()


def _layernorm(P, out, z, zr, lng, lnb, st, mv):
    for c in range(2):
        P.op("dve", lambda e: e.bn_stats(st[:, c * 6:(c + 1) * 6], z[:, c * 512:(c + 1) * 512]), R=[zr], W=[st])
    P.op("dve", lambda e: e.bn_aggr(mv[:, 0:2], st[:]), R=[st], W=[mv])
    P.act(mv[:, 2:3], mv[:, 1:2], AF.Sqrt, R=[mv], W=[mv], bias=LN_EPS)
    P.op("dve", lambda e: e.reciprocal(mv[:, 2:3], mv[:, 2:3]), R=[mv], W=[mv])
    P.ts(out, z, mv[:, 0:1], mv[:, 2:3], ALU.subtract, ALU.mult, R=[zr, mv], W=[zr])
    P.tt(out, out, lng[:], ALU.mult, R=[zr, lng], W=[zr])
    P.tt(out, out, lnb[:], ALU.add, R=[zr, lnb], W=[zr])


def _gate_bcast(P, cst, modrow_d, c0, s, pG, gbc):
    mr = P.sb([3, 1024], F32, name="mr")
    P.dma(mr[:], modrow_d[:, c0:c0 + 1024], R=[P.dres("modrow")], W=[mr])
    for ri, r in enumerate((s, 2)):
        for half in range(2):
            P.mm(pG[:, half * 512:(half + 1) * 512], cst[0:3, C_SEL + r * 128:C_SEL + (r + 1) * 128], mr[0:3, half * 512:(half + 1) * 512],
                 R=[cst, mr], W=[pG])
        P.cp(gbc[:, ri, :], pG[:], R=[pG], W=[gbc])


def phase_merge(P, nc, l, s, last, Wd, cst, cstb, hT, modA, modB, modrow_d, ohgT, oatT, ombT, ymTd, x1s, xsrc_fn, sq):
    identB = cstb[:, 0:128]
    identF = cst[:, C_ID:C_ID + 128]
    tiles = list(range(TCTX, NT)) if last else list(range(NT))
    Wl = Wd["w_in"][l].rearrange("(kc p) n -> p kc n", p=128)
    with ExitStack() as ph:
        P.stack = ph
        Wgt = P.sb([128, 8, 3072], BF16, nres=12, name="Wgt")
        Wbh = P.sb([128, 8, 1024], BF16, nres=4, name="Wbh")
        Wba = P.sb([128, 8, 1024], BF16, nres=4, name="Wba")
        Wbm = P.sb([128, 16, 1024], BF16, nres=8, name="Wbm")
        Wgs = Wl[:, :, O_GT:O_GT + 3072]

        def ldg(b):
            for i, kc in enumerate(range(0, 8, 2)):
                P.dma(Wgt[:, kc:kc + 2, b * 1024:(b + 1) * 1024], Wgs[:, kc:kc + 2, b * 1024:(b + 1) * 1024], W=[Wgt.res[b * 4 + i]], q="pool")

        ldg(0)
        _loadw(P, Wbh, Wd["w_br_hg"][l].rearrange("(kc p) n -> p kc n", p=128), 1024)
        ldg(1)
        _loadw(P, Wba, Wd["w_br_at"][l].rearrange("(kc p) n -> p kc n", p=128), 1024)
        ldg(2)
        _loadw(P, Wbm, Wd["w_br_mb"][l].rearrange("(kc p) n -> p kc n", p=128), 1024)
        oh = [P.sb([128, 1024], BF16, name="oh%d" % i) for i in range(2)]
        oa = [P.sb([128, 8, 128], BF16, name="oam%d" % i) for i in range(2)]
        om = [P.sb([128, 2048], BF16, name="om%d" % i) for i in range(2)]
        sgt = P.sb([128, 1024], F32, name="sgt")
        ym = P.sb([128, 1024], F32, name="ym")
        tmp = P.sb([128, 1024], F32, name="tmpm")
        ymb = P.sb([128, 1024], BF16, name="ymb")
        ymT = [P.sb([128, 1024], BF16, name="ymT%d" % i) for i in range(2)]
        pG = P.ps([128, 1024], name="pGm")
        pB = P.ps([128, 1024], name="pBm")
        pT = P.ps([128, 1024], BF16, name="pTm")

        def ld_ma(i):
            t = tiles[i]
            P.dma(oh[i % 2][:], ohgT[t], R=[P.dres("ohgT", t)], W=[oh[i % 2]])
            P.dma(oa[i % 2][:], oatT[:, :, t * 128:(t + 1) * 128],
                  R=[P.dres("oatT", (h, q0)) for h in range(8) for q0 in (0, 256, 768, 1280, 1792)], W=[oa[i % 2]])
            P.dma(om[i % 2][:], ombT[t], R=[P.dres("ombT", t)], W=[om[i % 2]])

        for i, t in enumerate(tiles):
            tc = slice(t * 128, (t + 1) * 128)
            oh_, oa_, om_ = oh[i % 2], oa[i % 2], om[i % 2]
            if i == 0:
                ld_ma(0)
            if i + 1 < len(tiles):
                ld_ma(i + 1)
            srcs = [(lambda kc, o=oh_: o[:, kc * 128:(kc + 1) * 128], Wbh, 8, oh_),
                    (lambda kc, o=oa_: o[:, kc, :], Wba, 8, oa_),
                    (lambda kc, o=om_: o[:, kc * 128:(kc + 1) * 128], Wbm, 16, om_)]
            for b, (of, Wb, nk, ot) in enumerate(srcs):
                for half in range(2):
                    for kc in range(8):
                        P.mm(pG[:, half * 512:(half + 1) * 512], hT[:, kc, tc], Wgt[:, kc, b * 1024 + half * 512:b * 1024 + (half + 1) * 512],
                             start=(kc == 0), stop=(kc == 7), R=[hT.res[t]] + Wgt.res[b * 4:(b + 1) * 4], W=[pG])
                P.act(sgt[:], pG[:], AF.Sigmoid, R=[pG], W=[sgt])
                for half in range(2):
                    for kc in range(nk):
                        P.mm(pB[:, half * 512:(half + 1) * 512], of(kc), Wb[:, kc, half * 512:(half + 1) * 512],
                             start=(kc == 0), stop=(kc == nk - 1), R=[ot, Wb], W=[pB])
                if b == 0:
                    P.tt(ym[:], pB[:], sgt[:], ALU.mult, R=[pB, sgt], W=[ym])
                else:
                    P.tt(tmp[:], pB[:], sgt[:], ALU.mult, R=[pB, sgt], W=[tmp])
                    if b == 1:
                        P.tt(ym[:], ym[:], tmp[:], ALU.add, R=[ym, tmp], W=[ym])
                    else:
                        P.tt(ymb[:], ym[:], tmp[:], ALU.add, R=[ym, tmp], W=[ymb])
            for kc in range(8):
                P.tr(pT[:, kc * 128:(kc + 1) * 128], ymb[:, kc * 128:(kc + 1) * 128], identB, R=[ymb, cstb], W=[pT])
            yT = ymT[i % 2]
            P.cp(yT[:], pT[:], R=[pT], W=[yT], E="act")
            P.dma(ymTd[t], yT[:], R=[yT], W=[P.dres("ymT", t)])
        P.barrier()
    with ExitStack() as ph:
        P.stack = ph
        Wo = P.sb([128, 8, 1024], BF16, nres=4, name="Wo")
        _loadw(P, Wo, Wd["w_out"][l].rearrange("(kc p) n -> p kc n", p=128), 1024)
        pW = P.ps([128, 1024], name="pWm")
        pp = [P.ps([128, 512], name="pp%d" % i) for i in range(2)]
        gbc = P.sb([128, 2, 1024], F32, name="gbc")
        _gate_bcast(P, cst, modrow_d, 2048, s, pW, gbc)
        lng = P.sb([128, 1024], F32, name="lng")
        lnb = P.sb([128, 1024], F32, name="lnb")
        P.dma(lng[:], Wd["ln1_g"][l:l + 1, :].broadcast_to([128, 1024]), W=[lng])
        P.dma(lnb[:], Wd["ln1_b"][l:l + 1, :].broadcast_to([128, 1024]), W=[lnb])
        yT = [P.sb([128, 1024], BF16, name="yTb%d" % i) for i in range(2)]
        xt = [P.sb([128, 1024], F32, name="xtb%d" % i) for i in range(2)]
        z = [P.sb([128, 1024], F32, name="zb%d" % i) for i in range(2)]
        st = P.sb([128, 12], F32, name="st")
        mv = P.sb([128, 4], F32, name="mv")

        def ld_mb(i):
            t = tiles[i]
            P.dma(yT[i % 2][:], ymTd[t], R=[P.dres("ymT", t)], W=[yT[i % 2]])
            src, sres = xsrc_fn(t)
            P.dma(xt[i % 2][:], src, R=[sres], W=[xt[i % 2]])

        for i, t in enumerate(tiles):
            tc = slice(t * 128, (t + 1) * 128)
            r = 2 if t < TCTX else s
            ri = 1 if t < TCTX else 0
            yT_, xt_, z_ = yT[i % 2], xt[i % 2], z[i % 2]
            if i == 0:
                ld_mb(0)
            if i + 1 < len(tiles):
                ld_mb(i + 1)
            for half in range(2):
                for kc in range(8):
                    P.mm(pW[:, half * 512:(half + 1) * 512], yT_[:, kc * 128:(kc + 1) * 128], Wo[:, kc, half * 512:(half + 1) * 512],
                         start=(kc == 0), stop=(kc == 7), R=[yT_, Wo], W=[pW])
            P.tt(z_[:], pW[:], gbc[:, ri, :], ALU.mult, R=[pW, gbc], W=[z_])
            P.stt(z_[:], xt_[:], float(DN_ALPHA), z_[:], ALU.mult, ALU.add, R=[xt_, z_], W=[z_])
            _layernorm(P, z_[:], z_[:], z_, lng, lnb, st, mv)
            P.dma(x1s[t], z_[:], R=[z_], W=[P.dres("x1s", t)])
            for half in range(2):
                pb = pp[half]
                for q in range(4):
                    kc = half * 4 + q
                    P.tr(pb[:, q * 128:(q + 1) * 128], z_[:, kc * 128:(kc + 1) * 128], identF, R=[z_, cst], W=[pb])
                for q in range(4):
                    kc = half * 4 + q
                    o = hT[:, kc, tc]
                    i_ = pb[:, q * 128:(q + 1) * 128]
                    sc = modB[:, 32 + kc, r:r + 1]
                    sh = modA[:, 24 + kc, r:r + 1]
                    if q % 2 == 0:
                        P.ts(o, i_, sc, sh, ALU.mult, ALU.add, R=[pb, modA, modB], W=[hT.res[t]])
                    else:
                        P.act(o, i_, AF.Identity, R=[pb, modA, modB], W=[hT.res[t]], bias=sh, scale=sc)
        P.barrier()


def phase_ffn(P, nc, l, s, last, Wd, cst, cstb, hT, modrow_d, x1s, xres, out_d, ffn_act):
    blocks = ([] if last else [(0, TCTX)]) + [(TCTX + 4 * i, 4) for i in range(4)]
    with ExitStack() as ph0:
        P.stack = ph0
        W2 = P.sb([128, 22, 1024], BF16, nres=11, name="W2")
        with ExitStack() as ph:
            P.stack = ph
            W1 = P.sb([128, 8, 2 * FFH], BF16, nres=44, name="W1")
            _loadw_cols(P, W1, Wd["w_ffn_in"][l].rearrange("(kc p) n -> p kc n", p=128), 2 * FFH, 256,
                        order=[x for j in range(11) for x in (j, 11 + j)])
            _loadw(P, W2, Wd["w_ffn_out"][l].rearrange("(j p) n -> p j n", p=128), 1024)
            pGt = [P.ps([128, 512], name="pGt%d" % i) for i in range(2)]
            pUp = [P.ps([128, 512], name="pUp%d" % i) for i in range(2)]
            sgu = [P.sb([128, 512], F32, name="sgu%d" % i) for i in range(2)]
            aT = P.sb([128, 22, 512], BF16, name="actT")
            for bi, (t0, ntile) in enumerate(blocks):
                N = ntile * 128
                cols = slice(t0 * 128, t0 * 128 + N)
                tr_ = [hT.res[t] for t in range(t0, t0 + ntile)]
                for j in range(22):
                    pg, pu, sg_ = pGt[j % 2], pUp[j % 2], sgu[j % 2]
                    for kc in range(8):
                        P.mm(pg[:, 0:N], W1[:, kc, j * 128:(j + 1) * 128], hT[:, kc, cols], start=(kc == 0), stop=(kc == 7),
                             R=_cr(W1, j * 128, (j + 1) * 128) + tr_, W=[pg])
                    for kc in range(8):
                        P.mm(pu[:, 0:N], W1[:, kc, FFH + j * 128:FFH + (j + 1) * 128], hT[:, kc, cols], start=(kc == 0), stop=(kc == 7),
                             R=_cr(W1, FFH + j * 128, FFH + (j + 1) * 128) + tr_, W=[pu])
                    P.act(sg_[:, 0:N], pg[:, 0:N], AF.Silu, R=[pg], W=[sg_])
                    P.tt(aT[:, j, 0:N], sg_[:, 0:N], pu[:, 0:N], ALU.mult, R=[sg_, pu], W=[aT])
                for ti in range(ntile):
                    t = t0 + ti
                    P.dma(ffn_act[t].rearrange("p (j k) -> p j k", j=22), aT[:, :, ti * 128:(ti + 1) * 128], R=[aT], W=[P.dres("ffn_act", t)])
            P.barrier()
        with ExitStack() as ph:
            P.stack = ph
            pF = [P.ps([128, 1024], name="pF%d" % i) for i in range(2)]
            gbc = P.sb([128, 2, 1024], F32, name="gbc2")
            _gate_bcast(P, cst, modrow_d, 5120, s, pF[0], gbc)
            lng = P.sb([128, 1024], F32, name="lng2")
            lnb = P.sb([128, 1024], F32, name="lnb2")
            P.dma(lng[:], Wd["ln2_g"][l:l + 1, :].broadcast_to([128, 1024]), W=[lng])
            P.dma(lnb[:], Wd["ln2_b"][l:l + 1, :].broadcast_to([128, 1024]), W=[lnb])
            aTl = [P.sb([128, 22, 128], BF16, name="aTl%d" % i) for i in range(2)]
            xt = [P.sb([128, 1024], F32, name="x1l%d" % i) for i in range(2)]
            z = [P.sb([128, 1024], F32, name="z2%d" % i) for i in range(2)]
            st = P.sb([128, 12], F32, name="st2")
            mv = P.sb([128, 4], F32, name="mv2")
            tiles = [t0 + ti for (t0, ntile) in blocks for ti in range(ntile)]
            def ld_b(i):
                t = tiles[i]
                P.dma(aTl[i % 2][:], ffn_act[t].rearrange("p (j k) -> p j k", j=22), R=[P.dres("ffn_act", t)], W=[aTl[i % 2]])
                P.dma(xt[i % 2][:], x1s[t], R=[P.dres("x1s", t)], W=[xt[i % 2]])

            ld_b(0)
            for i, t in enumerate(tiles):
                ri = 1 if t < TCTX else 0
                a_, xt_, z_, pF_ = aTl[i % 2], xt[i % 2], z[i % 2], pF[i % 2]
                if i + 1 < len(tiles):
                    ld_b(i + 1)
                for j in range(22):
                    for half in range(2):
                        P.mm(pF_[:, half * 512:(half + 1) * 512], a_[:, j, :], W2[:, j, half * 512:(half + 1) * 512],
                             start=(j == 0), stop=(j == 21), R=[a_, W2], W=[pF_])
                P.tt(z_[:], pF_[:], gbc[:, ri, :], ALU.mult, R=[pF_, gbc], W=[z_])
                P.stt(z_[:], xt_[:], float(DN_ALPHA), z_[:], ALU.mult, ALU.add, R=[xt_, z_], W=[z_])
                _layernorm(P, z_[:], z_[:], z_, lng, lnb, st, mv)
                if last:
                    P.dma(out_d[s, (t - TCTX) * 128:(t - TCTX + 1) * 128, :], z_[:], R=[z_], W=[P.dres("out", (s, t))])
                else:
                    P.dma(xres[s, t], z_[:], R=[z_], W=[P.dres("xres", (s, t))])
            P.barrier()


def kernel(**inputs):
    inp = {k: np.asarray(v) for k, v in inputs.items()}
    nc = build(DEPTH)
    consts = make_consts()
    rope = make_rope()
    maps = []
    for core in range(8):
        b0 = 2 * core
        c3 = np.stack([inp["c"][b0], inp["c"][b0 + 1], inp["c_ctx"]])
        c3T = np.ascontiguousarray(c3.reshape(3, 8, 128).transpose(2, 1, 0))
        m = {"x": np.ascontiguousarray(inp["x"][b0:b0 + 2]), "ctx": np.ascontiguousarray(inp["ctx"][b0:b0 + 2]),
             "c3T": c3T, "consts": consts, "rope": rope, "hg_lb": inp["hg_lb"]}
        for n, _ in WSHAPES:
            m[n] = inp[n]
        maps.append(m)
    res = run_bass_kernel_spmd(nc, maps, core_ids=list(range(8)))
    out = np.concatenate([np.asarray(r["out"]) for r in res.results], axis=0)
    return out.astype(np.float32)
```
